# Optimizing a Trainium2 kernel written in Bass

```python
import math
import jax, jax.numpy as jnp
from jax import lax
import numpy as np

D_MODEL = 1024
BATCH = 8
SEQ = 4096
DEPTH = 2

N_EVEN = (DEPTH + 1) // 2
N_ODD = DEPTH // 2
N_SUB = 3
RMS_EPS = 1e-6
FFN_RES = 0.5
D_FF = ((8 * D_MODEL // 3 + 127) // 128) * 128
POOL_WINDOWS = (2, 4, 8, 16)
POOL_GROUPS = len(POOL_WINDOWS)
POOL_DIM = D_MODEL // 2
POOL_GROUP_DIM = POOL_DIM // POOL_GROUPS
D_SSM = D_MODEL
SSD_HEADDIM = 64
SSD_HEADS = D_SSM // SSD_HEADDIM
SSD_GROUPS = 2
D_STATE = 128
CONV_K = 4
CONV_DIM = D_SSM + 2 * SSD_GROUPS * D_STATE
SSD_CHUNK = 128
IN_AB = POOL_DIM + D_SSM + CONV_DIM + SSD_HEADS
OUT_AB = POOL_DIM + D_SSM
MLA_HEADS = 16
QK_NOPE = 64
QK_ROPE = 32
V_DIM = 64
Q_LORA = 768
KV_LORA = 256
IN_MLA = Q_LORA + KV_LORA + QK_ROPE
ROPE_THETA = 10000.0
Q_BLOCK = 128

kernel_name = "hybrid_pool_ssd_mla_macaron_adaln"


def rmsnorm(x, g):
    xf = x.astype(jnp.float32)
    y = xf * lax.rsqrt(jnp.mean(xf * xf, axis=-1, keepdims=True) + RMS_EPS)
    return (y * g.astype(jnp.float32)).astype(x.dtype)


def modulate(h, shift, scale):
    return h * (1 + scale[:, None, :]) + shift[:, None, :]


def swiglu(h, w13, w2):
    a, b = jnp.split(h @ w13, 2, axis=-1)
    return (jax.nn.silu(a) * b) @ w2


def multiscale_pool(u, w, scale):
    bsz, L, _ = u.shape
    uf = u.astype(jnp.float32)
    cs_pad = jnp.pad(jnp.cumsum(uf, axis=1), ((0, 0), (1, 0), (0, 0)))
    t = jnp.arange(L, dtype=jnp.float32)
    outs = []
    for gi, win in enumerate(POOL_WINDOWS):
        sl = slice(gi * POOL_GROUP_DIM, (gi + 1) * POOL_GROUP_DIM)
        c_g = cs_pad[:, :, sl]
        lag = jnp.pad(c_g[:, :L + 1 - win], ((0, 0), (win - 1, 0), (0, 0)))
        count = jnp.minimum(t + 1.0, float(win))[None, :, None]
        outs.append((c_g[:, 1:] - lag) / count - uf[:, :, sl])
    d = jnp.stack(outs, axis=2)
    y = jnp.einsum("blgc,gcd->blgd", d, w.astype(jnp.float32)).reshape(bsz, L, POOL_DIM)
    return (y * scale.astype(jnp.float32)).astype(u.dtype)


def causal_depthwise_conv(u, w, b):
    y = lax.conv_general_dilated(u, w[:, None, :], window_strides=(1,),
                                 padding=[(CONV_K - 1, 0)],
                                 dimension_numbers=("NWC", "WIO", "NWC"),
                                 feature_group_count=u.shape[-1])
    return y + b


def ssd_chunked(xh, dt, a, bm, cm):
    bsz, L, H, P = xh.shape
    nc = L // SSD_CHUNK
    r = H // SSD_GROUPS
    x = (xh * dt[..., None]).reshape(bsz, nc, SSD_CHUNK, SSD_GROUPS, r, P)
    adt = (dt * a).reshape(bsz, nc, SSD_CHUNK, SSD_GROUPS, r).transpose(0, 3, 4, 1, 2)
    bc = bm.reshape(bsz, nc, SSD_CHUNK, SSD_GROUPS, D_STATE)
    cc = cm.reshape(bsz, nc, SSD_CHUNK, SSD_GROUPS, D_STATE)
    a_cs = jnp.cumsum(adt, axis=-1)
    causal = jnp.tril(jnp.ones((SSD_CHUNK, SSD_CHUNK), dtype=bool))
    decay = jnp.exp(jnp.where(causal, a_cs[..., :, None] - a_cs[..., None, :], -jnp.inf))
    cb = jnp.einsum("bclgn,bcsgn->bgcls", cc, bc)
    y_diag = jnp.einsum("bgcls,bgrcls,bcsgrp->bclgrp", cb, decay, x)
    decay_to_end = jnp.exp(a_cs[..., -1:] - a_cs)
    states = jnp.einsum("bcsgn,bgrcs,bcsgrp->cbgrpn", bc, decay_to_end, x)
    chunk_decay = jnp.exp(a_cs[..., -1]).transpose(3, 0, 1, 2)

    def step(h, inp):
        s_c, d_c = inp
        return d_c[..., None, None] * h + s_c, h

    h0 = jnp.zeros(states.shape[1:], states.dtype)
    _, prev = lax.scan(step, h0, (states, chunk_decay))
    y_off = jnp.einsum("bclgn,cbgrpn,bgrcl->bclgrp", cc, prev, jnp.exp(a_cs))
    return (y_diag + y_off).reshape(bsz, L, H, P)


def pool_ssd_mixer(h, w_in, pool_w, pool_scale, conv_w, conv_b, dt_bias, a_log, d_skip, norm_g, w_out):
    bsz, L, _ = h.shape
    proj = h @ w_in
    u_pool, z, xbc, dt_raw = jnp.split(
        proj, [POOL_DIM, POOL_DIM + D_SSM, POOL_DIM + D_SSM + CONV_DIM], axis=-1)
    y_pool = multiscale_pool(u_pool, pool_w, pool_scale)
    xbc = jax.nn.silu(causal_depthwise_conv(xbc, conv_w, conv_b))
    xs, bm, cm = jnp.split(xbc, [D_SSM, D_SSM + SSD_GROUPS * D_STATE], axis=-1)
    f32 = jnp.float32
    xh = xs.astype(f32).reshape(bsz, L, SSD_HEADS, SSD_HEADDIM)
    dt = jax.nn.softplus(dt_raw.astype(f32) + dt_bias.astype(f32))
    a = -jnp.exp(a_log.astype(f32))
    y = ssd_chunked(xh, dt, a,
                    bm.astype(f32).reshape(bsz, L, SSD_GROUPS, D_STATE),
                    cm.astype(f32).reshape(bsz, L, SSD_GROUPS, D_STATE))
    y = y + d_skip.astype(f32)[:, None] * xh
    y = y.reshape(bsz, L, D_SSM) * jax.nn.silu(z.astype(f32))
    yg = y.reshape(bsz, L, SSD_GROUPS, D_SSM // SSD_GROUPS)
    yg = yg * lax.rsqrt(jnp.mean(yg * yg, axis=-1, keepdims=True) + RMS_EPS)
    y = (yg.reshape(bsz, L, D_SSM) * norm_g.astype(f32)).astype(h.dtype)
    return jnp.concatenate([y_pool, y], axis=-1) @ w_out


def rope_tables(positions):
    inv_freq = ROPE_THETA ** (-jnp.arange(0, QK_ROPE, 2, dtype=jnp.float32) / QK_ROPE)
    ang = positions.astype(jnp.float32)[..., None] * inv_freq
    return jnp.cos(ang), jnp.sin(ang)


def apply_rope(x, cos, sin):
    xf = x.astype(jnp.float32)
    x1, x2 = jnp.split(xf, 2, axis=-1)
    return jnp.concatenate([x1 * cos - x2 * sin, x2 * cos + x1 * sin], axis=-1).astype(x.dtype)


def mla_mixer(h, positions, w_in, q_norm_g, w_uq, kv_norm_g, w_ukv, w_o):
    bsz, L, _ = h.shape
    q_a, kv_a, k_rope = jnp.split(h @ w_in, [Q_LORA, Q_LORA + KV_LORA], axis=-1)
    q = (rmsnorm(q_a, q_norm_g) @ w_uq).reshape(bsz, L, MLA_HEADS, QK_NOPE + QK_ROPE)
    q_nope, q_rope = q[..., :QK_NOPE], q[..., QK_NOPE:]
    kv = (rmsnorm(kv_a, kv_norm_g) @ w_ukv).reshape(bsz, L, MLA_HEADS, QK_NOPE + V_DIM)
    k_nope, v = kv[..., :QK_NOPE], kv[..., QK_NOPE:]
    cos, sin = rope_tables(positions)
    q_rope = apply_rope(q_rope, cos[:, :, None], sin[:, :, None])
    k_rope = apply_rope(k_rope, cos, sin)
    scale = 1.0 / math.sqrt(QK_NOPE + QK_ROPE)
    nb = L // Q_BLOCK
    k_idx = jnp.arange(L)

    def blocks(t):
        return jnp.moveaxis(t.reshape(bsz, nb, Q_BLOCK, *t.shape[2:]), 1, 0)

    def attend(args):
        qn, qr, start = args
        s = (jnp.einsum("bqhd,bkhd->bhqk", qn, k_nope, preferred_element_type=jnp.float32)
             + jnp.einsum("bqhd,bkd->bhqk", qr, k_rope, preferred_element_type=jnp.float32)) * scale
        q_idx = start + jnp.arange(Q_BLOCK)
        s = jnp.where(k_idx[None, :] <= q_idx[:, None], s, -jnp.inf)
        p = jax.nn.softmax(s, axis=-1).astype(v.dtype)
        return jnp.einsum("bhqk,bkhd->bqhd", p, v)

    starts = jnp.arange(nb, dtype=jnp.int32) * Q_BLOCK
    o = lax.map(attend, (blocks(q_nope), blocks(q_rope), starts))
    o = jnp.moveaxis(o, 0, 1).reshape(bsz, L, MLA_HEADS * V_DIM)
    return o @ w_o


def setup_inputs(seed: int = 0) -> dict:
    key = jax.random.key(seed)
    ks = jax.random.split(key, 32)
    f32 = jnp.float32

    def nrm(k, shape, std):
        return jax.random.normal(k, shape, f32) * std

    def gain(k, shape):
        return 1.0 + 0.05 * jax.random.normal(k, shape, f32)

    x = jax.random.normal(ks[0], (BATCH, SEQ, D_MODEL), f32)
    c = jax.random.normal(ks[1], (BATCH, D_MODEL), f32)
    positions = (jnp.arange(SEQ, dtype=jnp.int32)[None, :]
                 + jax.random.randint(ks[2], (BATCH, 1), 0, 1024, dtype=jnp.int32))
    mod_w = nrm(ks[3], (DEPTH, D_MODEL, 3 * N_SUB * D_MODEL), 0.5 * D_MODEL ** -0.5)
    mod_b = nrm(ks[4], (DEPTH, 3 * N_SUB * D_MODEL), 0.02)
    norm_g = gain(ks[5], (DEPTH, N_SUB, D_MODEL))
    ffn_w13 = nrm(ks[6], (DEPTH, 2, D_MODEL, 2 * D_FF), D_MODEL ** -0.5)
    ffn_w2 = nrm(ks[7], (DEPTH, 2, D_FF, D_MODEL), D_FF ** -0.5)
    ab_w_in = nrm(ks[8], (N_EVEN, D_MODEL, IN_AB), D_MODEL ** -0.5)
    pool_w = nrm(ks[9], (N_EVEN, POOL_GROUPS, POOL_GROUP_DIM, POOL_GROUP_DIM), POOL_GROUP_DIM ** -0.5)
    pool_scale = gain(ks[10], (N_EVEN, POOL_DIM))
    ssd_conv_w = nrm(ks[11], (N_EVEN, CONV_K, CONV_DIM), CONV_K ** -0.5)
    ssd_conv_b = nrm(ks[12], (N_EVEN, CONV_DIM), 0.02)
    dt0 = jnp.exp(jax.random.uniform(ks[13], (N_EVEN, SSD_HEADS), f32,
                                     minval=math.log(1e-3), maxval=math.log(1e-1)))
    ssd_dt_bias = dt0 + jnp.log(-jnp.expm1(-dt0))
    ssd_a_log = jnp.log(jax.random.uniform(ks[14], (N_EVEN, SSD_HEADS), f32, minval=1.0, maxval=16.0))
    ssd_d = gain(ks[15], (N_EVEN, SSD_HEADS))
    ssd_norm_g = gain(ks[16], (N_EVEN, D_SSM))
    ab_w_out = nrm(ks[17], (N_EVEN, OUT_AB, D_MODEL), OUT_AB ** -0.5)
    mla_w_in = nrm(ks[18], (N_ODD, D_MODEL, IN_MLA), D_MODEL ** -0.5)
    mla_q_norm_g = gain(ks[19], (N_ODD, Q_LORA))
    mla_w_uq = nrm(ks[20], (N_ODD, Q_LORA, MLA_HEADS * (QK_NOPE + QK_ROPE)), Q_LORA ** -0.5)
    mla_kv_norm_g = gain(ks[21], (N_ODD, KV_LORA))
    mla_w_ukv = nrm(ks[22], (N_ODD, KV_LORA, MLA_HEADS * (QK_NOPE + V_DIM)), KV_LORA ** -0.5)
    mla_w_o = nrm(ks[23], (N_ODD, MLA_HEADS * V_DIM, D_MODEL), (MLA_HEADS * V_DIM) ** -0.5)
    final_norm_g = gain(ks[24], (D_MODEL,))
    return {"x": x, "c": c, "positions": positions, "mod_w": mod_w, "mod_b": mod_b,
            "norm_g": norm_g, "ffn_w13": ffn_w13, "ffn_w2": ffn_w2, "ab_w_in": ab_w_in,
            "pool_w": pool_w, "pool_scale": pool_scale, "ssd_conv_w": ssd_conv_w,
            "ssd_conv_b": ssd_conv_b, "ssd_dt_bias": ssd_dt_bias, "ssd_a_log": ssd_a_log,
            "ssd_d": ssd_d, "ssd_norm_g": ssd_norm_g, "ab_w_out": ab_w_out,
            "mla_w_in": mla_w_in, "mla_q_norm_g": mla_q_norm_g, "mla_w_uq": mla_w_uq,
            "mla_kv_norm_g": mla_kv_norm_g, "mla_w_ukv": mla_w_ukv, "mla_w_o": mla_w_o,
            "final_norm_g": final_norm_g}


def reference(x, c, positions, mod_w, mod_b, norm_g, ffn_w13, ffn_w2, ab_w_in, pool_w,
              pool_scale, ssd_conv_w, ssd_conv_b, ssd_dt_bias, ssd_a_log, ssd_d, ssd_norm_g,
              ab_w_out, mla_w_in, mla_q_norm_g, mla_w_uq, mla_kv_norm_g, mla_w_ukv, mla_w_o,
              final_norm_g):
    c_act = jax.nn.silu(c)
    for i in range(DEPTH):
        mod = c_act @ mod_w[i] + mod_b[i]
        sh1, sc1, g1, sh2, sc2, g2, sh3, sc3, g3 = jnp.split(mod, 3 * N_SUB, axis=-1)
        h = modulate(rmsnorm(x, norm_g[i, 0]), sh1, sc1)
        x = x + FFN_RES * g1[:, None, :] * swiglu(h, ffn_w13[i, 0], ffn_w2[i, 0])
        h = modulate(rmsnorm(x, norm_g[i, 1]), sh2, sc2)
        j = i // 2
        if i % 2 == 0:
            m = pool_ssd_mixer(h, ab_w_in[j], pool_w[j], pool_scale[j], ssd_conv_w[j],
                               ssd_conv_b[j], ssd_dt_bias[j], ssd_a_log[j], ssd_d[j],
                               ssd_norm_g[j], ab_w_out[j])
        else:
            m = mla_mixer(h, positions, mla_w_in[j], mla_q_norm_g[j], mla_w_uq[j],
                          mla_kv_norm_g[j], mla_w_ukv[j], mla_w_o[j])
        x = x + g2[:, None, :] * m
        h = modulate(rmsnorm(x, norm_g[i, 2]), sh3, sc3)
        x = x + FFN_RES * g3[:, None, :] * swiglu(h, ffn_w13[i, 1], ffn_w2[i, 1])
    return rmsnorm(x, final_norm_g)
```

```python
import numpy as np
import concourse.bass as bass
import concourse.mybir as mybir
from concourse.bass_utils import run_bass_kernel_spmd

F32 = mybir.dt.float32
BF16 = mybir.dt.bfloat16
I32 = mybir.dt.int32
AF = mybir.ActivationFunctionType
ALU = mybir.AluOpType
AX = mybir.AxisListType

D = 1024
KC = 8
DFF = 2816
NF = 22
EPS = 1e-6
SEQ_FULL = 4096
IN_AB = 3088
IN_MLA = 1056

COMPUTE = ("pe", "act", "dve", "pool")
INV_FREQ = (np.float32(10000.0) ** (-np.arange(0, 32, 2, dtype=np.float32) / np.float32(32))).astype(np.float32)


class Op:
    __slots__ = ("q", "fn", "reads", "writes", "dma", "semkey", "deps", "signal", "idx", "need_sig")


class Sched:
    def __init__(self, nc):
        self.nc = nc
        self.ops = []
        self.last_w = {}
        self.readers = {}
        self.sems = {}
        self.dma_count = {}
        self.bar_deps = []

    def op(self, q, fn, reads=(), writes=(), dma=False, semkey=None):
        o = Op()
        o.q, o.fn, o.reads, o.writes, o.dma = q, fn, tuple(reads), tuple(writes), dma
        o.idx = len(self.ops)
        o.need_sig = False
        deps = set(self.bar_deps)
        for r in o.reads:
            w = self.last_w.get(r)
            if w is not None:
                deps.add(w)
        for w_ in o.writes:
            w = self.last_w.get(w_)
            if w is not None:
                deps.add(w)
            for rd in self.readers.get(w_, ()):
                deps.add(rd)
        deps.discard(o.idx)
        o.deps = deps
        if dma:
            if semkey is None:
                semkey = ("dma",) + tuple(o.writes[:1])
            o.semkey = semkey
            n = self.dma_count.get(semkey, 0) + 1
            self.dma_count[semkey] = n
            o.signal = (semkey, 16 * n)
        else:
            o.semkey = q
            o.signal = None
        for r in o.reads:
            self.readers.setdefault(r, []).append(o.idx)
        for w_ in o.writes:
            self.last_w[w_] = o.idx
            self.readers[w_] = []
        self.ops.append(o)
        return o

    def barrier(self):
        lastq = {}
        for o in self.ops:
            if o.dma:
                lastq[("d", o.semkey)] = o.idx
            elif o.fn is not None:
                lastq[("c", o.q)] = o.idx
        self.bar_deps = list(lastq.values())
        self.last_w = {}
        self.readers = {}

    def emit(self):
        nc = self.nc
        ops = self.ops
        for o in ops:
            for d in o.deps:
                p = ops[d]
                if p.q == "pe" and o.q == "pe" and not p.dma:
                    continue
                p.need_sig = True
        cnt = {q: 0 for q in COMPUTE}
        for o in ops:
            if not o.dma and o.fn is not None and o.need_sig:
                cnt[o.q] += 1
                o.signal = (o.q, cnt[o.q])
        def sem(key):
            s = self.sems.get(key)
            if s is None:
                s = nc.alloc_semaphore("s%d" % len(self.sems))
                self.sems[key] = s
            return s
        queues = {}
        for o in ops:
            queues.setdefault(o.q, []).append(o)
        dma_before = {}
        run = {}
        for o in ops:
            dma_before[o.idx] = dict(run) if False else None
        dma_positions = {}
        for o in ops:
            if o.dma:
                dma_positions.setdefault(o.semkey, []).append(o.idx)
        import bisect

        def emit_queue(qname, eng):
            waited = {}
            for o in queues.get(qname, []):
                need = {}
                for d in o.deps:
                    p = ops[d]
                    if p.dma:
                        pos = dma_positions[p.semkey]
                        n = bisect.bisect_left(pos, o.idx)
                        key, val = p.semkey, 16 * n
                    else:
                        if p.fn is None:
                            continue
                        if p.q == "pe" and o.q == "pe" and not o.dma:
                            continue
                        key, val = p.signal
                    if need.get(key, 0) < val:
                        need[key] = val
                for key, val in need.items():
                    if waited.get(key, 0) >= val:
                        continue
                    waited[key] = val
                    eng.wait_ge(sem(key), val)
                if o.fn is None:
                    continue
                ins = o.fn(eng)
                if o.dma:
                    ins.then_inc(sem(o.semkey), 16)
                elif o.need_sig:
                    ins.then_inc(sem(o.q), 1)

        with nc.Block() as block:
            @block.tensor
            def _(e):
                emit_queue("pe", e)

            @block.scalar
            def _(e):
                emit_queue("act", e)

            @block.vector
            def _(e):
                emit_queue("dve", e)

            @block.gpsimd
            def _(e):
                emit_queue("pool", e)

            @block.sync
            def _(e):
                emit_queue("sp", e)


class Builder:
    def __init__(self, seq, subs=None, debug_out=None):
        self.seq = seq
        self.NT = seq // 512
        self.NB = seq // 128
        self.subs = subs
        nc = bass.Bass("TRN2", target_bir_lowering=False)
        self.nc = nc
        self.S = Sched(nc)
        self.sb_off = 16640
        self.sb_top = 229376
        self.nalloc = 0

    def sb(self, name, shape, dtype):
        esz = 4 if dtype in (F32, I32) else 2
        n = 1
        for s in shape[1:]:
            n *= s
        nbytes = (n * esz + 63) // 64 * 64
        off = self.sb_off
        assert off + nbytes <= self.sb_top, "SBUF overflow at %s: need %d have %d" % (name, nbytes, self.sb_top - off)
        self.sb_off += nbytes
        self.nalloc += 1
        return self.nc.alloc_sbuf_tensor_at("%s_%d" % (name, self.nalloc), list(shape), dtype, offset=off)

    def mark(self):
        return self.sb_off

    def release(self, m):
        self.sb_off = m

    def dram(self, name, shape, dtype, kind="Internal"):
        return self.nc.dram_tensor(name, list(shape), dtype, kind=kind).ap()

    def mm(self, out, lhsT, rhs, start, stop, reads, writes):
        self.S.op("pe", lambda e: e.matmul(out, lhsT, rhs, start=start, stop=stop), reads, writes)

    def tr(self, out, in_, ident, reads, writes):
        self.S.op("pe", lambda e: e.transpose(out, in_, ident), reads, writes)

    def act(self, out, in_, func, reads, writes, bias=None, scale=None, accum_out=None):
        kw = {}
        if bias is not None:
            kw["bias"] = bias
        if scale is not None:
            kw["scale"] = scale
        if accum_out is not None:
            kw["accum_out"] = accum_out
        self.S.op("act", lambda e: e.activation(out=out, in_=in_, func=func, **kw), reads, writes)

    def vec(self, q, method, reads, writes, *args, **kw):
        self.S.op(q, lambda e: getattr(e, method)(*args, **kw), reads, writes)

    def dma(self, q, out, in_, reads, writes, semkey=None):
        self.S.op(q, lambda e: e.dma_start(out=out, in_=in_), reads, writes, dma=True, semkey=semkey)

    def build(self):
        nc = self.nc
        seq, NT, NB = self.seq, self.NT, self.NB
        S = self.S
        x_in = self.dram("x", [seq, D], F32, "ExternalInput")
        c_in = self.dram("c", [128, KC], F32, "ExternalInput")
        pos_in = self.dram("pos", [1, seq], I32, "ExternalInput")
        mod_w = self.dram("mod_w", [2, D, 9 * D], F32, "ExternalInput")
        mod_b = self.dram("mod_b", [2, 9 * D], F32, "ExternalInput")
        ngP_in = self.dram("ngP", [128, 2 * 3 * KC], F32, "ExternalInput")
        w13_in = self.dram("ffn_w13", [4, D, 2 * DFF], F32, "ExternalInput")
        w2_in = self.dram("ffn_w2", [4, DFF, D], F32, "ExternalInput")
        abin_in = self.dram("ab_w_in", [D, IN_AB], F32, "ExternalInput")
        poolw_in = self.dram("pool_w", [512, 128], F32, "ExternalInput")
        abP_in = self.dram("abP", [128, 80], F32, "ExternalInput")
        abR_in = self.dram("abR", [1, 64], F32, "ExternalInput")
        about_in = self.dram("ab_w_out", [1536, D], F32, "ExternalInput")
        mlain_in = self.dram("mla_w_in", [D, IN_MLA], F32, "ExternalInput")
        wuq_in = self.dram("mla_w_uq", [768, 1536], F32, "ExternalInput")
        wukv_in = self.dram("mla_w_ukv", [256, 2048], F32, "ExternalInput")
        wo_in = self.dram("mla_w_o", [D, D], F32, "ExternalInput")
        mlaP_in = self.dram("mlaP", [128, 16], F32, "ExternalInput")
        fin_in = self.dram("final_g", [1, D], F32, "ExternalInput")
        cst_in = self.dram("consts", [128, 512], F32, "ExternalInput")
        sel_in = self.dram("sel", [16, 2048], F32, "ExternalInput")
        cmask_in = self.dram("cmask", [128, 512], F32, "ExternalInput")
        self.xs_d = self.dram("xs_d", [seq, D], F32)
        self.ins = dict(abin=abin_in, poolw=poolw_in, abP=abP_in, abR=abR_in, about=about_in, mlain=mlain_in,
                        wuq=wuq_in, wukv=wukv_in, wo=wo_in, mlaP=mlaP_in, sel=sel_in, cmask=cmask_in, pos=pos_in)
        y_out = self.dram("y", [seq, D], F32, "ExternalOutput")
        self.x_in, self.y_out = x_in, y_out

        w13_s = self.dram("w13_s", [4, D, 2 * DFF], BF16)
        w2_s = self.dram("w2_s", [4, DFF, D], BF16)
        mod_d = self.dram("mod_d", [2, 9 * D], F32)
        self.w13_s, self.w2_s, self.mod_d = w13_s, w2_s, mod_d

        self.ident = self.sb("ident", [128, 128], BF16)
        self.cst = self.sb("cst", [128, 512], F32)
        self.modP = self.sb("modP", [128, 2, 9, KC], F32)
        self.ngP = self.sb("ngP", [128, 2, 3, KC], F32)
        self.aP = self.sb("aP", [128, 2, 3, KC], F32)
        self.stat = self.sb("stat", [128, 64], F32)
        self.x_off = self.sb_off
        self.xres = self.sb("xres", [128, max(NB, 32), D], F32)
        self.ps = [nc.alloc_psum_tensor("ps%d" % i, [128, 512], F32) for i in range(8)]

        self.dma("sp", self.cst[:], cst_in, ["cst_in"], ["cst"])
        self.vec("dve", "tensor_copy", ["cst"], ["ident"], out=self.ident[:], in_=self.cst[:, 0:128])
        self.dma("sp", self.ngP[:].rearrange("p a b c -> p (a b c)"), ngP_in, [], ["ngP"])
        for j in range(NB):
            self.dma("sp", self.xres[:, j, :], x_in[j * 128:(j + 1) * 128, :], [], [("x", j)], semkey=("xload", j % 4))
        self.modulation(c_in, mod_w, mod_b)
        for i in range(4):
            self.precast(w13_s[i], w13_in[i], D, 2 * DFF, ("w13s", i))
            self.precast(w2_s[i], w2_in[i], DFF, D, ("w2s", i))

        def want(l, s):
            return self.subs is None or (l, s) in self.subs

        for l in range(2):
            if want(l, 0):
                self.ffn(l, 0)
            if want(l, 1):
                self.spill_x()
                if l == 0:
                    self.mixer_ab()
                else:
                    self.mixer_mla()
                self.reload_x()
            if want(l, 2):
                self.ffn(l, 1)
        self.final_norm(fin_in)
        S.op("sp", None, reads=[("yout", j) for j in range(NB)])
        S.emit()
        return nc

    def precast(self, dst, src, rows, cols, res):
        a = rows // 128
        half = max(1, a // 2)
        for h0 in range(0, a, half):
            h1 = min(a, h0 + half)
            d = dst.rearrange("(p a) c -> p a c", p=128)[:, h0:h1, :]
            s = src.rearrange("(p a) c -> p a c", p=128)[:, h0:h1, :]
            self.dma("pool", d, s, [], [res], semkey=("pc",) + tuple(res))

    def modulation(self, c_in, mod_w, mod_b):
        m = self.mark()
        cact = self.sb("cact", [128, KC], F32)
        mrow = self.sb("mrow", [1, 9 * D], F32)
        modT = self.sb("modT", [128, 128], F32)
        wst = [self.sb("modw%d" % i, [128, KC, 512], F32) for i in range(2)]
        self.dma("sp", cact[:], c_in, [], ["cact"])
        self.act(cact[:], cact[:], AF.Silu, ["cact"], ["cact"])
        for l in range(2):
            self.dma("sp", mrow[:], mod_b[l:l + 1, :], [], ["mrow"])
            for cb in range(18):
                slot = (l * 18 + cb) % 2
                self.dma("sp", wst[slot][:], mod_w[l, :, cb * 512:(cb + 1) * 512].rearrange("(k p) c -> p k c", p=128),
                         [], [("modw", slot)])
                pst = self.ps[cb % 2]
                for k in range(KC):
                    self.mm(pst[0:1, :], cact[:, k:k + 1], wst[slot][:, k, :], k == 0, k == KC - 1,
                            ["cact", ("modw", slot)], [("ps", cb % 2)])
                self.vec("dve", "tensor_tensor", [("ps", cb % 2), "mrow"], ["mrow"],
                         out=mrow[0:1, cb * 512:(cb + 1) * 512], in0=pst[0:1, :], in1=mrow[0:1, cb * 512:(cb + 1) * 512],
                         op=ALU.add)
            self.dma("sp", self.mod_d[l:l + 1, :], mrow[:], ["mrow"], [("mod_d", l)])
            self.dma("sp", modT[0:72, :], self.mod_d[l, :].rearrange("(r p) -> r p", p=128), [("mod_d", l)], ["modT"])
            self.tr(self.ps[2][:, 0:72], modT[0:72, :], self.cst[0:72, 0:72], ["modT", "cst"], [("ps", 2)])
            self.vec("dve", "tensor_copy", [("ps", 2)], ["modP"],
                     out=self.modP[:, l, :, :].rearrange("p j k -> p (j k)"), in_=self.ps[2][:, 0:72])
        for l in range(2):
            for s in range(3):
                self.vec("dve", "scalar_tensor_tensor", ["modP", "ngP"], ["aP"],
                         out=self.aP[:, l, s, :], in0=self.modP[:, l, 3 * s + 1, :], scalar=1.0,
                         in1=self.ngP[:, l, s, :], op0=ALU.add, op1=ALU.mult)
        self.S.barrier()
        self.release(m)

    def norm_mod_T(self, t, l, s, hT, hres, xn, junk, xsrc=None, xr=None):
        xts, xrs = [], []
        for jj in range(4):
            j = t * 4 + jj
            xt = self.xres[:, j, :] if xsrc is None else xsrc(jj)
            xres_ = ("x", j) if xr is None else xr(jj)
            xts.append(xt)
            xrs.append(xres_)
            self.act(junk[:], xt, AF.Square, [xres_], ["junk", "ss"], accum_out=self.stat[:, jj:jj + 1])
        rs4 = self.stat[:, 8:12]
        self.vec("dve", "tensor_scalar", ["ss"], ["rs"], out=rs4, in0=self.stat[:, 0:4], scalar1=1.0 / D, scalar2=EPS,
                 op0=ALU.mult, op1=ALU.add)
        self.act(rs4, rs4, AF.Sqrt, ["rs"], ["rs"])
        self.vec("dve", "reciprocal", ["rs"], ["rs"], out=rs4, in_=rs4)
        for jj in range(4):
            self.vec("dve", "tensor_scalar", [xrs[jj], "rs"], [("xn", jj)], out=xn[:, jj, :], in0=xts[jj],
                     scalar1=self.stat[:, 8 + jj:9 + jj], scalar2=None, op0=ALU.mult)
        for k in range(KC):
            bank = 4 + (k % 4)
            pv = self.ps[bank][:].bitcast(BF16)
            for jj in range(4):
                self.tr(pv[:, jj * 128:(jj + 1) * 128], xn[:, jj, k * 128:(k + 1) * 128], self.ident[:],
                        [("xn", jj), "ident"], [("ps", bank)])
            self.act(hT[:, k, :], pv[:, 0:512], AF.Identity, [("ps", bank), "aP", "modP"], [hres],
                     scale=self.aP[:, l, s, k:k + 1], bias=self.modP[:, l, 3 * s, k:k + 1])

    def load_gate(self, G, l, s):
        self.dma("sp", G[:], self.mod_d[l:l + 1, (3 * s + 2) * D:(3 * s + 3) * D].partition_broadcast(128),
                 [("mod_d", l)], ["G"])

    def ffn(self, l, s2):
        S = self.S
        fi = l * 2 + s2
        s = 0 if s2 == 0 else 2
        NT = self.NT
        m = self.mark()
        hT = self.sb("hT", [128, KC, 512], BF16)
        gT = self.sb("gT", [128, NF, 512], BF16)
        xn = self.sb("xn", [128, 4, D], BF16)
        junk = self.sb("junk", [128, D], BF16)
        G = self.sb("G", [128, D], F32)
        sg = [self.sb("sg%d" % i, [128, 512], F32) for i in range(2)]
        tmp = [self.sb("tmp0", [128, 512], F32)] * 2
        w13 = [self.sb("w13_%d" % i, [128, KC, 2, 256], BF16) for i in range(2)]
        w2g = [3, 3, 3, 3, 3, 3, 2, 2]
        w2o = [0, 3, 6, 9, 12, 15, 18, 20]
        w2 = [self.sb("w2_%d" % i, [128, 3, 512], BF16) for i in range(2)]
        self.load_gate(G, l, s)
        pieces = []
        for t in range(NT):
            for pc in range(11):
                pieces.append(("w13", pc))
            for n in range(2):
                for g in range(8):
                    pieces.append(("w2", n, g))
        loaded = [0]
        cnt = {"w13": 0, "w2": 0}
        slot_of = {}

        def ensure(i):
            while loaded[0] < len(pieces) and loaded[0] <= i + 1:
                p = pieces[loaded[0]]
                kind = p[0]
                sl = cnt[kind] % 2
                cnt[kind] += 1
                slot_of[loaded[0]] = sl
                if kind == "w13":
                    pc = p[1]
                    for ab in range(2):
                        src = self.w13_s[fi, :, ab * DFF + pc * 256: ab * DFF + (pc + 1) * 256]
                        self.dma("sp", w13[sl][:, :, ab, :], src.rearrange("(k p) c -> p k c", p=128),
                                 [("w13s", fi)], [("w13", sl, ab)])
                else:
                    n, g = p[1], p[2]
                    src = self.w2_s[fi, w2o[g] * 128:(w2o[g] + w2g[g]) * 128, n * 512:(n + 1) * 512]
                    self.dma("sp", w2[sl][:, 0:w2g[g], :], src.rearrange("(f p) c -> p f c", p=128),
                             [("w2s", fi)], [("w2", sl)])
                loaded[0] += 1

        pi = 0
        for t in range(NT):
            self.norm_mod_T(t, l, s, hT, "hT", xn, junk)
            for pc in range(11):
                ensure(pi)
                sl = slot_of[pi]
                pi += 1
                for ff in range(2):
                    f = pc * 2 + ff
                    pa, pb = self.ps[ff * 2], self.ps[ff * 2 + 1]
                    for k in range(KC):
                        self.mm(pa[:], w13[sl][:, k, 0, ff * 128:(ff + 1) * 128], hT[:, k, :], k == 0, k == KC - 1,
                                [("w13", sl, 0), "hT"], [("ps", ff * 2)])
                    for k in range(KC):
                        self.mm(pb[:], w13[sl][:, k, 1, ff * 128:(ff + 1) * 128], hT[:, k, :], k == 0, k == KC - 1,
                                [("w13", sl, 1), "hT"], [("ps", ff * 2 + 1)])
                    self.act(sg[ff][:], pa[:], AF.Silu, [("ps", ff * 2)], [("sg", ff)])
                    self.vec("dve", "tensor_tensor", [("sg", ff), ("ps", ff * 2 + 1)], [("gT", f)],
                             out=gT[:, f, :], in0=sg[ff][:], in1=pb[:], op=ALU.mult)
            for n in range(2):
                for g in range(8):
                    ensure(pi)
                    sl = slot_of[pi]
                    pi += 1
                    for fl in range(w2g[g]):
                        f = w2o[g] + fl
                        for jj in range(4):
                            self.mm(self.ps[4 + jj][:], gT[:, f, jj * 128:(jj + 1) * 128], w2[sl][:, fl, :],
                                    f == 0, f == NF - 1, [("gT", f), ("w2", sl)], [("ps", 4 + jj)])
                for jj in range(4):
                    j = t * 4 + jj
                    xs = self.xres[:, j, n * 512:(n + 1) * 512]
                    self.vec("dve", "tensor_tensor", [("ps", 4 + jj), "G"], [("tmp", 0)],
                             out=tmp[jj % 2][:], in0=self.ps[4 + jj][:], in1=G[:, n * 512:(n + 1) * 512], op=ALU.mult)
                    self.vec("dve", "scalar_tensor_tensor", [("tmp", 0), ("x", j)], [("x", j)],
                             out=xs, in0=tmp[jj % 2][:], scalar=0.5, in1=xs, op0=ALU.mult, op1=ALU.add)
        S.barrier()
        self.release(m)

    def spill_x(self):
        for j in range(self.NB):
            self.dma("sp", self.xs_d[j * 128:(j + 1) * 128, :], self.xres[:, j, :], [("x", j)], [("xd", j)],
                     semkey=("xsp", j % 4))
        self.S.barrier()

    def reload_x(self):
        self.S.barrier()
        for j in range(self.NB):
            self.dma("sp", self.xres[:, j, :], self.xs_d[j * 128:(j + 1) * 128, :], [("xd", j)], [("x", j)],
                     semkey=("xload", j % 4))

    def mixer_ab(self):
        S = self.S
        ins = self.ins
        l, s = 0, 1
        NT = self.NT
        keep = self.sb_off
        self.sb_off = self.x_off
        cst = self.cst
        w_in = self.sb("abwin", [128, KC, IN_AB], BF16)
        w_out = self.sb("abwout", [128, 12, D], BF16)
        pw = self.sb("poolw", [128, 4, 128], BF16)
        abP = self.sb("abP", [128, 80], F32)
        rowp = self.sb("rowp", [128, 48], F32)
        a_bc = self.sb("a_bc", [128, 16], F32)
        sel = self.sb("sel", [16, 16, 128], BF16)
        negm = self.sb("negm", [128, 512], BF16)
        G = self.sb("G", [128, D], F32)
        hT = self.sb("hT", [128, KC, 512], BF16)
        xn = self.sb("xn", [128, 4, D], BF16)
        junk = self.sb("junk", [128, D], BF16)
        xt = [self.sb("xt0", [128, 4, D], F32)] * 2
        Up = self.sb("Up", [128, 528], F32)
        sA = self.sb("sA", [128, 528], F32)
        sB = self.sb("sB", [128, 528], F32)
        phalo = self.sb("phalo", [128, 4, 16], F32)
        dT = [self.sb("dT%d" % i, [128, 512], BF16) for i in range(2)]
        dfix = self.sb("dfix", [128, 16], F32)
        ypT = self.sb("ypT", [128, 4, 512], BF16)
        Uc = [self.sb("Uc0", [128, 515], F32)] * 2
        chalo = self.sb("chalo", [128, 12, 3], F32)
        acc = [self.sb("acc0", [128, 512], F32)] * 2
        xbcT = self.sb("xbcT", [128, 12, 512], BF16)
        zs = self.sb("zs", [128, D], F32)
        dtt = self.sb("dtt", [128, 96], F32)
        sm2 = self.sb("sm2", [128, 48], F32)
        acsT = self.sb("acsT", [16, 3, 128], F32)
        acsHL = self.sb("acsHL", [16, 2, 128], BF16)
        xs_tok = self.sb("xs_tok", [128, D], BF16)
        xw = self.sb("xw", [128, D], BF16)
        xsD = self.sb("xsD", [128, D], BF16)
        Btok = self.sb("Btok", [128, 2, 128], BF16)
        cb = self.sb("cb", [128, 2, 128], F32)
        Eexp = [self.sb("Eexp%d" % i, [128, 512], F32) for i in range(2)]
        Mt = [self.sb("Mt%d" % i, [128, 512], BF16) for i in range(2)]
        H = self.sb("H", [128, D], F32)
        Hb = self.sb("Hb", [128, D], BF16)
        yA = self.sb("yA", [128, D], F32)
        ssg = self.sb("ssg", [128, 4], F32)
        yn = self.sb("yn", [128, D], BF16)
        yT = self.sb("yT", [128, KC, 512], BF16)
        tmp = [self.sb("tmpo%d" % i, [128, 512], F32) for i in range(2)]
        ps = self.ps
        ident = self.ident

        self.dma("pool", w_in[:], ins["abin"].rearrange("(k p) c -> p k c", p=128), [], ["abwin"])
        self.dma("pool", w_out[:], ins["about"].rearrange("(k p) c -> p k c", p=128), [], ["abwout"])
        self.dma("pool", pw[:], ins["poolw"].rearrange("(g p) c -> p g c", p=128), [], ["poolw"])
        self.dma("pool", sel[:].rearrange("p a b -> p (a b)"), ins["sel"], [], ["sel"])
        self.dma("pool", negm[:], ins["cmask"], [], ["negm"])
        self.dma("sp", abP[:], ins["abP"], [], ["abP"])
        self.dma("sp", rowp[:], ins["abR"][0:1, 0:48].partition_broadcast(128), [], ["rowp"])
        self.act(a_bc[:], rowp[:, 16:32], AF.Exp, ["rowp"], ["a_bc"])
        self.vec("dve", "tensor_scalar", ["a_bc"], ["a_bc"], out=a_bc[:], in0=a_bc[:], scalar1=-1.0, scalar2=None,
                 op0=ALU.mult)
        self.load_gate(G, l, s)
        self.vec("dve", "memset", [], ["phalo"], phalo[:], 0.0)
        self.vec("dve", "memset", [], ["chalo"], chalo[:], 0.0)
        self.vec("dve", "memset", [], ["H"], H[:], 0.0)
        self.vec("dve", "memset", [], ["Hb"], Hb[:], 0.0)
        D_bc = rowp[:, 32:48]
        dtb_bc = rowp[:, 0:16]
        wins = (2, 4, 8, 16)

        for t in range(NT):
            xb = xt[t % 2]
            for jj in range(4):
                j = t * 4 + jj
                self.dma("sp", xb[:, jj, :], self.xs_d[j * 128:(j + 1) * 128, :], [("xd", j)], [("xt", 0, jj)],
                         semkey=("xtl", t % 2))
            self.norm_mod_T(t, l, s, hT, "hT", xn, junk, xsrc=lambda jj: xb[:, jj, :],
                            xr=lambda jj: ("xt", 0, jj))
            for g in range(4):
                bk = g % 2
                for k in range(KC):
                    self.mm(ps[bk][:], w_in[:, k, g * 128:(g + 1) * 128], hT[:, k, :], k == 0, k == KC - 1,
                            ["abwin", "hT"], [("ps", bk)])
                self.vec("dve", "tensor_copy", ["phalo"], ["Up"], out=Up[:, 0:16], in_=phalo[:, g, :])
                self.act(Up[:, 16:528], ps[bk][:], AF.Copy, [("ps", bk)], ["Up"])
                self.vec("dve", "tensor_copy", ["Up"], ["phalo"], out=phalo[:, g, :], in_=Up[:, 512:528])
                self.vec("dve", "tensor_tensor", ["Up"], ["sA"], out=sA[:, 1:528], in0=Up[:, 1:528], in1=Up[:, 0:527],
                         op=ALU.add)
                lvl = sA
                if g >= 1:
                    self.vec("dve", "tensor_tensor", ["sA"], ["sB"], out=sB[:, 3:528], in0=sA[:, 3:528],
                             in1=sA[:, 1:526], op=ALU.add)
                    lvl = sB
                if g >= 2:
                    self.vec("dve", "tensor_tensor", ["sB"], ["sA"], out=sA[:, 7:528], in0=sB[:, 7:528],
                             in1=sB[:, 3:524], op=ALU.add)
                    lvl = sA
                if g >= 3:
                    self.vec("dve", "tensor_tensor", ["sA"], ["sB"], out=sB[:, 15:528], in0=sA[:, 15:528],
                             in1=sA[:, 7:520], op=ALU.add)
                    lvl = sB
                lres = "sA" if lvl is sA else "sB"
                dd = dT[g % 2]
                self.vec("dve", "scalar_tensor_tensor", [lres, "Up"], [("dT", g % 2)], out=dd[:], in0=lvl[:, 16:528],
                         scalar=1.0 / wins[g], in1=Up[:, 16:528], op0=ALU.mult, op1=ALU.subtract)
                if t == 0:
                    self.vec("dve", "tensor_tensor", [lres, "cst"], ["dfix"], out=dfix[:], in0=lvl[:, 16:32],
                             in1=cst[:, 384 + g * 16:384 + (g + 1) * 16], op=ALU.mult)
                    self.vec("dve", "tensor_tensor", ["dfix", "Up"], [("dT", g % 2)], out=dd[:, 0:16], in0=dfix[:],
                             in1=Up[:, 16:32], op=ALU.subtract)
                self.mm(ps[2 + bk][:], pw[:, g, :], dd[:], True, True, ["poolw", ("dT", g % 2)], [("ps", 2 + bk)])
                self.act(ypT[:, g, :], ps[2 + bk][:], AF.Copy, [("ps", 2 + bk), "abP"], [("ypT", g)],
                         scale=abP[:, g:g + 1])
            for cc in range(12):
                bk = cc % 2
                c0 = 1536 + cc * 128
                for k in range(KC):
                    self.mm(ps[bk][:], w_in[:, k, c0:c0 + 128], hT[:, k, :], k == 0, k == KC - 1,
                            ["abwin", "hT"], [("ps", bk)])
                U = Uc[bk]
                ur = ("Uc", 0)
                self.vec("dve", "tensor_copy", ["chalo"], [ur], out=U[:, 0:3], in_=chalo[:, cc, :])
                self.act(U[:, 3:515], ps[bk][:], AF.Copy, [("ps", bk)], [ur])
                self.vec("dve", "tensor_copy", [ur], ["chalo"], out=chalo[:, cc, :], in_=U[:, 512:515])
                a_ = acc[bk]
                ar = ("acc", 0)
                self.vec("dve", "tensor_scalar", [ur, "abP"], [ar], out=a_[:], in0=U[:, 0:512],
                         scalar1=abP[:, 4 + cc * 4:5 + cc * 4], scalar2=abP[:, 52 + cc:53 + cc], op0=ALU.mult, op1=ALU.add)
                for kk in range(1, 4):
                    self.vec("dve", "scalar_tensor_tensor", [ur, "abP", ar], [ar], out=a_[:], in0=U[:, kk:kk + 512],
                             scalar=abP[:, 4 + cc * 4 + kk:5 + cc * 4 + kk], in1=a_[:], op0=ALU.mult, op1=ALU.add)
                self.act(xbcT[:, cc, :], a_[:], AF.Silu, [ar], [("xbcT", cc)])
            for jj in range(4):
                j = t * 4 + jj
                tok = slice(jj * 128, (jj + 1) * 128)
                for n in range(2):
                    for k in range(KC):
                        self.mm(ps[2 + n][:], hT[:, k, tok], w_in[:, k, 512 + n * 512:1024 + n * 512], k == 0,
                                k == KC - 1, ["hT", "abwin"], [("ps", 2 + n)])
                    self.act(zs[:, n * 512:(n + 1) * 512], ps[2 + n][:], AF.Silu, [("ps", 2 + n)], ["zs"])
                for k in range(KC):
                    self.mm(ps[4][:, 0:16], hT[:, k, tok], w_in[:, k, 3072:3088], k == 0, k == KC - 1,
                            ["hT", "abwin"], [("ps", 4)])
                dt_, lndt, adt, nb, wst, eacs = (dtt[:, i * 16:(i + 1) * 16] for i in range(6))
                cd_bc, acs_sb, arg = (sm2[:, i * 16:(i + 1) * 16] for i in range(3))
                self.vec("dve", "tensor_tensor", [("ps", 4), "rowp"], ["dt"], out=dt_, in0=ps[4][:, 0:16], in1=dtb_bc,
                         op=ALU.add)
                self.act(dt_, dt_, AF.Exp, ["dt"], ["dt"])
                self.act(dt_, dt_, AF.Ln, ["dt"], ["dt"], bias=1.0)
                self.act(lndt, dt_, AF.Ln, ["dt"], ["lndt"])
                self.vec("dve", "tensor_tensor", ["dt", "a_bc"], ["adt"], out=adt, in0=dt_, in1=a_bc[:], op=ALU.mult)
                self.mm(ps[4][:, 16:32], cst[:, 128:256], adt, True, True, ["cst", "adt"], [("ps", 4)])
                self.mm(ps[4][0:16, 128:256], adt, cst[:, 128:256], True, True, ["cst", "adt"], [("ps", 4)])
                self.mm(ps[4][:, 32:48], cst[:, 256:384], adt, True, True, ["cst", "adt"], [("ps", 4)])
                self.vec("dve", "tensor_copy", [("ps", 4)], ["acs_sb"], out=acs_sb, in_=ps[4][:, 16:32])
                self.vec("dve", "tensor_tensor", ["lndt", "acs_sb"], ["nb"], out=nb, in0=lndt, in1=acs_sb, op=ALU.subtract)
                self.act(eacs, ps[4][:, 16:32], AF.Exp, [("ps", 4)], ["eacs"])
                self.vec("dve", "tensor_tensor", [("ps", 4), "nb"], ["arg"], out=arg, in0=ps[4][:, 32:48], in1=nb,
                         op=ALU.add)
                self.act(wst, arg, AF.Exp, ["arg"], ["wst"])
                self.act(cd_bc, ps[4][:, 32:48], AF.Exp, [("ps", 4)], ["cd_bc"])
                self.vec("dve", "tensor_copy", [("ps", 4)], ["acsT0"], out=acsT[:, 0, :], in_=ps[4][0:16, 128:256])
                self.vec("dve", "tensor_copy", ["acsT0"], ["acsHL0"], out=acsHL[:, 0, :], in_=acsT[:, 0, :])
                self.vec("dve", "tensor_copy", ["acsHL0"], ["acsT1"], out=acsT[:, 1, :], in_=acsHL[:, 0, :])
                self.vec("dve", "tensor_tensor", ["acsT0", "acsT1"], ["acsHL1"], out=acsHL[:, 1, :], in0=acsT[:, 0, :],
                         in1=acsT[:, 1, :], op=ALU.subtract)
                pv5 = ps[5][:].bitcast(BF16)
                for cc in range(8):
                    self.tr(pv5[:, cc * 128:(cc + 1) * 128], xbcT[:, cc, tok], ident[:], [("xbcT", cc), "ident"],
                            [("ps", 5)])
                self.act(xs_tok[:], pv5[:, 0:1024], AF.Copy, [("ps", 5)], ["xs_tok"])
                self.vec("dve", "tensor_tensor", [("ps", 5), "wst"], ["xw"],
                         out=xw[:].rearrange("p (h q) -> p h q", h=16), in0=pv5[:, 0:1024].rearrange("p (h q) -> p h q", h=16),
                         in1=wst.unsqueeze(2).to_broadcast([128, 16, 64]), op=ALU.mult)
                self.vec("dve", "tensor_tensor", ["xs_tok", "rowp"], ["xsD"],
                         out=xsD[:].rearrange("p (h q) -> p h q", h=16), in0=xs_tok[:].rearrange("p (h q) -> p h q", h=16),
                         in1=D_bc.unsqueeze(2).to_broadcast([128, 16, 64]), op=ALU.mult)
                pv7 = ps[7][:].bitcast(BF16)
                for g in range(2):
                    self.tr(pv7[:, g * 128:(g + 1) * 128], xbcT[:, 8 + g, tok], ident[:], [("xbcT", 8 + g), "ident"],
                            [("ps", 7)])
                self.act(Btok[:].rearrange("p a b -> p (a b)"), pv7[:, 0:256], AF.Copy, [("ps", 7)], ["Btok"])
                for g in range(2):
                    self.mm(ps[4][:, 256 + g * 128:384 + g * 128], xbcT[:, 8 + g, tok], xbcT[:, 10 + g, tok], True, True,
                            [("xbcT", 8 + g), ("xbcT", 10 + g)], [("ps", 4)])
                self.vec("dve", "tensor_copy", [("ps", 4)], ["cb"], out=cb[:].rearrange("p a b -> p (a b)"),
                         in_=ps[4][:, 256:512])
                for g in range(2):
                    ydb = ps[g]
                    self.mm(ydb[:], ident[:], xsD[:, g * 512:(g + 1) * 512], True, False, ["ident", "xsD"], [("ps", g)])
                    for qq in range(2):
                        q4 = g * 2 + qq
                        eb = Eexp[q4 % 2]
                        mb = Mt[q4 % 2]
                        self.mm(ps[6][:], ident[:], negm[:], True, False, ["ident", "negm"], [("ps", 6)])
                        for hh in range(4):
                            h = q4 * 4 + hh
                            for part in range(2):
                                self.mm(ps[6][:, hh * 128:(hh + 1) * 128], sel[:, h, :], acsHL[:, part, :], False,
                                        hh == 3 and part == 1, ["sel", "acsHL0", "acsHL1"], [("ps", 6)])
                        for hh in range(4):
                            h = q4 * 4 + hh
                            self.act(eb[:, hh * 128:(hh + 1) * 128], ps[6][:, hh * 128:(hh + 1) * 128], AF.Exp,
                                     [("ps", 6), "nb"], [("Eexp", q4 % 2)], bias=nb[:, h:h + 1])
                        self.vec("dve", "tensor_tensor", [("Eexp", q4 % 2), "cb"], [("Mt", q4 % 2)],
                                 out=mb[:].rearrange("p (a b) -> p a b", a=4), in0=eb[:].rearrange("p (a b) -> p a b", a=4),
                                 in1=cb[:, g:g + 1, :].to_broadcast([128, 4, 128]), op=ALU.mult)
                        for hh in range(4):
                            h = q4 * 4 + hh
                            hl = h - g * 8
                            self.mm(ydb[:, hl * 64:(hl + 1) * 64], mb[:, hh * 128:(hh + 1) * 128],
                                    xs_tok[:, h * 64:(h + 1) * 64], False, qq == 1 and hh == 3,
                                    [("Mt", q4 % 2), "xs_tok"], [("ps", g)])
                for g in range(2):
                    gs = slice(g * 512, (g + 1) * 512)
                    self.mm(ps[2 + g][:], xbcT[:, 10 + g, tok], Hb[:, gs], True, True, [("xbcT", 10 + g), "Hb"],
                            [("ps", 2 + g)])
                    self.vec("dve", "tensor_tensor", [("ps", 2 + g), "eacs"], ["yA"],
                             out=yA[:, gs].rearrange("p (h q) -> p h q", h=8),
                             in0=ps[2 + g][:].rearrange("p (h q) -> p h q", h=8),
                             in1=eacs[:, g * 8:(g + 1) * 8].unsqueeze(2).to_broadcast([128, 8, 64]), op=ALU.mult)
                    self.vec("dve", "tensor_tensor", [("ps", g), "yA"], ["yA"], out=yA[:, gs], in0=ps[g][:], in1=yA[:, gs],
                             op=ALU.add)
                    self.mm(ps[7][:], Btok[:, g, :], xw[:, gs], True, True, ["Btok", "xw"], [("ps", 7)])
                    self.vec("dve", "tensor_tensor", ["H", "cd_bc"], ["H"], out=H[:, gs].rearrange("p (h q) -> p h q", h=8),
                             in0=H[:, gs].rearrange("p (h q) -> p h q", h=8),
                             in1=cd_bc[:, g * 8:(g + 1) * 8].unsqueeze(2).to_broadcast([128, 8, 64]), op=ALU.mult)
                    self.vec("dve", "tensor_tensor", [("ps", 7), "H"], ["H"], out=H[:, gs], in0=ps[7][:], in1=H[:, gs],
                             op=ALU.add)
                    self.act(Hb[:, gs], H[:, gs], AF.Copy, ["H"], ["Hb"])
                self.vec("dve", "tensor_tensor", ["yA", "zs"], ["yA"], out=yA[:], in0=yA[:], in1=zs[:], op=ALU.mult)
                for g in range(2):
                    gs = slice(g * 512, (g + 1) * 512)
                    self.act(junk[:, gs], yA[:, gs], AF.Square, ["yA"], ["junk", "ssg"], accum_out=ssg[:, g:g + 1])
                self.vec("dve", "tensor_scalar", ["ssg"], ["ssg"], out=ssg[:, 2:4], in0=ssg[:, 0:2], scalar1=1.0 / 512,
                         scalar2=EPS, op0=ALU.mult, op1=ALU.add)
                self.act(ssg[:, 2:4], ssg[:, 2:4], AF.Sqrt, ["ssg"], ["ssg"])
                self.vec("dve", "reciprocal", ["ssg"], ["ssg"], out=ssg[:, 2:4], in_=ssg[:, 2:4])
                for g in range(2):
                    gs = slice(g * 512, (g + 1) * 512)
                    self.vec("dve", "tensor_scalar", ["yA", "ssg"], ["yn"], out=yn[:, gs], in0=yA[:, gs],
                             scalar1=ssg[:, 2 + g:3 + g], scalar2=None, op0=ALU.mult)
                for cc in range(8):
                    self.tr(pv5[:, cc * 128:(cc + 1) * 128], yn[:, cc * 128:(cc + 1) * 128], ident[:], ["yn", "ident"],
                            [("ps", 5)])
                for cc in range(8):
                    self.act(yT[:, cc, tok], pv5[:, cc * 128:(cc + 1) * 128], AF.Copy, [("ps", 5), "abP"], [("yT", jj)],
                             scale=abP[:, 64 + cc:65 + cc])
            for jj in range(4):
                j = t * 4 + jj
                tok = slice(jj * 128, (jj + 1) * 128)
                for n in range(2):
                    pb = ps[6 + n]
                    for k in range(12):
                        lhs = ypT[:, k, tok] if k < 4 else yT[:, k - 4, tok]
                        rr = [("ypT", k)] if k < 4 else [("yT", jj)]
                        self.mm(pb[:], lhs, w_out[:, k, n * 512:(n + 1) * 512], k == 0, k == 11, rr + ["abwout"],
                                [("ps", 6 + n)])
                    xs_ = xb[:, jj, n * 512:(n + 1) * 512]
                    self.vec("dve", "tensor_tensor", [("ps", 6 + n), "G"], [("tmpo", n)], out=tmp[n][:], in0=pb[:],
                             in1=G[:, n * 512:(n + 1) * 512], op=ALU.mult)
                    self.vec("dve", "tensor_tensor", [("tmpo", n), ("xt", 0, jj)], [("xt", 0, jj)], out=xs_,
                             in0=tmp[n][:], in1=xs_, op=ALU.add)
                self.dma("sp", self.xs_d[j * 128:(j + 1) * 128, :], xb[:, jj, :], [("xt", 0, jj)], [("xd", j)],
                         semkey=("xts", t % 2))
        self.sb_off = keep


    def mixer_mla(self):
        import math
        S = self.S
        ins = self.ins
        l, s = 1, 1
        NT, NB, seq = self.NT, self.NB, self.seq
        keep = self.sb_off
        self.sb_off = self.x_off
        cst, ident, ps = self.cst, self.ident, self.ps
        QN_d = self.dram("QN_d", [1024, seq], BF16)
        QR_d = self.dram("QR_d", [512, seq], BF16)
        KN_d = self.dram("KN_d", [1024, seq], BF16)
        KR_d = self.dram("KR_d", [32, seq], BF16)
        V_d = self.dram("V_d", [seq, 1024], BF16)
        OT_d = self.dram("OT_d", [1024, seq], BF16)
        w_in = self.sb("mwin", [128, KC, IN_MLA], BF16)
        w_uq = self.sb("mwuq", [128, 6, 1536], BF16)
        w_ukv = self.sb("mwukv", [128, 2, 2048], BF16)
        w_o = self.sb("mwo", [128, KC, D], BF16)
        mlaP = self.sb("mlaP", [128, 16], F32)
        G = self.sb("G", [128, D], F32)
        tri = self.sb("tri", [128, 128], BF16)
        xt = self.sb("xt", [128, 4, D], F32)
        self.dma("pool", w_in[:], ins["mlain"].rearrange("(k p) c -> p k c", p=128), [], ["mwin"])
        self.dma("pool", w_uq[:], ins["wuq"].rearrange("(k p) c -> p k c", p=128), [], ["mwuq"])
        self.dma("pool", w_ukv[:], ins["wukv"].rearrange("(k p) c -> p k c", p=128), [], ["mwukv"])
        self.dma("pool", w_o[:], ins["wo"].rearrange("(k p) c -> p k c", p=128), [], ["mwo"])
        self.dma("sp", mlaP[:], ins["mlaP"], [], ["mlaP"])
        self.load_gate(G, l, s)
        self.vec("dve", "tensor_copy", ["cst"], ["tri"], out=tri[:], in_=cst[:, 128:256])
        mA = self.mark()
        hT = self.sb("hT", [128, KC, 512], BF16)
        xn = self.sb("xn", [128, 4, D], BF16)
        junk = self.sb("junk", [128, D], BF16)
        qnT = self.sb("qnT", [128, 6, 512], BF16)
        kvnT = self.sb("kvnT", [128, 2, 512], BF16)
        posi = self.sb("posi", [128, 512], I32)
        ang = self.sb("ang", [128, 512], F32)
        rr = [self.sb("rr%d" % i, [128, 512], F32) for i in range(2)]
        cosT = self.sb("cosT", [128, 512], F32)
        sinT = self.sb("sinT", [128, 512], F32)
        x1s = self.sb("x1s", [128, 512], F32)
        x2s = self.sb("x2s", [128, 512], F32)
        t1 = self.sb("t1", [128, 512], F32)
        t2 = self.sb("t2", [128, 512], F32)
        ro = [self.sb("ro%d" % i, [128, 512], BF16) for i in range(2)]
        qsb = [self.sb("qsb%d" % i, [128, 512], BF16) for i in range(2)]
        vsb = [self.sb("vsb%d" % i, [128, D], BF16) for i in range(2)]
        st = self.sb("mst", [128, 16], F32)
        PI = math.pi
        pib = self.sb("pib", [128, 1], F32)
        self.vec("dve", "memset", [], ["pib"], pib[:], PI)

        def rope(x1ps, x2ps, P, res1, res2, o1, o2, o1res, o2res):
            self.act(x1s[0:P, :], x1ps, AF.Copy, [res1], ["x1s"])
            self.act(x2s[0:P, :], x2ps, AF.Copy, [res2], ["x2s"])
            self.vec("dve", "tensor_tensor", ["x1s", "cosT"], ["t1"], out=t1[0:P, :], in0=x1s[0:P, :], in1=cosT[0:P, :], op=ALU.mult)
            self.vec("dve", "tensor_tensor", ["x2s", "sinT"], ["t2"], out=t2[0:P, :], in0=x2s[0:P, :], in1=sinT[0:P, :], op=ALU.mult)
            self.vec("dve", "tensor_tensor", ["t1", "t2"], [o1res], out=o1, in0=t1[0:P, :], in1=t2[0:P, :], op=ALU.subtract)
            self.vec("dve", "tensor_tensor", ["x2s", "cosT"], ["t1"], out=t1[0:P, :], in0=x2s[0:P, :], in1=cosT[0:P, :], op=ALU.mult)
            self.vec("dve", "tensor_tensor", ["x1s", "sinT"], ["t2"], out=t2[0:P, :], in0=x1s[0:P, :], in1=sinT[0:P, :], op=ALU.mult)
            self.vec("dve", "tensor_tensor", ["t1", "t2"], [o2res], out=o2, in0=t1[0:P, :], in1=t2[0:P, :], op=ALU.add)

        for t in range(NT):
            cols = slice(t * 512, (t + 1) * 512)
            for jj in range(4):
                j = t * 4 + jj
                self.dma("sp", xt[:, jj, :], self.xs_d[j * 128:(j + 1) * 128, :], [("xd", j)], [("xt", jj)], semkey=("xtl", 0))
            self.norm_mod_T(t, l, s, hT, "hT", xn, junk, xsrc=lambda jj: xt[:, jj, :], xr=lambda jj: ("xt", jj))
            self.dma("sp", posi[:], ins["pos"][0:1, cols].partition_broadcast(128), [], ["posi"])
            self.vec("dve", "tensor_copy", ["posi"], ["ang"], out=ang[:], in_=posi[:])
            self.vec("dve", "tensor_scalar", ["ang", "mlaP"], ["ang"], out=ang[:], in0=ang[:], scalar1=mlaP[:, 8:9], scalar2=None,
                     op0=ALU.mult)
            C1 = 6.28125
            C2 = 2 * PI - C1
            for which, (dst, dres) in enumerate(((sinT, "sinT"), (cosT, "cosT"))):
                r = rr[which]
                rres = ("rr", which)
                src = ang
                if which == 1:
                    self.vec("dve", "tensor_scalar", ["ang"], ["t1"], out=t1[:], in0=ang[:], scalar1=PI / 2, scalar2=None, op0=ALU.add)
                    src = t1
                sres = "ang" if which == 0 else "t1"
                self.vec("dve", "tensor_scalar", [sres], ["t2"], out=t2[:], in0=src[:], scalar1=1.0 / (2 * PI), scalar2=None, op0=ALU.mult)
                self.vec("dve", "tensor_copy", ["t2"], ["posi"], out=posi[:], in_=t2[:])
                self.vec("dve", "tensor_copy", ["posi"], ["t2"], out=t2[:], in_=posi[:])
                self.vec("dve", "scalar_tensor_tensor", ["t2", sres], [rres], out=r[:], in0=t2[:], scalar=-C1, in1=src[:],
                         op0=ALU.mult, op1=ALU.add)
                self.vec("dve", "scalar_tensor_tensor", ["t2", rres], [rres], out=r[:], in0=t2[:], scalar=-C2, in1=r[:],
                         op0=ALU.mult, op1=ALU.add)
                self.vec("dve", "tensor_scalar", [rres], ["x1s"], out=x1s[:], in0=r[:], scalar1=PI, scalar2=-2 * PI, op0=ALU.is_gt,
                         op1=ALU.mult)
                self.vec("dve", "tensor_scalar", [rres], ["x2s"], out=x2s[:], in0=r[:], scalar1=-PI, scalar2=2 * PI, op0=ALU.is_lt,
                         op1=ALU.mult)
                self.vec("dve", "tensor_tensor", [rres, "x1s"], [rres], out=r[:], in0=r[:], in1=x1s[:], op=ALU.add)
                self.vec("dve", "tensor_tensor", [rres, "x2s"], [rres], out=r[:], in0=r[:], in1=x2s[:], op=ALU.add)
                self.act(dst[:], r[:], AF.Sin, [rres], [dres])
            for jj in range(4):
                tok = slice(jj * 128, (jj + 1) * 128)
                for n in range(2):
                    for k in range(KC):
                        self.mm(ps[n][:], hT[:, k, tok], w_in[:, k, n * 512:(n + 1) * 512], k == 0, k == KC - 1,
                                ["hT", "mwin"], [("ps", n)])
                self.act(junk[:, 0:512], ps[0][:], AF.Square, [("ps", 0)], ["junk", "mst"], accum_out=st[:, 0:1])
                self.act(junk[:, 512:768], ps[1][:, 0:256], AF.Square, [("ps", 1)], ["junk", "mst"], accum_out=st[:, 1:2])
                self.act(junk[:, 768:1024], ps[1][:, 256:512], AF.Square, [("ps", 1)], ["junk", "mst"], accum_out=st[:, 2:3])
                self.vec("dve", "tensor_tensor", ["mst"], ["mst"], out=st[:, 3:4], in0=st[:, 0:1], in1=st[:, 1:2], op=ALU.add)
                self.vec("dve", "tensor_scalar", ["mst"], ["mst"], out=st[:, 4:5], in0=st[:, 3:4], scalar1=1.0 / 768, scalar2=EPS,
                         op0=ALU.mult, op1=ALU.add)
                self.vec("dve", "tensor_scalar", ["mst"], ["mst"], out=st[:, 5:6], in0=st[:, 2:3], scalar1=1.0 / 256, scalar2=EPS,
                         op0=ALU.mult, op1=ALU.add)
                self.act(st[:, 4:6], st[:, 4:6], AF.Sqrt, ["mst"], ["mst"])
                self.vec("dve", "reciprocal", ["mst"], ["mst"], out=st[:, 4:6], in_=st[:, 4:6])
                self.vec("dve", "tensor_scalar", [("ps", 0), "mst"], [("xn", jj)], out=xn[:, jj, 0:512], in0=ps[0][:],
                         scalar1=st[:, 4:5], scalar2=None, op0=ALU.mult)
                self.vec("dve", "tensor_scalar", [("ps", 1), "mst"], [("xn", jj)], out=xn[:, jj, 512:768], in0=ps[1][:, 0:256],
                         scalar1=st[:, 4:5], scalar2=None, op0=ALU.mult)
                self.vec("dve", "tensor_scalar", [("ps", 1), "mst"], [("xn", jj)], out=xn[:, jj, 768:1024], in0=ps[1][:, 256:512],
                         scalar1=st[:, 5:6], scalar2=None, op0=ALU.mult)
            for k in range(8):
                bank = 4 + (k % 4)
                pv = ps[bank][:].bitcast(BF16)
                for jj in range(4):
                    self.tr(pv[:, jj * 128:(jj + 1) * 128], xn[:, jj, k * 128:(k + 1) * 128], ident[:], [("xn", jj), "ident"],
                            [("ps", bank)])
                dst = qnT[:, k, :] if k < 6 else kvnT[:, k - 6, :]
                self.act(dst, pv[:, 0:512], AF.Copy, [("ps", bank), "mlaP"], ["qnT" if k < 6 else "kvnT"], scale=mlaP[:, k:k + 1])
            for half in range(2):
                for k in range(KC):
                    self.mm(ps[2 + half][0:16, :], w_in[:, k, 1024 + half * 16:1040 + half * 16], hT[:, k, :], k == 0, k == KC - 1,
                            ["mwin", "hT"], [("ps", 2 + half)])
            rope(ps[2][0:16, :], ps[3][0:16, :], 16, ("ps", 2), ("ps", 3), ro[0][0:16, :], ro[1][0:16, :], ("ro", 0), ("ro", 1))
            self.dma("sp", KR_d[0:16, cols], ro[0][0:16, :], [("ro", 0)], [("KR_d", t)], semkey=("scr", 0))
            self.dma("sp", KR_d[16:32, cols], ro[1][0:16, :], [("ro", 1)], [("KR_d", t)], semkey=("scr", 0))
            for c in range(8):
                bk = c % 2
                for k in range(6):
                    self.mm(ps[bk][:], w_uq[:, k, c * 128:(c + 1) * 128], qnT[:, k, :], k == 0, k == 5, ["mwuq", "qnT"], [("ps", bk)])
                self.act(qsb[bk][:], ps[bk][:], AF.Copy, [("ps", bk)], [("qsb", bk)])
                self.dma("sp", QN_d[c * 128:(c + 1) * 128, cols], qsb[bk][:], [("qsb", bk)], [("QN_d", t)], semkey=("scr", 1 + bk))
            for hg in range(2):
                for k in range(6):
                    self.mm(ps[2][:], w_uq[:, k, 1024 + hg * 128:1152 + hg * 128], qnT[:, k, :], k == 0, k == 5, ["mwuq", "qnT"],
                            [("ps", 2)])
                for k in range(6):
                    self.mm(ps[3][:], w_uq[:, k, 1280 + hg * 128:1408 + hg * 128], qnT[:, k, :], k == 0, k == 5, ["mwuq", "qnT"],
                            [("ps", 3)])
                rope(ps[2][:], ps[3][:], 128, ("ps", 2), ("ps", 3), ro[0][:], ro[1][:], ("ro", 0), ("ro", 1))
                self.dma("sp", QR_d[hg * 128:(hg + 1) * 128, cols], ro[0][:], [("ro", 0)], [("QR_d", t)], semkey=("scr", 0))
                self.dma("sp", QR_d[256 + hg * 128:256 + (hg + 1) * 128, cols], ro[1][:], [("ro", 1)], [("QR_d", t)], semkey=("scr", 0))
            for c in range(8):
                bk = c % 2
                for k in range(2):
                    self.mm(ps[bk][:], w_ukv[:, k, c * 128:(c + 1) * 128], kvnT[:, k, :], k == 0, k == 1, ["mwukv", "kvnT"], [("ps", bk)])
                self.vec("dve", "tensor_copy", [("ps", bk)], [("qsb", bk)], out=qsb[bk][:], in_=ps[bk][:])
                self.dma("sp", KN_d[c * 128:(c + 1) * 128, cols], qsb[bk][:], [("qsb", bk)], [("KN_d", t)], semkey=("scr", 1 + bk))
            for jj in range(4):
                j = t * 4 + jj
                tok = slice(jj * 128, (jj + 1) * 128)
                vb = vsb[jj % 2]
                for n in range(2):
                    for k in range(2):
                        self.mm(ps[2 + n][:], kvnT[:, k, tok], w_ukv[:, k, 1024 + n * 512:1536 + n * 512], k == 0, k == 1,
                                ["kvnT", "mwukv"], [("ps", 2 + n)])
                    if n == 0:
                        self.act(vb[:, 0:512], ps[2][:], AF.Copy, [("ps", 2)], [("vsb", jj % 2)])
                    else:
                        self.vec("dve", "tensor_copy", [("ps", 3)], [("vsb", jj % 2)], out=vb[:, 512:1024], in_=ps[3][:])
                self.dma("sp", V_d[j * 128:(j + 1) * 128, :], vb[:], [("vsb", jj % 2)], [("V_d", j)], semkey=("scr", 3 + jj % 2))
        S.barrier()
        self.release(mA)
        KhT = [self.sb("KhT%d" % i, [96, seq], BF16) for i in range(2)]
        QhT = [self.sb("QhT%d" % i, [96, seq], BF16) for i in range(2)]
        V1 = [self.sb("V1_%d" % i, [128, NB, 65], BF16) for i in range(2)]
        PT = [self.sb("PT%d" % i, [128, 512], BF16) for i in range(3)]
        rd = self.sb("rd", [128, 512], F32)
        bcs = self.sb("bcs", [64, 512], F32)
        osb = [self.sb("osb%d" % i, [64, 512], BF16) for i in range(2)]
        for i in range(2):
            self.vec("dve", "memset", [], [("V1", i)], V1[i][:, :, 64:65], 1.0)
        scale = 1.0 / math.sqrt(96.0)
        it = 0
        for h in range(16):
            hb = h % 2
            self.dma("sp", KhT[hb][0:64, :], KN_d[h * 64:(h + 1) * 64, :], [], [("KhT", hb, 0)], semkey=("ld", hb))
            self.dma("sp", KhT[hb][64:96, :], KR_d[0:32, :], [], [("KhT", hb, 1)], semkey=("ld", hb))
            self.dma("sp", QhT[hb][0:64, :], QN_d[h * 64:(h + 1) * 64, :], [], [("QhT", hb, 0)], semkey=("ld", hb))
            self.dma("sp", QhT[hb][64:80, :], QR_d[h * 16:(h + 1) * 16, :], [], [("QhT", hb, 1)], semkey=("ld", hb))
            self.dma("sp", QhT[hb][80:96, :], QR_d[256 + h * 16:256 + (h + 1) * 16, :], [], [("QhT", hb, 2)], semkey=("ld", hb))
            self.dma("sp", V1[hb][:, :, 0:64], V_d[:, h * 64:(h + 1) * 64].rearrange("(j p) c -> p j c", p=128), [],
                     [("V1", hb)], semkey=("ld", hb))
            kres = [("KhT", hb, 0), ("KhT", hb, 1)]
            qres = [("QhT", hb, 0), ("QhT", hb, 1), ("QhT", hb, 2)]
            for qt in range(NT):
                ob = 4 + (qt % 2)
                nkb = 4 * (qt + 1)
                for kb in range(nkb):
                    jd = kb - 4 * qt
                    c0 = 128 * jd if jd > 0 else 0
                    sbk = it % 3
                    it += 1
                    pt = PT[sbk]
                    self.mm(ps[sbk][:, c0:512], KhT[hb][:, kb * 128:(kb + 1) * 128], QhT[hb][:, qt * 512 + c0:(qt + 1) * 512],
                            True, True, kres + qres, [("ps", sbk)])
                    self.act(pt[:, c0:512], ps[sbk][:, c0:512], AF.Exp, [("ps", sbk)], [("PT", sbk)], scale=scale)
                    if jd >= 0:
                        self.vec("dve", "tensor_tensor", [("PT", sbk), "tri"], [("PT", sbk)], out=pt[:, c0:c0 + 128],
                                 in0=pt[:, c0:c0 + 128], in1=tri[:], op=ALU.mult)
                    self.mm(ps[ob][0:65, c0:512], V1[hb][:, kb, :], pt[:, c0:512], kb == 0, kb == nkb - 1,
                            [("V1", hb), ("PT", sbk)], [("ps", ob)])
                self.vec("dve", "reciprocal", [("ps", ob)], ["rd"], out=rd[64:65, :], in_=ps[ob][64:65, :])
                self.mm(ps[6][0:64, :], cst[64:65, 256:320], rd[64:65, :], True, True, ["cst", "rd"], [("ps", 6)])
                self.act(bcs[:], ps[6][0:64, :], AF.Copy, [("ps", 6)], ["bcs"])
                o_ = osb[qt % 2]
                self.vec("dve", "tensor_tensor", [("ps", ob), "bcs"], [("osb", qt % 2)], out=o_[:], in0=ps[ob][0:64, :], in1=bcs[:],
                         op=ALU.mult)
                self.dma("sp", OT_d[h * 64:(h + 1) * 64, qt * 512:(qt + 1) * 512], o_[:], [("osb", qt % 2)], [("OT_d", qt)],
                         semkey=("ost", qt % 2))
        S.barrier()
        self.release(mA)
        oT = [self.sb("oT%d" % i, [128, KC, 512], BF16) for i in range(2)]
        tmp = [self.sb("tmpm%d" % i, [128, 512], F32) for i in range(2)]
        for t in range(NT):
            cols = slice(t * 512, (t + 1) * 512)
            ot = oT[t % 2]
            self.dma("sp", ot[:], OT_d[:, cols].rearrange("(k p) t -> p k t", p=128), [], [("oT", t % 2)], semkey=("otl", t % 2))
            for jj in range(4):
                j = t * 4 + jj
                self.dma("sp", xt[:, jj, :], self.xs_d[j * 128:(j + 1) * 128, :], [("xd", j)], [("xt", jj)], semkey=("xtl", 0))
            for jj in range(4):
                j = t * 4 + jj
                tok = slice(jj * 128, (jj + 1) * 128)
                for n in range(2):
                    pb = ps[n]
                    for k in range(KC):
                        self.mm(pb[:], ot[:, k, tok], w_o[:, k, n * 512:(n + 1) * 512], k == 0, k == KC - 1, [("oT", t % 2), "mwo"],
                                [("ps", n)])
                    xs_ = xt[:, jj, n * 512:(n + 1) * 512]
                    self.vec("dve", "tensor_tensor", [("ps", n), "G"], [("tmpm", n)], out=tmp[n][:], in0=pb[:],
                             in1=G[:, n * 512:(n + 1) * 512], op=ALU.mult)
                    self.vec("dve", "tensor_tensor", [("tmpm", n), ("xt", jj)], [("xt", jj)], out=xs_, in0=tmp[n][:], in1=xs_,
                             op=ALU.add)
                self.dma("sp", self.xs_d[j * 128:(j + 1) * 128, :], xt[:, jj, :], [("xt", jj)], [("xd", j)], semkey=("xts", 0))
        self.sb_off = keep


    def final_norm(self, fin_in):
        m = self.mark()
        Gf = self.sb("Gf", [128, D], F32)
        junk = self.sb("junkf", [128, D], BF16)
        ob = [self.sb("ob%d" % i, [128, D], F32) for i in range(2)]
        self.dma("sp", Gf[:], fin_in.partition_broadcast(128), [], ["Gf"])
        for j in range(self.NB):
            xt = self.xres[:, j, :]
            ss = self.stat[:, 16 + j % 2:17 + j % 2]
            rs = self.stat[:, 18 + j % 2:19 + j % 2]
            o = ob[j % 2]
            self.act(junk[:], xt, AF.Square, [("x", j)], ["junkf", ("fss", j % 2)], accum_out=ss)
            self.vec("dve", "tensor_scalar", [("fss", j % 2)], [("frs", j % 2)], out=rs, in0=ss, scalar1=1.0 / D,
                     scalar2=EPS, op0=ALU.mult, op1=ALU.add)
            self.act(rs, rs, AF.Sqrt, [("frs", j % 2)], [("frs", j % 2)])
            self.vec("dve", "reciprocal", [("frs", j % 2)], [("frs", j % 2)], out=rs, in_=rs)
            self.vec("dve", "scalar_tensor_tensor", [("x", j), ("frs", j % 2), "Gf"], [("ob", j % 2)],
                     out=o[:], in0=xt, scalar=rs, in1=Gf[:], op0=ALU.mult, op1=ALU.mult)
            self.dma("sp", self.y_out[j * 128:(j + 1) * 128, :], o[:], [("ob", j % 2)], [("yout", j)],
                     semkey=("yout", j % 2))
        self.release(m)


def make_consts():
    c = np.zeros((128, 512), np.float32)
    c[:, 0:128] = np.eye(128, dtype=np.float32)
    ii = np.arange(128)
    c[:, 128:256] = (ii[:, None] <= ii[None, :]).astype(np.float32)
    c[:, 256:384] = 1.0
    for g, w in enumerate((2, 4, 8, 16)):
        c[:, 384 + g * 16: 384 + (g + 1) * 16] = 1.0 / np.minimum(np.arange(16) + 1.0, float(w))
    return c


def host_inputs(b, seq, x, c, positions, mod_w, mod_b, norm_g, ffn_w13, ffn_w2, ab_w_in, pool_w, pool_scale,
                ssd_conv_w, ssd_conv_b, ssd_dt_bias, ssd_a_log, ssd_d, ssd_norm_g, ab_w_out, mla_w_in,
                mla_q_norm_g, mla_w_uq, mla_kv_norm_g, mla_w_ukv, mla_w_o, final_norm_g):
    f = np.float32
    m = {}
    m["x"] = np.ascontiguousarray(x[b], dtype=f)
    m["c"] = np.ascontiguousarray(c[b].reshape(KC, 128).T, dtype=f)
    m["pos"] = np.ascontiguousarray(positions[b].reshape(1, seq), dtype=np.int32)
    m["mod_w"] = np.ascontiguousarray(mod_w, dtype=f)
    m["mod_b"] = np.ascontiguousarray(mod_b, dtype=f)
    m["ngP"] = np.ascontiguousarray(norm_g.reshape(2, 3, KC, 128).transpose(3, 0, 1, 2).reshape(128, 48), dtype=f)
    m["ffn_w13"] = np.ascontiguousarray(ffn_w13.reshape(4, D, 2 * DFF), dtype=f)
    m["ffn_w2"] = np.ascontiguousarray(ffn_w2.reshape(4, DFF, D), dtype=f)
    m["ab_w_in"] = np.ascontiguousarray(ab_w_in[0], dtype=f)
    m["pool_w"] = np.ascontiguousarray(pool_w[0].reshape(512, 128), dtype=f)
    abP = np.zeros((128, 80), f)
    abP[:, 0:4] = pool_scale[0].reshape(4, 128).T
    abP[:, 4:52] = ssd_conv_w[0].reshape(4, 12, 128).transpose(2, 1, 0).reshape(128, 48)
    abP[:, 52:64] = ssd_conv_b[0].reshape(12, 128).T
    abP[:, 64:72] = ssd_norm_g[0].reshape(8, 128).T
    m["abP"] = abP
    abR = np.zeros((1, 64), f)
    abR[0, 0:16] = ssd_dt_bias[0]
    abR[0, 16:32] = ssd_a_log[0]
    abR[0, 32:48] = ssd_d[0]
    m["abR"] = abR
    sel = np.zeros((16, 16, 128), f)
    for h in range(16):
        sel[h, h, :] = 1.0
    m["sel"] = sel.reshape(16, 2048)
    ii = np.arange(128)
    negm = np.where(ii[None, :] < ii[:, None], -30000.0, 0.0).astype(f)
    m["cmask"] = np.ascontiguousarray(np.tile(negm, (1, 4)))
    m["ab_w_out"] = np.ascontiguousarray(ab_w_out[0], dtype=f)
    m["mla_w_in"] = np.ascontiguousarray(mla_w_in[0], dtype=f)
    wq = mla_w_uq[0].reshape(768, 16, 96)
    m["mla_w_uq"] = np.ascontiguousarray(np.concatenate(
        [wq[:, :, 0:64].reshape(768, 1024), wq[:, :, 64:80].reshape(768, 256), wq[:, :, 80:96].reshape(768, 256)], axis=1), dtype=f)
    wkv = mla_w_ukv[0].reshape(256, 16, 128)
    m["mla_w_ukv"] = np.ascontiguousarray(np.concatenate(
        [wkv[:, :, 0:64].reshape(256, 1024), wkv[:, :, 64:128].reshape(256, 1024)], axis=1), dtype=f)
    m["mla_w_o"] = np.ascontiguousarray(mla_w_o[0], dtype=f)
    mp = np.zeros((128, 16), f)
    mp[:, 0:6] = mla_q_norm_g[0].reshape(6, 128).T
    mp[:, 6:8] = mla_kv_norm_g[0].reshape(2, 128).T
    mp[:, 8] = np.tile(INV_FREQ, 8)
    m["mlaP"] = mp
    m["final_g"] = np.ascontiguousarray(final_norm_g.reshape(1, D), dtype=f)
    m["consts"] = make_consts()
    return m


_NC_CACHE = {}


def kernel(**inputs):
    inputs = {k: np.asarray(v) for k, v in inputs.items()}
    B, seq, _ = inputs["x"].shape
    key = (seq,)
    if key not in _NC_CACHE:
        _NC_CACHE[key] = Builder(seq).build()
    nc = _NC_CACHE[key]
    in_maps = [host_inputs(b, seq, **inputs) for b in range(B)]
    res = run_bass_kernel_spmd(nc, in_maps, core_ids=list(range(B)))
    return np.stack([np.asarray(r["y"], dtype=np.float32) for r in res.results], axis=0)
```

```python
import numpy as np
import concourse.bass as bass
import concourse.mybir as mybir
from concourse.bass_utils import run_bass_kernel_spmd

F32 = mybir.dt.float32
BF16 = mybir.dt.bfloat16
I32 = mybir.dt.int32
AF = mybir.ActivationFunctionType
ALU = mybir.AluOpType
AX = mybir.AxisListType

D = 1024
KC = 8
DFF = 2816
NF = 22
EPS = 1e-6
SEQ_FULL = 4096
IN_AB = 3088
IN_MLA = 1056

COMPUTE = ("pe", "act", "dve", "pool")
INV_FREQ = (np.float32(10000.0) ** (-np.arange(0, 32, 2, dtype=np.float32) / np.float32(32))).astype(np.float32)


class Op:
    __slots__ = ("q", "fn", "reads", "writes", "dma", "semkey", "deps", "signal", "idx", "need_sig")


class Sched:
    def __init__(self, nc):
        self.nc = nc
        self.ops = []
        self.last_w = {}
        self.readers = {}
        self.sems = {}
        self.dma_count = {}
        self.bar_deps = []

    def op(self, q, fn, reads=(), writes=(), dma=False, semkey=None):
        o = Op()
        o.q, o.fn, o.reads, o.writes, o.dma = q, fn, tuple(reads), tuple(writes), dma
        o.idx = len(self.ops)
        o.need_sig = False
        deps = set(self.bar_deps)
        for r in o.reads:
            w = self.last_w.get(r)
            if w is not None:
                deps.add(w)
        for w_ in o.writes:
            w = self.last_w.get(w_)
            if w is not None:
                deps.add(w)
            for rd in self.readers.get(w_, ()):
                deps.add(rd)
        deps.discard(o.idx)
        o.deps = deps
        if dma:
            if semkey is None:
                semkey = ("dma",) + tuple(o.writes[:1])
            o.semkey = semkey
            n = self.dma_count.get(semkey, 0) + 1
            self.dma_count[semkey] = n
            o.signal = (semkey, 16 * n)
        else:
            o.semkey = q
            o.signal = None
        for r in o.reads:
            self.readers.setdefault(r, []).append(o.idx)
        for w_ in o.writes:
            self.last_w[w_] = o.idx
            self.readers[w_] = []
        self.ops.append(o)
        return o

    def barrier(self):
        lastq = {}
        for o in self.ops:
            if o.dma:
                lastq[("d", o.semkey)] = o.idx
            elif o.fn is not None:
                lastq[("c", o.q)] = o.idx
        self.bar_deps = list(lastq.values())
        self.last_w = {}
        self.readers = {}

    def emit(self):
        nc = self.nc
        ops = self.ops
        for o in ops:
            for d in o.deps:
                p = ops[d]
                if p.q == "pe" and o.q == "pe" and not p.dma:
                    continue
                p.need_sig = True
        cnt = {q: 0 for q in COMPUTE}
        for o in ops:
            if not o.dma and o.fn is not None and o.need_sig:
                cnt[o.q] += 1
                o.signal = (o.q, cnt[o.q])
        def sem(key):
            s = self.sems.get(key)
            if s is None:
                s = nc.alloc_semaphore("s%d" % len(self.sems))
                self.sems[key] = s
            return s
        queues = {}
        for o in ops:
            queues.setdefault(o.q, []).append(o)
        dma_before = {}
        run = {}
        for o in ops:
            dma_before[o.idx] = dict(run) if False else None
        dma_positions = {}
        for o in ops:
            if o.dma:
                dma_positions.setdefault(o.semkey, []).append(o.idx)
        import bisect

        def emit_queue(qname, eng):
            waited = {}
            for o in queues.get(qname, []):
                need = {}
                for d in o.deps:
                    p = ops[d]
                    if p.dma:
                        pos = dma_positions[p.semkey]
                        n = bisect.bisect_left(pos, o.idx)
                        key, val = p.semkey, 16 * n
                    else:
                        if p.fn is None:
                            continue
                        if p.q == "pe" and o.q == "pe" and not o.dma:
                            continue
                        key, val = p.signal
                    if need.get(key, 0) < val:
                        need[key] = val
                for key, val in need.items():
                    if waited.get(key, 0) >= val:
                        continue
                    waited[key] = val
                    eng.wait_ge(sem(key), val)
                if o.fn is None:
                    continue
                ins = o.fn(eng)
                if o.dma:
                    ins.then_inc(sem(o.semkey), 16)
                elif o.need_sig:
                    ins.then_inc(sem(o.q), 1)

        with nc.Block() as block:
            @block.tensor
            def _(e):
                emit_queue("pe", e)

            @block.scalar
            def _(e):
                emit_queue("act", e)

            @block.vector
            def _(e):
                emit_queue("dve", e)

            @block.gpsimd
            def _(e):
                emit_queue("pool", e)

            @block.sync
            def _(e):
                emit_queue("sp", e)


class Builder:
    def __init__(self, seq, subs=None, debug_out=None):
        self.seq = seq
        self.NT = seq // 512
        self.NB = seq // 128
        self.subs = subs
        nc = bass.Bass("TRN2", target_bir_lowering=False)
        self.nc = nc
        self.S = Sched(nc)
        self.sb_off = 16640
        self.sb_top = 229376
        self.nalloc = 0

    def sb(self, name, shape, dtype):
        esz = 4 if dtype in (F32, I32) else 2
        n = 1
        for s in shape[1:]:
            n *= s
        nbytes = (n * esz + 63) // 64 * 64
        off = self.sb_off
        assert off + nbytes <= self.sb_top, "SBUF overflow at %s: need %d have %d" % (name, nbytes, self.sb_top - off)
        self.sb_off += nbytes
        self.nalloc += 1
        return self.nc.alloc_sbuf_tensor_at("%s_%d" % (name, self.nalloc), list(shape), dtype, offset=off)

    def mark(self):
        return self.sb_off

    def release(self, m):
        self.sb_off = m

    def dram(self, name, shape, dtype, kind="Internal"):
        return self.nc.dram_tensor(name, list(shape), dtype, kind=kind).ap()

    def mm(self, out, lhsT, rhs, start, stop, reads, writes):
        self.S.op("pe", lambda e: e.matmul(out, lhsT, rhs, start=start, stop=stop), reads, writes)

    def tr(self, out, in_, ident, reads, writes):
        self.S.op("pe", lambda e: e.transpose(out, in_, ident), reads, writes)

    def act(self, out, in_, func, reads, writes, bias=None, scale=None, accum_out=None):
        kw = {}
        if bias is not None:
            kw["bias"] = bias
        if scale is not None:
            kw["scale"] = scale
        if accum_out is not None:
            kw["accum_out"] = accum_out
        self.S.op("act", lambda e: e.activation(out=out, in_=in_, func=func, **kw), reads, writes)

    def vec(self, q, method, reads, writes, *args, **kw):
        self.S.op(q, lambda e: getattr(e, method)(*args, **kw), reads, writes)

    def dma(self, q, out, in_, reads, writes, semkey=None):
        self.S.op(q, lambda e: e.dma_start(out=out, in_=in_), reads, writes, dma=True, semkey=semkey)

    def build(self):
        nc = self.nc
        seq, NT, NB = self.seq, self.NT, self.NB
        S = self.S
        x_in = self.dram("x", [seq, D], F32, "ExternalInput")
        c_in = self.dram("c", [128, KC], F32, "ExternalInput")
        pos_in = self.dram("pos", [1, seq], I32, "ExternalInput")
        mod_w = self.dram("mod_w", [2, D, 9 * D], F32, "ExternalInput")
        mod_b = self.dram("mod_b", [2, 9 * D], F32, "ExternalInput")
        ngP_in = self.dram("ngP", [128, 2 * 3 * KC], F32, "ExternalInput")
        w13_in = self.dram("ffn_w13", [4, D, 2 * DFF], F32, "ExternalInput")
        w2_in = self.dram("ffn_w2", [4, DFF, D], F32, "ExternalInput")
        abin_in = self.dram("ab_w_in", [D, IN_AB], F32, "ExternalInput")
        poolw_in = self.dram("pool_w", [512, 128], F32, "ExternalInput")
        abP_in = self.dram("abP", [128, 80], F32, "ExternalInput")
        abR_in = self.dram("abR", [1, 64], F32, "ExternalInput")
        about_in = self.dram("ab_w_out", [1536, D], F32, "ExternalInput")
        mlain_in = self.dram("mla_w_in", [D, IN_MLA], F32, "ExternalInput")
        wuq_in = self.dram("mla_w_uq", [768, 1536], F32, "ExternalInput")
        wukv_in = self.dram("mla_w_ukv", [256, 2048], F32, "ExternalInput")
        wo_in = self.dram("mla_w_o", [D, D], F32, "ExternalInput")
        mlaP_in = self.dram("mlaP", [128, 16], F32, "ExternalInput")
        fin_in = self.dram("final_g", [1, D], F32, "ExternalInput")
        cst_in = self.dram("consts", [128, 512], F32, "ExternalInput")
        sel_in = self.dram("sel", [16, 2048], F32, "ExternalInput")
        cmask_in = self.dram("cmask", [128, 512], F32, "ExternalInput")
        self.xs_d = self.dram("xs_d", [seq, D], F32)
        self.ins = dict(abin=abin_in, poolw=poolw_in, abP=abP_in, abR=abR_in, about=about_in, mlain=mlain_in,
                        wuq=wuq_in, wukv=wukv_in, wo=wo_in, mlaP=mlaP_in, sel=sel_in, cmask=cmask_in, pos=pos_in)
        y_out = self.dram("y", [seq, D], F32, "ExternalOutput")
        self.x_in, self.y_out = x_in, y_out

        w13_s = self.dram("w13_s", [4, D, 2 * DFF], BF16)
        w2_s = self.dram("w2_s", [4, DFF, D], BF16)
        mod_d = self.dram("mod_d", [2, 9 * D], F32)
        self.w13_s, self.w2_s, self.mod_d = w13_s, w2_s, mod_d

        self.ident = self.sb("ident", [128, 128], BF16)
        self.cst = self.sb("cst", [128, 512], F32)
        self.modP = self.sb("modP", [128, 2, 9, KC], F32)
        self.ngP = self.sb("ngP", [128, 2, 3, KC], F32)
        self.aP = self.sb("aP", [128, 2, 3, KC], F32)
        self.stat = self.sb("stat", [128, 64], F32)
        self.x_off = self.sb_off
        self.xres = self.sb("xres", [128, max(NB, 32), D], F32)
        self.ps = [nc.alloc_psum_tensor("ps%d" % i, [128, 512], F32) for i in range(8)]

        self.dma("sp", self.cst[:], cst_in, ["cst_in"], ["cst"])
        self.vec("dve", "tensor_copy", ["cst"], ["ident"], out=self.ident[:], in_=self.cst[:, 0:128])
        self.dma("sp", self.ngP[:].rearrange("p a b c -> p (a b c)"), ngP_in, [], ["ngP"])
        for j in range(NB):
            self.dma("sp", self.xres[:, j, :], x_in[j * 128:(j + 1) * 128, :], [], [("x", j)], semkey=("xload", j % 4))
        self.modulation(c_in, mod_w, mod_b)
        for i in range(4):
            self.precast(w13_s[i], w13_in[i], D, 2 * DFF, ("w13s", i))
            self.precast(w2_s[i], w2_in[i], DFF, D, ("w2s", i))

        def want(l, s):
            return self.subs is None or (l, s) in self.subs

        for l in range(2):
            if want(l, 0):
                self.ffn(l, 0)
            if want(l, 1):
                self.spill_x()
                if l == 0:
                    self.mixer_ab()
                else:
                    self.mixer_mla()
                self.reload_x()
            if want(l, 2):
                self.ffn(l, 1)
        self.final_norm(fin_in)
        S.op("sp", None, reads=[("yout", j) for j in range(NB)])
        S.emit()
        return nc

    def precast(self, dst, src, rows, cols, res):
        a = rows // 128
        half = max(1, a // 2)
        for h0 in range(0, a, half):
            h1 = min(a, h0 + half)
            d = dst.rearrange("(p a) c -> p a c", p=128)[:, h0:h1, :]
            s = src.rearrange("(p a) c -> p a c", p=128)[:, h0:h1, :]
            self.dma("pool", d, s, [], [res], semkey=("pc",) + tuple(res))

    def modulation(self, c_in, mod_w, mod_b):
        m = self.mark()
        cact = self.sb("cact", [128, KC], F32)
        mrow = self.sb("mrow", [1, 9 * D], F32)
        modT = self.sb("modT", [128, 128], F32)
        wst = [self.sb("modw%d" % i, [128, KC, 512], F32) for i in range(2)]
        self.dma("sp", cact[:], c_in, [], ["cact"])
        self.act(cact[:], cact[:], AF.Silu, ["cact"], ["cact"])
        for l in range(2):
            self.dma("sp", mrow[:], mod_b[l:l + 1, :], [], ["mrow"])
            for cb in range(18):
                slot = (l * 18 + cb) % 2
                self.dma("sp", wst[slot][:], mod_w[l, :, cb * 512:(cb + 1) * 512].rearrange("(k p) c -> p k c", p=128),
                         [], [("modw", slot)])
                pst = self.ps[cb % 2]
                for k in range(KC):
                    self.mm(pst[0:1, :], cact[:, k:k + 1], wst[slot][:, k, :], k == 0, k == KC - 1,
                            ["cact", ("modw", slot)], [("ps", cb % 2)])
                self.vec("dve", "tensor_tensor", [("ps", cb % 2), "mrow"], ["mrow"],
                         out=mrow[0:1, cb * 512:(cb + 1) * 512], in0=pst[0:1, :], in1=mrow[0:1, cb * 512:(cb + 1) * 512],
                         op=ALU.add)
            self.dma("sp", self.mod_d[l:l + 1, :], mrow[:], ["mrow"], [("mod_d", l)])
            self.dma("sp", modT[0:72, :], self.mod_d[l, :].rearrange("(r p) -> r p", p=128), [("mod_d", l)], ["modT"])
            self.tr(self.ps[2][:, 0:72], modT[0:72, :], self.cst[0:72, 0:72], ["modT", "cst"], [("ps", 2)])
            self.vec("dve", "tensor_copy", [("ps", 2)], ["modP"],
                     out=self.modP[:, l, :, :].rearrange("p j k -> p (j k)"), in_=self.ps[2][:, 0:72])
        for l in range(2):
            for s in range(3):
                self.vec("dve", "scalar_tensor_tensor", ["modP", "ngP"], ["aP"],
                         out=self.aP[:, l, s, :], in0=self.modP[:, l, 3 * s + 1, :], scalar=1.0,
                         in1=self.ngP[:, l, s, :], op0=ALU.add, op1=ALU.mult)
        self.S.barrier()
        self.release(m)

    def norm_mod_T(self, t, l, s, hT, hres, xn, junk, xsrc=None, xr=None):
        xts, xrs = [], []
        for jj in range(4):
            j = t * 4 + jj
            xt = self.xres[:, j, :] if xsrc is None else xsrc(jj)
            xres_ = ("x", j) if xr is None else xr(jj)
            xts.append(xt)
            xrs.append(xres_)
            if junk is None:
                self.act(xn[:, jj, :], xt, AF.Square, [xres_], [("xn", jj), "ss"], accum_out=self.stat[:, jj:jj + 1])
            else:
                self.act(junk[:], xt, AF.Square, [xres_], ["junk", "ss"], accum_out=self.stat[:, jj:jj + 1])
        rs4 = self.stat[:, 8:12]
        self.vec("dve", "tensor_scalar", ["ss"], ["rs"], out=rs4, in0=self.stat[:, 0:4], scalar1=1.0 / D, scalar2=EPS,
                 op0=ALU.mult, op1=ALU.add)
        self.act(rs4, rs4, AF.Sqrt, ["rs"], ["rs"])
        self.vec("dve", "reciprocal", ["rs"], ["rs"], out=rs4, in_=rs4)
        for jj in range(4):
            self.vec("dve", "tensor_scalar", [xrs[jj], "rs"], [("xn", jj)], out=xn[:, jj, :], in0=xts[jj],
                     scalar1=self.stat[:, 8 + jj:9 + jj], scalar2=None, op0=ALU.mult)
        for k in range(KC):
            bank = 4 + (k % 4)
            pv = self.ps[bank][:].bitcast(BF16)
            for jj in range(4):
                self.tr(pv[:, jj * 128:(jj + 1) * 128], xn[:, jj, k * 128:(k + 1) * 128], self.ident[:],
                        [("xn", jj), "ident"], [("ps", bank)])
            self.act(hT[:, k, :], pv[:, 0:512], AF.Identity, [("ps", bank), "aP", "modP"], [hres],
                     scale=self.aP[:, l, s, k:k + 1], bias=self.modP[:, l, 3 * s, k:k + 1])

    def load_gate(self, G, l, s):
        self.dma("sp", G[:], self.mod_d[l:l + 1, (3 * s + 2) * D:(3 * s + 3) * D].partition_broadcast(128),
                 [("mod_d", l)], ["G"])

    def ffn(self, l, s2):
        S = self.S
        fi = l * 2 + s2
        s = 0 if s2 == 0 else 2
        NT = self.NT
        m = self.mark()
        hT = self.sb("hT", [128, KC, 512], BF16)
        gT = self.sb("gT", [128, NF, 512], BF16)
        xn = self.sb("xn", [128, 4, D], BF16)
        junk = self.sb("junk", [128, D], BF16)
        G = self.sb("G", [128, D], F32)
        sg = [self.sb("sg%d" % i, [128, 512], F32) for i in range(2)]
        tmp = [self.sb("tmp0", [128, 512], F32)] * 2
        w13 = [self.sb("w13_%d" % i, [128, KC, 2, 256], BF16) for i in range(2)]
        w2g = [3, 3, 3, 3, 3, 3, 2, 2]
        w2o = [0, 3, 6, 9, 12, 15, 18, 20]
        w2 = [self.sb("w2_%d" % i, [128, 3, 512], BF16) for i in range(2)]
        self.load_gate(G, l, s)
        pieces = []
        for t in range(NT):
            for pc in range(11):
                pieces.append(("w13", pc))
            for n in range(2):
                for g in range(8):
                    pieces.append(("w2", n, g))
        loaded = [0]
        cnt = {"w13": 0, "w2": 0}
        slot_of = {}

        def ensure(i):
            while loaded[0] < len(pieces) and loaded[0] <= i + 1:
                p = pieces[loaded[0]]
                kind = p[0]
                sl = cnt[kind] % 2
                cnt[kind] += 1
                slot_of[loaded[0]] = sl
                if kind == "w13":
                    pc = p[1]
                    for ab in range(2):
                        src = self.w13_s[fi, :, ab * DFF + pc * 256: ab * DFF + (pc + 1) * 256]
                        self.dma("sp", w13[sl][:, :, ab, :], src.rearrange("(k p) c -> p k c", p=128),
                                 [("w13s", fi)], [("w13", sl, ab)])
                else:
                    n, g = p[1], p[2]
                    src = self.w2_s[fi, w2o[g] * 128:(w2o[g] + w2g[g]) * 128, n * 512:(n + 1) * 512]
                    self.dma("sp", w2[sl][:, 0:w2g[g], :], src.rearrange("(f p) c -> p f c", p=128),
                             [("w2s", fi)], [("w2", sl)])
                loaded[0] += 1

        pi = 0
        for t in range(NT):
            self.norm_mod_T(t, l, s, hT, "hT", xn, junk)
            for pc in range(11):
                ensure(pi)
                sl = slot_of[pi]
                pi += 1
                for ff in range(2):
                    f = pc * 2 + ff
                    pa, pb = self.ps[ff * 2], self.ps[ff * 2 + 1]
                    for k in range(KC):
                        self.mm(pa[:], w13[sl][:, k, 0, ff * 128:(ff + 1) * 128], hT[:, k, :], k == 0, k == KC - 1,
                                [("w13", sl, 0), "hT"], [("ps", ff * 2)])
                    for k in range(KC):
                        self.mm(pb[:], w13[sl][:, k, 1, ff * 128:(ff + 1) * 128], hT[:, k, :], k == 0, k == KC - 1,
                                [("w13", sl, 1), "hT"], [("ps", ff * 2 + 1)])
                    self.act(sg[ff][:], pa[:], AF.Silu, [("ps", ff * 2)], [("sg", ff)])
                    self.vec("dve", "tensor_tensor", [("sg", ff), ("ps", ff * 2 + 1)], [("gT", f)],
                             out=gT[:, f, :], in0=sg[ff][:], in1=pb[:], op=ALU.mult)
            for n in range(2):
                for g in range(8):
                    ensure(pi)
                    sl = slot_of[pi]
                    pi += 1
                    for fl in range(w2g[g]):
                        f = w2o[g] + fl
                        for jj in range(4):
                            self.mm(self.ps[4 + jj][:], gT[:, f, jj * 128:(jj + 1) * 128], w2[sl][:, fl, :],
                                    f == 0, f == NF - 1, [("gT", f), ("w2", sl)], [("ps", 4 + jj)])
                for jj in range(4):
                    j = t * 4 + jj
                    xs = self.xres[:, j, n * 512:(n + 1) * 512]
                    self.vec("dve", "tensor_tensor", [("ps", 4 + jj), "G"], [("tmp", 0)],
                             out=tmp[jj % 2][:], in0=self.ps[4 + jj][:], in1=G[:, n * 512:(n + 1) * 512], op=ALU.mult)
                    self.vec("dve", "scalar_tensor_tensor", [("tmp", 0), ("x", j)], [("x", j)],
                             out=xs, in0=tmp[jj % 2][:], scalar=0.5, in1=xs, op0=ALU.mult, op1=ALU.add)
        S.barrier()
        self.release(m)

    def spill_x(self):
        for j in range(self.NB):
            self.dma("sp", self.xs_d[j * 128:(j + 1) * 128, :], self.xres[:, j, :], [("x", j)], [("xd", j)],
                     semkey=("xsp", j % 4))
        self.S.barrier()

    def reload_x(self):
        self.S.barrier()
        for j in range(self.NB):
            self.dma("sp", self.xres[:, j, :], self.xs_d[j * 128:(j + 1) * 128, :], [("xd", j)], [("x", j)],
                     semkey=("xload", j % 4))

    def mixer_ab(self):
        S = self.S
        ins = self.ins
        l, s = 0, 1
        NT = self.NT
        keep = self.sb_off
        self.sb_off = self.x_off
        cst = self.cst
        w_in = self.sb("abwin", [128, KC, IN_AB], BF16)
        w_out = self.sb("abwout", [128, 12, D], BF16)
        pw = self.sb("poolw", [128, 4, 128], BF16)
        abP = self.sb("abP", [128, 80], F32)
        rowp = self.sb("rowp", [128, 48], F32)
        a_bc = self.sb("a_bc", [128, 16], F32)
        sel = self.sb("sel", [16, 16, 128], BF16)
        negm = self.sb("negm", [128, 512], BF16)
        G = self.sb("G", [128, D], F32)
        hT = self.sb("hT", [128, KC, 512], BF16)
        xn = self.sb("xn", [128, 4, D], BF16)
        junk = self.sb("junk", [128, D], BF16)
        xt = [self.sb("xt0", [128, 4, D], F32)] * 2
        Up = self.sb("Up", [128, 528], F32)
        sA = self.sb("sA", [128, 528], F32)
        sB = self.sb("sB", [128, 528], F32)
        phalo = self.sb("phalo", [128, 4, 16], F32)
        dT = [self.sb("dT%d" % i, [128, 512], BF16) for i in range(2)]
        dfix = self.sb("dfix", [128, 16], F32)
        ypT = self.sb("ypT", [128, 4, 512], BF16)
        Uc = [self.sb("Uc0", [128, 515], F32)] * 2
        chalo = self.sb("chalo", [128, 12, 3], F32)
        acc = [self.sb("acc0", [128, 512], F32)] * 2
        xbcT = self.sb("xbcT", [128, 12, 512], BF16)
        zs = self.sb("zs", [128, D], F32)
        dtt = self.sb("dtt", [128, 96], F32)
        sm2 = self.sb("sm2", [128, 48], F32)
        acsT = self.sb("acsT", [16, 3, 128], F32)
        acsHL = self.sb("acsHL", [16, 2, 128], BF16)
        xs_tok = self.sb("xs_tok", [128, D], BF16)
        xw = self.sb("xw", [128, D], BF16)
        xsD = self.sb("xsD", [128, D], BF16)
        Btok = self.sb("Btok", [128, 2, 128], BF16)
        cb = self.sb("cb", [128, 2, 128], F32)
        Eexp = [self.sb("Eexp%d" % i, [128, 512], F32) for i in range(2)]
        Mt = [self.sb("Mt%d" % i, [128, 512], BF16) for i in range(2)]
        H = self.sb("H", [128, D], F32)
        Hb = self.sb("Hb", [128, D], BF16)
        yA = self.sb("yA", [128, D], F32)
        ssg = self.sb("ssg", [128, 4], F32)
        yn = self.sb("yn", [128, D], BF16)
        yT = self.sb("yT", [128, KC, 512], BF16)
        tmp = [self.sb("tmpo%d" % i, [128, 512], F32) for i in range(2)]
        ps = self.ps
        ident = self.ident

        self.dma("pool", w_in[:], ins["abin"].rearrange("(k p) c -> p k c", p=128), [], ["abwin"])
        self.dma("pool", w_out[:], ins["about"].rearrange("(k p) c -> p k c", p=128), [], ["abwout"])
        self.dma("pool", pw[:], ins["poolw"].rearrange("(g p) c -> p g c", p=128), [], ["poolw"])
        self.dma("pool", sel[:].rearrange("p a b -> p (a b)"), ins["sel"], [], ["sel"])
        self.dma("pool", negm[:], ins["cmask"], [], ["negm"])
        self.dma("sp", abP[:], ins["abP"], [], ["abP"])
        self.dma("sp", rowp[:], ins["abR"][0:1, 0:48].partition_broadcast(128), [], ["rowp"])
        self.act(a_bc[:], rowp[:, 16:32], AF.Exp, ["rowp"], ["a_bc"])
        self.vec("dve", "tensor_scalar", ["a_bc"], ["a_bc"], out=a_bc[:], in0=a_bc[:], scalar1=-1.0, scalar2=None,
                 op0=ALU.mult)
        self.load_gate(G, l, s)
        self.vec("dve", "memset", [], ["phalo"], phalo[:], 0.0)
        self.vec("dve", "memset", [], ["chalo"], chalo[:], 0.0)
        self.vec("dve", "memset", [], ["H"], H[:], 0.0)
        self.vec("dve", "memset", [], ["Hb"], Hb[:], 0.0)
        D_bc = rowp[:, 32:48]
        dtb_bc = rowp[:, 0:16]
        wins = (2, 4, 8, 16)

        for t in range(NT):
            xb = xt[t % 2]
            for jj in range(4):
                j = t * 4 + jj
                self.dma("sp", xb[:, jj, :], self.xs_d[j * 128:(j + 1) * 128, :], [("xd", j)], [("xt", 0, jj)],
                         semkey=("xtl", t % 2))
            self.norm_mod_T(t, l, s, hT, "hT", xn, junk, xsrc=lambda jj: xb[:, jj, :],
                            xr=lambda jj: ("xt", 0, jj))
            for g in range(4):
                bk = g % 2
                for k in range(KC):
                    self.mm(ps[bk][:], w_in[:, k, g * 128:(g + 1) * 128], hT[:, k, :], k == 0, k == KC - 1,
                            ["abwin", "hT"], [("ps", bk)])
                self.vec("dve", "tensor_copy", ["phalo"], ["Up"], out=Up[:, 0:16], in_=phalo[:, g, :])
                self.act(Up[:, 16:528], ps[bk][:], AF.Copy, [("ps", bk)], ["Up"])
                self.vec("dve", "tensor_copy", ["Up"], ["phalo"], out=phalo[:, g, :], in_=Up[:, 512:528])
                self.vec("dve", "tensor_tensor", ["Up"], ["sA"], out=sA[:, 1:528], in0=Up[:, 1:528], in1=Up[:, 0:527],
                         op=ALU.add)
                lvl = sA
                if g >= 1:
                    self.vec("dve", "tensor_tensor", ["sA"], ["sB"], out=sB[:, 3:528], in0=sA[:, 3:528],
                             in1=sA[:, 1:526], op=ALU.add)
                    lvl = sB
                if g >= 2:
                    self.vec("dve", "tensor_tensor", ["sB"], ["sA"], out=sA[:, 7:528], in0=sB[:, 7:528],
                             in1=sB[:, 3:524], op=ALU.add)
                    lvl = sA
                if g >= 3:
                    self.vec("dve", "tensor_tensor", ["sA"], ["sB"], out=sB[:, 15:528], in0=sA[:, 15:528],
                             in1=sA[:, 7:520], op=ALU.add)
                    lvl = sB
                lres = "sA" if lvl is sA else "sB"
                dd = dT[g % 2]
                self.vec("dve", "scalar_tensor_tensor", [lres, "Up"], [("dT", g % 2)], out=dd[:], in0=lvl[:, 16:528],
                         scalar=1.0 / wins[g], in1=Up[:, 16:528], op0=ALU.mult, op1=ALU.subtract)
                if t == 0:
                    self.vec("dve", "tensor_tensor", [lres, "cst"], ["dfix"], out=dfix[:], in0=lvl[:, 16:32],
                             in1=cst[:, 384 + g * 16:384 + (g + 1) * 16], op=ALU.mult)
                    self.vec("dve", "tensor_tensor", ["dfix", "Up"], [("dT", g % 2)], out=dd[:, 0:16], in0=dfix[:],
                             in1=Up[:, 16:32], op=ALU.subtract)
                self.mm(ps[2 + bk][:], pw[:, g, :], dd[:], True, True, ["poolw", ("dT", g % 2)], [("ps", 2 + bk)])
                self.act(ypT[:, g, :], ps[2 + bk][:], AF.Copy, [("ps", 2 + bk), "abP"], [("ypT", g)],
                         scale=abP[:, g:g + 1])
            for cc in range(12):
                bk = cc % 2
                c0 = 1536 + cc * 128
                for k in range(KC):
                    self.mm(ps[bk][:], w_in[:, k, c0:c0 + 128], hT[:, k, :], k == 0, k == KC - 1,
                            ["abwin", "hT"], [("ps", bk)])
                U = Uc[bk]
                ur = ("Uc", 0)
                self.vec("dve", "tensor_copy", ["chalo"], [ur], out=U[:, 0:3], in_=chalo[:, cc, :])
                self.act(U[:, 3:515], ps[bk][:], AF.Copy, [("ps", bk)], [ur])
                self.vec("dve", "tensor_copy", [ur], ["chalo"], out=chalo[:, cc, :], in_=U[:, 512:515])
                a_ = acc[bk]
                ar = ("acc", 0)
                self.vec("dve", "tensor_scalar", [ur, "abP"], [ar], out=a_[:], in0=U[:, 0:512],
                         scalar1=abP[:, 4 + cc * 4:5 + cc * 4], scalar2=abP[:, 52 + cc:53 + cc], op0=ALU.mult, op1=ALU.add)
                for kk in range(1, 4):
                    self.vec("dve", "scalar_tensor_tensor", [ur, "abP", ar], [ar], out=a_[:], in0=U[:, kk:kk + 512],
                             scalar=abP[:, 4 + cc * 4 + kk:5 + cc * 4 + kk], in1=a_[:], op0=ALU.mult, op1=ALU.add)
                self.act(xbcT[:, cc, :], a_[:], AF.Silu, [ar], [("xbcT", cc)])
            for jj in range(4):
                j = t * 4 + jj
                tok = slice(jj * 128, (jj + 1) * 128)
                for n in range(2):
                    for k in range(KC):
                        self.mm(ps[2 + n][:], hT[:, k, tok], w_in[:, k, 512 + n * 512:1024 + n * 512], k == 0,
                                k == KC - 1, ["hT", "abwin"], [("ps", 2 + n)])
                    self.act(zs[:, n * 512:(n + 1) * 512], ps[2 + n][:], AF.Silu, [("ps", 2 + n)], ["zs"])
                for k in range(KC):
                    self.mm(ps[4][:, 0:16], hT[:, k, tok], w_in[:, k, 3072:3088], k == 0, k == KC - 1,
                            ["hT", "abwin"], [("ps", 4)])
                dt_, lndt, adt, nb, wst, eacs = (dtt[:, i * 16:(i + 1) * 16] for i in range(6))
                cd_bc, acs_sb, arg = (sm2[:, i * 16:(i + 1) * 16] for i in range(3))
                self.vec("dve", "tensor_tensor", [("ps", 4), "rowp"], ["dt"], out=dt_, in0=ps[4][:, 0:16], in1=dtb_bc,
                         op=ALU.add)
                self.act(dt_, dt_, AF.Exp, ["dt"], ["dt"])
                self.act(dt_, dt_, AF.Ln, ["dt"], ["dt"], bias=1.0)
                self.act(lndt, dt_, AF.Ln, ["dt"], ["lndt"])
                self.vec("dve", "tensor_tensor", ["dt", "a_bc"], ["adt"], out=adt, in0=dt_, in1=a_bc[:], op=ALU.mult)
                self.mm(ps[4][:, 16:32], cst[:, 128:256], adt, True, True, ["cst", "adt"], [("ps", 4)])
                self.mm(ps[4][0:16, 128:256], adt, cst[:, 128:256], True, True, ["cst", "adt"], [("ps", 4)])
                self.mm(ps[4][:, 32:48], cst[:, 256:384], adt, True, True, ["cst", "adt"], [("ps", 4)])
                self.vec("dve", "tensor_copy", [("ps", 4)], ["acs_sb"], out=acs_sb, in_=ps[4][:, 16:32])
                self.vec("dve", "tensor_tensor", ["lndt", "acs_sb"], ["nb"], out=nb, in0=lndt, in1=acs_sb, op=ALU.subtract)
                self.act(eacs, ps[4][:, 16:32], AF.Exp, [("ps", 4)], ["eacs"])
                self.vec("dve", "tensor_tensor", [("ps", 4), "nb"], ["arg"], out=arg, in0=ps[4][:, 32:48], in1=nb,
                         op=ALU.add)
                self.act(wst, arg, AF.Exp, ["arg"], ["wst"])
                self.act(cd_bc, ps[4][:, 32:48], AF.Exp, [("ps", 4)], ["cd_bc"])
                self.vec("dve", "tensor_copy", [("ps", 4)], ["acsT0"], out=acsT[:, 0, :], in_=ps[4][0:16, 128:256])
                self.vec("dve", "tensor_copy", ["acsT0"], ["acsHL0"], out=acsHL[:, 0, :], in_=acsT[:, 0, :])
                self.vec("dve", "tensor_copy", ["acsHL0"], ["acsT1"], out=acsT[:, 1, :], in_=acsHL[:, 0, :])
                self.vec("dve", "tensor_tensor", ["acsT0", "acsT1"], ["acsHL1"], out=acsHL[:, 1, :], in0=acsT[:, 0, :],
                         in1=acsT[:, 1, :], op=ALU.subtract)
                pv5 = ps[5][:].bitcast(BF16)
                for cc in range(8):
                    self.tr(pv5[:, cc * 128:(cc + 1) * 128], xbcT[:, cc, tok], ident[:], [("xbcT", cc), "ident"],
                            [("ps", 5)])
                self.act(xs_tok[:], pv5[:, 0:1024], AF.Copy, [("ps", 5)], ["xs_tok"])
                self.vec("dve", "tensor_tensor", [("ps", 5), "wst"], ["xw"],
                         out=xw[:].rearrange("p (h q) -> p h q", h=16), in0=pv5[:, 0:1024].rearrange("p (h q) -> p h q", h=16),
                         in1=wst.unsqueeze(2).to_broadcast([128, 16, 64]), op=ALU.mult)
                self.vec("dve", "tensor_tensor", ["xs_tok", "rowp"], ["xsD"],
                         out=xsD[:].rearrange("p (h q) -> p h q", h=16), in0=xs_tok[:].rearrange("p (h q) -> p h q", h=16),
                         in1=D_bc.unsqueeze(2).to_broadcast([128, 16, 64]), op=ALU.mult)
                pv7 = ps[7][:].bitcast(BF16)
                for g in range(2):
                    self.tr(pv7[:, g * 128:(g + 1) * 128], xbcT[:, 8 + g, tok], ident[:], [("xbcT", 8 + g), "ident"],
                            [("ps", 7)])
                self.act(Btok[:].rearrange("p a b -> p (a b)"), pv7[:, 0:256], AF.Copy, [("ps", 7)], ["Btok"])
                for g in range(2):
                    self.mm(ps[4][:, 256 + g * 128:384 + g * 128], xbcT[:, 8 + g, tok], xbcT[:, 10 + g, tok], True, True,
                            [("xbcT", 8 + g), ("xbcT", 10 + g)], [("ps", 4)])
                self.vec("dve", "tensor_copy", [("ps", 4)], ["cb"], out=cb[:].rearrange("p a b -> p (a b)"),
                         in_=ps[4][:, 256:512])
                for g in range(2):
                    ydb = ps[g]
                    self.mm(ydb[:], ident[:], xsD[:, g * 512:(g + 1) * 512], True, False, ["ident", "xsD"], [("ps", g)])
                    for qq in range(2):
                        q4 = g * 2 + qq
                        eb = Eexp[q4 % 2]
                        mb = Mt[q4 % 2]
                        self.mm(ps[6][:], ident[:], negm[:], True, False, ["ident", "negm"], [("ps", 6)])
                        for hh in range(4):
                            h = q4 * 4 + hh
                            for part in range(2):
                                self.mm(ps[6][:, hh * 128:(hh + 1) * 128], sel[:, h, :], acsHL[:, part, :], False,
                                        hh == 3 and part == 1, ["sel", "acsHL0", "acsHL1"], [("ps", 6)])
                        for hh in range(4):
                            h = q4 * 4 + hh
                            self.act(eb[:, hh * 128:(hh + 1) * 128], ps[6][:, hh * 128:(hh + 1) * 128], AF.Exp,
                                     [("ps", 6), "nb"], [("Eexp", q4 % 2)], bias=nb[:, h:h + 1])
                        self.vec("dve", "tensor_tensor", [("Eexp", q4 % 2), "cb"], [("Mt", q4 % 2)],
                                 out=mb[:].rearrange("p (a b) -> p a b", a=4), in0=eb[:].rearrange("p (a b) -> p a b", a=4),
                                 in1=cb[:, g:g + 1, :].to_broadcast([128, 4, 128]), op=ALU.mult)
                        for hh in range(4):
                            h = q4 * 4 + hh
                            hl = h - g * 8
                            self.mm(ydb[:, hl * 64:(hl + 1) * 64], mb[:, hh * 128:(hh + 1) * 128],
                                    xs_tok[:, h * 64:(h + 1) * 64], False, qq == 1 and hh == 3,
                                    [("Mt", q4 % 2), "xs_tok"], [("ps", g)])
                for g in range(2):
                    gs = slice(g * 512, (g + 1) * 512)
                    self.mm(ps[2 + g][:], xbcT[:, 10 + g, tok], Hb[:, gs], True, True, [("xbcT", 10 + g), "Hb"],
                            [("ps", 2 + g)])
                    self.vec("dve", "tensor_tensor", [("ps", 2 + g), "eacs"], ["yA"],
                             out=yA[:, gs].rearrange("p (h q) -> p h q", h=8),
                             in0=ps[2 + g][:].rearrange("p (h q) -> p h q", h=8),
                             in1=eacs[:, g * 8:(g + 1) * 8].unsqueeze(2).to_broadcast([128, 8, 64]), op=ALU.mult)
                    self.vec("dve", "tensor_tensor", [("ps", g), "yA"], ["yA"], out=yA[:, gs], in0=ps[g][:], in1=yA[:, gs],
                             op=ALU.add)
                    self.mm(ps[7][:], Btok[:, g, :], xw[:, gs], True, True, ["Btok", "xw"], [("ps", 7)])
                    self.vec("dve", "tensor_tensor", ["H", "cd_bc"], ["H"], out=H[:, gs].rearrange("p (h q) -> p h q", h=8),
                             in0=H[:, gs].rearrange("p (h q) -> p h q", h=8),
                             in1=cd_bc[:, g * 8:(g + 1) * 8].unsqueeze(2).to_broadcast([128, 8, 64]), op=ALU.mult)
                    self.vec("dve", "tensor_tensor", [("ps", 7), "H"], ["H"], out=H[:, gs], in0=ps[7][:], in1=H[:, gs],
                             op=ALU.add)
                    self.act(Hb[:, gs], H[:, gs], AF.Copy, ["H"], ["Hb"])
                self.vec("dve", "tensor_tensor", ["yA", "zs"], ["yA"], out=yA[:], in0=yA[:], in1=zs[:], op=ALU.mult)
                for g in range(2):
                    gs = slice(g * 512, (g + 1) * 512)
                    self.act(junk[:, gs], yA[:, gs], AF.Square, ["yA"], ["junk", "ssg"], accum_out=ssg[:, g:g + 1])
                self.vec("dve", "tensor_scalar", ["ssg"], ["ssg"], out=ssg[:, 2:4], in0=ssg[:, 0:2], scalar1=1.0 / 512,
                         scalar2=EPS, op0=ALU.mult, op1=ALU.add)
                self.act(ssg[:, 2:4], ssg[:, 2:4], AF.Sqrt, ["ssg"], ["ssg"])
                self.vec("dve", "reciprocal", ["ssg"], ["ssg"], out=ssg[:, 2:4], in_=ssg[:, 2:4])
                for g in range(2):
                    gs = slice(g * 512, (g + 1) * 512)
                    self.vec("dve", "tensor_scalar", ["yA", "ssg"], ["yn"], out=yn[:, gs], in0=yA[:, gs],
                             scalar1=ssg[:, 2 + g:3 + g], scalar2=None, op0=ALU.mult)
                for cc in range(8):
                    self.tr(pv5[:, cc * 128:(cc + 1) * 128], yn[:, cc * 128:(cc + 1) * 128], ident[:], ["yn", "ident"],
                            [("ps", 5)])
                for cc in range(8):
                    self.act(yT[:, cc, tok], pv5[:, cc * 128:(cc + 1) * 128], AF.Copy, [("ps", 5), "abP"], [("yT", jj)],
                             scale=abP[:, 64 + cc:65 + cc])
            for jj in range(4):
                j = t * 4 + jj
                tok = slice(jj * 128, (jj + 1) * 128)
                for n in range(2):
                    pb = ps[6 + n]
                    for k in range(12):
                        lhs = ypT[:, k, tok] if k < 4 else yT[:, k - 4, tok]
                        rr = [("ypT", k)] if k < 4 else [("yT", jj)]
                        self.mm(pb[:], lhs, w_out[:, k, n * 512:(n + 1) * 512], k == 0, k == 11, rr + ["abwout"],
                                [("ps", 6 + n)])
                    xs_ = xb[:, jj, n * 512:(n + 1) * 512]
                    self.vec("dve", "tensor_tensor", [("ps", 6 + n), "G"], [("tmpo", n)], out=tmp[n][:], in0=pb[:],
                             in1=G[:, n * 512:(n + 1) * 512], op=ALU.mult)
                    self.vec("dve", "tensor_tensor", [("tmpo", n), ("xt", 0, jj)], [("xt", 0, jj)], out=xs_,
                             in0=tmp[n][:], in1=xs_, op=ALU.add)
                self.dma("sp", self.xs_d[j * 128:(j + 1) * 128, :], xb[:, jj, :], [("xt", 0, jj)], [("xd", j)],
                         semkey=("xts", t % 2))
        self.sb_off = keep


    def mixer_mla(self):
        import math
        S = self.S
        ins = self.ins
        l, s = 1, 1
        NT, NB, seq = self.NT, self.NB, self.seq
        keep = self.sb_off
        self.sb_off = self.x_off
        cst, ident, ps = self.cst, self.ident, self.ps
        QN_d = self.dram("QN_d", [1024, seq], BF16)
        QR_d = self.dram("QR_d", [512, seq], BF16)
        KN_d = self.dram("KN_d", [1024, seq], BF16)
        KR_d = self.dram("KR_d", [32, seq], BF16)
        V_d = self.dram("V_d", [seq, 1024], BF16)
        OT_d = self.dram("OT_d", [1024, seq], BF16)
        w_in = self.sb("mwin", [128, KC, IN_MLA], BF16)
        w_uq = self.sb("mwuq", [128, 6, 1536], BF16)
        w_ukv = self.sb("mwukv", [128, 2, 2048], BF16)
        w_o = self.sb("mwo", [128, KC, D], BF16)
        mlaP = self.sb("mlaP", [128, 16], F32)
        G = self.sb("G", [128, D], F32)
        tri = self.sb("tri", [128, 128], BF16)
        xt = self.sb("xt", [128, 4, D], F32)
        self.dma("pool", w_in[:], ins["mlain"].rearrange("(k p) c -> p k c", p=128), [], ["mwin"])
        self.dma("pool", w_uq[:], ins["wuq"].rearrange("(k p) c -> p k c", p=128), [], ["mwuq"])
        self.dma("pool", w_ukv[:], ins["wukv"].rearrange("(k p) c -> p k c", p=128), [], ["mwukv"])
        self.dma("pool", w_o[:], ins["wo"].rearrange("(k p) c -> p k c", p=128), [], ["mwo"])
        self.dma("sp", mlaP[:], ins["mlaP"], [], ["mlaP"])
        self.load_gate(G, l, s)
        self.vec("dve", "tensor_copy", ["cst"], ["tri"], out=tri[:], in_=cst[:, 128:256])
        mA = self.mark()
        hT = self.sb("hT", [128, KC, 512], BF16)
        xn = self.sb("xn", [128, 4, D], BF16)
        junk = self.sb("junk", [128, D], BF16)
        qnT = self.sb("qnT", [128, 6, 512], BF16)
        kvnT = self.sb("kvnT", [128, 2, 512], BF16)
        posi = self.sb("posi", [128, 512], I32)
        ang = self.sb("ang", [128, 512], F32)
        rr = [self.sb("rr%d" % i, [128, 512], F32) for i in range(2)]
        cosT = self.sb("cosT", [128, 512], F32)
        sinT = self.sb("sinT", [128, 512], F32)
        x1s = self.sb("x1s", [128, 512], F32)
        x2s = self.sb("x2s", [128, 512], F32)
        t1 = self.sb("t1", [128, 512], F32)
        t2 = self.sb("t2", [128, 512], F32)
        ro = [self.sb("ro%d" % i, [128, 512], BF16) for i in range(2)]
        qsb = [self.sb("qsb%d" % i, [128, 512], BF16) for i in range(2)]
        vsb = [self.sb("vsb%d" % i, [128, D], BF16) for i in range(2)]
        st = self.sb("mst", [128, 16], F32)
        PI = math.pi
        pib = self.sb("pib", [128, 1], F32)
        self.vec("dve", "memset", [], ["pib"], pib[:], PI)

        def rope(x1ps, x2ps, P, res1, res2, o1, o2, o1res, o2res):
            self.act(x1s[0:P, :], x1ps, AF.Copy, [res1], ["x1s"])
            self.act(x2s[0:P, :], x2ps, AF.Copy, [res2], ["x2s"])
            self.vec("dve", "tensor_tensor", ["x1s", "cosT"], ["t1"], out=t1[0:P, :], in0=x1s[0:P, :], in1=cosT[0:P, :], op=ALU.mult)
            self.vec("dve", "tensor_tensor", ["x2s", "sinT"], ["t2"], out=t2[0:P, :], in0=x2s[0:P, :], in1=sinT[0:P, :], op=ALU.mult)
            self.vec("dve", "tensor_tensor", ["t1", "t2"], [o1res], out=o1, in0=t1[0:P, :], in1=t2[0:P, :], op=ALU.subtract)
            self.vec("dve", "tensor_tensor", ["x2s", "cosT"], ["t1"], out=t1[0:P, :], in0=x2s[0:P, :], in1=cosT[0:P, :], op=ALU.mult)
            self.vec("dve", "tensor_tensor", ["x1s", "sinT"], ["t2"], out=t2[0:P, :], in0=x1s[0:P, :], in1=sinT[0:P, :], op=ALU.mult)
            self.vec("dve", "tensor_tensor", ["t1", "t2"], [o2res], out=o2, in0=t1[0:P, :], in1=t2[0:P, :], op=ALU.add)

        for t in range(NT):
            cols = slice(t * 512, (t + 1) * 512)
            for jj in range(4):
                j = t * 4 + jj
                self.dma("sp", xt[:, jj, :], self.xs_d[j * 128:(j + 1) * 128, :], [("xd", j)], [("xt", jj)], semkey=("xtl", 0))
            self.norm_mod_T(t, l, s, hT, "hT", xn, junk, xsrc=lambda jj: xt[:, jj, :], xr=lambda jj: ("xt", jj))
            self.dma("sp", posi[:], ins["pos"][0:1, cols].partition_broadcast(128), [], ["posi"])
            self.vec("dve", "tensor_copy", ["posi"], ["ang"], out=ang[:], in_=posi[:])
            self.vec("dve", "tensor_scalar", ["ang", "mlaP"], ["ang"], out=ang[:], in0=ang[:], scalar1=mlaP[:, 8:9], scalar2=None,
                     op0=ALU.mult)
            C1 = 6.28125
            C2 = 2 * PI - C1
            for which, (dst, dres) in enumerate(((sinT, "sinT"), (cosT, "cosT"))):
                r = rr[which]
                rres = ("rr", which)
                src = ang
                if which == 1:
                    self.vec("dve", "tensor_scalar", ["ang"], ["t1"], out=t1[:], in0=ang[:], scalar1=PI / 2, scalar2=None, op0=ALU.add)
                    src = t1
                sres = "ang" if which == 0 else "t1"
                self.vec("dve", "tensor_scalar", [sres], ["t2"], out=t2[:], in0=src[:], scalar1=1.0 / (2 * PI), scalar2=None, op0=ALU.mult)
                self.vec("dve", "tensor_copy", ["t2"], ["posi"], out=posi[:], in_=t2[:])
                self.vec("dve", "tensor_copy", ["posi"], ["t2"], out=t2[:], in_=posi[:])
                self.vec("dve", "scalar_tensor_tensor", ["t2", sres], [rres], out=r[:], in0=t2[:], scalar=-C1, in1=src[:],
                         op0=ALU.mult, op1=ALU.add)
                self.vec("dve", "scalar_tensor_tensor", ["t2", rres], [rres], out=r[:], in0=t2[:], scalar=-C2, in1=r[:],
                         op0=ALU.mult, op1=ALU.add)
                self.vec("dve", "tensor_scalar", [rres], ["x1s"], out=x1s[:], in0=r[:], scalar1=PI, scalar2=-2 * PI, op0=ALU.is_gt,
                         op1=ALU.mult)
                self.vec("dve", "tensor_scalar", [rres], ["x2s"], out=x2s[:], in0=r[:], scalar1=-PI, scalar2=2 * PI, op0=ALU.is_lt,
                         op1=ALU.mult)
                self.vec("dve", "tensor_tensor", [rres, "x1s"], [rres], out=r[:], in0=r[:], in1=x1s[:], op=ALU.add)
                self.vec("dve", "tensor_tensor", [rres, "x2s"], [rres], out=r[:], in0=r[:], in1=x2s[:], op=ALU.add)
                self.act(dst[:], r[:], AF.Sin, [rres], [dres])
            for jj in range(4):
                tok = slice(jj * 128, (jj + 1) * 128)
                for n in range(2):
                    for k in range(KC):
                        self.mm(ps[n][:], hT[:, k, tok], w_in[:, k, n * 512:(n + 1) * 512], k == 0, k == KC - 1,
                                ["hT", "mwin"], [("ps", n)])
                self.act(junk[:, 0:512], ps[0][:], AF.Square, [("ps", 0)], ["junk", "mst"], accum_out=st[:, 0:1])
                self.act(junk[:, 512:768], ps[1][:, 0:256], AF.Square, [("ps", 1)], ["junk", "mst"], accum_out=st[:, 1:2])
                self.act(junk[:, 768:1024], ps[1][:, 256:512], AF.Square, [("ps", 1)], ["junk", "mst"], accum_out=st[:, 2:3])
                self.vec("dve", "tensor_tensor", ["mst"], ["mst"], out=st[:, 3:4], in0=st[:, 0:1], in1=st[:, 1:2], op=ALU.add)
                self.vec("dve", "tensor_scalar", ["mst"], ["mst"], out=st[:, 4:5], in0=st[:, 3:4], scalar1=1.0 / 768, scalar2=EPS,
                         op0=ALU.mult, op1=ALU.add)
                self.vec("dve", "tensor_scalar", ["mst"], ["mst"], out=st[:, 5:6], in0=st[:, 2:3], scalar1=1.0 / 256, scalar2=EPS,
                         op0=ALU.mult, op1=ALU.add)
                self.act(st[:, 4:6], st[:, 4:6], AF.Sqrt, ["mst"], ["mst"])
                self.vec("dve", "reciprocal", ["mst"], ["mst"], out=st[:, 4:6], in_=st[:, 4:6])
                self.vec("dve", "tensor_scalar", [("ps", 0), "mst"], [("xn", jj)], out=xn[:, jj, 0:512], in0=ps[0][:],
                         scalar1=st[:, 4:5], scalar2=None, op0=ALU.mult)
                self.vec("dve", "tensor_scalar", [("ps", 1), "mst"], [("xn", jj)], out=xn[:, jj, 512:768], in0=ps[1][:, 0:256],
                         scalar1=st[:, 4:5], scalar2=None, op0=ALU.mult)
                self.vec("dve", "tensor_scalar", [("ps", 1), "mst"], [("xn", jj)], out=xn[:, jj, 768:1024], in0=ps[1][:, 256:512],
                         scalar1=st[:, 5:6], scalar2=None, op0=ALU.mult)
            for k in range(8):
                bank = 4 + (k % 4)
                pv = ps[bank][:].bitcast(BF16)
                for jj in range(4):
                    self.tr(pv[:, jj * 128:(jj + 1) * 128], xn[:, jj, k * 128:(k + 1) * 128], ident[:], [("xn", jj), "ident"],
                            [("ps", bank)])
                dst = qnT[:, k, :] if k < 6 else kvnT[:, k - 6, :]
                self.act(dst, pv[:, 0:512], AF.Copy, [("ps", bank), "mlaP"], ["qnT" if k < 6 else "kvnT"], scale=mlaP[:, k:k + 1])
            for half in range(2):
                for k in range(KC):
                    self.mm(ps[2 + half][0:16, :], w_in[:, k, 1024 + half * 16:1040 + half * 16], hT[:, k, :], k == 0, k == KC - 1,
                            ["mwin", "hT"], [("ps", 2 + half)])
            rope(ps[2][0:16, :], ps[3][0:16, :], 16, ("ps", 2), ("ps", 3), ro[0][0:16, :], ro[1][0:16, :], ("ro", 0), ("ro", 1))
            self.dma("sp", KR_d[0:16, cols], ro[0][0:16, :], [("ro", 0)], [("KR_d", t)], semkey=("scr", 0))
            self.dma("sp", KR_d[16:32, cols], ro[1][0:16, :], [("ro", 1)], [("KR_d", t)], semkey=("scr", 0))
            for c in range(8):
                bk = c % 2
                for k in range(6):
                    self.mm(ps[bk][:], w_uq[:, k, c * 128:(c + 1) * 128], qnT[:, k, :], k == 0, k == 5, ["mwuq", "qnT"], [("ps", bk)])
                self.act(qsb[bk][:], ps[bk][:], AF.Copy, [("ps", bk)], [("qsb", bk)])
                self.dma("sp", QN_d[c * 128:(c + 1) * 128, cols], qsb[bk][:], [("qsb", bk)], [("QN_d", t)], semkey=("scr", 1 + bk))
            for hg in range(2):
                for k in range(6):
                    self.mm(ps[2][:], w_uq[:, k, 1024 + hg * 128:1152 + hg * 128], qnT[:, k, :], k == 0, k == 5, ["mwuq", "qnT"],
                            [("ps", 2)])
                for k in range(6):
                    self.mm(ps[3][:], w_uq[:, k, 1280 + hg * 128:1408 + hg * 128], qnT[:, k, :], k == 0, k == 5, ["mwuq", "qnT"],
                            [("ps", 3)])
                rope(ps[2][:], ps[3][:], 128, ("ps", 2), ("ps", 3), ro[0][:], ro[1][:], ("ro", 0), ("ro", 1))
                self.dma("sp", QR_d[hg * 128:(hg + 1) * 128, cols], ro[0][:], [("ro", 0)], [("QR_d", t)], semkey=("scr", 0))
                self.dma("sp", QR_d[256 + hg * 128:256 + (hg + 1) * 128, cols], ro[1][:], [("ro", 1)], [("QR_d", t)], semkey=("scr", 0))
            for c in range(8):
                bk = c % 2
                for k in range(2):
                    self.mm(ps[bk][:], w_ukv[:, k, c * 128:(c + 1) * 128], kvnT[:, k, :], k == 0, k == 1, ["mwukv", "kvnT"], [("ps", bk)])
                self.vec("dve", "tensor_copy", [("ps", bk)], [("qsb", bk)], out=qsb[bk][:], in_=ps[bk][:])
                self.dma("sp", KN_d[c * 128:(c + 1) * 128, cols], qsb[bk][:], [("qsb", bk)], [("KN_d", t)], semkey=("scr", 1 + bk))
            for jj in range(4):
                j = t * 4 + jj
                tok = slice(jj * 128, (jj + 1) * 128)
                vb = vsb[jj % 2]
                for n in range(2):
                    for k in range(2):
                        self.mm(ps[2 + n][:], kvnT[:, k, tok], w_ukv[:, k, 1024 + n * 512:1536 + n * 512], k == 0, k == 1,
                                ["kvnT", "mwukv"], [("ps", 2 + n)])
                    if n == 0:
                        self.act(vb[:, 0:512], ps[2][:], AF.Copy, [("ps", 2)], [("vsb", jj % 2)])
                    else:
                        self.vec("dve", "tensor_copy", [("ps", 3)], [("vsb", jj % 2)], out=vb[:, 512:1024], in_=ps[3][:])
                self.dma("sp", V_d[j * 128:(j + 1) * 128, :], vb[:], [("vsb", jj % 2)], [("V_d", j)], semkey=("scr", 3 + jj % 2))
        S.barrier()
        self.release(mA)
        KhT = [self.sb("KhT%d" % i, [96, seq], BF16) for i in range(2)]
        QhT = [self.sb("QhT%d" % i, [96, seq], BF16) for i in range(2)]
        V1 = [self.sb("V1_%d" % i, [128, NB, 65], BF16) for i in range(2)]
        PT = [self.sb("PT%d" % i, [128, 512], BF16) for i in range(3)]
        rd = self.sb("rd", [128, 512], F32)
        bcs = self.sb("bcs", [64, 512], F32)
        osb = [self.sb("osb%d" % i, [64, 512], BF16) for i in range(2)]
        for i in range(2):
            self.vec("dve", "memset", [], [("V1", i)], V1[i][:, :, 64:65], 1.0)
        scale = 1.0 / math.sqrt(96.0)
        blocks = []
        for h in range(16):
            for qt in range(NT):
                nkb = 4 * (qt + 1)
                for kb in range(nkb):
                    blocks.append((h, qt, kb, nkb))
        loaded_heads = set()

        def load_head(h):
            if h in loaded_heads or h >= 16:
                return
            loaded_heads.add(h)
            hb = h % 2
            self.dma("sp", KhT[hb][0:64, :], KN_d[h * 64:(h + 1) * 64, :], [], [("KhT", hb, 0)], semkey=("ld", hb))
            self.dma("sp", KhT[hb][64:96, :], KR_d[0:32, :], [], [("KhT", hb, 1)], semkey=("ld", hb))
            self.dma("sp", QhT[hb][0:64, :], QN_d[h * 64:(h + 1) * 64, :], [], [("QhT", hb, 0)], semkey=("ld", hb))
            self.dma("sp", QhT[hb][64:80, :], QR_d[h * 16:(h + 1) * 16, :], [], [("QhT", hb, 1)], semkey=("ld", hb))
            self.dma("sp", QhT[hb][80:96, :], QR_d[256 + h * 16:256 + (h + 1) * 16, :], [], [("QhT", hb, 2)], semkey=("ld", hb))
            self.dma("sp", V1[hb][:, :, 0:64], V_d[:, h * 64:(h + 1) * 64].rearrange("(j p) c -> p j c", p=128), [],
                     [("V1", hb)], semkey=("ld", hb))

        def geom(i):
            h, qt, kb, nkb = blocks[i]
            jd = kb - 4 * qt
            c0 = 128 * jd if jd > 0 else 0
            return h, qt, kb, nkb, jd, c0, i % 3

        def stage1(i):
            h, qt, kb, nkb, jd, c0, sbk = geom(i)
            hb = h % 2
            kres = [("KhT", hb, 0), ("KhT", hb, 1)]
            qres = [("QhT", hb, 0), ("QhT", hb, 1), ("QhT", hb, 2)]
            pt = PT[sbk]
            self.mm(ps[sbk][:, c0:512], KhT[hb][:, kb * 128:(kb + 1) * 128], QhT[hb][:, qt * 512 + c0:(qt + 1) * 512],
                    True, True, kres + qres, [("ps", sbk)])
            self.act(pt[:, c0:512], ps[sbk][:, c0:512], AF.Exp, [("ps", sbk)], [("PT", sbk)], scale=scale)
            if jd >= 0:
                self.vec("dve", "tensor_tensor", [("PT", sbk), "tri"], [("PT", sbk)], out=pt[:, c0:c0 + 128],
                         in0=pt[:, c0:c0 + 128], in1=tri[:], op=ALU.mult)

        def stage2(i):
            h, qt, kb, nkb, jd, c0, sbk = geom(i)
            hb = h % 2
            ob = 4 + (qt % 2)
            pt = PT[sbk]
            self.mm(ps[ob][0:65, c0:512], V1[hb][:, kb, :], pt[:, c0:512], kb == 0, kb == nkb - 1,
                    [("V1", hb), ("PT", sbk)], [("ps", ob)])
            if kb == nkb - 1:
                self.vec("dve", "reciprocal", [("ps", ob)], ["rd"], out=rd[64:65, :], in_=ps[ob][64:65, :])
                self.mm(ps[6][0:64, :], cst[64:65, 256:320], rd[64:65, :], True, True, ["cst", "rd"], [("ps", 6)])
                self.act(bcs[:], ps[6][0:64, :], AF.Copy, [("ps", 6)], ["bcs"])
                o_ = osb[qt % 2]
                self.vec("dve", "tensor_tensor", [("ps", ob), "bcs"], [("osb", qt % 2)], out=o_[:], in0=ps[ob][0:64, :], in1=bcs[:],
                         op=ALU.mult)
                self.dma("sp", OT_d[h * 64:(h + 1) * 64, qt * 512:(qt + 1) * 512], o_[:], [("osb", qt % 2)], [("OT_d", qt)],
                         semkey=("ost", qt % 2))

        load_head(0)
        load_head(1)
        stage1(0)
        for i in range(len(blocks)):
            if i + 1 < len(blocks):
                stage1(i + 1)
            stage2(i)
            if i + 1 == len(blocks) or blocks[i + 1][0] != blocks[i][0]:
                load_head(blocks[i][0] + 2)
        S.barrier()
        self.release(mA)
        oT = [self.sb("oT%d" % i, [128, KC, 512], BF16) for i in range(2)]
        tmp = [self.sb("tmpm%d" % i, [128, 512], F32) for i in range(2)]
        for t in range(NT):
            cols = slice(t * 512, (t + 1) * 512)
            ot = oT[t % 2]
            self.dma("sp", ot[:], OT_d[:, cols].rearrange("(k p) t -> p k t", p=128), [], [("oT", t % 2)], semkey=("otl", t % 2))
            for jj in range(4):
                j = t * 4 + jj
                self.dma("sp", xt[:, jj, :], self.xs_d[j * 128:(j + 1) * 128, :], [("xd", j)], [("xt", jj)], semkey=("xtl", 0))
            for jj in range(4):
                j = t * 4 + jj
                tok = slice(jj * 128, (jj + 1) * 128)
                for n in range(2):
                    pb = ps[n]
                    for k in range(KC):
                        self.mm(pb[:], ot[:, k, tok], w_o[:, k, n * 512:(n + 1) * 512], k == 0, k == KC - 1, [("oT", t % 2), "mwo"],
                                [("ps", n)])
                    xs_ = xt[:, jj, n * 512:(n + 1) * 512]
                    self.vec("dve", "tensor_tensor", [("ps", n), "G"], [("tmpm", n)], out=tmp[n][:], in0=pb[:],
                             in1=G[:, n * 512:(n + 1) * 512], op=ALU.mult)
                    self.vec("dve", "tensor_tensor", [("tmpm", n), ("xt", jj)], [("xt", jj)], out=xs_, in0=tmp[n][:], in1=xs_,
                             op=ALU.add)
                self.dma("sp", self.xs_d[j * 128:(j + 1) * 128, :], xt[:, jj, :], [("xt", jj)], [("xd", j)], semkey=("xts", 0))
        self.sb_off = keep


    def final_norm(self, fin_in):
        m = self.mark()
        Gf = self.sb("Gf", [128, D], F32)
        junk = self.sb("junkf", [128, D], BF16)
        ob = [self.sb("ob%d" % i, [128, D], F32) for i in range(2)]
        self.dma("sp", Gf[:], fin_in.partition_broadcast(128), [], ["Gf"])
        for j in range(self.NB):
            xt = self.xres[:, j, :]
            ss = self.stat[:, 16 + j % 2:17 + j % 2]
            rs = self.stat[:, 18 + j % 2:19 + j % 2]
            o = ob[j % 2]
            self.act(junk[:], xt, AF.Square, [("x", j)], ["junkf", ("fss", j % 2)], accum_out=ss)
            self.vec("dve", "tensor_scalar", [("fss", j % 2)], [("frs", j % 2)], out=rs, in0=ss, scalar1=1.0 / D,
                     scalar2=EPS, op0=ALU.mult, op1=ALU.add)
            self.act(rs, rs, AF.Sqrt, [("frs", j % 2)], [("frs", j % 2)])
            self.vec("dve", "reciprocal", [("frs", j % 2)], [("frs", j % 2)], out=rs, in_=rs)
            self.vec("dve", "scalar_tensor_tensor", [("x", j), ("frs", j % 2), "Gf"], [("ob", j % 2)],
                     out=o[:], in0=xt, scalar=rs, in1=Gf[:], op0=ALU.mult, op1=ALU.mult)
            self.dma("sp", self.y_out[j * 128:(j + 1) * 128, :], o[:], [("ob", j % 2)], [("yout", j)],
                     semkey=("yout", j % 2))
        self.release(m)


def make_consts():
    c = np.zeros((128, 512), np.float32)
    c[:, 0:128] = np.eye(128, dtype=np.float32)
    ii = np.arange(128)
    c[:, 128:256] = (ii[:, None] <= ii[None, :]).astype(np.float32)
    c[:, 256:384] = 1.0
    for g, w in enumerate((2, 4, 8, 16)):
        c[:, 384 + g * 16: 384 + (g + 1) * 16] = 1.0 / np.minimum(np.arange(16) + 1.0, float(w))
    return c


def host_inputs(b, seq, x, c, positions, mod_w, mod_b, norm_g, ffn_w13, ffn_w2, ab_w_in, pool_w, pool_scale,
                ssd_conv_w, ssd_conv_b, ssd_dt_bias, ssd_a_log, ssd_d, ssd_norm_g, ab_w_out, mla_w_in,
                mla_q_norm_g, mla_w_uq, mla_kv_norm_g, mla_w_ukv, mla_w_o, final_norm_g):
    f = np.float32
    m = {}
    m["x"] = np.ascontiguousarray(x[b], dtype=f)
    m["c"] = np.ascontiguousarray(c[b].reshape(KC, 128).T, dtype=f)
    m["pos"] = np.ascontiguousarray(positions[b].reshape(1, seq), dtype=np.int32)
    m["mod_w"] = np.ascontiguousarray(mod_w, dtype=f)
    m["mod_b"] = np.ascontiguousarray(mod_b, dtype=f)
    m["ngP"] = np.ascontiguousarray(norm_g.reshape(2, 3, KC, 128).transpose(3, 0, 1, 2).reshape(128, 48), dtype=f)
    m["ffn_w13"] = np.ascontiguousarray(ffn_w13.reshape(4, D, 2 * DFF), dtype=f)
    m["ffn_w2"] = np.ascontiguousarray(ffn_w2.reshape(4, DFF, D), dtype=f)
    m["ab_w_in"] = np.ascontiguousarray(ab_w_in[0], dtype=f)
    m["pool_w"] = np.ascontiguousarray(pool_w[0].reshape(512, 128), dtype=f)
    abP = np.zeros((128, 80), f)
    abP[:, 0:4] = pool_scale[0].reshape(4, 128).T
    abP[:, 4:52] = ssd_conv_w[0].reshape(4, 12, 128).transpose(2, 1, 0).reshape(128, 48)
    abP[:, 52:64] = ssd_conv_b[0].reshape(12, 128).T
    abP[:, 64:72] = ssd_norm_g[0].reshape(8, 128).T
    m["abP"] = abP
    abR = np.zeros((1, 64), f)
    abR[0, 0:16] = ssd_dt_bias[0]
    abR[0, 16:32] = ssd_a_log[0]
    abR[0, 32:48] = ssd_d[0]
    m["abR"] = abR
    sel = np.zeros((16, 16, 128), f)
    for h in range(16):
        sel[h, h, :] = 1.0
    m["sel"] = sel.reshape(16, 2048)
    ii = np.arange(128)
    negm = np.where(ii[None, :] < ii[:, None], -30000.0, 0.0).astype(f)
    m["cmask"] = np.ascontiguousarray(np.tile(negm, (1, 4)))
    m["ab_w_out"] = np.ascontiguousarray(ab_w_out[0], dtype=f)
    m["mla_w_in"] = np.ascontiguousarray(mla_w_in[0], dtype=f)
    wq = mla_w_uq[0].reshape(768, 16, 96)
    m["mla_w_uq"] = np.ascontiguousarray(np.concatenate(
        [wq[:, :, 0:64].reshape(768, 1024), wq[:, :, 64:80].reshape(768, 256), wq[:, :, 80:96].reshape(768, 256)], axis=1), dtype=f)
    wkv = mla_w_ukv[0].reshape(256, 16, 128)
    m["mla_w_ukv"] = np.ascontiguousarray(np.concatenate(
        [wkv[:, :, 0:64].reshape(256, 1024), wkv[:, :, 64:128].reshape(256, 1024)], axis=1), dtype=f)
    m["mla_w_o"] = np.ascontiguousarray(mla_w_o[0], dtype=f)
    mp = np.zeros((128, 16), f)
    mp[:, 0:6] = mla_q_norm_g[0].reshape(6, 128).T
    mp[:, 6:8] = mla_kv_norm_g[0].reshape(2, 128).T
    mp[:, 8] = np.tile(INV_FREQ, 8)
    m["mlaP"] = mp
    m["final_g"] = np.ascontiguousarray(final_norm_g.reshape(1, D), dtype=f)
    m["consts"] = make_consts()
    return m


_NC_CACHE = {}


def kernel(**inputs):
    inputs = {k: np.asarray(v) for k, v in inputs.items()}
    B, seq, _ = inputs["x"].shape
    key = (seq,)
    if key not in _NC_CACHE:
        _NC_CACHE[key] = Builder(seq).build()
    nc = _NC_CACHE[key]
    in_maps = [host_inputs(b, seq, **inputs) for b in range(B)]
    res = run_bass_kernel_spmd(nc, in_maps, core_ids=list(range(B)))
    return np.stack([np.asarray(r["y"], dtype=np.float32) for r in res.results], axis=0)
```

```python
import numpy as np
import concourse.bass as bass
import concourse.mybir as mybir
from concourse.bass_utils import run_bass_kernel_spmd

F32 = mybir.dt.float32
BF16 = mybir.dt.bfloat16
I32 = mybir.dt.int32
AF = mybir.ActivationFunctionType
ALU = mybir.AluOpType
AX = mybir.AxisListType

D = 1024
KC = 8
DFF = 2816
NF = 22
EPS = 1e-6
SEQ_FULL = 4096
IN_AB = 3088
IN_MLA = 1056

COMPUTE = ("pe", "act", "dve", "pool")
INV_FREQ = (np.float32(10000.0) ** (-np.arange(0, 32, 2, dtype=np.float32) / np.float32(32))).astype(np.float32)


class Op:
    __slots__ = ("q", "fn", "reads", "writes", "dma", "semkey", "deps", "signal", "idx", "need_sig")


class Sched:
    def __init__(self, nc):
        self.nc = nc
        self.ops = []
        self.last_w = {}
        self.readers = {}
        self.sems = {}
        self.dma_count = {}
        self.bar_deps = []

    def op(self, q, fn, reads=(), writes=(), dma=False, semkey=None):
        o = Op()
        o.q, o.fn, o.reads, o.writes, o.dma = q, fn, tuple(reads), tuple(writes), dma
        o.idx = len(self.ops)
        o.need_sig = False
        deps = set(self.bar_deps)
        for r in o.reads:
            w = self.last_w.get(r)
            if w is not None:
                deps.add(w)
        for w_ in o.writes:
            w = self.last_w.get(w_)
            if w is not None:
                deps.add(w)
            for rd in self.readers.get(w_, ()):
                deps.add(rd)
        deps.discard(o.idx)
        o.deps = deps
        if dma:
            if semkey is None:
                semkey = ("dma",) + tuple(o.writes[:1])
            o.semkey = semkey
            n = self.dma_count.get(semkey, 0) + 1
            self.dma_count[semkey] = n
            o.signal = (semkey, 16 * n)
        else:
            o.semkey = q
            o.signal = None
        for r in o.reads:
            self.readers.setdefault(r, []).append(o.idx)
        for w_ in o.writes:
            self.last_w[w_] = o.idx
            self.readers[w_] = []
        self.ops.append(o)
        return o

    def barrier(self):
        lastq = {}
        for o in self.ops:
            if o.dma:
                lastq[("d", o.semkey)] = o.idx
            elif o.fn is not None:
                lastq[("c", o.q)] = o.idx
        self.bar_deps = list(lastq.values())
        self.last_w = {}
        self.readers = {}

    def emit(self):
        nc = self.nc
        ops = self.ops
        for o in ops:
            for d in o.deps:
                p = ops[d]
                if p.q == "pe" and o.q == "pe" and not p.dma:
                    continue
                p.need_sig = True
        cnt = {q: 0 for q in COMPUTE}
        for o in ops:
            if not o.dma and o.fn is not None and o.need_sig:
                cnt[o.q] += 1
                o.signal = (o.q, cnt[o.q])
        def sem(key):
            s = self.sems.get(key)
            if s is None:
                s = nc.alloc_semaphore("s%d" % len(self.sems))
                self.sems[key] = s
            return s
        queues = {}
        for o in ops:
            queues.setdefault(o.q, []).append(o)
        dma_before = {}
        run = {}
        for o in ops:
            dma_before[o.idx] = dict(run) if False else None
        dma_positions = {}
        for o in ops:
            if o.dma:
                dma_positions.setdefault(o.semkey, []).append(o.idx)
        import bisect

        def emit_queue(qname, eng):
            waited = {}
            for o in queues.get(qname, []):
                need = {}
                for d in o.deps:
                    p = ops[d]
                    if p.dma:
                        pos = dma_positions[p.semkey]
                        n = bisect.bisect_left(pos, o.idx)
                        key, val = p.semkey, 16 * n
                    else:
                        if p.fn is None:
                            continue
                        if p.q == "pe" and o.q == "pe" and not o.dma:
                            continue
                        key, val = p.signal
                    if need.get(key, 0) < val:
                        need[key] = val
                for key, val in need.items():
                    if waited.get(key, 0) >= val:
                        continue
                    waited[key] = val
                    eng.wait_ge(sem(key), val)
                if o.fn is None:
                    continue
                ins = o.fn(eng)
                if o.dma:
                    ins.then_inc(sem(o.semkey), 16)
                elif o.need_sig:
                    ins.then_inc(sem(o.q), 1)

        with nc.Block() as block:
            @block.tensor
            def _(e):
                emit_queue("pe", e)

            @block.scalar
            def _(e):
                emit_queue("act", e)

            @block.vector
            def _(e):
                emit_queue("dve", e)

            @block.gpsimd
            def _(e):
                emit_queue("pool", e)

            @block.sync
            def _(e):
                emit_queue("sp", e)


class Builder:
    def __init__(self, seq, subs=None, debug_out=None):
        self.seq = seq
        self.NT = seq // 512
        self.NB = seq // 128
        self.subs = subs
        nc = bass.Bass("TRN2", target_bir_lowering=False)
        self.nc = nc
        self.S = Sched(nc)
        self.sb_off = 16640
        self.sb_top = 229376
        self.nalloc = 0

    def sb(self, name, shape, dtype):
        esz = 4 if dtype in (F32, I32) else 2
        n = 1
        for s in shape[1:]:
            n *= s
        nbytes = (n * esz + 63) // 64 * 64
        off = self.sb_off
        assert off + nbytes <= self.sb_top, "SBUF overflow at %s: need %d have %d" % (name, nbytes, self.sb_top - off)
        self.sb_off += nbytes
        self.nalloc += 1
        return self.nc.alloc_sbuf_tensor_at("%s_%d" % (name, self.nalloc), list(shape), dtype, offset=off)

    def mark(self):
        return self.sb_off

    def release(self, m):
        self.sb_off = m

    def dram(self, name, shape, dtype, kind="Internal"):
        return self.nc.dram_tensor(name, list(shape), dtype, kind=kind).ap()

    def mm(self, out, lhsT, rhs, start, stop, reads, writes):
        self.S.op("pe", lambda e: e.matmul(out, lhsT, rhs, start=start, stop=stop), reads, writes)

    def tr(self, out, in_, ident, reads, writes):
        self.S.op("pe", lambda e: e.transpose(out, in_, ident), reads, writes)

    def act(self, out, in_, func, reads, writes, bias=None, scale=None, accum_out=None):
        kw = {}
        if bias is not None:
            kw["bias"] = bias
        if scale is not None:
            kw["scale"] = scale
        if accum_out is not None:
            kw["accum_out"] = accum_out
        self.S.op("act", lambda e: e.activation(out=out, in_=in_, func=func, **kw), reads, writes)

    def vec(self, q, method, reads, writes, *args, **kw):
        self.S.op(q, lambda e: getattr(e, method)(*args, **kw), reads, writes)

    def dma(self, q, out, in_, reads, writes, semkey=None):
        self.S.op(q, lambda e: e.dma_start(out=out, in_=in_), reads, writes, dma=True, semkey=semkey)

    def build(self):
        nc = self.nc
        seq, NT, NB = self.seq, self.NT, self.NB
        S = self.S
        x_in = self.dram("x", [seq, D], F32, "ExternalInput")
        c_in = self.dram("c", [128, KC], F32, "ExternalInput")
        pos_in = self.dram("pos", [1, seq], I32, "ExternalInput")
        mod_w = self.dram("mod_w", [2, D, 9 * D], F32, "ExternalInput")
        mod_b = self.dram("mod_b", [2, 9 * D], F32, "ExternalInput")
        ngP_in = self.dram("ngP", [128, 2 * 3 * KC], F32, "ExternalInput")
        w13_in = self.dram("ffn_w13", [4, D, 2 * DFF], F32, "ExternalInput")
        w2_in = self.dram("ffn_w2", [4, DFF, D], F32, "ExternalInput")
        abin_in = self.dram("ab_w_in", [D, IN_AB], F32, "ExternalInput")
        poolw_in = self.dram("pool_w", [512, 128], F32, "ExternalInput")
        abP_in = self.dram("abP", [128, 80], F32, "ExternalInput")
        abR_in = self.dram("abR", [1, 64], F32, "ExternalInput")
        about_in = self.dram("ab_w_out", [1536, D], F32, "ExternalInput")
        mlain_in = self.dram("mla_w_in", [D, IN_MLA], F32, "ExternalInput")
        wuq_in = self.dram("mla_w_uq", [768, 1536], F32, "ExternalInput")
        wukv_in = self.dram("mla_w_ukv", [256, 2048], F32, "ExternalInput")
        wo_in = self.dram("mla_w_o", [D, D], F32, "ExternalInput")
        mlaP_in = self.dram("mlaP", [128, 16], F32, "ExternalInput")
        fin_in = self.dram("final_g", [1, D], F32, "ExternalInput")
        cst_in = self.dram("consts", [128, 512], F32, "ExternalInput")
        sel_in = self.dram("sel", [16, 2048], F32, "ExternalInput")
        cmask_in = self.dram("cmask", [128, 512], F32, "ExternalInput")
        self.xs_d = self.dram("xs_d", [seq, D], F32)
        self.ins = dict(abin=abin_in, poolw=poolw_in, abP=abP_in, abR=abR_in, about=about_in, mlain=mlain_in,
                        wuq=wuq_in, wukv=wukv_in, wo=wo_in, mlaP=mlaP_in, sel=sel_in, cmask=cmask_in, pos=pos_in)
        y_out = self.dram("y", [seq, D], F32, "ExternalOutput")
        self.x_in, self.y_out = x_in, y_out

        w13_s = self.dram("w13_s", [4, D, 2 * DFF], BF16)
        w2_s = self.dram("w2_s", [4, DFF, D], BF16)
        mod_d = self.dram("mod_d", [2, 9 * D], F32)
        self.w13_s, self.w2_s, self.mod_d = w13_s, w2_s, mod_d

        self.ident = self.sb("ident", [128, 128], BF16)
        self.cst = self.sb("cst", [128, 512], F32)
        self.modP = self.sb("modP", [128, 2, 9, KC], F32)
        self.ngP = self.sb("ngP", [128, 2, 3, KC], F32)
        self.aP = self.sb("aP", [128, 2, 3, KC], F32)
        self.stat = self.sb("stat", [128, 64], F32)
        self.x_off = self.sb_off
        self.xres = self.sb("xres", [128, max(NB, 32), D], F32)
        self.ps = [nc.alloc_psum_tensor("ps%d" % i, [128, 512], F32) for i in range(8)]

        self.dma("sp", self.cst[:], cst_in, ["cst_in"], ["cst"])
        self.vec("dve", "tensor_copy", ["cst"], ["ident"], out=self.ident[:], in_=self.cst[:, 0:128])
        self.dma("sp", self.ngP[:].rearrange("p a b c -> p (a b c)"), ngP_in, [], ["ngP"])
        for j in range(NB):
            self.dma("sp", self.xres[:, j, :], x_in[j * 128:(j + 1) * 128, :], [], [("x", j)], semkey=("xload", j % 4))
        self.modulation(c_in, mod_w, mod_b)
        for i in range(4):
            self.precast(w13_s[i], w13_in[i], D, 2 * DFF, ("w13s", i))
            self.precast(w2_s[i], w2_in[i], DFF, D, ("w2s", i))

        def want(l, s):
            return self.subs is None or (l, s) in self.subs

        for l in range(2):
            if want(l, 0):
                self.ffn(l, 0)
            if want(l, 1):
                self.spill_x()
                if l == 0:
                    self.mixer_ab()
                else:
                    self.mixer_mla()
                self.reload_x()
            if want(l, 2):
                self.ffn(l, 1)
        self.final_norm(fin_in)
        S.op("sp", None, reads=[("yout", j) for j in range(NB)])
        S.emit()
        return nc

    def precast(self, dst, src, rows, cols, res):
        a = rows // 128
        half = max(1, a // 2)
        for h0 in range(0, a, half):
            h1 = min(a, h0 + half)
            d = dst.rearrange("(p a) c -> p a c", p=128)[:, h0:h1, :]
            s = src.rearrange("(p a) c -> p a c", p=128)[:, h0:h1, :]
            self.dma("pool", d, s, [], [res], semkey=("pc",) + tuple(res))

    def modulation(self, c_in, mod_w, mod_b):
        m = self.mark()
        cact = self.sb("cact", [128, KC], F32)
        mrow = self.sb("mrow", [1, 9 * D], F32)
        modT = self.sb("modT", [128, 128], F32)
        wst = [self.sb("modw%d" % i, [128, KC, 512], F32) for i in range(2)]
        self.dma("sp", cact[:], c_in, [], ["cact"])
        self.act(cact[:], cact[:], AF.Silu, ["cact"], ["cact"])
        for l in range(2):
            self.dma("sp", mrow[:], mod_b[l:l + 1, :], [], ["mrow"])
            for cb in range(18):
                slot = (l * 18 + cb) % 2
                self.dma("sp", wst[slot][:], mod_w[l, :, cb * 512:(cb + 1) * 512].rearrange("(k p) c -> p k c", p=128),
                         [], [("modw", slot)])
                pst = self.ps[cb % 2]
                for k in range(KC):
                    self.mm(pst[0:1, :], cact[:, k:k + 1], wst[slot][:, k, :], k == 0, k == KC - 1,
                            ["cact", ("modw", slot)], [("ps", cb % 2)])
                self.vec("dve", "tensor_tensor", [("ps", cb % 2), "mrow"], ["mrow"],
                         out=mrow[0:1, cb * 512:(cb + 1) * 512], in0=pst[0:1, :], in1=mrow[0:1, cb * 512:(cb + 1) * 512],
                         op=ALU.add)
            self.dma("sp", self.mod_d[l:l + 1, :], mrow[:], ["mrow"], [("mod_d", l)])
            self.dma("sp", modT[0:72, :], self.mod_d[l, :].rearrange("(r p) -> r p", p=128), [("mod_d", l)], ["modT"])
            self.tr(self.ps[2][:, 0:72], modT[0:72, :], self.cst[0:72, 0:72], ["modT", "cst"], [("ps", 2)])
            self.vec("dve", "tensor_copy", [("ps", 2)], ["modP"],
                     out=self.modP[:, l, :, :].rearrange("p j k -> p (j k)"), in_=self.ps[2][:, 0:72])
        for l in range(2):
            for s in range(3):
                self.vec("dve", "scalar_tensor_tensor", ["modP", "ngP"], ["aP"],
                         out=self.aP[:, l, s, :], in0=self.modP[:, l, 3 * s + 1, :], scalar=1.0,
                         in1=self.ngP[:, l, s, :], op0=ALU.add, op1=ALU.mult)
        self.S.barrier()
        self.release(m)

    def norm_mod_T(self, t, l, s, hT, hres, xn, junk, xsrc=None, xr=None):
        xts, xrs = [], []
        for jj in range(4):
            j = t * 4 + jj
            xt = self.xres[:, j, :] if xsrc is None else xsrc(jj)
            xres_ = ("x", j) if xr is None else xr(jj)
            xts.append(xt)
            xrs.append(xres_)
            if junk is None:
                self.act(xn[:, jj, :], xt, AF.Square, [xres_], [("xn", jj), "ss"], accum_out=self.stat[:, jj:jj + 1])
            else:
                self.act(junk[:], xt, AF.Square, [xres_], ["junk", "ss"], accum_out=self.stat[:, jj:jj + 1])
        rs4 = self.stat[:, 8:12]
        self.vec("dve", "tensor_scalar", ["ss"], ["rs"], out=rs4, in0=self.stat[:, 0:4], scalar1=1.0 / D, scalar2=EPS,
                 op0=ALU.mult, op1=ALU.add)
        self.act(rs4, rs4, AF.Sqrt, ["rs"], ["rs"])
        self.vec("dve", "reciprocal", ["rs"], ["rs"], out=rs4, in_=rs4)
        for jj in range(4):
            self.vec("dve", "tensor_scalar", [xrs[jj], "rs"], [("xn", jj)], out=xn[:, jj, :], in0=xts[jj],
                     scalar1=self.stat[:, 8 + jj:9 + jj], scalar2=None, op0=ALU.mult)
        for k in range(KC):
            bank = 4 + (k % 4)
            pv = self.ps[bank][:].bitcast(BF16)
            for jj in range(4):
                self.tr(pv[:, jj * 128:(jj + 1) * 128], xn[:, jj, k * 128:(k + 1) * 128], self.ident[:],
                        [("xn", jj), "ident"], [("ps", bank)])
            self.act(hT[:, k, :], pv[:, 0:512], AF.Identity, [("ps", bank), "aP", "modP"], [hres],
                     scale=self.aP[:, l, s, k:k + 1], bias=self.modP[:, l, 3 * s, k:k + 1])

    def load_gate(self, G, l, s):
        self.dma("sp", G[:], self.mod_d[l:l + 1, (3 * s + 2) * D:(3 * s + 3) * D].partition_broadcast(128),
                 [("mod_d", l)], ["G"])

    def ffn(self, l, s2):
        S = self.S
        fi = l * 2 + s2
        s = 0 if s2 == 0 else 2
        NT = self.NT
        m = self.mark()
        hT = self.sb("hT", [128, KC, 512], BF16)
        gT = self.sb("gT", [128, NF, 512], BF16)
        xn = self.sb("xn", [128, 4, D], BF16)
        G = self.sb("G", [128, D], F32)
        sg = [self.sb("sg%d" % i, [128, 512], F32) for i in range(2)]
        NSL = 3
        ring = [self.sb("ring%d" % i, [128, 4096], BF16) for i in range(NSL)]
        self.load_gate(G, l, s)
        w2grp = [(0, 8), (8, 8), (16, 6)]
        pieces = []
        for t in range(NT):
            for pc in range(11):
                pieces.append(("w13", pc))
            for n in range(2):
                for g in range(3):
                    pieces.append(("w2", n, g))
        issued = [0]

        def issue():
            gi = issued[0]
            p = pieces[gi]
            sl = gi % NSL
            issued[0] += 1
            if p[0] == "w13":
                pc = p[1]
                v = ring[sl][:].rearrange("p (k a c) -> p k a c", k=KC, a=2)
                for ab in range(2):
                    src = self.w13_s[fi, :, ab * DFF + pc * 256: ab * DFF + (pc + 1) * 256]
                    self.dma("sp", v[:, :, ab, :], src.rearrange("(k p) c -> p k c", p=128),
                             [("w13s", fi)], [("ring", sl, ab)])
            else:
                n, g = p[1], p[2]
                f0, nf = w2grp[g]
                v = ring[sl][:].rearrange("p (f c) -> p f c", f=8)
                src = self.w2_s[fi, f0 * 128:(f0 + nf) * 128, n * 512:(n + 1) * 512]
                self.dma("sp", v[:, 0:nf, :], src.rearrange("(f p) c -> p f c", p=128), [("w2s", fi)],
                         [("ring", sl, 0), ("ring", sl, 1)])

        def ensure(gi):
            while issued[0] < len(pieces) and issued[0] < gi + NSL:
                issue()
            return gi % NSL

        pi = 0
        for t in range(NT):
            self.norm_mod_T(t, l, s, hT, "hT", xn, None)
            for pc in range(11):
                sl = ensure(pi)
                pi += 1
                wv = ring[sl][:].rearrange("p (k a c) -> p k a c", k=KC, a=2)
                for ff in range(2):
                    f = pc * 2 + ff
                    pa, pb = self.ps[ff * 2], self.ps[ff * 2 + 1]
                    for k in range(KC):
                        self.mm(pa[:], wv[:, k, 0, ff * 128:(ff + 1) * 128], hT[:, k, :], k == 0, k == KC - 1,
                                [("ring", sl, 0), "hT"], [("ps", ff * 2)])
                    for k in range(KC):
                        self.mm(pb[:], wv[:, k, 1, ff * 128:(ff + 1) * 128], hT[:, k, :], k == 0, k == KC - 1,
                                [("ring", sl, 1), "hT"], [("ps", ff * 2 + 1)])
                    self.act(sg[ff][:], pa[:], AF.Silu, [("ps", ff * 2)], [("sg", ff)])
                    self.vec("dve", "tensor_tensor", [("sg", ff), ("ps", ff * 2 + 1)], [("gT", f)],
                             out=gT[:, f, :], in0=sg[ff][:], in1=pb[:], op=ALU.mult)
            for n in range(2):
                for g in range(3):
                    sl = ensure(pi)
                    pi += 1
                    f0, nf = w2grp[g]
                    wv = ring[sl][:].rearrange("p (f c) -> p f c", f=8)
                    for fl in range(nf):
                        f = f0 + fl
                        for jj in range(4):
                            self.mm(self.ps[4 + jj][:], gT[:, f, jj * 128:(jj + 1) * 128], wv[:, fl, :],
                                    f == 0, f == NF - 1, [("gT", f), ("ring", sl, 0), ("ring", sl, 1)], [("ps", 4 + jj)])
                for jj in range(4):
                    j = t * 4 + jj
                    xs = self.xres[:, j, n * 512:(n + 1) * 512]
                    tb = sg[jj % 2]
                    self.vec("dve", "tensor_tensor", [("ps", 4 + jj), "G"], [("sg", jj % 2)],
                             out=tb[:], in0=self.ps[4 + jj][:], in1=G[:, n * 512:(n + 1) * 512], op=ALU.mult)
                    self.vec("dve", "scalar_tensor_tensor", [("sg", jj % 2), ("x", j)], [("x", j)],
                             out=xs, in0=tb[:], scalar=0.5, in1=xs, op0=ALU.mult, op1=ALU.add)
        S.barrier()
        self.release(m)

    def spill_x(self):
        for j in range(self.NB):
            self.dma("sp", self.xs_d[j * 128:(j + 1) * 128, :], self.xres[:, j, :], [("x", j)], [("xd", j)],
                     semkey=("xsp", j % 4))
        self.S.barrier()

    def reload_x(self):
        self.S.barrier()
        for j in range(self.NB):
            self.dma("sp", self.xres[:, j, :], self.xs_d[j * 128:(j + 1) * 128, :], [("xd", j)], [("x", j)],
                     semkey=("xload", j % 4))

    def mixer_ab(self):
        S = self.S
        ins = self.ins
        l, s = 0, 1
        NT = self.NT
        keep = self.sb_off
        self.sb_off = self.x_off
        cst = self.cst
        w_in = self.sb("abwin", [128, KC, IN_AB], BF16)
        w_out = self.sb("abwout", [128, 12, D], BF16)
        pw = self.sb("poolw", [128, 4, 128], BF16)
        abP = self.sb("abP", [128, 80], F32)
        rowp = self.sb("rowp", [128, 48], F32)
        a_bc = self.sb("a_bc", [128, 16], F32)
        sel = self.sb("sel", [16, 16, 128], BF16)
        negm = self.sb("negm", [128, 512], BF16)
        G = self.sb("G", [128, D], F32)
        hT = self.sb("hT", [128, KC, 512], BF16)
        xn = self.sb("xn", [128, 4, D], BF16)
        junk = self.sb("junk", [128, D], BF16)
        xt = [self.sb("xt0", [128, 4, D], F32)] * 2
        Up = self.sb("Up", [128, 528], F32)
        sA = self.sb("sA", [128, 528], F32)
        sB = self.sb("sB", [128, 528], F32)
        phalo = self.sb("phalo", [128, 4, 16], F32)
        dT = [self.sb("dT%d" % i, [128, 512], BF16) for i in range(2)]
        dfix = self.sb("dfix", [128, 16], F32)
        ypT = self.sb("ypT", [128, 4, 512], BF16)
        Uc = [self.sb("Uc0", [128, 515], F32)] * 2
        chalo = self.sb("chalo", [128, 12, 3], F32)
        acc = [self.sb("acc0", [128, 512], F32)] * 2
        xbcT = self.sb("xbcT", [128, 12, 512], BF16)
        zs = self.sb("zs", [128, D], F32)
        dtt = self.sb("dtt", [128, 96], F32)
        sm2 = self.sb("sm2", [128, 48], F32)
        acsT = self.sb("acsT", [16, 3, 128], F32)
        acsHL = self.sb("acsHL", [16, 2, 128], BF16)
        xs_tok = self.sb("xs_tok", [128, D], BF16)
        xw = self.sb("xw", [128, D], BF16)
        xsD = self.sb("xsD", [128, D], BF16)
        Btok = self.sb("Btok", [128, 2, 128], BF16)
        cb = self.sb("cb", [128, 2, 128], F32)
        Eexp = [self.sb("Eexp%d" % i, [128, 512], F32) for i in range(2)]
        Mt = [self.sb("Mt%d" % i, [128, 512], BF16) for i in range(2)]
        H = self.sb("H", [128, D], F32)
        Hb = self.sb("Hb", [128, D], BF16)
        yA = self.sb("yA", [128, D], F32)
        ssg = self.sb("ssg", [128, 4], F32)
        yn = self.sb("yn", [128, D], BF16)
        yT = self.sb("yT", [128, KC, 512], BF16)
        tmp = [self.sb("tmpo%d" % i, [128, 512], F32) for i in range(2)]
        ps = self.ps
        ident = self.ident

        self.dma("pool", w_in[:], ins["abin"].rearrange("(k p) c -> p k c", p=128), [], ["abwin"])
        self.dma("pool", w_out[:], ins["about"].rearrange("(k p) c -> p k c", p=128), [], ["abwout"])
        self.dma("pool", pw[:], ins["poolw"].rearrange("(g p) c -> p g c", p=128), [], ["poolw"])
        self.dma("pool", sel[:].rearrange("p a b -> p (a b)"), ins["sel"], [], ["sel"])
        self.dma("pool", negm[:], ins["cmask"], [], ["negm"])
        self.dma("sp", abP[:], ins["abP"], [], ["abP"])
        self.dma("sp", rowp[:], ins["abR"][0:1, 0:48].partition_broadcast(128), [], ["rowp"])
        self.act(a_bc[:], rowp[:, 16:32], AF.Exp, ["rowp"], ["a_bc"])
        self.vec("dve", "tensor_scalar", ["a_bc"], ["a_bc"], out=a_bc[:], in0=a_bc[:], scalar1=-1.0, scalar2=None,
                 op0=ALU.mult)
        self.load_gate(G, l, s)
        self.vec("dve", "memset", [], ["phalo"], phalo[:], 0.0)
        self.vec("dve", "memset", [], ["chalo"], chalo[:], 0.0)
        self.vec("dve", "memset", [], ["H"], H[:], 0.0)
        self.vec("dve", "memset", [], ["Hb"], Hb[:], 0.0)
        D_bc = rowp[:, 32:48]
        dtb_bc = rowp[:, 0:16]
        wins = (2, 4, 8, 16)

        for t in range(NT):
            xb = xt[t % 2]
            for jj in range(4):
                j = t * 4 + jj
                self.dma("sp", xb[:, jj, :], self.xs_d[j * 128:(j + 1) * 128, :], [("xd", j)], [("xt", 0, jj)],
                         semkey=("xtl", t % 2))
            self.norm_mod_T(t, l, s, hT, "hT", xn, junk, xsrc=lambda jj: xb[:, jj, :],
                            xr=lambda jj: ("xt", 0, jj))
            for g in range(4):
                bk = g % 2
                for k in range(KC):
                    self.mm(ps[bk][:], w_in[:, k, g * 128:(g + 1) * 128], hT[:, k, :], k == 0, k == KC - 1,
                            ["abwin", "hT"], [("ps", bk)])
                self.vec("dve", "tensor_copy", ["phalo"], ["Up"], out=Up[:, 0:16], in_=phalo[:, g, :])
                self.act(Up[:, 16:528], ps[bk][:], AF.Copy, [("ps", bk)], ["Up"])
                self.vec("dve", "tensor_copy", ["Up"], ["phalo"], out=phalo[:, g, :], in_=Up[:, 512:528])
                self.vec("dve", "tensor_tensor", ["Up"], ["sA"], out=sA[:, 1:528], in0=Up[:, 1:528], in1=Up[:, 0:527],
                         op=ALU.add)
                lvl = sA
                if g >= 1:
                    self.vec("dve", "tensor_tensor", ["sA"], ["sB"], out=sB[:, 3:528], in0=sA[:, 3:528],
                             in1=sA[:, 1:526], op=ALU.add)
                    lvl = sB
                if g >= 2:
                    self.vec("dve", "tensor_tensor", ["sB"], ["sA"], out=sA[:, 7:528], in0=sB[:, 7:528],
                             in1=sB[:, 3:524], op=ALU.add)
                    lvl = sA
                if g >= 3:
                    self.vec("dve", "tensor_tensor", ["sA"], ["sB"], out=sB[:, 15:528], in0=sA[:, 15:528],
                             in1=sA[:, 7:520], op=ALU.add)
                    lvl = sB
                lres = "sA" if lvl is sA else "sB"
                dd = dT[g % 2]
                self.vec("dve", "scalar_tensor_tensor", [lres, "Up"], [("dT", g % 2)], out=dd[:], in0=lvl[:, 16:528],
                         scalar=1.0 / wins[g], in1=Up[:, 16:528], op0=ALU.mult, op1=ALU.subtract)
                if t == 0:
                    self.vec("dve", "tensor_tensor", [lres, "cst"], ["dfix"], out=dfix[:], in0=lvl[:, 16:32],
                             in1=cst[:, 384 + g * 16:384 + (g + 1) * 16], op=ALU.mult)
                    self.vec("dve", "tensor_tensor", ["dfix", "Up"], [("dT", g % 2)], out=dd[:, 0:16], in0=dfix[:],
                             in1=Up[:, 16:32], op=ALU.subtract)
                self.mm(ps[2 + bk][:], pw[:, g, :], dd[:], True, True, ["poolw", ("dT", g % 2)], [("ps", 2 + bk)])
                self.act(ypT[:, g, :], ps[2 + bk][:], AF.Copy, [("ps", 2 + bk), "abP"], [("ypT", g)],
                         scale=abP[:, g:g + 1])
            for cc in range(12):
                bk = cc % 2
                c0 = 1536 + cc * 128
                for k in range(KC):
                    self.mm(ps[bk][:], w_in[:, k, c0:c0 + 128], hT[:, k, :], k == 0, k == KC - 1,
                            ["abwin", "hT"], [("ps", bk)])
                U = Uc[bk]
                ur = ("Uc", 0)
                self.vec("dve", "tensor_copy", ["chalo"], [ur], out=U[:, 0:3], in_=chalo[:, cc, :])
                self.act(U[:, 3:515], ps[bk][:], AF.Copy, [("ps", bk)], [ur])
                self.vec("dve", "tensor_copy", [ur], ["chalo"], out=chalo[:, cc, :], in_=U[:, 512:515])
                a_ = acc[bk]
                ar = ("acc", 0)
                self.vec("dve", "tensor_scalar", [ur, "abP"], [ar], out=a_[:], in0=U[:, 0:512],
                         scalar1=abP[:, 4 + cc * 4:5 + cc * 4], scalar2=abP[:, 52 + cc:53 + cc], op0=ALU.mult, op1=ALU.add)
                for kk in range(1, 4):
                    self.vec("dve", "scalar_tensor_tensor", [ur, "abP", ar], [ar], out=a_[:], in0=U[:, kk:kk + 512],
                             scalar=abP[:, 4 + cc * 4 + kk:5 + cc * 4 + kk], in1=a_[:], op0=ALU.mult, op1=ALU.add)
                self.act(xbcT[:, cc, :], a_[:], AF.Silu, [ar], [("xbcT", cc)])
            for jj in range(4):
                j = t * 4 + jj
                tok = slice(jj * 128, (jj + 1) * 128)
                for n in range(2):
                    for k in range(KC):
                        self.mm(ps[2 + n][:], hT[:, k, tok], w_in[:, k, 512 + n * 512:1024 + n * 512], k == 0,
                                k == KC - 1, ["hT", "abwin"], [("ps", 2 + n)])
                    self.act(zs[:, n * 512:(n + 1) * 512], ps[2 + n][:], AF.Silu, [("ps", 2 + n)], ["zs"])
                for k in range(KC):
                    self.mm(ps[4][:, 0:16], hT[:, k, tok], w_in[:, k, 3072:3088], k == 0, k == KC - 1,
                            ["hT", "abwin"], [("ps", 4)])
                dt_, lndt, adt, nb, wst, eacs = (dtt[:, i * 16:(i + 1) * 16] for i in range(6))
                cd_bc, acs_sb, arg = (sm2[:, i * 16:(i + 1) * 16] for i in range(3))
                self.vec("dve", "tensor_tensor", [("ps", 4), "rowp"], ["dt"], out=dt_, in0=ps[4][:, 0:16], in1=dtb_bc,
                         op=ALU.add)
                self.act(dt_, dt_, AF.Exp, ["dt"], ["dt"])
                self.act(dt_, dt_, AF.Ln, ["dt"], ["dt"], bias=1.0)
                self.act(lndt, dt_, AF.Ln, ["dt"], ["lndt"])
                self.vec("dve", "tensor_tensor", ["dt", "a_bc"], ["adt"], out=adt, in0=dt_, in1=a_bc[:], op=ALU.mult)
                self.mm(ps[4][:, 16:32], cst[:, 128:256], adt, True, True, ["cst", "adt"], [("ps", 4)])
                self.mm(ps[4][0:16, 128:256], adt, cst[:, 128:256], True, True, ["cst", "adt"], [("ps", 4)])
                self.mm(ps[4][:, 32:48], cst[:, 256:384], adt, True, True, ["cst", "adt"], [("ps", 4)])
                self.vec("dve", "tensor_copy", [("ps", 4)], ["acs_sb"], out=acs_sb, in_=ps[4][:, 16:32])
                self.vec("dve", "tensor_tensor", ["lndt", "acs_sb"], ["nb"], out=nb, in0=lndt, in1=acs_sb, op=ALU.subtract)
                self.act(eacs, ps[4][:, 16:32], AF.Exp, [("ps", 4)], ["eacs"])
                self.vec("dve", "tensor_tensor", [("ps", 4), "nb"], ["arg"], out=arg, in0=ps[4][:, 32:48], in1=nb,
                         op=ALU.add)
                self.act(wst, arg, AF.Exp, ["arg"], ["wst"])
                self.act(cd_bc, ps[4][:, 32:48], AF.Exp, [("ps", 4)], ["cd_bc"])
                self.vec("dve", "tensor_copy", [("ps", 4)], ["acsT0"], out=acsT[:, 0, :], in_=ps[4][0:16, 128:256])
                self.vec("dve", "tensor_copy", ["acsT0"], ["acsHL0"], out=acsHL[:, 0, :], in_=acsT[:, 0, :])
                self.vec("dve", "tensor_copy", ["acsHL0"], ["acsT1"], out=acsT[:, 1, :], in_=acsHL[:, 0, :])
                self.vec("dve", "tensor_tensor", ["acsT0", "acsT1"], ["acsHL1"], out=acsHL[:, 1, :], in0=acsT[:, 0, :],
                         in1=acsT[:, 1, :], op=ALU.subtract)
                pv5 = ps[5][:].bitcast(BF16)
                for cc in range(8):
                    self.tr(pv5[:, cc * 128:(cc + 1) * 128], xbcT[:, cc, tok], ident[:], [("xbcT", cc), "ident"],
                            [("ps", 5)])
                self.act(xs_tok[:], pv5[:, 0:1024], AF.Copy, [("ps", 5)], ["xs_tok"])
                self.vec("dve", "tensor_tensor", [("ps", 5), "wst"], ["xw"],
                         out=xw[:].rearrange("p (h q) -> p h q", h=16), in0=pv5[:, 0:1024].rearrange("p (h q) -> p h q", h=16),
                         in1=wst.unsqueeze(2).to_broadcast([128, 16, 64]), op=ALU.mult)
                self.vec("dve", "tensor_tensor", ["xs_tok", "rowp"], ["xsD"],
                         out=xsD[:].rearrange("p (h q) -> p h q", h=16), in0=xs_tok[:].rearrange("p (h q) -> p h q", h=16),
                         in1=D_bc.unsqueeze(2).to_broadcast([128, 16, 64]), op=ALU.mult)
                pv7 = ps[7][:].bitcast(BF16)
                for g in range(2):
                    self.tr(pv7[:, g * 128:(g + 1) * 128], xbcT[:, 8 + g, tok], ident[:], [("xbcT", 8 + g), "ident"],
                            [("ps", 7)])
                self.act(Btok[:].rearrange("p a b -> p (a b)"), pv7[:, 0:256], AF.Copy, [("ps", 7)], ["Btok"])
                for g in range(2):
                    self.mm(ps[4][:, 256 + g * 128:384 + g * 128], xbcT[:, 8 + g, tok], xbcT[:, 10 + g, tok], True, True,
                            [("xbcT", 8 + g), ("xbcT", 10 + g)], [("ps", 4)])
                self.vec("dve", "tensor_copy", [("ps", 4)], ["cb"], out=cb[:].rearrange("p a b -> p (a b)"),
                         in_=ps[4][:, 256:512])
                for g in range(2):
                    ydb = ps[g]
                    self.mm(ydb[:], ident[:], xsD[:, g * 512:(g + 1) * 512], True, False, ["ident", "xsD"], [("ps", g)])
                    for qq in range(2):
                        q4 = g * 2 + qq
                        eb = Eexp[q4 % 2]
                        mb = Mt[q4 % 2]
                        self.mm(ps[6][:], ident[:], negm[:], True, False, ["ident", "negm"], [("ps", 6)])
                        for hh in range(4):
                            h = q4 * 4 + hh
                            for part in range(2):
                                self.mm(ps[6][:, hh * 128:(hh + 1) * 128], sel[:, h, :], acsHL[:, part, :], False,
                                        hh == 3 and part == 1, ["sel", "acsHL0", "acsHL1"], [("ps", 6)])
                        for hh in range(4):
                            h = q4 * 4 + hh
                            self.act(eb[:, hh * 128:(hh + 1) * 128], ps[6][:, hh * 128:(hh + 1) * 128], AF.Exp,
                                     [("ps", 6), "nb"], [("Eexp", q4 % 2)], bias=nb[:, h:h + 1])
                        self.vec("dve", "tensor_tensor", [("Eexp", q4 % 2), "cb"], [("Mt", q4 % 2)],
                                 out=mb[:].rearrange("p (a b) -> p a b", a=4), in0=eb[:].rearrange("p (a b) -> p a b", a=4),
                                 in1=cb[:, g:g + 1, :].to_broadcast([128, 4, 128]), op=ALU.mult)
                        for hh in range(4):
                            h = q4 * 4 + hh
                            hl = h - g * 8
                            self.mm(ydb[:, hl * 64:(hl + 1) * 64], mb[:, hh * 128:(hh + 1) * 128],
                                    xs_tok[:, h * 64:(h + 1) * 64], False, qq == 1 and hh == 3,
                                    [("Mt", q4 % 2), "xs_tok"], [("ps", g)])
                for g in range(2):
                    gs = slice(g * 512, (g + 1) * 512)
                    self.mm(ps[2 + g][:], xbcT[:, 10 + g, tok], Hb[:, gs], True, True, [("xbcT", 10 + g), "Hb"],
                            [("ps", 2 + g)])
                    self.vec("dve", "tensor_tensor", [("ps", 2 + g), "eacs"], ["yA"],
                             out=yA[:, gs].rearrange("p (h q) -> p h q", h=8),
                             in0=ps[2 + g][:].rearrange("p (h q) -> p h q", h=8),
                             in1=eacs[:, g * 8:(g + 1) * 8].unsqueeze(2).to_broadcast([128, 8, 64]), op=ALU.mult)
                    self.vec("dve", "tensor_tensor", [("ps", g), "yA"], ["yA"], out=yA[:, gs], in0=ps[g][:], in1=yA[:, gs],
                             op=ALU.add)
                    self.mm(ps[7][:], Btok[:, g, :], xw[:, gs], True, True, ["Btok", "xw"], [("ps", 7)])
                    self.vec("dve", "tensor_tensor", ["H", "cd_bc"], ["H"], out=H[:, gs].rearrange("p (h q) -> p h q", h=8),
                             in0=H[:, gs].rearrange("p (h q) -> p h q", h=8),
                             in1=cd_bc[:, g * 8:(g + 1) * 8].unsqueeze(2).to_broadcast([128, 8, 64]), op=ALU.mult)
                    self.vec("dve", "tensor_tensor", [("ps", 7), "H"], ["H"], out=H[:, gs], in0=ps[7][:], in1=H[:, gs],
                             op=ALU.add)
                    self.act(Hb[:, gs], H[:, gs], AF.Copy, ["H"], ["Hb"])
                self.vec("dve", "tensor_tensor", ["yA", "zs"], ["yA"], out=yA[:], in0=yA[:], in1=zs[:], op=ALU.mult)
                for g in range(2):
                    gs = slice(g * 512, (g + 1) * 512)
                    self.act(junk[:, gs], yA[:, gs], AF.Square, ["yA"], ["junk", "ssg"], accum_out=ssg[:, g:g + 1])
                self.vec("dve", "tensor_scalar", ["ssg"], ["ssg"], out=ssg[:, 2:4], in0=ssg[:, 0:2], scalar1=1.0 / 512,
                         scalar2=EPS, op0=ALU.mult, op1=ALU.add)
                self.act(ssg[:, 2:4], ssg[:, 2:4], AF.Sqrt, ["ssg"], ["ssg"])
                self.vec("dve", "reciprocal", ["ssg"], ["ssg"], out=ssg[:, 2:4], in_=ssg[:, 2:4])
                for g in range(2):
                    gs = slice(g * 512, (g + 1) * 512)
                    self.vec("dve", "tensor_scalar", ["yA", "ssg"], ["yn"], out=yn[:, gs], in0=yA[:, gs],
                             scalar1=ssg[:, 2 + g:3 + g], scalar2=None, op0=ALU.mult)
                for cc in range(8):
                    self.tr(pv5[:, cc * 128:(cc + 1) * 128], yn[:, cc * 128:(cc + 1) * 128], ident[:], ["yn", "ident"],
                            [("ps", 5)])
                for cc in range(8):
                    self.act(yT[:, cc, tok], pv5[:, cc * 128:(cc + 1) * 128], AF.Copy, [("ps", 5), "abP"], [("yT", jj)],
                             scale=abP[:, 64 + cc:65 + cc])
            for jj in range(4):
                j = t * 4 + jj
                tok = slice(jj * 128, (jj + 1) * 128)
                for n in range(2):
                    pb = ps[6 + n]
                    for k in range(12):
                        lhs = ypT[:, k, tok] if k < 4 else yT[:, k - 4, tok]
                        rr = [("ypT", k)] if k < 4 else [("yT", jj)]
                        self.mm(pb[:], lhs, w_out[:, k, n * 512:(n + 1) * 512], k == 0, k == 11, rr + ["abwout"],
                                [("ps", 6 + n)])
                    xs_ = xb[:, jj, n * 512:(n + 1) * 512]
                    self.vec("dve", "tensor_tensor", [("ps", 6 + n), "G"], [("tmpo", n)], out=tmp[n][:], in0=pb[:],
                             in1=G[:, n * 512:(n + 1) * 512], op=ALU.mult)
                    self.vec("dve", "tensor_tensor", [("tmpo", n), ("xt", 0, jj)], [("xt", 0, jj)], out=xs_,
                             in0=tmp[n][:], in1=xs_, op=ALU.add)
                self.dma("sp", self.xs_d[j * 128:(j + 1) * 128, :], xb[:, jj, :], [("xt", 0, jj)], [("xd", j)],
                         semkey=("xts", t % 2))
        self.sb_off = keep


    def mixer_mla(self):
        import math
        S = self.S
        ins = self.ins
        l, s = 1, 1
        NT, NB, seq = self.NT, self.NB, self.seq
        keep = self.sb_off
        self.sb_off = self.x_off
        cst, ident, ps = self.cst, self.ident, self.ps
        QN_d = self.dram("QN_d", [1024, seq], BF16)
        QR_d = self.dram("QR_d", [512, seq], BF16)
        KN_d = self.dram("KN_d", [1024, seq], BF16)
        KR_d = self.dram("KR_d", [32, seq], BF16)
        V_d = self.dram("V_d", [seq, 1024], BF16)
        OT_d = self.dram("OT_d", [1024, seq], BF16)
        w_in = self.sb("mwin", [128, KC, IN_MLA], BF16)
        w_uq = self.sb("mwuq", [128, 6, 1536], BF16)
        w_ukv = self.sb("mwukv", [128, 2, 2048], BF16)
        w_o = self.sb("mwo", [128, KC, D], BF16)
        mlaP = self.sb("mlaP", [128, 16], F32)
        G = self.sb("G", [128, D], F32)
        tri = self.sb("tri", [128, 128], BF16)
        xt = self.sb("xt", [128, 4, D], F32)
        self.dma("pool", w_in[:], ins["mlain"].rearrange("(k p) c -> p k c", p=128), [], ["mwin"])
        self.dma("pool", w_uq[:], ins["wuq"].rearrange("(k p) c -> p k c", p=128), [], ["mwuq"])
        self.dma("pool", w_ukv[:], ins["wukv"].rearrange("(k p) c -> p k c", p=128), [], ["mwukv"])
        self.dma("pool", w_o[:], ins["wo"].rearrange("(k p) c -> p k c", p=128), [], ["mwo"])
        self.dma("sp", mlaP[:], ins["mlaP"], [], ["mlaP"])
        self.load_gate(G, l, s)
        self.vec("dve", "tensor_copy", ["cst"], ["tri"], out=tri[:], in_=cst[:, 128:256])
        mA = self.mark()
        hT = self.sb("hT", [128, KC, 512], BF16)
        xn = self.sb("xn", [128, 4, D], BF16)
        junk = self.sb("junk", [128, D], BF16)
        qnT = self.sb("qnT", [128, 6, 512], BF16)
        kvnT = self.sb("kvnT", [128, 2, 512], BF16)
        posi = self.sb("posi", [128, 512], I32)
        ang = self.sb("ang", [128, 512], F32)
        rr = [self.sb("rr%d" % i, [128, 512], F32) for i in range(2)]
        cosT = self.sb("cosT", [128, 512], F32)
        sinT = self.sb("sinT", [128, 512], F32)
        x1s = self.sb("x1s", [128, 512], F32)
        x2s = self.sb("x2s", [128, 512], F32)
        t1 = self.sb("t1", [128, 512], F32)
        t2 = self.sb("t2", [128, 512], F32)
        ro = [self.sb("ro%d" % i, [128, 512], BF16) for i in range(2)]
        qsb = [self.sb("qsb%d" % i, [128, 512], BF16) for i in range(2)]
        vsb = [self.sb("vsb%d" % i, [128, D], BF16) for i in range(2)]
        st = self.sb("mst", [128, 16], F32)
        PI = math.pi
        pib = self.sb("pib", [128, 1], F32)
        self.vec("dve", "memset", [], ["pib"], pib[:], PI)

        def rope(x1ps, x2ps, P, res1, res2, o1, o2, o1res, o2res):
            self.act(x1s[0:P, :], x1ps, AF.Copy, [res1], ["x1s"])
            self.act(x2s[0:P, :], x2ps, AF.Copy, [res2], ["x2s"])
            self.vec("dve", "tensor_tensor", ["x1s", "cosT"], ["t1"], out=t1[0:P, :], in0=x1s[0:P, :], in1=cosT[0:P, :], op=ALU.mult)
            self.vec("dve", "tensor_tensor", ["x2s", "sinT"], ["t2"], out=t2[0:P, :], in0=x2s[0:P, :], in1=sinT[0:P, :], op=ALU.mult)
            self.vec("dve", "tensor_tensor", ["t1", "t2"], [o1res], out=o1, in0=t1[0:P, :], in1=t2[0:P, :], op=ALU.subtract)
            self.vec("dve", "tensor_tensor", ["x2s", "cosT"], ["t1"], out=t1[0:P, :], in0=x2s[0:P, :], in1=cosT[0:P, :], op=ALU.mult)
            self.vec("dve", "tensor_tensor", ["x1s", "sinT"], ["t2"], out=t2[0:P, :], in0=x1s[0:P, :], in1=sinT[0:P, :], op=ALU.mult)
            self.vec("dve", "tensor_tensor", ["t1", "t2"], [o2res], out=o2, in0=t1[0:P, :], in1=t2[0:P, :], op=ALU.add)

        for t in range(NT):
            cols = slice(t * 512, (t + 1) * 512)
            for jj in range(4):
                j = t * 4 + jj
                self.dma("sp", xt[:, jj, :], self.xs_d[j * 128:(j + 1) * 128, :], [("xd", j)], [("xt", jj)], semkey=("xtl", 0))
            self.norm_mod_T(t, l, s, hT, "hT", xn, junk, xsrc=lambda jj: xt[:, jj, :], xr=lambda jj: ("xt", jj))
            self.dma("sp", posi[:], ins["pos"][0:1, cols].partition_broadcast(128), [], ["posi"])
            self.vec("dve", "tensor_copy", ["posi"], ["ang"], out=ang[:], in_=posi[:])
            self.vec("dve", "tensor_scalar", ["ang", "mlaP"], ["ang"], out=ang[:], in0=ang[:], scalar1=mlaP[:, 8:9], scalar2=None,
                     op0=ALU.mult)
            C1 = 6.28125
            C2 = 2 * PI - C1
            for which, (dst, dres) in enumerate(((sinT, "sinT"), (cosT, "cosT"))):
                r = rr[which]
                rres = ("rr", which)
                src = ang
                if which == 1:
                    self.vec("dve", "tensor_scalar", ["ang"], ["t1"], out=t1[:], in0=ang[:], scalar1=PI / 2, scalar2=None, op0=ALU.add)
                    src = t1
                sres = "ang" if which == 0 else "t1"
                self.vec("dve", "tensor_scalar", [sres], ["t2"], out=t2[:], in0=src[:], scalar1=1.0 / (2 * PI), scalar2=None, op0=ALU.mult)
                self.vec("dve", "tensor_copy", ["t2"], ["posi"], out=posi[:], in_=t2[:])
                self.vec("dve", "tensor_copy", ["posi"], ["t2"], out=t2[:], in_=posi[:])
                self.vec("dve", "scalar_tensor_tensor", ["t2", sres], [rres], out=r[:], in0=t2[:], scalar=-C1, in1=src[:],
                         op0=ALU.mult, op1=ALU.add)
                self.vec("dve", "scalar_tensor_tensor", ["t2", rres], [rres], out=r[:], in0=t2[:], scalar=-C2, in1=r[:],
                         op0=ALU.mult, op1=ALU.add)
                self.vec("dve", "tensor_scalar", [rres], ["x1s"], out=x1s[:], in0=r[:], scalar1=PI, scalar2=-2 * PI, op0=ALU.is_gt,
                         op1=ALU.mult)
                self.vec("dve", "tensor_scalar", [rres], ["x2s"], out=x2s[:], in0=r[:], scalar1=-PI, scalar2=2 * PI, op0=ALU.is_lt,
                         op1=ALU.mult)
                self.vec("dve", "tensor_tensor", [rres, "x1s"], [rres], out=r[:], in0=r[:], in1=x1s[:], op=ALU.add)
                self.vec("dve", "tensor_tensor", [rres, "x2s"], [rres], out=r[:], in0=r[:], in1=x2s[:], op=ALU.add)
                self.act(dst[:], r[:], AF.Sin, [rres], [dres])
            for jj in range(4):
                tok = slice(jj * 128, (jj + 1) * 128)
                for n in range(2):
                    for k in range(KC):
                        self.mm(ps[n][:], hT[:, k, tok], w_in[:, k, n * 512:(n + 1) * 512], k == 0, k == KC - 1,
                                ["hT", "mwin"], [("ps", n)])
                self.act(junk[:, 0:512], ps[0][:], AF.Square, [("ps", 0)], ["junk", "mst"], accum_out=st[:, 0:1])
                self.act(junk[:, 512:768], ps[1][:, 0:256], AF.Square, [("ps", 1)], ["junk", "mst"], accum_out=st[:, 1:2])
                self.act(junk[:, 768:1024], ps[1][:, 256:512], AF.Square, [("ps", 1)], ["junk", "mst"], accum_out=st[:, 2:3])
                self.vec("dve", "tensor_tensor", ["mst"], ["mst"], out=st[:, 3:4], in0=st[:, 0:1], in1=st[:, 1:2], op=ALU.add)
                self.vec("dve", "tensor_scalar", ["mst"], ["mst"], out=st[:, 4:5], in0=st[:, 3:4], scalar1=1.0 / 768, scalar2=EPS,
                         op0=ALU.mult, op1=ALU.add)
                self.vec("dve", "tensor_scalar", ["mst"], ["mst"], out=st[:, 5:6], in0=st[:, 2:3], scalar1=1.0 / 256, scalar2=EPS,
                         op0=ALU.mult, op1=ALU.add)
                self.act(st[:, 4:6], st[:, 4:6], AF.Sqrt, ["mst"], ["mst"])
                self.vec("dve", "reciprocal", ["mst"], ["mst"], out=st[:, 4:6], in_=st[:, 4:6])
                self.vec("dve", "tensor_scalar", [("ps", 0), "mst"], [("xn", jj)], out=xn[:, jj, 0:512], in0=ps[0][:],
                         scalar1=st[:, 4:5], scalar2=None, op0=ALU.mult)
                self.vec("dve", "tensor_scalar", [("ps", 1), "mst"], [("xn", jj)], out=xn[:, jj, 512:768], in0=ps[1][:, 0:256],
                         scalar1=st[:, 4:5], scalar2=None, op0=ALU.mult)
                self.vec("dve", "tensor_scalar", [("ps", 1), "mst"], [("xn", jj)], out=xn[:, jj, 768:1024], in0=ps[1][:, 256:512],
                         scalar1=st[:, 5:6], scalar2=None, op0=ALU.mult)
            for k in range(8):
                bank = 4 + (k % 4)
                pv = ps[bank][:].bitcast(BF16)
                for jj in range(4):
                    self.tr(pv[:, jj * 128:(jj + 1) * 128], xn[:, jj, k * 128:(k + 1) * 128], ident[:], [("xn", jj), "ident"],
                            [("ps", bank)])
                dst = qnT[:, k, :] if k < 6 else kvnT[:, k - 6, :]
                self.act(dst, pv[:, 0:512], AF.Copy, [("ps", bank), "mlaP"], ["qnT" if k < 6 else "kvnT"], scale=mlaP[:, k:k + 1])
            for half in range(2):
                for k in range(KC):
                    self.mm(ps[2 + half][0:16, :], w_in[:, k, 1024 + half * 16:1040 + half * 16], hT[:, k, :], k == 0, k == KC - 1,
                            ["mwin", "hT"], [("ps", 2 + half)])
            rope(ps[2][0:16, :], ps[3][0:16, :], 16, ("ps", 2), ("ps", 3), ro[0][0:16, :], ro[1][0:16, :], ("ro", 0), ("ro", 1))
            self.dma("sp", KR_d[0:16, cols], ro[0][0:16, :], [("ro", 0)], [("KR_d", t)], semkey=("scr", 0))
            self.dma("sp", KR_d[16:32, cols], ro[1][0:16, :], [("ro", 1)], [("KR_d", t)], semkey=("scr", 0))
            for c in range(8):
                bk = c % 2
                for k in range(6):
                    self.mm(ps[bk][:], w_uq[:, k, c * 128:(c + 1) * 128], qnT[:, k, :], k == 0, k == 5, ["mwuq", "qnT"], [("ps", bk)])
                self.act(qsb[bk][:], ps[bk][:], AF.Copy, [("ps", bk)], [("qsb", bk)])
                self.dma("sp", QN_d[c * 128:(c + 1) * 128, cols], qsb[bk][:], [("qsb", bk)], [("QN_d", t)], semkey=("scr", 1 + bk))
            for hg in range(2):
                for k in range(6):
                    self.mm(ps[2][:], w_uq[:, k, 1024 + hg * 128:1152 + hg * 128], qnT[:, k, :], k == 0, k == 5, ["mwuq", "qnT"],
                            [("ps", 2)])
                for k in range(6):
                    self.mm(ps[3][:], w_uq[:, k, 1280 + hg * 128:1408 + hg * 128], qnT[:, k, :], k == 0, k == 5, ["mwuq", "qnT"],
                            [("ps", 3)])
                rope(ps[2][:], ps[3][:], 128, ("ps", 2), ("ps", 3), ro[0][:], ro[1][:], ("ro", 0), ("ro", 1))
                self.dma("sp", QR_d[hg * 128:(hg + 1) * 128, cols], ro[0][:], [("ro", 0)], [("QR_d", t)], semkey=("scr", 0))
                self.dma("sp", QR_d[256 + hg * 128:256 + (hg + 1) * 128, cols], ro[1][:], [("ro", 1)], [("QR_d", t)], semkey=("scr", 0))
            for c in range(8):
                bk = c % 2
                for k in range(2):
                    self.mm(ps[bk][:], w_ukv[:, k, c * 128:(c + 1) * 128], kvnT[:, k, :], k == 0, k == 1, ["mwukv", "kvnT"], [("ps", bk)])
                self.vec("dve", "tensor_copy", [("ps", bk)], [("qsb", bk)], out=qsb[bk][:], in_=ps[bk][:])
                self.dma("sp", KN_d[c * 128:(c + 1) * 128, cols], qsb[bk][:], [("qsb", bk)], [("KN_d", t)], semkey=("scr", 1 + bk))
            for jj in range(4):
                j = t * 4 + jj
                tok = slice(jj * 128, (jj + 1) * 128)
                vb = vsb[jj % 2]
                for n in range(2):
                    for k in range(2):
                        self.mm(ps[2 + n][:], kvnT[:, k, tok], w_ukv[:, k, 1024 + n * 512:1536 + n * 512], k == 0, k == 1,
                                ["kvnT", "mwukv"], [("ps", 2 + n)])
                    if n == 0:
                        self.act(vb[:, 0:512], ps[2][:], AF.Copy, [("ps", 2)], [("vsb", jj % 2)])
                    else:
                        self.vec("dve", "tensor_copy", [("ps", 3)], [("vsb", jj % 2)], out=vb[:, 512:1024], in_=ps[3][:])
                self.dma("sp", V_d[j * 128:(j + 1) * 128, :], vb[:], [("vsb", jj % 2)], [("V_d", j)], semkey=("scr", 3 + jj % 2))
        S.barrier()
        self.release(mA)
        KhT = [self.sb("KhT%d" % i, [96, seq], BF16) for i in range(2)]
        QhT = [self.sb("QhT%d" % i, [96, seq], BF16) for i in range(2)]
        V1 = [self.sb("V1_%d" % i, [128, NB, 65], BF16) for i in range(2)]
        PT = [self.sb("PT%d" % i, [128, 512], BF16) for i in range(3)]
        rd = self.sb("rd", [128, 512], F32)
        bcs = self.sb("bcs", [64, 512], F32)
        osb = [self.sb("osb%d" % i, [64, 512], BF16) for i in range(2)]
        for i in range(2):
            self.vec("dve", "memset", [], [("V1", i)], V1[i][:, :, 64:65], 1.0)
        scale = 1.0 / math.sqrt(96.0)
        blocks = []
        for h in range(16):
            for qt in range(NT):
                nkb = 4 * (qt + 1)
                for kb in range(nkb):
                    blocks.append((h, qt, kb, nkb))
        loaded_heads = set()

        def load_head(h):
            if h in loaded_heads or h >= 16:
                return
            loaded_heads.add(h)
            hb = h % 2
            self.dma("sp", KhT[hb][0:64, :], KN_d[h * 64:(h + 1) * 64, :], [], [("KhT", hb, 0)], semkey=("ld", hb))
            self.dma("sp", KhT[hb][64:96, :], KR_d[0:32, :], [], [("KhT", hb, 1)], semkey=("ld", hb))
            self.dma("sp", QhT[hb][0:64, :], QN_d[h * 64:(h + 1) * 64, :], [], [("QhT", hb, 0)], semkey=("ld", hb))
            self.dma("sp", QhT[hb][64:80, :], QR_d[h * 16:(h + 1) * 16, :], [], [("QhT", hb, 1)], semkey=("ld", hb))
            self.dma("sp", QhT[hb][80:96, :], QR_d[256 + h * 16:256 + (h + 1) * 16, :], [], [("QhT", hb, 2)], semkey=("ld", hb))
            self.dma("sp", V1[hb][:, :, 0:64], V_d[:, h * 64:(h + 1) * 64].rearrange("(j p) c -> p j c", p=128), [],
                     [("V1", hb)], semkey=("ld", hb))

        def geom(i):
            h, qt, kb, nkb = blocks[i]
            jd = kb - 4 * qt
            c0 = 128 * jd if jd > 0 else 0
            return h, qt, kb, nkb, jd, c0, i % 3

        def stage1(i):
            h, qt, kb, nkb, jd, c0, sbk = geom(i)
            hb = h % 2
            kres = [("KhT", hb, 0), ("KhT", hb, 1)]
            qres = [("QhT", hb, 0), ("QhT", hb, 1), ("QhT", hb, 2)]
            pt = PT[sbk]
            self.mm(ps[sbk][:, c0:512], KhT[hb][:, kb * 128:(kb + 1) * 128], QhT[hb][:, qt * 512 + c0:(qt + 1) * 512],
                    True, True, kres + qres, [("ps", sbk)])
            self.act(pt[:, c0:512], ps[sbk][:, c0:512], AF.Exp, [("ps", sbk)], [("PT", sbk)], scale=scale)
            if jd >= 0:
                self.vec("dve", "tensor_tensor", [("PT", sbk), "tri"], [("PT", sbk)], out=pt[:, c0:c0 + 128],
                         in0=pt[:, c0:c0 + 128], in1=tri[:], op=ALU.mult)

        def stage2(i):
            h, qt, kb, nkb, jd, c0, sbk = geom(i)
            hb = h % 2
            ob = 4 + (qt % 2)
            pt = PT[sbk]
            self.mm(ps[ob][0:65, c0:512], V1[hb][:, kb, :], pt[:, c0:512], kb == 0, kb == nkb - 1,
                    [("V1", hb), ("PT", sbk)], [("ps", ob)])
            if kb == nkb - 1:
                self.vec("dve", "reciprocal", [("ps", ob)], ["rd"], out=rd[64:65, :], in_=ps[ob][64:65, :])
                self.mm(ps[6][0:64, :], cst[64:65, 256:320], rd[64:65, :], True, True, ["cst", "rd"], [("ps", 6)])
                self.act(bcs[:], ps[6][0:64, :], AF.Copy, [("ps", 6)], ["bcs"])
                o_ = osb[qt % 2]
                self.vec("dve", "tensor_tensor", [("ps", ob), "bcs"], [("osb", qt % 2)], out=o_[:], in0=ps[ob][0:64, :], in1=bcs[:],
                         op=ALU.mult)
                self.dma("sp", OT_d[h * 64:(h + 1) * 64, qt * 512:(qt + 1) * 512], o_[:], [("osb", qt % 2)], [("OT_d", qt)],
                         semkey=("ost", qt % 2))

        load_head(0)
        load_head(1)
        stage1(0)
        for i in range(len(blocks)):
            if i + 1 < len(blocks):
                stage1(i + 1)
            stage2(i)
            if i + 1 == len(blocks) or blocks[i + 1][0] != blocks[i][0]:
                load_head(blocks[i][0] + 2)
        S.barrier()
        self.release(mA)
        oT = [self.sb("oT%d" % i, [128, KC, 512], BF16) for i in range(2)]
        tmp = [self.sb("tmpm%d" % i, [128, 512], F32) for i in range(2)]
        for t in range(NT):
            cols = slice(t * 512, (t + 1) * 512)
            ot = oT[t % 2]
            self.dma("sp", ot[:], OT_d[:, cols].rearrange("(k p) t -> p k t", p=128), [], [("oT", t % 2)], semkey=("otl", t % 2))
            for jj in range(4):
                j = t * 4 + jj
                self.dma("sp", xt[:, jj, :], self.xs_d[j * 128:(j + 1) * 128, :], [("xd", j)], [("xt", jj)], semkey=("xtl", 0))
            for jj in range(4):
                j = t * 4 + jj
                tok = slice(jj * 128, (jj + 1) * 128)
                for n in range(2):
                    pb = ps[n]
                    for k in range(KC):
                        self.mm(pb[:], ot[:, k, tok], w_o[:, k, n * 512:(n + 1) * 512], k == 0, k == KC - 1, [("oT", t % 2), "mwo"],
                                [("ps", n)])
                    xs_ = xt[:, jj, n * 512:(n + 1) * 512]
                    self.vec("dve", "tensor_tensor", [("ps", n), "G"], [("tmpm", n)], out=tmp[n][:], in0=pb[:],
                             in1=G[:, n * 512:(n + 1) * 512], op=ALU.mult)
                    self.vec("dve", "tensor_tensor", [("tmpm", n), ("xt", jj)], [("xt", jj)], out=xs_, in0=tmp[n][:], in1=xs_,
                             op=ALU.add)
                self.dma("sp", self.xs_d[j * 128:(j + 1) * 128, :], xt[:, jj, :], [("xt", jj)], [("xd", j)], semkey=("xts", 0))
        self.sb_off = keep


    def final_norm(self, fin_in):
        m = self.mark()
        Gf = self.sb("Gf", [128, D], F32)
        junk = self.sb("junkf", [128, D], BF16)
        ob = [self.sb("ob%d" % i, [128, D], F32) for i in range(2)]
        self.dma("sp", Gf[:], fin_in.partition_broadcast(128), [], ["Gf"])
        for j in range(self.NB):
            xt = self.xres[:, j, :]
            ss = self.stat[:, 16 + j % 2:17 + j % 2]
            rs = self.stat[:, 18 + j % 2:19 + j % 2]
            o = ob[j % 2]
            self.act(junk[:], xt, AF.Square, [("x", j)], ["junkf", ("fss", j % 2)], accum_out=ss)
            self.vec("dve", "tensor_scalar", [("fss", j % 2)], [("frs", j % 2)], out=rs, in0=ss, scalar1=1.0 / D,
                     scalar2=EPS, op0=ALU.mult, op1=ALU.add)
            self.act(rs, rs, AF.Sqrt, [("frs", j % 2)], [("frs", j % 2)])
            self.vec("dve", "reciprocal", [("frs", j % 2)], [("frs", j % 2)], out=rs, in_=rs)
            self.vec("dve", "scalar_tensor_tensor", [("x", j), ("frs", j % 2), "Gf"], [("ob", j % 2)],
                     out=o[:], in0=xt, scalar=rs, in1=Gf[:], op0=ALU.mult, op1=ALU.mult)
            self.dma("sp", self.y_out[j * 128:(j + 1) * 128, :], o[:], [("ob", j % 2)], [("yout", j)],
                     semkey=("yout", j % 2))
        self.release(m)


def make_consts():
    c = np.zeros((128, 512), np.float32)
    c[:, 0:128] = np.eye(128, dtype=np.float32)
    ii = np.arange(128)
    c[:, 128:256] = (ii[:, None] <= ii[None, :]).astype(np.float32)
    c[:, 256:384] = 1.0
    for g, w in enumerate((2, 4, 8, 16)):
        c[:, 384 + g * 16: 384 + (g + 1) * 16] = 1.0 / np.minimum(np.arange(16) + 1.0, float(w))
    return c


def host_inputs(b, seq, x, c, positions, mod_w, mod_b, norm_g, ffn_w13, ffn_w2, ab_w_in, pool_w, pool_scale,
                ssd_conv_w, ssd_conv_b, ssd_dt_bias, ssd_a_log, ssd_d, ssd_norm_g, ab_w_out, mla_w_in,
                mla_q_norm_g, mla_w_uq, mla_kv_norm_g, mla_w_ukv, mla_w_o, final_norm_g):
    f = np.float32
    m = {}
    m["x"] = np.ascontiguousarray(x[b], dtype=f)
    m["c"] = np.ascontiguousarray(c[b].reshape(KC, 128).T, dtype=f)
    m["pos"] = np.ascontiguousarray(positions[b].reshape(1, seq), dtype=np.int32)
    m["mod_w"] = np.ascontiguousarray(mod_w, dtype=f)
    m["mod_b"] = np.ascontiguousarray(mod_b, dtype=f)
    m["ngP"] = np.ascontiguousarray(norm_g.reshape(2, 3, KC, 128).transpose(3, 0, 1, 2).reshape(128, 48), dtype=f)
    m["ffn_w13"] = np.ascontiguousarray(ffn_w13.reshape(4, D, 2 * DFF), dtype=f)
    m["ffn_w2"] = np.ascontiguousarray(ffn_w2.reshape(4, DFF, D), dtype=f)
    m["ab_w_in"] = np.ascontiguousarray(ab_w_in[0], dtype=f)
    m["pool_w"] = np.ascontiguousarray(pool_w[0].reshape(512, 128), dtype=f)
    abP = np.zeros((128, 80), f)
    abP[:, 0:4] = pool_scale[0].reshape(4, 128).T
    abP[:, 4:52] = ssd_conv_w[0].reshape(4, 12, 128).transpose(2, 1, 0).reshape(128, 48)
    abP[:, 52:64] = ssd_conv_b[0].reshape(12, 128).T
    abP[:, 64:72] = ssd_norm_g[0].reshape(8, 128).T
    m["abP"] = abP
    abR = np.zeros((1, 64), f)
    abR[0, 0:16] = ssd_dt_bias[0]
    abR[0, 16:32] = ssd_a_log[0]
    abR[0, 32:48] = ssd_d[0]
    m["abR"] = abR
    sel = np.zeros((16, 16, 128), f)
    for h in range(16):
        sel[h, h, :] = 1.0
    m["sel"] = sel.reshape(16, 2048)
    ii = np.arange(128)
    negm = np.where(ii[None, :] < ii[:, None], -30000.0, 0.0).astype(f)
    m["cmask"] = np.ascontiguousarray(np.tile(negm, (1, 4)))
    m["ab_w_out"] = np.ascontiguousarray(ab_w_out[0], dtype=f)
    m["mla_w_in"] = np.ascontiguousarray(mla_w_in[0], dtype=f)
    wq = mla_w_uq[0].reshape(768, 16, 96)
    m["mla_w_uq"] = np.ascontiguousarray(np.concatenate(
        [wq[:, :, 0:64].reshape(768, 1024), wq[:, :, 64:80].reshape(768, 256), wq[:, :, 80:96].reshape(768, 256)], axis=1), dtype=f)
    wkv = mla_w_ukv[0].reshape(256, 16, 128)
    m["mla_w_ukv"] = np.ascontiguousarray(np.concatenate(
        [wkv[:, :, 0:64].reshape(256, 1024), wkv[:, :, 64:128].reshape(256, 1024)], axis=1), dtype=f)
    m["mla_w_o"] = np.ascontiguousarray(mla_w_o[0], dtype=f)
    mp = np.zeros((128, 16), f)
    mp[:, 0:6] = mla_q_norm_g[0].reshape(6, 128).T
    mp[:, 6:8] = mla_kv_norm_g[0].reshape(2, 128).T
    mp[:, 8] = np.tile(INV_FREQ, 8)
    m["mlaP"] = mp
    m["final_g"] = np.ascontiguousarray(final_norm_g.reshape(1, D), dtype=f)
    m["consts"] = make_consts()
    return m


_NC_CACHE = {}


def kernel(**inputs):
    inputs = {k: np.asarray(v) for k, v in inputs.items()}
    B, seq, _ = inputs["x"].shape
    key = (seq,)
    if key not in _NC_CACHE:
        _NC_CACHE[key] = Builder(seq).build()
    nc = _NC_CACHE[key]
    in_maps = [host_inputs(b, seq, **inputs) for b in range(B)]
    res = run_bass_kernel_spmd(nc, in_maps, core_ids=list(range(B)))
    return np.stack([np.asarray(r["y"], dtype=np.float32) for r in res.results], axis=0)
```

```python
import numpy as np
import concourse.bass as bass
import concourse.mybir as mybir
from concourse.bass_utils import run_bass_kernel_spmd

F32 = mybir.dt.float32
BF16 = mybir.dt.bfloat16
I32 = mybir.dt.int32
AF = mybir.ActivationFunctionType
ALU = mybir.AluOpType
AX = mybir.AxisListType

D = 1024
KC = 8
DFF = 2816
NF = 22
EPS = 1e-6
SEQ_FULL = 4096
IN_AB = 3088
IN_MLA = 1056

COMPUTE = ("pe", "act", "dve", "pool")
INV_FREQ = (np.float32(10000.0) ** (-np.arange(0, 32, 2, dtype=np.float32) / np.float32(32))).astype(np.float32)


class Op:
    __slots__ = ("q", "fn", "reads", "writes", "dma", "semkey", "deps", "signal", "idx", "need_sig", "bar_dma")


class Sched:
    def __init__(self, nc):
        self.nc = nc
        self.ops = []
        self.last_w = {}
        self.readers = {}
        self.sems = {}
        self.dma_count = {}
        self.bar_deps = []
        self.bar_dma = {}

    def op(self, q, fn, reads=(), writes=(), dma=False, semkey=None):
        o = Op()
        o.q, o.fn, o.reads, o.writes, o.dma = q, fn, tuple(reads), tuple(writes), dma
        o.idx = len(self.ops)
        o.need_sig = False
        o.bar_dma = self.bar_dma
        deps = set(self.bar_deps)
        for r in o.reads:
            w = self.last_w.get(r)
            if w is not None:
                deps.add(w)
        for w_ in o.writes:
            w = self.last_w.get(w_)
            if w is not None:
                deps.add(w)
            for rd in self.readers.get(w_, ()):
                deps.add(rd)
        deps.discard(o.idx)
        o.deps = deps
        if dma:
            if semkey is None:
                semkey = ("dma",) + tuple(o.writes[:1])
            o.semkey = semkey
            n = self.dma_count.get(semkey, 0) + 1
            self.dma_count[semkey] = n
            o.signal = (semkey, 16 * n)
        else:
            o.semkey = q
            o.signal = None
        for r in o.reads:
            self.readers.setdefault(r, []).append(o.idx)
        for w_ in o.writes:
            self.last_w[w_] = o.idx
            self.readers[w_] = []
        self.ops.append(o)
        return o

    def barrier(self):
        lastq = {}
        for o in self.ops:
            if (not o.dma) and o.fn is not None:
                lastq[o.q] = o.idx
        self.bar_deps = list(lastq.values())
        self.bar_dma = {k: 16 * n for k, n in self.dma_count.items()}
        self.last_w = {}
        self.readers = {}

    def emit(self):
        nc = self.nc
        ops = self.ops
        for o in ops:
            for d in o.deps:
                p = ops[d]
                if p.q == "pe" and o.q == "pe" and not p.dma:
                    continue
                p.need_sig = True
        cnt = {q: 0 for q in COMPUTE}
        for o in ops:
            if not o.dma and o.fn is not None and o.need_sig:
                cnt[o.q] += 1
                o.signal = (o.q, cnt[o.q])
        def sem(key):
            s = self.sems.get(key)
            if s is None:
                s = nc.alloc_semaphore("s%d" % len(self.sems))
                self.sems[key] = s
            return s
        queues = {}
        for o in ops:
            queues.setdefault(o.q, []).append(o)
        dma_before = {}
        run = {}
        for o in ops:
            dma_before[o.idx] = dict(run) if False else None
        dma_positions = {}
        for o in ops:
            if o.dma:
                dma_positions.setdefault(o.semkey, []).append(o.idx)
        import bisect

        def emit_queue(qname, eng):
            waited = {}
            for o in queues.get(qname, []):
                need = dict(o.bar_dma)
                for d in o.deps:
                    p = ops[d]
                    if p.dma:
                        pos = dma_positions[p.semkey]
                        n = bisect.bisect_left(pos, o.idx)
                        key, val = p.semkey, 16 * n
                    else:
                        if p.fn is None:
                            continue
                        if p.q == "pe" and o.q == "pe" and not o.dma:
                            continue
                        key, val = p.signal
                    if need.get(key, 0) < val:
                        need[key] = val
                for key, val in need.items():
                    if waited.get(key, 0) >= val:
                        continue
                    waited[key] = val
                    eng.wait_ge(sem(key), val)
                if o.fn is None:
                    continue
                ins = o.fn(eng)
                if o.dma:
                    ins.then_inc(sem(o.semkey), 16)
                elif o.need_sig:
                    ins.then_inc(sem(o.q), 1)

        with nc.Block() as block:
            @block.tensor
            def _(e):
                emit_queue("pe", e)

            @block.scalar
            def _(e):
                emit_queue("act", e)

            @block.vector
            def _(e):
                emit_queue("dve", e)

            @block.gpsimd
            def _(e):
                emit_queue("pool", e)

            @block.sync
            def _(e):
                emit_queue("sp", e)


class Builder:
    def __init__(self, seq, subs=None, debug_out=None):
        self.seq = seq
        self.NT = seq // 512
        self.NB = seq // 128
        self.subs = subs
        nc = bass.Bass("TRN2", target_bir_lowering=False)
        self.nc = nc
        self.S = Sched(nc)
        self.sb_off = 16640
        self.sb_top = 229376
        self.nalloc = 0

    def sb(self, name, shape, dtype):
        esz = 4 if dtype in (F32, I32) else 2
        n = 1
        for s in shape[1:]:
            n *= s
        nbytes = (n * esz + 63) // 64 * 64
        off = self.sb_off
        assert off + nbytes <= self.sb_top, "SBUF overflow at %s: need %d have %d" % (name, nbytes, self.sb_top - off)
        self.sb_off += nbytes
        self.nalloc += 1
        return self.nc.alloc_sbuf_tensor_at("%s_%d" % (name, self.nalloc), list(shape), dtype, offset=off)

    def mark(self):
        return self.sb_off

    def release(self, m):
        self.sb_off = m

    def dram(self, name, shape, dtype, kind="Internal"):
        return self.nc.dram_tensor(name, list(shape), dtype, kind=kind).ap()

    def mm(self, out, lhsT, rhs, start, stop, reads, writes):
        self.S.op("pe", lambda e: e.matmul(out, lhsT, rhs, start=start, stop=stop), reads, writes)

    def tr(self, out, in_, ident, reads, writes):
        self.S.op("pe", lambda e: e.transpose(out, in_, ident), reads, writes)

    def act(self, out, in_, func, reads, writes, bias=None, scale=None, accum_out=None):
        kw = {}
        if bias is not None:
            kw["bias"] = bias
        if scale is not None:
            kw["scale"] = scale
        if accum_out is not None:
            kw["accum_out"] = accum_out
        self.S.op("act", lambda e: e.activation(out=out, in_=in_, func=func, **kw), reads, writes)

    def vec(self, q, method, reads, writes, *args, **kw):
        self.S.op(q, lambda e: getattr(e, method)(*args, **kw), reads, writes)

    def dma(self, q, out, in_, reads, writes, semkey=None):
        self.S.op(q, lambda e: e.dma_start(out=out, in_=in_), reads, writes, dma=True, semkey=semkey)

    def build(self):
        nc = self.nc
        seq, NT, NB = self.seq, self.NT, self.NB
        S = self.S
        x_in = self.dram("x", [seq, D], F32, "ExternalInput")
        c_in = self.dram("c", [128, KC], F32, "ExternalInput")
        pos_in = self.dram("pos", [1, seq], I32, "ExternalInput")
        mod_w = self.dram("mod_w", [2, D, 9 * D], F32, "ExternalInput")
        mod_b = self.dram("mod_b", [2, 9 * D], F32, "ExternalInput")
        ngP_in = self.dram("ngP", [128, 2 * 3 * KC], F32, "ExternalInput")
        w13_in = self.dram("ffn_w13", [4, D, 2 * DFF], F32, "ExternalInput")
        w2_in = self.dram("ffn_w2", [4, DFF, D], F32, "ExternalInput")
        abin_in = self.dram("ab_w_in", [D, IN_AB], F32, "ExternalInput")
        poolw_in = self.dram("pool_w", [512, 128], F32, "ExternalInput")
        abP_in = self.dram("abP", [128, 80], F32, "ExternalInput")
        abR_in = self.dram("abR", [1, 64], F32, "ExternalInput")
        about_in = self.dram("ab_w_out", [1536, D], F32, "ExternalInput")
        mlain_in = self.dram("mla_w_in", [D, IN_MLA], F32, "ExternalInput")
        wuq_in = self.dram("mla_w_uq", [768, 1536], F32, "ExternalInput")
        wukv_in = self.dram("mla_w_ukv", [256, 2048], F32, "ExternalInput")
        wo_in = self.dram("mla_w_o", [D, D], F32, "ExternalInput")
        mlaP_in = self.dram("mlaP", [128, 16], F32, "ExternalInput")
        fin_in = self.dram("final_g", [1, D], F32, "ExternalInput")
        cst_in = self.dram("consts", [128, 512], F32, "ExternalInput")
        sel_in = self.dram("sel", [16, 2048], F32, "ExternalInput")
        cmask_in = self.dram("cmask", [128, 512], F32, "ExternalInput")
        self.xs_d = self.dram("xs_d", [seq, D], F32)
        self.ins = dict(abin=abin_in, poolw=poolw_in, abP=abP_in, abR=abR_in, about=about_in, mlain=mlain_in,
                        wuq=wuq_in, wukv=wukv_in, wo=wo_in, mlaP=mlaP_in, sel=sel_in, cmask=cmask_in, pos=pos_in)
        y_out = self.dram("y", [seq, D], F32, "ExternalOutput")
        self.x_in, self.y_out = x_in, y_out

        w13_s = self.dram("w13_s", [4, D, 2 * DFF], BF16)
        w2_s = self.dram("w2_s", [4, DFF, D], BF16)
        mod_d = self.dram("mod_d", [2, 9 * D], F32)
        self.w13_s, self.w2_s, self.mod_d = w13_s, w2_s, mod_d

        self.ident = self.sb("ident", [128, 128], BF16)
        self.cst = self.sb("cst", [128, 512], F32)
        self.modP = self.sb("modP", [128, 2, 9, KC], F32)
        self.ngP = self.sb("ngP", [128, 2, 3, KC], F32)
        self.aP = self.sb("aP", [128, 2, 3, KC], F32)
        self.stat = self.sb("stat", [128, 64], F32)
        self.x_off = self.sb_off
        self.xres = self.sb("xres", [128, max(NB, 32), D], F32)
        self.ps = [nc.alloc_psum_tensor("ps%d" % i, [128, 512], F32) for i in range(8)]

        self.dma("sp", self.cst[:], cst_in, ["cst_in"], ["cst"])
        self.vec("dve", "tensor_copy", ["cst"], ["ident"], out=self.ident[:], in_=self.cst[:, 0:128])
        self.dma("sp", self.ngP[:].rearrange("p a b c -> p (a b c)"), ngP_in, [], ["ngP"])
        for j in range(NB):
            self.dma("sp", self.xres[:, j, :], x_in[j * 128:(j + 1) * 128, :], [], [("x", j)], semkey=("xload", j % 4))
        self.modulation(c_in, mod_w, mod_b)
        for i in range(4):
            self.precast(w13_s[i], w13_in[i], D, 2 * DFF, ("w13s", i))
            self.precast(w2_s[i], w2_in[i], DFF, D, ("w2s", i))

        def want(l, s):
            return self.subs is None or (l, s) in self.subs

        for l in range(2):
            if want(l, 0):
                self.ffn(l, 0)
            if want(l, 1):
                self.spill_x()
                if l == 0:
                    self.mixer_ab()
                else:
                    self.mixer_mla()
                self.reload_x()
            if want(l, 2):
                self.ffn(l, 1)
        self.final_norm(fin_in)
        S.op("sp", None, reads=[("yout", j) for j in range(NB)])
        S.emit()
        return nc

    def precast(self, dst, src, rows, cols, res):
        a = rows // 128
        half = max(1, a // 2)
        for h0 in range(0, a, half):
            h1 = min(a, h0 + half)
            d = dst.rearrange("(p a) c -> p a c", p=128)[:, h0:h1, :]
            s = src.rearrange("(p a) c -> p a c", p=128)[:, h0:h1, :]
            self.dma("pool", d, s, [], [res], semkey=("pc",) + tuple(res))

    def modulation(self, c_in, mod_w, mod_b):
        m = self.mark()
        cact = self.sb("cact", [128, KC], F32)
        mrow = self.sb("mrow", [1, 9 * D], F32)
        modT = self.sb("modT", [128, 128], F32)
        wst = [self.sb("modw%d" % i, [128, KC, 512], F32) for i in range(2)]
        self.dma("sp", cact[:], c_in, [], ["cact"])
        self.act(cact[:], cact[:], AF.Silu, ["cact"], ["cact"])
        for l in range(2):
            self.dma("sp", mrow[:], mod_b[l:l + 1, :], [], ["mrow"])
            for cb in range(18):
                slot = (l * 18 + cb) % 2
                self.dma("sp", wst[slot][:], mod_w[l, :, cb * 512:(cb + 1) * 512].rearrange("(k p) c -> p k c", p=128),
                         [], [("modw", slot)])
                pst = self.ps[cb % 2]
                for k in range(KC):
                    self.mm(pst[0:1, :], cact[:, k:k + 1], wst[slot][:, k, :], k == 0, k == KC - 1,
                            ["cact", ("modw", slot)], [("ps", cb % 2)])
                self.vec("dve", "tensor_tensor", [("ps", cb % 2), "mrow"], ["mrow"],
                         out=mrow[0:1, cb * 512:(cb + 1) * 512], in0=pst[0:1, :], in1=mrow[0:1, cb * 512:(cb + 1) * 512],
                         op=ALU.add)
            self.dma("sp", self.mod_d[l:l + 1, :], mrow[:], ["mrow"], [("mod_d", l)])
            self.dma("sp", modT[0:72, :], self.mod_d[l, :].rearrange("(r p) -> r p", p=128), [("mod_d", l)], ["modT"])
            self.tr(self.ps[2][:, 0:72], modT[0:72, :], self.cst[0:72, 0:72], ["modT", "cst"], [("ps", 2)])
            self.vec("dve", "tensor_copy", [("ps", 2)], ["modP"],
                     out=self.modP[:, l, :, :].rearrange("p j k -> p (j k)"), in_=self.ps[2][:, 0:72])
        for l in range(2):
            for s in range(3):
                self.vec("dve", "scalar_tensor_tensor", ["modP", "ngP"], ["aP"],
                         out=self.aP[:, l, s, :], in0=self.modP[:, l, 3 * s + 1, :], scalar=1.0,
                         in1=self.ngP[:, l, s, :], op0=ALU.add, op1=ALU.mult)
        self.S.barrier()
        self.release(m)

    def norm_mod_T(self, t, l, s, hT, hres, xn, junk, xsrc=None, xr=None):
        xts, xrs = [], []
        for jj in range(4):
            j = t * 4 + jj
            xt = self.xres[:, j, :] if xsrc is None else xsrc(jj)
            xres_ = ("x", j) if xr is None else xr(jj)
            xts.append(xt)
            xrs.append(xres_)
            if junk is None:
                self.act(xn[:, jj, :], xt, AF.Square, [xres_], [("xn", jj), "ss"], accum_out=self.stat[:, jj:jj + 1])
            else:
                self.act(junk[:], xt, AF.Square, [xres_], ["junk", "ss"], accum_out=self.stat[:, jj:jj + 1])
        rs4 = self.stat[:, 8:12]
        self.vec("dve", "tensor_scalar", ["ss"], ["rs"], out=rs4, in0=self.stat[:, 0:4], scalar1=1.0 / D, scalar2=EPS,
                 op0=ALU.mult, op1=ALU.add)
        self.act(rs4, rs4, AF.Sqrt, ["rs"], ["rs"])
        self.vec("dve", "reciprocal", ["rs"], ["rs"], out=rs4, in_=rs4)
        for jj in range(4):
            self.vec("dve", "tensor_scalar", [xrs[jj], "rs"], [("xn", jj)], out=xn[:, jj, :], in0=xts[jj],
                     scalar1=self.stat[:, 8 + jj:9 + jj], scalar2=None, op0=ALU.mult)
        for k in range(KC):
            bank = 4 + (k % 4)
            pv = self.ps[bank][:].bitcast(BF16)
            for jj in range(4):
                self.tr(pv[:, jj * 128:(jj + 1) * 128], xn[:, jj, k * 128:(k + 1) * 128], self.ident[:],
                        [("xn", jj), "ident"], [("ps", bank)])
            self.act(hT[:, k, :], pv[:, 0:512], AF.Identity, [("ps", bank), "aP", "modP"], [hres],
                     scale=self.aP[:, l, s, k:k + 1], bias=self.modP[:, l, 3 * s, k:k + 1])

    def load_gate(self, G, l, s):
        self.dma("sp", G[:], self.mod_d[l:l + 1, (3 * s + 2) * D:(3 * s + 3) * D].partition_broadcast(128),
                 [("mod_d", l)], ["G"])

    def ffn(self, l, s2):
        S = self.S
        fi = l * 2 + s2
        s = 0 if s2 == 0 else 2
        NT = self.NT
        m = self.mark()
        hT = self.sb("hT", [128, KC, 512], BF16)
        gT = self.sb("gT", [128, NF, 512], BF16)
        xn = self.sb("xn", [128, 4, D], BF16)
        G = self.sb("G", [128, D], F32)
        sg = [self.sb("sg%d" % i, [128, 512], F32) for i in range(2)]
        NSL = 3
        ring = [self.sb("ring%d" % i, [128, 4096], BF16) for i in range(NSL)]
        self.load_gate(G, l, s)
        w2grp = [(0, 8), (8, 8), (16, 6)]
        pieces = []
        for t in range(NT):
            for pc in range(11):
                pieces.append(("w13", pc))
            for n in range(2):
                for g in range(3):
                    pieces.append(("w2", n, g))
        issued = [0]

        def issue():
            gi = issued[0]
            p = pieces[gi]
            sl = gi % NSL
            issued[0] += 1
            if p[0] == "w13":
                pc = p[1]
                v = ring[sl][:].rearrange("p (k a c) -> p k a c", k=KC, a=2)
                for ab in range(2):
                    src = self.w13_s[fi, :, ab * DFF + pc * 256: ab * DFF + (pc + 1) * 256]
                    self.dma("sp", v[:, :, ab, :], src.rearrange("(k p) c -> p k c", p=128),
                             [("w13s", fi)], [("ring", sl, ab)])
            else:
                n, g = p[1], p[2]
                f0, nf = w2grp[g]
                v = ring[sl][:].rearrange("p (f c) -> p f c", f=8)
                src = self.w2_s[fi, f0 * 128:(f0 + nf) * 128, n * 512:(n + 1) * 512]
                self.dma("sp", v[:, 0:nf, :], src.rearrange("(f p) c -> p f c", p=128), [("w2s", fi)],
                         [("ring", sl, 0), ("ring", sl, 1)])

        def ensure(gi):
            while issued[0] < len(pieces) and issued[0] < gi + NSL:
                issue()
            return gi % NSL

        pi = 0
        for t in range(NT):
            self.norm_mod_T(t, l, s, hT, "hT", xn, None)
            for pc in range(11):
                sl = ensure(pi)
                pi += 1
                wv = ring[sl][:].rearrange("p (k a c) -> p k a c", k=KC, a=2)
                for ff in range(2):
                    f = pc * 2 + ff
                    pa, pb = self.ps[ff * 2], self.ps[ff * 2 + 1]
                    for k in range(KC):
                        self.mm(pa[:], wv[:, k, 0, ff * 128:(ff + 1) * 128], hT[:, k, :], k == 0, k == KC - 1,
                                [("ring", sl, 0), "hT"], [("ps", ff * 2)])
                    for k in range(KC):
                        self.mm(pb[:], wv[:, k, 1, ff * 128:(ff + 1) * 128], hT[:, k, :], k == 0, k == KC - 1,
                                [("ring", sl, 1), "hT"], [("ps", ff * 2 + 1)])
                    self.act(sg[ff][:], pa[:], AF.Silu, [("ps", ff * 2)], [("sg", ff)])
                    self.vec("dve", "tensor_tensor", [("sg", ff), ("ps", ff * 2 + 1)], [("gT", f)],
                             out=gT[:, f, :], in0=sg[ff][:], in1=pb[:], op=ALU.mult)
            for n in range(2):
                for g in range(3):
                    sl = ensure(pi)
                    pi += 1
                    f0, nf = w2grp[g]
                    wv = ring[sl][:].rearrange("p (f c) -> p f c", f=8)
                    for fl in range(nf):
                        f = f0 + fl
                        for jj in range(4):
                            self.mm(self.ps[4 + jj][:], gT[:, f, jj * 128:(jj + 1) * 128], wv[:, fl, :],
                                    f == 0, f == NF - 1, [("gT", f), ("ring", sl, 0), ("ring", sl, 1)], [("ps", 4 + jj)])
                for jj in range(4):
                    j = t * 4 + jj
                    xs = self.xres[:, j, n * 512:(n + 1) * 512]
                    tb = sg[jj % 2]
                    self.vec("dve", "tensor_tensor", [("ps", 4 + jj), "G"], [("sg", jj % 2)],
                             out=tb[:], in0=self.ps[4 + jj][:], in1=G[:, n * 512:(n + 1) * 512], op=ALU.mult)
                    self.vec("dve", "scalar_tensor_tensor", [("sg", jj % 2), ("x", j)], [("x", j)],
                             out=xs, in0=tb[:], scalar=0.5, in1=xs, op0=ALU.mult, op1=ALU.add)
        S.barrier()
        self.release(m)

    def spill_x(self):
        for j in range(self.NB):
            self.dma("sp", self.xs_d[j * 128:(j + 1) * 128, :], self.xres[:, j, :], [("x", j)], [("xd", j)],
                     semkey=("xsp", j % 4))
        self.S.barrier()

    def reload_x(self):
        self.S.barrier()
        for j in range(self.NB):
            self.dma("sp", self.xres[:, j, :], self.xs_d[j * 128:(j + 1) * 128, :], [("xd", j)], [("x", j)],
                     semkey=("xload", j % 4))

    def mixer_ab(self):
        S = self.S
        ins = self.ins
        l, s = 0, 1
        NT = self.NT
        keep = self.sb_off
        self.sb_off = self.x_off
        cst = self.cst
        w_in = self.sb("abwin", [128, KC, IN_AB], BF16)
        w_out = self.sb("abwout", [128, 12, D], BF16)
        pw = self.sb("poolw", [128, 4, 128], BF16)
        abP = self.sb("abP", [128, 80], F32)
        rowp = self.sb("rowp", [128, 48], F32)
        a_bc = self.sb("a_bc", [128, 16], F32)
        sel = self.sb("sel", [16, 16, 128], BF16)
        negm = self.sb("negm", [128, 512], BF16)
        G = self.sb("G", [128, D], F32)
        hT = self.sb("hT", [128, KC, 512], BF16)
        xn = self.sb("xn", [128, 4, D], BF16)
        junk = self.sb("junk", [128, D], BF16)
        xt = [self.sb("xt0", [128, 4, D], F32)] * 2
        Up = self.sb("Up", [128, 528], F32)
        sA = self.sb("sA", [128, 528], F32)
        sB = self.sb("sB", [128, 528], F32)
        phalo = self.sb("phalo", [128, 4, 16], F32)
        dT = [self.sb("dT%d" % i, [128, 512], BF16) for i in range(2)]
        dfix = self.sb("dfix", [128, 16], F32)
        ypT = self.sb("ypT", [128, 4, 512], BF16)
        Uc = [self.sb("Uc0", [128, 515], F32)] * 2
        chalo = self.sb("chalo", [128, 12, 3], F32)
        acc = [self.sb("acc0", [128, 512], F32)] * 2
        xbcT = self.sb("xbcT", [128, 12, 512], BF16)
        zs = self.sb("zs", [128, D], F32)
        dtt = self.sb("dtt", [128, 96], F32)
        sm2 = self.sb("sm2", [128, 48], F32)
        acsT = self.sb("acsT", [16, 3, 128], F32)
        acsHL = self.sb("acsHL", [16, 2, 128], BF16)
        xs_tok = self.sb("xs_tok", [128, D], BF16)
        xw = self.sb("xw", [128, D], BF16)
        xsD = self.sb("xsD", [128, D], BF16)
        Btok = self.sb("Btok", [128, 2, 128], BF16)
        cb = self.sb("cb", [128, 2, 128], F32)
        Eexp = [self.sb("Eexp%d" % i, [128, 512], F32) for i in range(2)]
        Mt = [self.sb("Mt%d" % i, [128, 512], BF16) for i in range(2)]
        H = self.sb("H", [128, D], F32)
        Hb = self.sb("Hb", [128, D], BF16)
        yA = self.sb("yA", [128, D], F32)
        ssg = self.sb("ssg", [128, 4], F32)
        yn = self.sb("yn", [128, D], BF16)
        yT = self.sb("yT", [128, KC, 512], BF16)
        tmp = [self.sb("tmpo%d" % i, [128, 512], F32) for i in range(2)]
        ps = self.ps
        ident = self.ident

        self.dma("pool", w_in[:], ins["abin"].rearrange("(k p) c -> p k c", p=128), [], ["abwin"])
        self.dma("pool", w_out[:], ins["about"].rearrange("(k p) c -> p k c", p=128), [], ["abwout"])
        self.dma("pool", pw[:], ins["poolw"].rearrange("(g p) c -> p g c", p=128), [], ["poolw"])
        self.dma("pool", sel[:].rearrange("p a b -> p (a b)"), ins["sel"], [], ["sel"])
        self.dma("pool", negm[:], ins["cmask"], [], ["negm"])
        self.dma("sp", abP[:], ins["abP"], [], ["abP"])
        self.dma("sp", rowp[:], ins["abR"][0:1, 0:48].partition_broadcast(128), [], ["rowp"])
        self.act(a_bc[:], rowp[:, 16:32], AF.Exp, ["rowp"], ["a_bc"])
        self.vec("dve", "tensor_scalar", ["a_bc"], ["a_bc"], out=a_bc[:], in0=a_bc[:], scalar1=-1.0, scalar2=None,
                 op0=ALU.mult)
        self.load_gate(G, l, s)
        self.vec("dve", "memset", [], ["phalo"], phalo[:], 0.0)
        self.vec("dve", "memset", [], ["chalo"], chalo[:], 0.0)
        self.vec("dve", "memset", [], ["H"], H[:], 0.0)
        self.vec("dve", "memset", [], ["Hb"], Hb[:], 0.0)
        D_bc = rowp[:, 32:48]
        dtb_bc = rowp[:, 0:16]
        wins = (2, 4, 8, 16)

        for t in range(NT):
            xb = xt[t % 2]
            for jj in range(4):
                j = t * 4 + jj
                self.dma("sp", xb[:, jj, :], self.xs_d[j * 128:(j + 1) * 128, :], [("xd", j)], [("xt", 0, jj)],
                         semkey=("xtl", t % 2))
            self.norm_mod_T(t, l, s, hT, "hT", xn, junk, xsrc=lambda jj: xb[:, jj, :],
                            xr=lambda jj: ("xt", 0, jj))
            for g in range(4):
                bk = g % 2
                for k in range(KC):
                    self.mm(ps[bk][:], w_in[:, k, g * 128:(g + 1) * 128], hT[:, k, :], k == 0, k == KC - 1,
                            ["abwin", "hT"], [("ps", bk)])
                self.vec("dve", "tensor_copy", ["phalo"], ["Up"], out=Up[:, 0:16], in_=phalo[:, g, :])
                self.act(Up[:, 16:528], ps[bk][:], AF.Copy, [("ps", bk)], ["Up"])
                self.vec("dve", "tensor_copy", ["Up"], ["phalo"], out=phalo[:, g, :], in_=Up[:, 512:528])
                self.vec("dve", "tensor_tensor", ["Up"], ["sA"], out=sA[:, 1:528], in0=Up[:, 1:528], in1=Up[:, 0:527],
                         op=ALU.add)
                lvl = sA
                if g >= 1:
                    self.vec("dve", "tensor_tensor", ["sA"], ["sB"], out=sB[:, 3:528], in0=sA[:, 3:528],
                             in1=sA[:, 1:526], op=ALU.add)
                    lvl = sB
                if g >= 2:
                    self.vec("dve", "tensor_tensor", ["sB"], ["sA"], out=sA[:, 7:528], in0=sB[:, 7:528],
                             in1=sB[:, 3:524], op=ALU.add)
                    lvl = sA
                if g >= 3:
                    self.vec("dve", "tensor_tensor", ["sA"], ["sB"], out=sB[:, 15:528], in0=sA[:, 15:528],
                             in1=sA[:, 7:520], op=ALU.add)
                    lvl = sB
                lres = "sA" if lvl is sA else "sB"
                dd = dT[g % 2]
                self.vec("dve", "scalar_tensor_tensor", [lres, "Up"], [("dT", g % 2)], out=dd[:], in0=lvl[:, 16:528],
                         scalar=1.0 / wins[g], in1=Up[:, 16:528], op0=ALU.mult, op1=ALU.subtract)
                if t == 0:
                    self.vec("dve", "tensor_tensor", [lres, "cst"], ["dfix"], out=dfix[:], in0=lvl[:, 16:32],
                             in1=cst[:, 384 + g * 16:384 + (g + 1) * 16], op=ALU.mult)
                    self.vec("dve", "tensor_tensor", ["dfix", "Up"], [("dT", g % 2)], out=dd[:, 0:16], in0=dfix[:],
                             in1=Up[:, 16:32], op=ALU.subtract)
                self.mm(ps[2 + bk][:], pw[:, g, :], dd[:], True, True, ["poolw", ("dT", g % 2)], [("ps", 2 + bk)])
                self.act(ypT[:, g, :], ps[2 + bk][:], AF.Copy, [("ps", 2 + bk), "abP"], [("ypT", g)],
                         scale=abP[:, g:g + 1])
            for cc in range(12):
                bk = cc % 2
                c0 = 1536 + cc * 128
                for k in range(KC):
                    self.mm(ps[bk][:], w_in[:, k, c0:c0 + 128], hT[:, k, :], k == 0, k == KC - 1,
                            ["abwin", "hT"], [("ps", bk)])
                U = Uc[bk]
                ur = ("Uc", 0)
                self.vec("dve", "tensor_copy", ["chalo"], [ur], out=U[:, 0:3], in_=chalo[:, cc, :])
                self.act(U[:, 3:515], ps[bk][:], AF.Copy, [("ps", bk)], [ur])
                self.vec("dve", "tensor_copy", [ur], ["chalo"], out=chalo[:, cc, :], in_=U[:, 512:515])
                a_ = acc[bk]
                ar = ("acc", 0)
                self.vec("dve", "tensor_scalar", [ur, "abP"], [ar], out=a_[:], in0=U[:, 0:512],
                         scalar1=abP[:, 4 + cc * 4:5 + cc * 4], scalar2=abP[:, 52 + cc:53 + cc], op0=ALU.mult, op1=ALU.add)
                for kk in range(1, 4):
                    self.vec("dve", "scalar_tensor_tensor", [ur, "abP", ar], [ar], out=a_[:], in0=U[:, kk:kk + 512],
                             scalar=abP[:, 4 + cc * 4 + kk:5 + cc * 4 + kk], in1=a_[:], op0=ALU.mult, op1=ALU.add)
                self.act(xbcT[:, cc, :], a_[:], AF.Silu, [ar], [("xbcT", cc)])
            for jj in range(4):
                j = t * 4 + jj
                tok = slice(jj * 128, (jj + 1) * 128)
                for n in range(2):
                    for k in range(KC):
                        self.mm(ps[2 + n][:], hT[:, k, tok], w_in[:, k, 512 + n * 512:1024 + n * 512], k == 0,
                                k == KC - 1, ["hT", "abwin"], [("ps", 2 + n)])
                    self.act(zs[:, n * 512:(n + 1) * 512], ps[2 + n][:], AF.Silu, [("ps", 2 + n)], ["zs"])
                for k in range(KC):
                    self.mm(ps[4][:, 0:16], hT[:, k, tok], w_in[:, k, 3072:3088], k == 0, k == KC - 1,
                            ["hT", "abwin"], [("ps", 4)])
                dt_, lndt, adt, nb, wst, eacs = (dtt[:, i * 16:(i + 1) * 16] for i in range(6))
                cd_bc, acs_sb, arg = (sm2[:, i * 16:(i + 1) * 16] for i in range(3))
                self.vec("dve", "tensor_tensor", [("ps", 4), "rowp"], ["dt"], out=dt_, in0=ps[4][:, 0:16], in1=dtb_bc,
                         op=ALU.add)
                self.act(dt_, dt_, AF.Exp, ["dt"], ["dt"])
                self.act(dt_, dt_, AF.Ln, ["dt"], ["dt"], bias=1.0)
                self.act(lndt, dt_, AF.Ln, ["dt"], ["lndt"])
                self.vec("dve", "tensor_tensor", ["dt", "a_bc"], ["adt"], out=adt, in0=dt_, in1=a_bc[:], op=ALU.mult)
                self.mm(ps[4][:, 16:32], cst[:, 128:256], adt, True, True, ["cst", "adt"], [("ps", 4)])
                self.mm(ps[4][0:16, 128:256], adt, cst[:, 128:256], True, True, ["cst", "adt"], [("ps", 4)])
                self.mm(ps[4][:, 32:48], cst[:, 256:384], adt, True, True, ["cst", "adt"], [("ps", 4)])
                self.vec("dve", "tensor_copy", [("ps", 4)], ["acs_sb"], out=acs_sb, in_=ps[4][:, 16:32])
                self.vec("dve", "tensor_tensor", ["lndt", "acs_sb"], ["nb"], out=nb, in0=lndt, in1=acs_sb, op=ALU.subtract)
                self.act(eacs, ps[4][:, 16:32], AF.Exp, [("ps", 4)], ["eacs"])
                self.vec("dve", "tensor_tensor", [("ps", 4), "nb"], ["arg"], out=arg, in0=ps[4][:, 32:48], in1=nb,
                         op=ALU.add)
                self.act(wst, arg, AF.Exp, ["arg"], ["wst"])
                self.act(cd_bc, ps[4][:, 32:48], AF.Exp, [("ps", 4)], ["cd_bc"])
                self.vec("dve", "tensor_copy", [("ps", 4)], ["acsT0"], out=acsT[:, 0, :], in_=ps[4][0:16, 128:256])
                self.vec("dve", "tensor_copy", ["acsT0"], ["acsHL0"], out=acsHL[:, 0, :], in_=acsT[:, 0, :])
                self.vec("dve", "tensor_copy", ["acsHL0"], ["acsT1"], out=acsT[:, 1, :], in_=acsHL[:, 0, :])
                self.vec("dve", "tensor_tensor", ["acsT0", "acsT1"], ["acsHL1"], out=acsHL[:, 1, :], in0=acsT[:, 0, :],
                         in1=acsT[:, 1, :], op=ALU.subtract)
                pv5 = ps[5][:].bitcast(BF16)
                for cc in range(8):
                    self.tr(pv5[:, cc * 128:(cc + 1) * 128], xbcT[:, cc, tok], ident[:], [("xbcT", cc), "ident"],
                            [("ps", 5)])
                self.act(xs_tok[:], pv5[:, 0:1024], AF.Copy, [("ps", 5)], ["xs_tok"])
                self.vec("dve", "tensor_tensor", [("ps", 5), "wst"], ["xw"],
                         out=xw[:].rearrange("p (h q) -> p h q", h=16), in0=pv5[:, 0:1024].rearrange("p (h q) -> p h q", h=16),
                         in1=wst.unsqueeze(2).to_broadcast([128, 16, 64]), op=ALU.mult)
                self.vec("dve", "tensor_tensor", ["xs_tok", "rowp"], ["xsD"],
                         out=xsD[:].rearrange("p (h q) -> p h q", h=16), in0=xs_tok[:].rearrange("p (h q) -> p h q", h=16),
                         in1=D_bc.unsqueeze(2).to_broadcast([128, 16, 64]), op=ALU.mult)
                pv7 = ps[7][:].bitcast(BF16)
                for g in range(2):
                    self.tr(pv7[:, g * 128:(g + 1) * 128], xbcT[:, 8 + g, tok], ident[:], [("xbcT", 8 + g), "ident"],
                            [("ps", 7)])
                self.act(Btok[:].rearrange("p a b -> p (a b)"), pv7[:, 0:256], AF.Copy, [("ps", 7)], ["Btok"])
                for g in range(2):
                    self.mm(ps[4][:, 256 + g * 128:384 + g * 128], xbcT[:, 8 + g, tok], xbcT[:, 10 + g, tok], True, True,
                            [("xbcT", 8 + g), ("xbcT", 10 + g)], [("ps", 4)])
                self.vec("dve", "tensor_copy", [("ps", 4)], ["cb"], out=cb[:].rearrange("p a b -> p (a b)"),
                         in_=ps[4][:, 256:512])
                for g in range(2):
                    ydb = ps[g]
                    self.mm(ydb[:], ident[:], xsD[:, g * 512:(g + 1) * 512], True, False, ["ident", "xsD"], [("ps", g)])
                    for qq in range(2):
                        q4 = g * 2 + qq
                        eb = Eexp[q4 % 2]
                        mb = Mt[q4 % 2]
                        self.mm(ps[6][:], ident[:], negm[:], True, False, ["ident", "negm"], [("ps", 6)])
                        for hh in range(4):
                            h = q4 * 4 + hh
                            for part in range(2):
                                self.mm(ps[6][:, hh * 128:(hh + 1) * 128], sel[:, h, :], acsHL[:, part, :], False,
                                        hh == 3 and part == 1, ["sel", "acsHL0", "acsHL1"], [("ps", 6)])
                        for hh in range(4):
                            h = q4 * 4 + hh
                            self.act(eb[:, hh * 128:(hh + 1) * 128], ps[6][:, hh * 128:(hh + 1) * 128], AF.Exp,
                                     [("ps", 6), "nb"], [("Eexp", q4 % 2)], bias=nb[:, h:h + 1])
                        self.vec("dve", "tensor_tensor", [("Eexp", q4 % 2), "cb"], [("Mt", q4 % 2)],
                                 out=mb[:].rearrange("p (a b) -> p a b", a=4), in0=eb[:].rearrange("p (a b) -> p a b", a=4),
                                 in1=cb[:, g:g + 1, :].to_broadcast([128, 4, 128]), op=ALU.mult)
                        for hh in range(4):
                            h = q4 * 4 + hh
                            hl = h - g * 8
                            self.mm(ydb[:, hl * 64:(hl + 1) * 64], mb[:, hh * 128:(hh + 1) * 128],
                                    xs_tok[:, h * 64:(h + 1) * 64], False, qq == 1 and hh == 3,
                                    [("Mt", q4 % 2), "xs_tok"], [("ps", g)])
                for g in range(2):
                    gs = slice(g * 512, (g + 1) * 512)
                    self.mm(ps[2 + g][:], xbcT[:, 10 + g, tok], Hb[:, gs], True, True, [("xbcT", 10 + g), "Hb"],
                            [("ps", 2 + g)])
                    self.vec("dve", "tensor_tensor", [("ps", 2 + g), "eacs"], ["yA"],
                             out=yA[:, gs].rearrange("p (h q) -> p h q", h=8),
                             in0=ps[2 + g][:].rearrange("p (h q) -> p h q", h=8),
                             in1=eacs[:, g * 8:(g + 1) * 8].unsqueeze(2).to_broadcast([128, 8, 64]), op=ALU.mult)
                    self.vec("dve", "tensor_tensor", [("ps", g), "yA"], ["yA"], out=yA[:, gs], in0=ps[g][:], in1=yA[:, gs],
                             op=ALU.add)
                    self.mm(ps[7][:], Btok[:, g, :], xw[:, gs], True, True, ["Btok", "xw"], [("ps", 7)])
                    self.vec("dve", "tensor_tensor", ["H", "cd_bc"], ["H"], out=H[:, gs].rearrange("p (h q) -> p h q", h=8),
                             in0=H[:, gs].rearrange("p (h q) -> p h q", h=8),
                             in1=cd_bc[:, g * 8:(g + 1) * 8].unsqueeze(2).to_broadcast([128, 8, 64]), op=ALU.mult)
                    self.vec("dve", "tensor_tensor", [("ps", 7), "H"], ["H"], out=H[:, gs], in0=ps[7][:], in1=H[:, gs],
                             op=ALU.add)
                    self.act(Hb[:, gs], H[:, gs], AF.Copy, ["H"], ["Hb"])
                self.vec("dve", "tensor_tensor", ["yA", "zs"], ["yA"], out=yA[:], in0=yA[:], in1=zs[:], op=ALU.mult)
                for g in range(2):
                    gs = slice(g * 512, (g + 1) * 512)
                    self.act(junk[:, gs], yA[:, gs], AF.Square, ["yA"], ["junk", "ssg"], accum_out=ssg[:, g:g + 1])
                self.vec("dve", "tensor_scalar", ["ssg"], ["ssg"], out=ssg[:, 2:4], in0=ssg[:, 0:2], scalar1=1.0 / 512,
                         scalar2=EPS, op0=ALU.mult, op1=ALU.add)
                self.act(ssg[:, 2:4], ssg[:, 2:4], AF.Sqrt, ["ssg"], ["ssg"])
                self.vec("dve", "reciprocal", ["ssg"], ["ssg"], out=ssg[:, 2:4], in_=ssg[:, 2:4])
                for g in range(2):
                    gs = slice(g * 512, (g + 1) * 512)
                    self.vec("dve", "tensor_scalar", ["yA", "ssg"], ["yn"], out=yn[:, gs], in0=yA[:, gs],
                             scalar1=ssg[:, 2 + g:3 + g], scalar2=None, op0=ALU.mult)
                for cc in range(8):
                    self.tr(pv5[:, cc * 128:(cc + 1) * 128], yn[:, cc * 128:(cc + 1) * 128], ident[:], ["yn", "ident"],
                            [("ps", 5)])
                for cc in range(8):
                    self.act(yT[:, cc, tok], pv5[:, cc * 128:(cc + 1) * 128], AF.Copy, [("ps", 5), "abP"], [("yT", jj)],
                             scale=abP[:, 64 + cc:65 + cc])
            for jj in range(4):
                j = t * 4 + jj
                tok = slice(jj * 128, (jj + 1) * 128)
                for n in range(2):
                    pb = ps[6 + n]
                    for k in range(12):
                        lhs = ypT[:, k, tok] if k < 4 else yT[:, k - 4, tok]
                        rr = [("ypT", k)] if k < 4 else [("yT", jj)]
                        self.mm(pb[:], lhs, w_out[:, k, n * 512:(n + 1) * 512], k == 0, k == 11, rr + ["abwout"],
                                [("ps", 6 + n)])
                    xs_ = xb[:, jj, n * 512:(n + 1) * 512]
                    self.vec("dve", "tensor_tensor", [("ps", 6 + n), "G"], [("tmpo", n)], out=tmp[n][:], in0=pb[:],
                             in1=G[:, n * 512:(n + 1) * 512], op=ALU.mult)
                    self.vec("dve", "tensor_tensor", [("tmpo", n), ("xt", 0, jj)], [("xt", 0, jj)], out=xs_,
                             in0=tmp[n][:], in1=xs_, op=ALU.add)
                self.dma("sp", self.xs_d[j * 128:(j + 1) * 128, :], xb[:, jj, :], [("xt", 0, jj)], [("xd", j)],
                         semkey=("xts", t % 2))
        self.sb_off = keep


    def mixer_mla(self):
        import math
        S = self.S
        ins = self.ins
        l, s = 1, 1
        NT, NB, seq = self.NT, self.NB, self.seq
        keep = self.sb_off
        self.sb_off = self.x_off
        cst, ident, ps = self.cst, self.ident, self.ps
        QN_d = self.dram("QN_d", [1024, seq], BF16)
        QR_d = self.dram("QR_d", [512, seq], BF16)
        KN_d = self.dram("KN_d", [1024, seq], BF16)
        KR_d = self.dram("KR_d", [32, seq], BF16)
        V_d = self.dram("V_d", [seq, 1024], BF16)
        OT_d = self.dram("OT_d", [1024, seq], BF16)
        w_in = self.sb("mwin", [128, KC, IN_MLA], BF16)
        w_uq = self.sb("mwuq", [128, 6, 1536], BF16)
        w_ukv = self.sb("mwukv", [128, 2, 2048], BF16)
        w_o = self.sb("mwo", [128, KC, D], BF16)
        mlaP = self.sb("mlaP", [128, 16], F32)
        G = self.sb("G", [128, D], F32)
        tri = self.sb("tri", [128, 128], BF16)
        xt = self.sb("xt", [128, 4, D], F32)
        self.dma("pool", w_in[:], ins["mlain"].rearrange("(k p) c -> p k c", p=128), [], ["mwin"])
        self.dma("pool", w_uq[:], ins["wuq"].rearrange("(k p) c -> p k c", p=128), [], ["mwuq"])
        self.dma("pool", w_ukv[:], ins["wukv"].rearrange("(k p) c -> p k c", p=128), [], ["mwukv"])
        self.dma("pool", w_o[:], ins["wo"].rearrange("(k p) c -> p k c", p=128), [], ["mwo"])
        self.dma("sp", mlaP[:], ins["mlaP"], [], ["mlaP"])
        self.load_gate(G, l, s)
        self.vec("dve", "tensor_copy", ["cst"], ["tri"], out=tri[:], in_=cst[:, 128:256])
        mA = self.mark()
        hT = self.sb("hT", [128, KC, 512], BF16)
        xn = self.sb("xn", [128, 4, D], BF16)
        junk = self.sb("junk", [128, D], BF16)
        qnT = self.sb("qnT", [128, 6, 512], BF16)
        kvnT = self.sb("kvnT", [128, 2, 512], BF16)
        posi = self.sb("posi", [128, 512], I32)
        ang = self.sb("ang", [128, 512], F32)
        rr = [self.sb("rr%d" % i, [128, 512], F32) for i in range(2)]
        cosT = self.sb("cosT", [128, 512], F32)
        sinT = self.sb("sinT", [128, 512], F32)
        x1s = self.sb("x1s", [128, 512], F32)
        x2s = self.sb("x2s", [128, 512], F32)
        t1 = self.sb("t1", [128, 512], F32)
        t2 = self.sb("t2", [128, 512], F32)
        ro = [self.sb("ro%d" % i, [128, 512], BF16) for i in range(2)]
        qsb = [self.sb("qsb%d" % i, [128, 512], BF16) for i in range(2)]
        vsb = [self.sb("vsb%d" % i, [128, D], BF16) for i in range(2)]
        st = self.sb("mst", [128, 16], F32)
        PI = math.pi
        pib = self.sb("pib", [128, 1], F32)
        self.vec("dve", "memset", [], ["pib"], pib[:], PI)

        def rope(x1ps, x2ps, P, res1, res2, o1, o2, o1res, o2res):
            self.act(x1s[0:P, :], x1ps, AF.Copy, [res1], ["x1s"])
            self.act(x2s[0:P, :], x2ps, AF.Copy, [res2], ["x2s"])
            self.vec("dve", "tensor_tensor", ["x1s", "cosT"], ["t1"], out=t1[0:P, :], in0=x1s[0:P, :], in1=cosT[0:P, :], op=ALU.mult)
            self.vec("dve", "tensor_tensor", ["x2s", "sinT"], ["t2"], out=t2[0:P, :], in0=x2s[0:P, :], in1=sinT[0:P, :], op=ALU.mult)
            self.vec("dve", "tensor_tensor", ["t1", "t2"], [o1res], out=o1, in0=t1[0:P, :], in1=t2[0:P, :], op=ALU.subtract)
            self.vec("dve", "tensor_tensor", ["x2s", "cosT"], ["t1"], out=t1[0:P, :], in0=x2s[0:P, :], in1=cosT[0:P, :], op=ALU.mult)
            self.vec("dve", "tensor_tensor", ["x1s", "sinT"], ["t2"], out=t2[0:P, :], in0=x1s[0:P, :], in1=sinT[0:P, :], op=ALU.mult)
            self.vec("dve", "tensor_tensor", ["t1", "t2"], [o2res], out=o2, in0=t1[0:P, :], in1=t2[0:P, :], op=ALU.add)

        for t in range(NT):
            cols = slice(t * 512, (t + 1) * 512)
            for jj in range(4):
                j = t * 4 + jj
                self.dma("sp", xt[:, jj, :], self.xs_d[j * 128:(j + 1) * 128, :], [("xd", j)], [("xt", jj)], semkey=("xtl", 0))
            self.norm_mod_T(t, l, s, hT, "hT", xn, junk, xsrc=lambda jj: xt[:, jj, :], xr=lambda jj: ("xt", jj))
            self.dma("sp", posi[:], ins["pos"][0:1, cols].partition_broadcast(128), [], ["posi"])
            self.vec("dve", "tensor_copy", ["posi"], ["ang"], out=ang[:], in_=posi[:])
            self.vec("dve", "tensor_scalar", ["ang", "mlaP"], ["ang"], out=ang[:], in0=ang[:], scalar1=mlaP[:, 8:9], scalar2=None,
                     op0=ALU.mult)
            C1 = 6.28125
            C2 = 2 * PI - C1
            for which, (dst, dres) in enumerate(((sinT, "sinT"), (cosT, "cosT"))):
                r = rr[which]
                rres = ("rr", which)
                src = ang
                if which == 1:
                    self.vec("dve", "tensor_scalar", ["ang"], ["t1"], out=t1[:], in0=ang[:], scalar1=PI / 2, scalar2=None, op0=ALU.add)
                    src = t1
                sres = "ang" if which == 0 else "t1"
                self.vec("dve", "tensor_scalar", [sres], ["t2"], out=t2[:], in0=src[:], scalar1=1.0 / (2 * PI), scalar2=None, op0=ALU.mult)
                self.vec("dve", "tensor_copy", ["t2"], ["posi"], out=posi[:], in_=t2[:])
                self.vec("dve", "tensor_copy", ["posi"], ["t2"], out=t2[:], in_=posi[:])
                self.vec("dve", "scalar_tensor_tensor", ["t2", sres], [rres], out=r[:], in0=t2[:], scalar=-C1, in1=src[:],
                         op0=ALU.mult, op1=ALU.add)
                self.vec("dve", "scalar_tensor_tensor", ["t2", rres], [rres], out=r[:], in0=t2[:], scalar=-C2, in1=r[:],
                         op0=ALU.mult, op1=ALU.add)
                self.vec("dve", "tensor_scalar", [rres], ["x1s"], out=x1s[:], in0=r[:], scalar1=PI, scalar2=-2 * PI, op0=ALU.is_gt,
                         op1=ALU.mult)
                self.vec("dve", "tensor_scalar", [rres], ["x2s"], out=x2s[:], in0=r[:], scalar1=-PI, scalar2=2 * PI, op0=ALU.is_lt,
                         op1=ALU.mult)
                self.vec("dve", "tensor_tensor", [rres, "x1s"], [rres], out=r[:], in0=r[:], in1=x1s[:], op=ALU.add)
                self.vec("dve", "tensor_tensor", [rres, "x2s"], [rres], out=r[:], in0=r[:], in1=x2s[:], op=ALU.add)
                self.act(dst[:], r[:], AF.Sin, [rres], [dres])
            for jj in range(4):
                tok = slice(jj * 128, (jj + 1) * 128)
                for n in range(2):
                    for k in range(KC):
                        self.mm(ps[n][:], hT[:, k, tok], w_in[:, k, n * 512:(n + 1) * 512], k == 0, k == KC - 1,
                                ["hT", "mwin"], [("ps", n)])
                self.act(junk[:, 0:512], ps[0][:], AF.Square, [("ps", 0)], ["junk", "mst"], accum_out=st[:, 0:1])
                self.act(junk[:, 512:768], ps[1][:, 0:256], AF.Square, [("ps", 1)], ["junk", "mst"], accum_out=st[:, 1:2])
                self.act(junk[:, 768:1024], ps[1][:, 256:512], AF.Square, [("ps", 1)], ["junk", "mst"], accum_out=st[:, 2:3])
                self.vec("dve", "tensor_tensor", ["mst"], ["mst"], out=st[:, 3:4], in0=st[:, 0:1], in1=st[:, 1:2], op=ALU.add)
                self.vec("dve", "tensor_scalar", ["mst"], ["mst"], out=st[:, 4:5], in0=st[:, 3:4], scalar1=1.0 / 768, scalar2=EPS,
                         op0=ALU.mult, op1=ALU.add)
                self.vec("dve", "tensor_scalar", ["mst"], ["mst"], out=st[:, 5:6], in0=st[:, 2:3], scalar1=1.0 / 256, scalar2=EPS,
                         op0=ALU.mult, op1=ALU.add)
                self.act(st[:, 4:6], st[:, 4:6], AF.Sqrt, ["mst"], ["mst"])
                self.vec("dve", "reciprocal", ["mst"], ["mst"], out=st[:, 4:6], in_=st[:, 4:6])
                self.vec("dve", "tensor_scalar", [("ps", 0), "mst"], [("xn", jj)], out=xn[:, jj, 0:512], in0=ps[0][:],
                         scalar1=st[:, 4:5], scalar2=None, op0=ALU.mult)
                self.vec("dve", "tensor_scalar", [("ps", 1), "mst"], [("xn", jj)], out=xn[:, jj, 512:768], in0=ps[1][:, 0:256],
                         scalar1=st[:, 4:5], scalar2=None, op0=ALU.mult)
                self.vec("dve", "tensor_scalar", [("ps", 1), "mst"], [("xn", jj)], out=xn[:, jj, 768:1024], in0=ps[1][:, 256:512],
                         scalar1=st[:, 5:6], scalar2=None, op0=ALU.mult)
            for k in range(8):
                bank = 4 + (k % 4)
                pv = ps[bank][:].bitcast(BF16)
                for jj in range(4):
                    self.tr(pv[:, jj * 128:(jj + 1) * 128], xn[:, jj, k * 128:(k + 1) * 128], ident[:], [("xn", jj), "ident"],
                            [("ps", bank)])
                dst = qnT[:, k, :] if k < 6 else kvnT[:, k - 6, :]
                self.act(dst, pv[:, 0:512], AF.Copy, [("ps", bank), "mlaP"], ["qnT" if k < 6 else "kvnT"], scale=mlaP[:, k:k + 1])
            for half in range(2):
                for k in range(KC):
                    self.mm(ps[2 + half][0:16, :], w_in[:, k, 1024 + half * 16:1040 + half * 16], hT[:, k, :], k == 0, k == KC - 1,
                            ["mwin", "hT"], [("ps", 2 + half)])
            rope(ps[2][0:16, :], ps[3][0:16, :], 16, ("ps", 2), ("ps", 3), ro[0][0:16, :], ro[1][0:16, :], ("ro", 0), ("ro", 1))
            self.dma("sp", KR_d[0:16, cols], ro[0][0:16, :], [("ro", 0)], [("KR_d", t)], semkey=("scr", 0))
            self.dma("sp", KR_d[16:32, cols], ro[1][0:16, :], [("ro", 1)], [("KR_d", t)], semkey=("scr", 0))
            for c in range(8):
                bk = c % 2
                for k in range(6):
                    self.mm(ps[bk][:], w_uq[:, k, c * 128:(c + 1) * 128], qnT[:, k, :], k == 0, k == 5, ["mwuq", "qnT"], [("ps", bk)])
                self.act(qsb[bk][:], ps[bk][:], AF.Copy, [("ps", bk)], [("qsb", bk)])
                self.dma("sp", QN_d[c * 128:(c + 1) * 128, cols], qsb[bk][:], [("qsb", bk)], [("QN_d", t)], semkey=("scr", 1 + bk))
            for hg in range(2):
                for k in range(6):
                    self.mm(ps[2][:], w_uq[:, k, 1024 + hg * 128:1152 + hg * 128], qnT[:, k, :], k == 0, k == 5, ["mwuq", "qnT"],
                            [("ps", 2)])
                for k in range(6):
                    self.mm(ps[3][:], w_uq[:, k, 1280 + hg * 128:1408 + hg * 128], qnT[:, k, :], k == 0, k == 5, ["mwuq", "qnT"],
                            [("ps", 3)])
                rope(ps[2][:], ps[3][:], 128, ("ps", 2), ("ps", 3), ro[0][:], ro[1][:], ("ro", 0), ("ro", 1))
                self.dma("sp", QR_d[hg * 128:(hg + 1) * 128, cols], ro[0][:], [("ro", 0)], [("QR_d", t)], semkey=("scr", 0))
                self.dma("sp", QR_d[256 + hg * 128:256 + (hg + 1) * 128, cols], ro[1][:], [("ro", 1)], [("QR_d", t)], semkey=("scr", 0))
            for c in range(8):
                bk = c % 2
                for k in range(2):
                    self.mm(ps[bk][:], w_ukv[:, k, c * 128:(c + 1) * 128], kvnT[:, k, :], k == 0, k == 1, ["mwukv", "kvnT"], [("ps", bk)])
                self.vec("dve", "tensor_copy", [("ps", bk)], [("qsb", bk)], out=qsb[bk][:], in_=ps[bk][:])
                self.dma("sp", KN_d[c * 128:(c + 1) * 128, cols], qsb[bk][:], [("qsb", bk)], [("KN_d", t)], semkey=("scr", 1 + bk))
            for jj in range(4):
                j = t * 4 + jj
                tok = slice(jj * 128, (jj + 1) * 128)
                vb = vsb[jj % 2]
                for n in range(2):
                    for k in range(2):
                        self.mm(ps[2 + n][:], kvnT[:, k, tok], w_ukv[:, k, 1024 + n * 512:1536 + n * 512], k == 0, k == 1,
                                ["kvnT", "mwukv"], [("ps", 2 + n)])
                    if n == 0:
                        self.act(vb[:, 0:512], ps[2][:], AF.Copy, [("ps", 2)], [("vsb", jj % 2)])
                    else:
                        self.vec("dve", "tensor_copy", [("ps", 3)], [("vsb", jj % 2)], out=vb[:, 512:1024], in_=ps[3][:])
                self.dma("sp", V_d[j * 128:(j + 1) * 128, :], vb[:], [("vsb", jj % 2)], [("V_d", j)], semkey=("scr", 3 + jj % 2))
        S.barrier()
        self.release(mA)
        KhT = [self.sb("KhT%d" % i, [96, seq], BF16) for i in range(2)]
        QhT = [self.sb("QhT%d" % i, [96, seq], BF16) for i in range(2)]
        V1 = [self.sb("V1_%d" % i, [128, NB, 65], BF16) for i in range(2)]
        PT = [self.sb("PT%d" % i, [128, 512], BF16) for i in range(3)]
        rd = self.sb("rd", [128, 512], F32)
        bcs = self.sb("bcs", [64, 512], F32)
        osb = [self.sb("osb%d" % i, [64, 512], BF16) for i in range(2)]
        for i in range(2):
            self.vec("dve", "memset", [], [("V1", i)], V1[i][:, :, 64:65], 1.0)
        scale = 1.0 / math.sqrt(96.0)
        blocks = []
        for h in range(16):
            for qt in range(NT):
                nkb = 4 * (qt + 1)
                for kb in range(nkb):
                    blocks.append((h, qt, kb, nkb))
        loaded_heads = set()

        def load_head(h):
            if h in loaded_heads or h >= 16:
                return
            loaded_heads.add(h)
            hb = h % 2
            self.dma("sp", KhT[hb][0:64, :], KN_d[h * 64:(h + 1) * 64, :], [], [("KhT", hb, 0)], semkey=("ld", hb))
            self.dma("sp", KhT[hb][64:96, :], KR_d[0:32, :], [], [("KhT", hb, 1)], semkey=("ld", hb))
            self.dma("sp", QhT[hb][0:64, :], QN_d[h * 64:(h + 1) * 64, :], [], [("QhT", hb, 0)], semkey=("ld", hb))
            self.dma("sp", QhT[hb][64:80, :], QR_d[h * 16:(h + 1) * 16, :], [], [("QhT", hb, 1)], semkey=("ld", hb))
            self.dma("sp", QhT[hb][80:96, :], QR_d[256 + h * 16:256 + (h + 1) * 16, :], [], [("QhT", hb, 2)], semkey=("ld", hb))
            self.dma("sp", V1[hb][:, :, 0:64], V_d[:, h * 64:(h + 1) * 64].rearrange("(j p) c -> p j c", p=128), [],
                     [("V1", hb)], semkey=("ld", hb))

        def geom(i):
            h, qt, kb, nkb = blocks[i]
            jd = kb - 4 * qt
            c0 = 128 * jd if jd > 0 else 0
            return h, qt, kb, nkb, jd, c0, i % 3

        def stage1(i):
            h, qt, kb, nkb, jd, c0, sbk = geom(i)
            hb = h % 2
            kres = [("KhT", hb, 0), ("KhT", hb, 1)]
            qres = [("QhT", hb, 0), ("QhT", hb, 1), ("QhT", hb, 2)]
            pt = PT[sbk]
            self.mm(ps[sbk][:, c0:512], KhT[hb][:, kb * 128:(kb + 1) * 128], QhT[hb][:, qt * 512 + c0:(qt + 1) * 512],
                    True, True, kres + qres, [("ps", sbk)])
            self.act(pt[:, c0:512], ps[sbk][:, c0:512], AF.Exp, [("ps", sbk)], [("PT", sbk)], scale=scale)
            if jd >= 0:
                self.vec("dve", "tensor_tensor", [("PT", sbk), "tri"], [("PT", sbk)], out=pt[:, c0:c0 + 128],
                         in0=pt[:, c0:c0 + 128], in1=tri[:], op=ALU.mult)

        def stage2(i):
            h, qt, kb, nkb, jd, c0, sbk = geom(i)
            hb = h % 2
            ob = 4 + (qt % 2)
            pt = PT[sbk]
            self.mm(ps[ob][0:65, c0:512], V1[hb][:, kb, :], pt[:, c0:512], kb == 0, kb == nkb - 1,
                    [("V1", hb), ("PT", sbk)], [("ps", ob)])
            if kb == nkb - 1:
                self.vec("dve", "reciprocal", [("ps", ob)], ["rd"], out=rd[64:65, :], in_=ps[ob][64:65, :])
                self.mm(ps[6][0:64, :], cst[64:65, 256:320], rd[64:65, :], True, True, ["cst", "rd"], [("ps", 6)])
                self.act(bcs[:], ps[6][0:64, :], AF.Copy, [("ps", 6)], ["bcs"])
                o_ = osb[qt % 2]
                self.vec("dve", "tensor_tensor", [("ps", ob), "bcs"], [("osb", qt % 2)], out=o_[:], in0=ps[ob][0:64, :], in1=bcs[:],
                         op=ALU.mult)
                self.dma("sp", OT_d[h * 64:(h + 1) * 64, qt * 512:(qt + 1) * 512], o_[:], [("osb", qt % 2)], [("OT_d", qt)],
                         semkey=("ost", qt % 2))

        load_head(0)
        load_head(1)
        stage1(0)
        for i in range(len(blocks)):
            if i + 1 < len(blocks):
                stage1(i + 1)
            stage2(i)
            if i + 1 == len(blocks) or blocks[i + 1][0] != blocks[i][0]:
                load_head(blocks[i][0] + 2)
        S.barrier()
        self.release(mA)
        oT = [self.sb("oT%d" % i, [128, KC, 512], BF16) for i in range(2)]
        tmp = [self.sb("tmpm%d" % i, [128, 512], F32) for i in range(2)]
        for t in range(NT):
            cols = slice(t * 512, (t + 1) * 512)
            ot = oT[t % 2]
            self.dma("sp", ot[:], OT_d[:, cols].rearrange("(k p) t -> p k t", p=128), [], [("oT", t % 2)], semkey=("otl", t % 2))
            for jj in range(4):
                j = t * 4 + jj
                self.dma("sp", xt[:, jj, :], self.xs_d[j * 128:(j + 1) * 128, :], [("xd", j)], [("xt", jj)], semkey=("xtl", 0))
            for jj in range(4):
                j = t * 4 + jj
                tok = slice(jj * 128, (jj + 1) * 128)
                for n in range(2):
                    pb = ps[n]
                    for k in range(KC):
                        self.mm(pb[:], ot[:, k, tok], w_o[:, k, n * 512:(n + 1) * 512], k == 0, k == KC - 1, [("oT", t % 2), "mwo"],
                                [("ps", n)])
                    xs_ = xt[:, jj, n * 512:(n + 1) * 512]
                    self.vec("dve", "tensor_tensor", [("ps", n), "G"], [("tmpm", n)], out=tmp[n][:], in0=pb[:],
                             in1=G[:, n * 512:(n + 1) * 512], op=ALU.mult)
                    self.vec("dve", "tensor_tensor", [("tmpm", n), ("xt", jj)], [("xt", jj)], out=xs_, in0=tmp[n][:], in1=xs_,
                             op=ALU.add)
                self.dma("sp", self.xs_d[j * 128:(j + 1) * 128, :], xt[:, jj, :], [("xt", jj)], [("xd", j)], semkey=("xts", 0))
        self.sb_off = keep


    def final_norm(self, fin_in):
        m = self.mark()
        Gf = self.sb("Gf", [128, D], F32)
        junk = self.sb("junkf", [128, D], BF16)
        ob = [self.sb("ob%d" % i, [128, D], F32) for i in range(2)]
        self.dma("sp", Gf[:], fin_in.partition_broadcast(128), [], ["Gf"])
        for j in range(self.NB):
            xt = self.xres[:, j, :]
            ss = self.stat[:, 16 + j % 2:17 + j % 2]
            rs = self.stat[:, 18 + j % 2:19 + j % 2]
            o = ob[j % 2]
            self.act(junk[:], xt, AF.Square, [("x", j)], ["junkf", ("fss", j % 2)], accum_out=ss)
            self.vec("dve", "tensor_scalar", [("fss", j % 2)], [("frs", j % 2)], out=rs, in0=ss, scalar1=1.0 / D,
                     scalar2=EPS, op0=ALU.mult, op1=ALU.add)
            self.act(rs, rs, AF.Sqrt, [("frs", j % 2)], [("frs", j % 2)])
            self.vec("dve", "reciprocal", [("frs", j % 2)], [("frs", j % 2)], out=rs, in_=rs)
            self.vec("dve", "scalar_tensor_tensor", [("x", j), ("frs", j % 2), "Gf"], [("ob", j % 2)],
                     out=o[:], in0=xt, scalar=rs, in1=Gf[:], op0=ALU.mult, op1=ALU.mult)
            self.dma("sp", self.y_out[j * 128:(j + 1) * 128, :], o[:], [("ob", j % 2)], [("yout", j)],
                     semkey=("yout", j % 2))
        self.release(m)


def make_consts():
    c = np.zeros((128, 512), np.float32)
    c[:, 0:128] = np.eye(128, dtype=np.float32)
    ii = np.arange(128)
    c[:, 128:256] = (ii[:, None] <= ii[None, :]).astype(np.float32)
    c[:, 256:384] = 1.0
    for g, w in enumerate((2, 4, 8, 16)):
        c[:, 384 + g * 16: 384 + (g + 1) * 16] = 1.0 / np.minimum(np.arange(16) + 1.0, float(w))
    return c


def host_inputs(b, seq, x, c, positions, mod_w, mod_b, norm_g, ffn_w13, ffn_w2, ab_w_in, pool_w, pool_scale,
                ssd_conv_w, ssd_conv_b, ssd_dt_bias, ssd_a_log, ssd_d, ssd_norm_g, ab_w_out, mla_w_in,
                mla_q_norm_g, mla_w_uq, mla_kv_norm_g, mla_w_ukv, mla_w_o, final_norm_g):
    f = np.float32
    m = {}
    m["x"] = np.ascontiguousarray(x[b], dtype=f)
    m["c"] = np.ascontiguousarray(c[b].reshape(KC, 128).T, dtype=f)
    m["pos"] = np.ascontiguousarray(positions[b].reshape(1, seq), dtype=np.int32)
    m["mod_w"] = np.ascontiguousarray(mod_w, dtype=f)
    m["mod_b"] = np.ascontiguousarray(mod_b, dtype=f)
    m["ngP"] = np.ascontiguousarray(norm_g.reshape(2, 3, KC, 128).transpose(3, 0, 1, 2).reshape(128, 48), dtype=f)
    m["ffn_w13"] = np.ascontiguousarray(ffn_w13.reshape(4, D, 2 * DFF), dtype=f)
    m["ffn_w2"] = np.ascontiguousarray(ffn_w2.reshape(4, DFF, D), dtype=f)
    m["ab_w_in"] = np.ascontiguousarray(ab_w_in[0], dtype=f)
    m["pool_w"] = np.ascontiguousarray(pool_w[0].reshape(512, 128), dtype=f)
    abP = np.zeros((128, 80), f)
    abP[:, 0:4] = pool_scale[0].reshape(4, 128).T
    abP[:, 4:52] = ssd_conv_w[0].reshape(4, 12, 128).transpose(2, 1, 0).reshape(128, 48)
    abP[:, 52:64] = ssd_conv_b[0].reshape(12, 128).T
    abP[:, 64:72] = ssd_norm_g[0].reshape(8, 128).T
    m["abP"] = abP
    abR = np.zeros((1, 64), f)
    abR[0, 0:16] = ssd_dt_bias[0]
    abR[0, 16:32] = ssd_a_log[0]
    abR[0, 32:48] = ssd_d[0]
    m["abR"] = abR
    sel = np.zeros((16, 16, 128), f)
    for h in range(16):
        sel[h, h, :] = 1.0
    m["sel"] = sel.reshape(16, 2048)
    ii = np.arange(128)
    negm = np.where(ii[None, :] < ii[:, None], -30000.0, 0.0).astype(f)
    m["cmask"] = np.ascontiguousarray(np.tile(negm, (1, 4)))
    m["ab_w_out"] = np.ascontiguousarray(ab_w_out[0], dtype=f)
    m["mla_w_in"] = np.ascontiguousarray(mla_w_in[0], dtype=f)
    wq = mla_w_uq[0].reshape(768, 16, 96)
    m["mla_w_uq"] = np.ascontiguousarray(np.concatenate(
        [wq[:, :, 0:64].reshape(768, 1024), wq[:, :, 64:80].reshape(768, 256), wq[:, :, 80:96].reshape(768, 256)], axis=1), dtype=f)
    wkv = mla_w_ukv[0].reshape(256, 16, 128)
    m["mla_w_ukv"] = np.ascontiguousarray(np.concatenate(
        [wkv[:, :, 0:64].reshape(256, 1024), wkv[:, :, 64:128].reshape(256, 1024)], axis=1), dtype=f)
    m["mla_w_o"] = np.ascontiguousarray(mla_w_o[0], dtype=f)
    mp = np.zeros((128, 16), f)
    mp[:, 0:6] = mla_q_norm_g[0].reshape(6, 128).T
    mp[:, 6:8] = mla_kv_norm_g[0].reshape(2, 128).T
    mp[:, 8] = np.tile(INV_FREQ, 8)
    m["mlaP"] = mp
    m["final_g"] = np.ascontiguousarray(final_norm_g.reshape(1, D), dtype=f)
    m["consts"] = make_consts()
    return m


_NC_CACHE = {}


def kernel(**inputs):
    inputs = {k: np.asarray(v) for k, v in inputs.items()}
    B, seq, _ = inputs["x"].shape
    key = (seq,)
    if key not in _NC_CACHE:
        _NC_CACHE[key] = Builder(seq).build()
    nc = _NC_CACHE[key]
    in_maps = [host_inputs(b, seq, **inputs) for b in range(B)]
    res = run_bass_kernel_spmd(nc, in_maps, core_ids=list(range(B)))
    return np.stack([np.asarray(r["y"], dtype=np.float32) for r in res.results], axis=0)
```

```python
import numpy as np
import concourse.bass as bass
import concourse.mybir as mybir
from concourse.bass_utils import run_bass_kernel_spmd

F32 = mybir.dt.float32
BF16 = mybir.dt.bfloat16
I32 = mybir.dt.int32
AF = mybir.ActivationFunctionType
ALU = mybir.AluOpType
AX = mybir.AxisListType

D = 1024
KC = 8
DFF = 2816
NF = 22
EPS = 1e-6
SEQ_FULL = 4096
IN_AB = 3088
IN_MLA = 1056

COMPUTE = ("pe", "act", "dve", "pool")
INV_FREQ = (np.float32(10000.0) ** (-np.arange(0, 32, 2, dtype=np.float32) / np.float32(32))).astype(np.float32)


class Op:
    __slots__ = ("q", "fn", "reads", "writes", "dma", "semkey", "deps", "signal", "idx", "need_sig", "bar_dma")


class Sched:
    def __init__(self, nc):
        self.nc = nc
        self.ops = []
        self.last_w = {}
        self.readers = {}
        self.sems = {}
        self.dma_count = {}
        self.bar_deps = []
        self.bar_dma = {}

    def op(self, q, fn, reads=(), writes=(), dma=False, semkey=None):
        o = Op()
        o.q, o.fn, o.reads, o.writes, o.dma = q, fn, tuple(reads), tuple(writes), dma
        o.idx = len(self.ops)
        o.need_sig = False
        o.bar_dma = self.bar_dma
        deps = set(self.bar_deps)
        for r in o.reads:
            w = self.last_w.get(r)
            if w is not None:
                deps.add(w)
        for w_ in o.writes:
            w = self.last_w.get(w_)
            if w is not None:
                deps.add(w)
            for rd in self.readers.get(w_, ()):
                deps.add(rd)
        deps.discard(o.idx)
        o.deps = deps
        if dma:
            if semkey is None:
                semkey = ("dma",) + tuple(o.writes[:1])
            o.semkey = semkey
            n = self.dma_count.get(semkey, 0) + 1
            self.dma_count[semkey] = n
            o.signal = (semkey, 16 * n)
        else:
            o.semkey = q
            o.signal = None
        for r in o.reads:
            self.readers.setdefault(r, []).append(o.idx)
        for w_ in o.writes:
            self.last_w[w_] = o.idx
            self.readers[w_] = []
        self.ops.append(o)
        return o

    def barrier(self):
        lastq = {}
        for o in self.ops:
            if (not o.dma) and o.fn is not None:
                lastq[o.q] = o.idx
        self.bar_deps = list(lastq.values())
        self.bar_dma = {k: 16 * n for k, n in self.dma_count.items()}
        self.last_w = {}
        self.readers = {}

    def emit(self):
        nc = self.nc
        ops = self.ops
        for o in ops:
            for d in o.deps:
                p = ops[d]
                if p.q == "pe" and o.q == "pe" and not p.dma:
                    continue
                p.need_sig = True
        cnt = {q: 0 for q in COMPUTE}
        for o in ops:
            if not o.dma and o.fn is not None and o.need_sig:
                cnt[o.q] += 1
                o.signal = (o.q, cnt[o.q])
        def sem(key):
            s = self.sems.get(key)
            if s is None:
                s = nc.alloc_semaphore("s%d" % len(self.sems))
                self.sems[key] = s
            return s
        queues = {}
        for o in ops:
            queues.setdefault(o.q, []).append(o)
        dma_before = {}
        run = {}
        for o in ops:
            dma_before[o.idx] = dict(run) if False else None
        dma_positions = {}
        for o in ops:
            if o.dma:
                dma_positions.setdefault(o.semkey, []).append(o.idx)
        import bisect

        def emit_queue(qname, eng):
            waited = {}
            for o in queues.get(qname, []):
                need = dict(o.bar_dma)
                for d in o.deps:
                    p = ops[d]
                    if p.dma:
                        pos = dma_positions[p.semkey]
                        n = bisect.bisect_left(pos, o.idx)
                        key, val = p.semkey, 16 * n
                    else:
                        if p.fn is None:
                            continue
                        if p.q == "pe" and o.q == "pe" and not o.dma:
                            continue
                        key, val = p.signal
                    if need.get(key, 0) < val:
                        need[key] = val
                for key, val in need.items():
                    if waited.get(key, 0) >= val:
                        continue
                    waited[key] = val
                    eng.wait_ge(sem(key), val)
                if o.fn is None:
                    continue
                ins = o.fn(eng)
                if o.dma:
                    ins.then_inc(sem(o.semkey), 16)
                elif o.need_sig:
                    ins.then_inc(sem(o.q), 1)

        with nc.Block() as block:
            @block.tensor
            def _(e):
                emit_queue("pe", e)

            @block.scalar
            def _(e):
                emit_queue("act", e)

            @block.vector
            def _(e):
                emit_queue("dve", e)

            @block.gpsimd
            def _(e):
                emit_queue("pool", e)

            @block.sync
            def _(e):
                emit_queue("sp", e)


class Builder:
    def __init__(self, seq, subs=None, debug_out=None):
        self.seq = seq
        self.NT = seq // 512
        self.NB = seq // 128
        self.subs = subs
        nc = bass.Bass("TRN2", target_bir_lowering=False)
        self.nc = nc
        self.S = Sched(nc)
        self.sb_off = 16640
        self.sb_top = 229376
        self.nalloc = 0

    def sb(self, name, shape, dtype):
        esz = 4 if dtype in (F32, I32) else 2
        n = 1
        for s in shape[1:]:
            n *= s
        nbytes = (n * esz + 63) // 64 * 64
        off = self.sb_off
        assert off + nbytes <= self.sb_top, "SBUF overflow at %s: need %d have %d" % (name, nbytes, self.sb_top - off)
        self.sb_off += nbytes
        self.nalloc += 1
        return self.nc.alloc_sbuf_tensor_at("%s_%d" % (name, self.nalloc), list(shape), dtype, offset=off)

    def mark(self):
        return self.sb_off

    def release(self, m):
        self.sb_off = m

    def dram(self, name, shape, dtype, kind="Internal"):
        return self.nc.dram_tensor(name, list(shape), dtype, kind=kind).ap()

    def mm(self, out, lhsT, rhs, start, stop, reads, writes):
        self.S.op("pe", lambda e: e.matmul(out, lhsT, rhs, start=start, stop=stop), reads, writes)

    def tr(self, out, in_, ident, reads, writes):
        self.S.op("pe", lambda e: e.transpose(out, in_, ident), reads, writes)

    def act(self, out, in_, func, reads, writes, bias=None, scale=None, accum_out=None):
        kw = {}
        if bias is not None:
            kw["bias"] = bias
        if scale is not None:
            kw["scale"] = scale
        if accum_out is not None:
            kw["accum_out"] = accum_out
        self.S.op("act", lambda e: e.activation(out=out, in_=in_, func=func, **kw), reads, writes)

    def vec(self, q, method, reads, writes, *args, **kw):
        self.S.op(q, lambda e: getattr(e, method)(*args, **kw), reads, writes)

    def dma(self, q, out, in_, reads, writes, semkey=None):
        self.S.op(q, lambda e: e.dma_start(out=out, in_=in_), reads, writes, dma=True, semkey=semkey)

    def build(self):
        nc = self.nc
        seq, NT, NB = self.seq, self.NT, self.NB
        S = self.S
        x_in = self.dram("x", [seq, D], F32, "ExternalInput")
        c_in = self.dram("c", [128, KC], F32, "ExternalInput")
        pos_in = self.dram("pos", [1, seq], I32, "ExternalInput")
        mod_w = self.dram("mod_w", [2, D, 9 * D], F32, "ExternalInput")
        mod_b = self.dram("mod_b", [2, 9 * D], F32, "ExternalInput")
        ngP_in = self.dram("ngP", [128, 2 * 3 * KC], F32, "ExternalInput")
        w13_in = self.dram("ffn_w13", [4, D, 2 * DFF], F32, "ExternalInput")
        w2_in = self.dram("ffn_w2", [4, DFF, D], F32, "ExternalInput")
        abin_in = self.dram("ab_w_in", [D, IN_AB], F32, "ExternalInput")
        poolw_in = self.dram("pool_w", [512, 128], F32, "ExternalInput")
        abP_in = self.dram("abP", [128, 80], F32, "ExternalInput")
        abR_in = self.dram("abR", [1, 64], F32, "ExternalInput")
        about_in = self.dram("ab_w_out", [1536, D], F32, "ExternalInput")
        mlain_in = self.dram("mla_w_in", [D, IN_MLA], F32, "ExternalInput")
        wuq_in = self.dram("mla_w_uq", [768, 1536], F32, "ExternalInput")
        wukv_in = self.dram("mla_w_ukv", [256, 2048], F32, "ExternalInput")
        wo_in = self.dram("mla_w_o", [D, D], F32, "ExternalInput")
        mlaP_in = self.dram("mlaP", [128, 16], F32, "ExternalInput")
        fin_in = self.dram("final_g", [1, D], F32, "ExternalInput")
        cst_in = self.dram("consts", [128, 512], F32, "ExternalInput")
        sel_in = self.dram("sel", [16, 2048], F32, "ExternalInput")
        cmask_in = self.dram("cmask", [128, 512], F32, "ExternalInput")
        self.xs_d = self.dram("xs_d", [seq, D], F32)
        self.ins = dict(abin=abin_in, poolw=poolw_in, abP=abP_in, abR=abR_in, about=about_in, mlain=mlain_in,
                        wuq=wuq_in, wukv=wukv_in, wo=wo_in, mlaP=mlaP_in, sel=sel_in, cmask=cmask_in, pos=pos_in)
        y_out = self.dram("y", [seq, D], F32, "ExternalOutput")
        self.x_in, self.y_out = x_in, y_out

        w13_s = self.dram("w13_s", [4, D, 2 * DFF], BF16)
        w2_s = self.dram("w2_s", [4, DFF, D], BF16)
        mod_d = self.dram("mod_d", [2, 9 * D], F32)
        self.w13_s, self.w2_s, self.mod_d = w13_s, w2_s, mod_d

        self.ident = self.sb("ident", [128, 128], BF16)
        self.cst = self.sb("cst", [128, 512], F32)
        self.modP = self.sb("modP", [128, 2, 9, KC], F32)
        self.ngP = self.sb("ngP", [128, 2, 3, KC], F32)
        self.aP = self.sb("aP", [128, 2, 3, KC], F32)
        self.stat = self.sb("stat", [128, 64], F32)
        self.x_off = self.sb_off
        self.xres = self.sb("xres", [128, max(NB, 32), D], F32)
        self.ps = [nc.alloc_psum_tensor("ps%d" % i, [128, 512], F32) for i in range(8)]

        self.dma("sp", self.cst[:], cst_in, ["cst_in"], ["cst"])
        self.vec("dve", "tensor_copy", ["cst"], ["ident"], out=self.ident[:], in_=self.cst[:, 0:128])
        self.dma("sp", self.ngP[:].rearrange("p a b c -> p (a b c)"), ngP_in, [], ["ngP"])
        for j in range(NB):
            self.dma("sp", self.xres[:, j, :], x_in[j * 128:(j + 1) * 128, :], [], [("x", j)], semkey=("xload", j % 4))
        self.precast(w13_s[0], w13_in[0], D, 2 * DFF, ("w13s", 0))
        self.precast(w2_s[0], w2_in[0], DFF, D, ("w2s", 0))
        self.modulation(c_in, mod_w, mod_b)
        for i in range(1, 4):
            self.precast(w13_s[i], w13_in[i], D, 2 * DFF, ("w13s", i))
            self.precast(w2_s[i], w2_in[i], DFF, D, ("w2s", i))

        def want(l, s):
            return self.subs is None or (l, s) in self.subs

        for l in range(2):
            if want(l, 0):
                self.ffn(l, 0)
            if want(l, 1):
                self.spill_x()
                if l == 0:
                    self.mixer_ab()
                else:
                    self.mixer_mla()
                self.reload_x()
            if want(l, 2):
                self.ffn(l, 1)
        self.final_norm(fin_in)
        S.op("sp", None, reads=[("yout", j) for j in range(NB)])
        S.emit()
        return nc

    def precast(self, dst, src, rows, cols, res):
        a = rows // 128
        half = max(1, a // 2)
        for h0 in range(0, a, half):
            h1 = min(a, h0 + half)
            d = dst.rearrange("(p a) c -> p a c", p=128)[:, h0:h1, :]
            s = src.rearrange("(p a) c -> p a c", p=128)[:, h0:h1, :]
            self.dma("pool", d, s, [], [res], semkey=("pc",) + tuple(res))

    def modulation(self, c_in, mod_w, mod_b):
        m = self.mark()
        cact = self.sb("cact", [128, KC], F32)
        mrow = self.sb("mrow", [1, 9 * D], F32)
        modT = self.sb("modT", [128, 128], F32)
        wst = [self.sb("modw%d" % i, [128, KC, 512], F32) for i in range(2)]
        self.dma("sp", cact[:], c_in, [], ["cact"])
        self.act(cact[:], cact[:], AF.Silu, ["cact"], ["cact"])
        for l in range(2):
            self.dma("sp", mrow[:], mod_b[l:l + 1, :], [], ["mrow"])
            for cb in range(18):
                slot = (l * 18 + cb) % 2
                self.dma("sp", wst[slot][:], mod_w[l, :, cb * 512:(cb + 1) * 512].rearrange("(k p) c -> p k c", p=128),
                         [], [("modw", slot)])
                pst = self.ps[cb % 2]
                for k in range(KC):
                    self.mm(pst[0:1, :], cact[:, k:k + 1], wst[slot][:, k, :], k == 0, k == KC - 1,
                            ["cact", ("modw", slot)], [("ps", cb % 2)])
                self.vec("dve", "tensor_tensor", [("ps", cb % 2), "mrow"], ["mrow"],
                         out=mrow[0:1, cb * 512:(cb + 1) * 512], in0=pst[0:1, :], in1=mrow[0:1, cb * 512:(cb + 1) * 512],
                         op=ALU.add)
            self.dma("sp", self.mod_d[l:l + 1, :], mrow[:], ["mrow"], [("mod_d", l)])
            self.dma("sp", modT[0:72, :], self.mod_d[l, :].rearrange("(r p) -> r p", p=128), [("mod_d", l)], ["modT"])
            self.tr(self.ps[2][:, 0:72], modT[0:72, :], self.cst[0:72, 0:72], ["modT", "cst"], [("ps", 2)])
            self.vec("dve", "tensor_copy", [("ps", 2)], ["modP"],
                     out=self.modP[:, l, :, :].rearrange("p j k -> p (j k)"), in_=self.ps[2][:, 0:72])
        for l in range(2):
            for s in range(3):
                self.vec("dve", "scalar_tensor_tensor", ["modP", "ngP"], ["aP"],
                         out=self.aP[:, l, s, :], in0=self.modP[:, l, 3 * s + 1, :], scalar=1.0,
                         in1=self.ngP[:, l, s, :], op0=ALU.add, op1=ALU.mult)
        self.S.barrier()
        self.release(m)

    def norm_mod_T(self, t, l, s, hT, hres, xn, junk, xsrc=None, xr=None):
        xts, xrs = [], []
        for jj in range(4):
            j = t * 4 + jj
            xt = self.xres[:, j, :] if xsrc is None else xsrc(jj)
            xres_ = ("x", j) if xr is None else xr(jj)
            xts.append(xt)
            xrs.append(xres_)
            if junk is None:
                self.act(xn[:, jj, :], xt, AF.Square, [xres_], [("xn", jj), "ss"], accum_out=self.stat[:, jj:jj + 1])
            else:
                self.act(junk[:], xt, AF.Square, [xres_], ["junk", "ss"], accum_out=self.stat[:, jj:jj + 1])
        rs4 = self.stat[:, 8:12]
        self.vec("dve", "tensor_scalar", ["ss"], ["rs"], out=rs4, in0=self.stat[:, 0:4], scalar1=1.0 / D, scalar2=EPS,
                 op0=ALU.mult, op1=ALU.add)
        self.act(rs4, rs4, AF.Sqrt, ["rs"], ["rs"])
        self.vec("dve", "reciprocal", ["rs"], ["rs"], out=rs4, in_=rs4)
        for jj in range(4):
            self.vec("dve", "tensor_scalar", [xrs[jj], "rs"], [("xn", jj)], out=xn[:, jj, :], in0=xts[jj],
                     scalar1=self.stat[:, 8 + jj:9 + jj], scalar2=None, op0=ALU.mult)
        for k in range(KC):
            bank = 4 + (k % 4)
            pv = self.ps[bank][:].bitcast(BF16)
            for jj in range(4):
                self.tr(pv[:, jj * 128:(jj + 1) * 128], xn[:, jj, k * 128:(k + 1) * 128], self.ident[:],
                        [("xn", jj), "ident"], [("ps", bank)])
            self.act(hT[:, k, :], pv[:, 0:512], AF.Identity, [("ps", bank), "aP", "modP"], [hres],
                     scale=self.aP[:, l, s, k:k + 1], bias=self.modP[:, l, 3 * s, k:k + 1])

    def load_gate(self, G, l, s):
        self.dma("sp", G[:], self.mod_d[l:l + 1, (3 * s + 2) * D:(3 * s + 3) * D].partition_broadcast(128),
                 [("mod_d", l)], ["G"])

    def ffn(self, l, s2):
        S = self.S
        fi = l * 2 + s2
        s = 0 if s2 == 0 else 2
        NT = self.NT
        m = self.mark()
        hT = self.sb("hT", [128, KC, 512], BF16)
        gT = self.sb("gT", [128, NF, 512], BF16)
        xn = self.sb("xn", [128, 4, D], BF16)
        G = self.sb("G", [128, D], F32)
        sg = [self.sb("sg%d" % i, [128, 512], F32) for i in range(2)]
        NSL = 3
        ring = [self.sb("ring%d" % i, [128, 4096], BF16) for i in range(NSL)]
        self.load_gate(G, l, s)
        w2grp = [(0, 8), (8, 8), (16, 6)]
        pieces = []
        for t in range(NT):
            for pc in range(11):
                pieces.append(("w13", pc))
            for n in range(2):
                for g in range(3):
                    pieces.append(("w2", n, g))
        issued = [0]

        def issue():
            gi = issued[0]
            p = pieces[gi]
            sl = gi % NSL
            issued[0] += 1
            if p[0] == "w13":
                pc = p[1]
                v = ring[sl][:].rearrange("p (k a c) -> p k a c", k=KC, a=2)
                for ab in range(2):
                    src = self.w13_s[fi, :, ab * DFF + pc * 256: ab * DFF + (pc + 1) * 256]
                    self.dma("sp", v[:, :, ab, :], src.rearrange("(k p) c -> p k c", p=128),
                             [("w13s", fi)], [("ring", sl, ab)])
            else:
                n, g = p[1], p[2]
                f0, nf = w2grp[g]
                v = ring[sl][:].rearrange("p (f c) -> p f c", f=8)
                src = self.w2_s[fi, f0 * 128:(f0 + nf) * 128, n * 512:(n + 1) * 512]
                self.dma("sp", v[:, 0:nf, :], src.rearrange("(f p) c -> p f c", p=128), [("w2s", fi)],
                         [("ring", sl, 0), ("ring", sl, 1)])

        def ensure(gi):
            while issued[0] < len(pieces) and issued[0] < gi + NSL:
                issue()
            return gi % NSL

        pi = 0
        for t in range(NT):
            self.norm_mod_T(t, l, s, hT, "hT", xn, None)
            for pc in range(11):
                sl = ensure(pi)
                pi += 1
                wv = ring[sl][:].rearrange("p (k a c) -> p k a c", k=KC, a=2)
                for ff in range(2):
                    f = pc * 2 + ff
                    pa, pb = self.ps[ff * 2], self.ps[ff * 2 + 1]
                    for k in range(KC):
                        self.mm(pa[:], wv[:, k, 0, ff * 128:(ff + 1) * 128], hT[:, k, :], k == 0, k == KC - 1,
                                [("ring", sl, 0), "hT"], [("ps", ff * 2)])
                    for k in range(KC):
                        self.mm(pb[:], wv[:, k, 1, ff * 128:(ff + 1) * 128], hT[:, k, :], k == 0, k == KC - 1,
                                [("ring", sl, 1), "hT"], [("ps", ff * 2 + 1)])
                    self.act(sg[ff][:], pa[:], AF.Silu, [("ps", ff * 2)], [("sg", ff)])
                    self.vec("dve", "tensor_tensor", [("sg", ff), ("ps", ff * 2 + 1)], [("gT", f)],
                             out=gT[:, f, :], in0=sg[ff][:], in1=pb[:], op=ALU.mult)
            for n in range(2):
                for g in range(3):
                    sl = ensure(pi)
                    pi += 1
                    f0, nf = w2grp[g]
                    wv = ring[sl][:].rearrange("p (f c) -> p f c", f=8)
                    for fl in range(nf):
                        f = f0 + fl
                        for jj in range(4):
                            self.mm(self.ps[4 + jj][:], gT[:, f, jj * 128:(jj + 1) * 128], wv[:, fl, :],
                                    f == 0, f == NF - 1, [("gT", f), ("ring", sl, 0), ("ring", sl, 1)], [("ps", 4 + jj)])
                for jj in range(4):
                    j = t * 4 + jj
                    xs = self.xres[:, j, n * 512:(n + 1) * 512]
                    tb = sg[jj % 2]
                    self.vec("dve", "tensor_tensor", [("ps", 4 + jj), "G"], [("sg", jj % 2)],
                             out=tb[:], in0=self.ps[4 + jj][:], in1=G[:, n * 512:(n + 1) * 512], op=ALU.mult)
                    self.vec("dve", "scalar_tensor_tensor", [("sg", jj % 2), ("x", j)], [("x", j)],
                             out=xs, in0=tb[:], scalar=0.5, in1=xs, op0=ALU.mult, op1=ALU.add)
        S.barrier()
        self.release(m)

    def spill_x(self):
        for j in range(self.NB):
            self.dma("sp", self.xs_d[j * 128:(j + 1) * 128, :], self.xres[:, j, :], [("x", j)], [("xd", j)],
                     semkey=("xsp", j % 4))
        self.S.barrier()

    def reload_x(self):
        self.S.barrier()
        for j in range(self.NB):
            self.dma("sp", self.xres[:, j, :], self.xs_d[j * 128:(j + 1) * 128, :], [("xd", j)], [("x", j)],
                     semkey=("xload", j % 4))

    def mixer_ab(self):
        S = self.S
        ins = self.ins
        l, s = 0, 1
        NT = self.NT
        keep = self.sb_off
        self.sb_off = self.x_off
        cst = self.cst
        w_in = self.sb("abwin", [128, KC, IN_AB], BF16)
        w_out = self.sb("abwout", [128, 12, D], BF16)
        pw = self.sb("poolw", [128, 4, 128], BF16)
        abP = self.sb("abP", [128, 80], F32)
        rowp = self.sb("rowp", [128, 48], F32)
        a_bc = self.sb("a_bc", [128, 16], F32)
        sel = self.sb("sel", [16, 16, 128], BF16)
        negm = self.sb("negm", [128, 512], BF16)
        G = self.sb("G", [128, D], F32)
        hT = self.sb("hT", [128, KC, 512], BF16)
        xn = self.sb("xn", [128, 4, D], BF16)
        junk = self.sb("junk", [128, D], BF16)
        xt = [self.sb("xt0", [128, 4, D], F32)] * 2
        Up = self.sb("Up", [128, 528], F32)
        sA = self.sb("sA", [128, 528], F32)
        sB = self.sb("sB", [128, 528], F32)
        phalo = self.sb("phalo", [128, 4, 16], F32)
        dT = [self.sb("dT%d" % i, [128, 512], BF16) for i in range(2)]
        dfix = self.sb("dfix", [128, 16], F32)
        ypT = self.sb("ypT", [128, 4, 512], BF16)
        Uc = [self.sb("Uc0", [128, 515], F32)] * 2
        chalo = self.sb("chalo", [128, 12, 3], F32)
        acc = [self.sb("acc0", [128, 512], F32)] * 2
        xbcT = self.sb("xbcT", [128, 12, 512], BF16)
        zs = self.sb("zs", [128, D], F32)
        dtt = self.sb("dtt", [128, 96], F32)
        sm2 = self.sb("sm2", [128, 48], F32)
        acsT = self.sb("acsT", [16, 3, 128], F32)
        acsHL = self.sb("acsHL", [16, 2, 128], BF16)
        xs_tok = self.sb("xs_tok", [128, D], BF16)
        xw = self.sb("xw", [128, D], BF16)
        xsD = self.sb("xsD", [128, D], BF16)
        Btok = self.sb("Btok", [128, 2, 128], BF16)
        cb = self.sb("cb", [128, 2, 128], F32)
        Eexp = [self.sb("Eexp%d" % i, [128, 512], F32) for i in range(2)]
        Mt = [self.sb("Mt%d" % i, [128, 512], BF16) for i in range(2)]
        H = self.sb("H", [128, D], F32)
        Hb = self.sb("Hb", [128, D], BF16)
        yA = self.sb("yA", [128, D], F32)
        ssg = self.sb("ssg", [128, 4], F32)
        yn = self.sb("yn", [128, D], BF16)
        yT = self.sb("yT", [128, KC, 512], BF16)
        tmp = [self.sb("tmpo%d" % i, [128, 512], F32) for i in range(2)]
        ps = self.ps
        ident = self.ident

        self.dma("pool", w_in[:], ins["abin"].rearrange("(k p) c -> p k c", p=128), [], ["abwin"])
        self.dma("pool", w_out[:], ins["about"].rearrange("(k p) c -> p k c", p=128), [], ["abwout"])
        self.dma("pool", pw[:], ins["poolw"].rearrange("(g p) c -> p g c", p=128), [], ["poolw"])
        self.dma("pool", sel[:].rearrange("p a b -> p (a b)"), ins["sel"], [], ["sel"])
        self.dma("pool", negm[:], ins["cmask"], [], ["negm"])
        self.dma("sp", abP[:], ins["abP"], [], ["abP"])
        self.dma("sp", rowp[:], ins["abR"][0:1, 0:48].partition_broadcast(128), [], ["rowp"])
        self.act(a_bc[:], rowp[:, 16:32], AF.Exp, ["rowp"], ["a_bc"])
        self.vec("dve", "tensor_scalar", ["a_bc"], ["a_bc"], out=a_bc[:], in0=a_bc[:], scalar1=-1.0, scalar2=None,
                 op0=ALU.mult)
        self.load_gate(G, l, s)
        self.vec("dve", "memset", [], ["phalo"], phalo[:], 0.0)
        self.vec("dve", "memset", [], ["chalo"], chalo[:], 0.0)
        self.vec("dve", "memset", [], ["H"], H[:], 0.0)
        self.vec("dve", "memset", [], ["Hb"], Hb[:], 0.0)
        D_bc = rowp[:, 32:48]
        dtb_bc = rowp[:, 0:16]
        wins = (2, 4, 8, 16)

        for t in range(NT):
            xb = xt[t % 2]
            for jj in range(4):
                j = t * 4 + jj
                self.dma("sp", xb[:, jj, :], self.xs_d[j * 128:(j + 1) * 128, :], [("xd", j)], [("xt", 0, jj)],
                         semkey=("xtl", t % 2))
            self.norm_mod_T(t, l, s, hT, "hT", xn, junk, xsrc=lambda jj: xb[:, jj, :],
                            xr=lambda jj: ("xt", 0, jj))
            for g in range(4):
                bk = g % 2
                for k in range(KC):
                    self.mm(ps[bk][:], w_in[:, k, g * 128:(g + 1) * 128], hT[:, k, :], k == 0, k == KC - 1,
                            ["abwin", "hT"], [("ps", bk)])
                self.vec("dve", "tensor_copy", ["phalo"], ["Up"], out=Up[:, 0:16], in_=phalo[:, g, :])
                self.act(Up[:, 16:528], ps[bk][:], AF.Copy, [("ps", bk)], ["Up"])
                self.vec("dve", "tensor_copy", ["Up"], ["phalo"], out=phalo[:, g, :], in_=Up[:, 512:528])
                self.vec("dve", "tensor_tensor", ["Up"], ["sA"], out=sA[:, 1:528], in0=Up[:, 1:528], in1=Up[:, 0:527],
                         op=ALU.add)
                lvl = sA
                if g >= 1:
                    self.vec("dve", "tensor_tensor", ["sA"], ["sB"], out=sB[:, 3:528], in0=sA[:, 3:528],
                             in1=sA[:, 1:526], op=ALU.add)
                    lvl = sB
                if g >= 2:
                    self.vec("dve", "tensor_tensor", ["sB"], ["sA"], out=sA[:, 7:528], in0=sB[:, 7:528],
                             in1=sB[:, 3:524], op=ALU.add)
                    lvl = sA
                if g >= 3:
                    self.vec("dve", "tensor_tensor", ["sA"], ["sB"], out=sB[:, 15:528], in0=sA[:, 15:528],
                             in1=sA[:, 7:520], op=ALU.add)
                    lvl = sB
                lres = "sA" if lvl is sA else "sB"
                dd = dT[g % 2]
                self.vec("dve", "scalar_tensor_tensor", [lres, "Up"], [("dT", g % 2)], out=dd[:], in0=lvl[:, 16:528],
                         scalar=1.0 / wins[g], in1=Up[:, 16:528], op0=ALU.mult, op1=ALU.subtract)
                if t == 0:
                    self.vec("dve", "tensor_tensor", [lres, "cst"], ["dfix"], out=dfix[:], in0=lvl[:, 16:32],
                             in1=cst[:, 384 + g * 16:384 + (g + 1) * 16], op=ALU.mult)
                    self.vec("dve", "tensor_tensor", ["dfix", "Up"], [("dT", g % 2)], out=dd[:, 0:16], in0=dfix[:],
                             in1=Up[:, 16:32], op=ALU.subtract)
                self.mm(ps[2 + bk][:], pw[:, g, :], dd[:], True, True, ["poolw", ("dT", g % 2)], [("ps", 2 + bk)])
                self.act(ypT[:, g, :], ps[2 + bk][:], AF.Copy, [("ps", 2 + bk), "abP"], [("ypT", g)],
                         scale=abP[:, g:g + 1])
            for cc in range(12):
                bk = cc % 2
                c0 = 1536 + cc * 128
                for k in range(KC):
                    self.mm(ps[bk][:], w_in[:, k, c0:c0 + 128], hT[:, k, :], k == 0, k == KC - 1,
                            ["abwin", "hT"], [("ps", bk)])
                U = Uc[bk]
                ur = ("Uc", 0)
                self.vec("dve", "tensor_copy", ["chalo"], [ur], out=U[:, 0:3], in_=chalo[:, cc, :])
                self.act(U[:, 3:515], ps[bk][:], AF.Copy, [("ps", bk)], [ur])
                self.vec("dve", "tensor_copy", [ur], ["chalo"], out=chalo[:, cc, :], in_=U[:, 512:515])
                a_ = acc[bk]
                ar = ("acc", 0)
                self.vec("dve", "tensor_scalar", [ur, "abP"], [ar], out=a_[:], in0=U[:, 0:512],
                         scalar1=abP[:, 4 + cc * 4:5 + cc * 4], scalar2=abP[:, 52 + cc:53 + cc], op0=ALU.mult, op1=ALU.add)
                for kk in range(1, 4):
                    self.vec("dve", "scalar_tensor_tensor", [ur, "abP", ar], [ar], out=a_[:], in0=U[:, kk:kk + 512],
                             scalar=abP[:, 4 + cc * 4 + kk:5 + cc * 4 + kk], in1=a_[:], op0=ALU.mult, op1=ALU.add)
                self.act(xbcT[:, cc, :], a_[:], AF.Silu, [ar], [("xbcT", cc)])
            for jj in range(4):
                j = t * 4 + jj
                tok = slice(jj * 128, (jj + 1) * 128)
                for n in range(2):
                    for k in range(KC):
                        self.mm(ps[2 + n][:], hT[:, k, tok], w_in[:, k, 512 + n * 512:1024 + n * 512], k == 0,
                                k == KC - 1, ["hT", "abwin"], [("ps", 2 + n)])
                    self.act(zs[:, n * 512:(n + 1) * 512], ps[2 + n][:], AF.Silu, [("ps", 2 + n)], ["zs"])
                for k in range(KC):
                    self.mm(ps[4][:, 0:16], hT[:, k, tok], w_in[:, k, 3072:3088], k == 0, k == KC - 1,
                            ["hT", "abwin"], [("ps", 4)])
                dt_, lndt, adt, nb, wst, eacs = (dtt[:, i * 16:(i + 1) * 16] for i in range(6))
                cd_bc, acs_sb, arg = (sm2[:, i * 16:(i + 1) * 16] for i in range(3))
                self.vec("dve", "tensor_tensor", [("ps", 4), "rowp"], ["dt"], out=dt_, in0=ps[4][:, 0:16], in1=dtb_bc,
                         op=ALU.add)
                self.act(dt_, dt_, AF.Exp, ["dt"], ["dt"])
                self.act(dt_, dt_, AF.Ln, ["dt"], ["dt"], bias=1.0)
                self.act(lndt, dt_, AF.Ln, ["dt"], ["lndt"])
                self.vec("dve", "tensor_tensor", ["dt", "a_bc"], ["adt"], out=adt, in0=dt_, in1=a_bc[:], op=ALU.mult)
                self.mm(ps[4][:, 16:32], cst[:, 128:256], adt, True, True, ["cst", "adt"], [("ps", 4)])
                self.mm(ps[4][0:16, 128:256], adt, cst[:, 128:256], True, True, ["cst", "adt"], [("ps", 4)])
                self.mm(ps[4][:, 32:48], cst[:, 256:384], adt, True, True, ["cst", "adt"], [("ps", 4)])
                self.vec("dve", "tensor_copy", [("ps", 4)], ["acs_sb"], out=acs_sb, in_=ps[4][:, 16:32])
                self.vec("dve", "tensor_tensor", ["lndt", "acs_sb"], ["nb"], out=nb, in0=lndt, in1=acs_sb, op=ALU.subtract)
                self.act(eacs, ps[4][:, 16:32], AF.Exp, [("ps", 4)], ["eacs"])
                self.vec("dve", "tensor_tensor", [("ps", 4), "nb"], ["arg"], out=arg, in0=ps[4][:, 32:48], in1=nb,
                         op=ALU.add)
                self.act(wst, arg, AF.Exp, ["arg"], ["wst"])
                self.act(cd_bc, ps[4][:, 32:48], AF.Exp, [("ps", 4)], ["cd_bc"])
                self.vec("dve", "tensor_copy", [("ps", 4)], ["acsT0"], out=acsT[:, 0, :], in_=ps[4][0:16, 128:256])
                self.vec("dve", "tensor_copy", ["acsT0"], ["acsHL0"], out=acsHL[:, 0, :], in_=acsT[:, 0, :])
                self.vec("dve", "tensor_copy", ["acsHL0"], ["acsT1"], out=acsT[:, 1, :], in_=acsHL[:, 0, :])
                self.vec("dve", "tensor_tensor", ["acsT0", "acsT1"], ["acsHL1"], out=acsHL[:, 1, :], in0=acsT[:, 0, :],
                         in1=acsT[:, 1, :], op=ALU.subtract)
                pv5 = ps[5][:].bitcast(BF16)
                for cc in range(8):
                    self.tr(pv5[:, cc * 128:(cc + 1) * 128], xbcT[:, cc, tok], ident[:], [("xbcT", cc), "ident"],
                            [("ps", 5)])
                self.act(xs_tok[:], pv5[:, 0:1024], AF.Copy, [("ps", 5)], ["xs_tok"])
                self.vec("dve", "tensor_tensor", [("ps", 5), "wst"], ["xw"],
                         out=xw[:].rearrange("p (h q) -> p h q", h=16), in0=pv5[:, 0:1024].rearrange("p (h q) -> p h q", h=16),
                         in1=wst.unsqueeze(2).to_broadcast([128, 16, 64]), op=ALU.mult)
                self.vec("dve", "tensor_tensor", ["xs_tok", "rowp"], ["xsD"],
                         out=xsD[:].rearrange("p (h q) -> p h q", h=16), in0=xs_tok[:].rearrange("p (h q) -> p h q", h=16),
                         in1=D_bc.unsqueeze(2).to_broadcast([128, 16, 64]), op=ALU.mult)
                pv7 = ps[7][:].bitcast(BF16)
                for g in range(2):
                    self.tr(pv7[:, g * 128:(g + 1) * 128], xbcT[:, 8 + g, tok], ident[:], [("xbcT", 8 + g), "ident"],
                            [("ps", 7)])
                self.act(Btok[:].rearrange("p a b -> p (a b)"), pv7[:, 0:256], AF.Copy, [("ps", 7)], ["Btok"])
                for g in range(2):
                    self.mm(ps[4][:, 256 + g * 128:384 + g * 128], xbcT[:, 8 + g, tok], xbcT[:, 10 + g, tok], True, True,
                            [("xbcT", 8 + g), ("xbcT", 10 + g)], [("ps", 4)])
                self.vec("dve", "tensor_copy", [("ps", 4)], ["cb"], out=cb[:].rearrange("p a b -> p (a b)"),
                         in_=ps[4][:, 256:512])
                def e_phase(q4):
                    ebk = 6 + (q4 % 2)
                    eb = Eexp[q4 % 2]
                    mb = Mt[q4 % 2]
                    g = q4 // 2
                    self.mm(ps[ebk][:], ident[:], negm[:], True, False, ["ident", "negm"], [("ps", ebk)])
                    for hh in range(4):
                        h = q4 * 4 + hh
                        for part in range(2):
                            self.mm(ps[ebk][:, hh * 128:(hh + 1) * 128], sel[:, h, :], acsHL[:, part, :], False,
                                    hh == 3 and part == 1, ["sel", "acsHL0", "acsHL1"], [("ps", ebk)])
                    for hh in range(4):
                        h = q4 * 4 + hh
                        self.act(eb[:, hh * 128:(hh + 1) * 128], ps[ebk][:, hh * 128:(hh + 1) * 128], AF.Exp,
                                 [("ps", ebk), "nb"], [("Eexp", q4 % 2)], bias=nb[:, h:h + 1])
                    self.vec("dve", "tensor_tensor", [("Eexp", q4 % 2), "cb"], [("Mt", q4 % 2)],
                             out=mb[:].rearrange("p (a b) -> p a b", a=4), in0=eb[:].rearrange("p (a b) -> p a b", a=4),
                             in1=cb[:, g:g + 1, :].to_broadcast([128, 4, 128]), op=ALU.mult)

                def y_phase(q4):
                    g, qq = q4 // 2, q4 % 2
                    ydb = ps[g]
                    mb = Mt[q4 % 2]
                    if qq == 0:
                        self.mm(ydb[:], ident[:], xsD[:, g * 512:(g + 1) * 512], True, False, ["ident", "xsD"], [("ps", g)])
                    for hh in range(4):
                        h = q4 * 4 + hh
                        hl = h - g * 8
                        self.mm(ydb[:, hl * 64:(hl + 1) * 64], mb[:, hh * 128:(hh + 1) * 128],
                                xs_tok[:, h * 64:(h + 1) * 64], False, qq == 1 and hh == 3,
                                [("Mt", q4 % 2), "xs_tok"], [("ps", g)])

                e_phase(0)
                for q4 in range(4):
                    if q4 + 1 < 4:
                        e_phase(q4 + 1)
                    y_phase(q4)
                for g in range(2):
                    gs = slice(g * 512, (g + 1) * 512)
                    self.mm(ps[2 + g][:], xbcT[:, 10 + g, tok], Hb[:, gs], True, True, [("xbcT", 10 + g), "Hb"],
                            [("ps", 2 + g)])
                    self.vec("dve", "tensor_tensor", [("ps", 2 + g), "eacs"], ["yA"],
                             out=yA[:, gs].rearrange("p (h q) -> p h q", h=8),
                             in0=ps[2 + g][:].rearrange("p (h q) -> p h q", h=8),
                             in1=eacs[:, g * 8:(g + 1) * 8].unsqueeze(2).to_broadcast([128, 8, 64]), op=ALU.mult)
                    self.vec("dve", "tensor_tensor", [("ps", g), "yA"], ["yA"], out=yA[:, gs], in0=ps[g][:], in1=yA[:, gs],
                             op=ALU.add)
                    self.mm(ps[7][:], Btok[:, g, :], xw[:, gs], True, True, ["Btok", "xw"], [("ps", 7)])
                    self.vec("dve", "tensor_tensor", ["H", "cd_bc"], ["H"], out=H[:, gs].rearrange("p (h q) -> p h q", h=8),
                             in0=H[:, gs].rearrange("p (h q) -> p h q", h=8),
                             in1=cd_bc[:, g * 8:(g + 1) * 8].unsqueeze(2).to_broadcast([128, 8, 64]), op=ALU.mult)
                    self.vec("dve", "tensor_tensor", [("ps", 7), "H"], ["H"], out=H[:, gs], in0=ps[7][:], in1=H[:, gs],
                             op=ALU.add)
                    self.act(Hb[:, gs], H[:, gs], AF.Copy, ["H"], ["Hb"])
                self.vec("dve", "tensor_tensor", ["yA", "zs"], ["yA"], out=yA[:], in0=yA[:], in1=zs[:], op=ALU.mult)
                for g in range(2):
                    gs = slice(g * 512, (g + 1) * 512)
                    self.act(junk[:, gs], yA[:, gs], AF.Square, ["yA"], ["junk", "ssg"], accum_out=ssg[:, g:g + 1])
                self.vec("dve", "tensor_scalar", ["ssg"], ["ssg"], out=ssg[:, 2:4], in0=ssg[:, 0:2], scalar1=1.0 / 512,
                         scalar2=EPS, op0=ALU.mult, op1=ALU.add)
                self.act(ssg[:, 2:4], ssg[:, 2:4], AF.Sqrt, ["ssg"], ["ssg"])
                self.vec("dve", "reciprocal", ["ssg"], ["ssg"], out=ssg[:, 2:4], in_=ssg[:, 2:4])
                for g in range(2):
                    gs = slice(g * 512, (g + 1) * 512)
                    self.vec("dve", "tensor_scalar", ["yA", "ssg"], ["yn"], out=yn[:, gs], in0=yA[:, gs],
                             scalar1=ssg[:, 2 + g:3 + g], scalar2=None, op0=ALU.mult)
                for cc in range(8):
                    self.tr(pv5[:, cc * 128:(cc + 1) * 128], yn[:, cc * 128:(cc + 1) * 128], ident[:], ["yn", "ident"],
                            [("ps", 5)])
                for cc in range(8):
                    self.act(yT[:, cc, tok], pv5[:, cc * 128:(cc + 1) * 128], AF.Copy, [("ps", 5), "abP"], [("yT", jj)],
                             scale=abP[:, 64 + cc:65 + cc])
            for jj in range(4):
                j = t * 4 + jj
                tok = slice(jj * 128, (jj + 1) * 128)
                for n in range(2):
                    pb = ps[6 + n]
                    for k in range(12):
                        lhs = ypT[:, k, tok] if k < 4 else yT[:, k - 4, tok]
                        rr = [("ypT", k)] if k < 4 else [("yT", jj)]
                        self.mm(pb[:], lhs, w_out[:, k, n * 512:(n + 1) * 512], k == 0, k == 11, rr + ["abwout"],
                                [("ps", 6 + n)])
                    xs_ = xb[:, jj, n * 512:(n + 1) * 512]
                    self.vec("dve", "tensor_tensor", [("ps", 6 + n), "G"], [("tmpo", n)], out=tmp[n][:], in0=pb[:],
                             in1=G[:, n * 512:(n + 1) * 512], op=ALU.mult)
                    self.vec("dve", "tensor_tensor", [("tmpo", n), ("xt", 0, jj)], [("xt", 0, jj)], out=xs_,
                             in0=tmp[n][:], in1=xs_, op=ALU.add)
                self.dma("sp", self.xs_d[j * 128:(j + 1) * 128, :], xb[:, jj, :], [("xt", 0, jj)], [("xd", j)],
                         semkey=("xts", t % 2))
        self.sb_off = keep


    def mixer_mla(self):
        import math
        S = self.S
        ins = self.ins
        l, s = 1, 1
        NT, NB, seq = self.NT, self.NB, self.seq
        keep = self.sb_off
        self.sb_off = self.x_off
        cst, ident, ps = self.cst, self.ident, self.ps
        QN_d = self.dram("QN_d", [1024, seq], BF16)
        QR_d = self.dram("QR_d", [512, seq], BF16)
        KN_d = self.dram("KN_d", [1024, seq], BF16)
        KR_d = self.dram("KR_d", [32, seq], BF16)
        V_d = self.dram("V_d", [seq, 1024], BF16)
        OT_d = self.dram("OT_d", [1024, seq], BF16)
        w_in = self.sb("mwin", [128, KC, IN_MLA], BF16)
        w_uq = self.sb("mwuq", [128, 6, 1536], BF16)
        w_ukv = self.sb("mwukv", [128, 2, 2048], BF16)
        w_o = self.sb("mwo", [128, KC, D], BF16)
        mlaP = self.sb("mlaP", [128, 16], F32)
        G = self.sb("G", [128, D], F32)
        tri = self.sb("tri", [128, 128], BF16)
        xt = self.sb("xt", [128, 4, D], F32)
        self.dma("pool", w_in[:], ins["mlain"].rearrange("(k p) c -> p k c", p=128), [], ["mwin"])
        self.dma("pool", w_uq[:], ins["wuq"].rearrange("(k p) c -> p k c", p=128), [], ["mwuq"])
        self.dma("pool", w_ukv[:], ins["wukv"].rearrange("(k p) c -> p k c", p=128), [], ["mwukv"])
        self.dma("pool", w_o[:], ins["wo"].rearrange("(k p) c -> p k c", p=128), [], ["mwo"])
        self.dma("sp", mlaP[:], ins["mlaP"], [], ["mlaP"])
        self.load_gate(G, l, s)
        self.vec("dve", "tensor_copy", ["cst"], ["tri"], out=tri[:], in_=cst[:, 128:256])
        mA = self.mark()
        hT = self.sb("hT", [128, KC, 512], BF16)
        xn = self.sb("xn", [128, 4, D], BF16)
        junk = self.sb("junk", [128, D], BF16)
        qnT = self.sb("qnT", [128, 6, 512], BF16)
        kvnT = self.sb("kvnT", [128, 2, 512], BF16)
        posi = self.sb("posi", [128, 512], I32)
        ang = self.sb("ang", [128, 512], F32)
        rr = [self.sb("rr%d" % i, [128, 512], F32) for i in range(2)]
        cosT = self.sb("cosT", [128, 512], F32)
        sinT = self.sb("sinT", [128, 512], F32)
        x1s = self.sb("x1s", [128, 512], F32)
        x2s = self.sb("x2s", [128, 512], F32)
        t1 = self.sb("t1", [128, 512], F32)
        t2 = self.sb("t2", [128, 512], F32)
        ro = [self.sb("ro%d" % i, [128, 512], BF16) for i in range(2)]
        qsb = [self.sb("qsb%d" % i, [128, 512], BF16) for i in range(2)]
        vsb = [self.sb("vsb%d" % i, [128, D], BF16) for i in range(2)]
        st = self.sb("mst", [128, 16], F32)
        PI = math.pi
        pib = self.sb("pib", [128, 1], F32)
        self.vec("dve", "memset", [], ["pib"], pib[:], PI)

        def rope(x1ps, x2ps, P, res1, res2, o1, o2, o1res, o2res):
            self.act(x1s[0:P, :], x1ps, AF.Copy, [res1], ["x1s"])
            self.act(x2s[0:P, :], x2ps, AF.Copy, [res2], ["x2s"])
            self.vec("dve", "tensor_tensor", ["x1s", "cosT"], ["t1"], out=t1[0:P, :], in0=x1s[0:P, :], in1=cosT[0:P, :], op=ALU.mult)
            self.vec("dve", "tensor_tensor", ["x2s", "sinT"], ["t2"], out=t2[0:P, :], in0=x2s[0:P, :], in1=sinT[0:P, :], op=ALU.mult)
            self.vec("dve", "tensor_tensor", ["t1", "t2"], [o1res], out=o1, in0=t1[0:P, :], in1=t2[0:P, :], op=ALU.subtract)
            self.vec("dve", "tensor_tensor", ["x2s", "cosT"], ["t1"], out=t1[0:P, :], in0=x2s[0:P, :], in1=cosT[0:P, :], op=ALU.mult)
            self.vec("dve", "tensor_tensor", ["x1s", "sinT"], ["t2"], out=t2[0:P, :], in0=x1s[0:P, :], in1=sinT[0:P, :], op=ALU.mult)
            self.vec("dve", "tensor_tensor", ["t1", "t2"], [o2res], out=o2, in0=t1[0:P, :], in1=t2[0:P, :], op=ALU.add)

        for t in range(NT):
            cols = slice(t * 512, (t + 1) * 512)
            for jj in range(4):
                j = t * 4 + jj
                self.dma("sp", xt[:, jj, :], self.xs_d[j * 128:(j + 1) * 128, :], [("xd", j)], [("xt", jj)], semkey=("xtl", 0))
            self.norm_mod_T(t, l, s, hT, "hT", xn, junk, xsrc=lambda jj: xt[:, jj, :], xr=lambda jj: ("xt", jj))
            self.dma("sp", posi[:], ins["pos"][0:1, cols].partition_broadcast(128), [], ["posi"])
            self.vec("dve", "tensor_copy", ["posi"], ["ang"], out=ang[:], in_=posi[:])
            self.vec("dve", "tensor_scalar", ["ang", "mlaP"], ["ang"], out=ang[:], in0=ang[:], scalar1=mlaP[:, 8:9], scalar2=None,
                     op0=ALU.mult)
            C1 = 6.28125
            C2 = 2 * PI - C1
            for which, (dst, dres) in enumerate(((sinT, "sinT"), (cosT, "cosT"))):
                r = rr[which]
                rres = ("rr", which)
                src = ang
                if which == 1:
                    self.vec("dve", "tensor_scalar", ["ang"], ["t1"], out=t1[:], in0=ang[:], scalar1=PI / 2, scalar2=None, op0=ALU.add)
                    src = t1
                sres = "ang" if which == 0 else "t1"
                self.vec("dve", "tensor_scalar", [sres], ["t2"], out=t2[:], in0=src[:], scalar1=1.0 / (2 * PI), scalar2=None, op0=ALU.mult)
                self.vec("dve", "tensor_copy", ["t2"], ["posi"], out=posi[:], in_=t2[:])
                self.vec("dve", "tensor_copy", ["posi"], ["t2"], out=t2[:], in_=posi[:])
                self.vec("dve", "scalar_tensor_tensor", ["t2", sres], [rres], out=r[:], in0=t2[:], scalar=-C1, in1=src[:],
                         op0=ALU.mult, op1=ALU.add)
                self.vec("dve", "scalar_tensor_tensor", ["t2", rres], [rres], out=r[:], in0=t2[:], scalar=-C2, in1=r[:],
                         op0=ALU.mult, op1=ALU.add)
                self.vec("dve", "tensor_scalar", [rres], ["x1s"], out=x1s[:], in0=r[:], scalar1=PI, scalar2=-2 * PI, op0=ALU.is_gt,
                         op1=ALU.mult)
                self.vec("dve", "tensor_scalar", [rres], ["x2s"], out=x2s[:], in0=r[:], scalar1=-PI, scalar2=2 * PI, op0=ALU.is_lt,
                         op1=ALU.mult)
                self.vec("dve", "tensor_tensor", [rres, "x1s"], [rres], out=r[:], in0=r[:], in1=x1s[:], op=ALU.add)
                self.vec("dve", "tensor_tensor", [rres, "x2s"], [rres], out=r[:], in0=r[:], in1=x2s[:], op=ALU.add)
                self.act(dst[:], r[:], AF.Sin, [rres], [dres])
            for jj in range(4):
                tok = slice(jj * 128, (jj + 1) * 128)
                for n in range(2):
                    for k in range(KC):
                        self.mm(ps[n][:], hT[:, k, tok], w_in[:, k, n * 512:(n + 1) * 512], k == 0, k == KC - 1,
                                ["hT", "mwin"], [("ps", n)])
                self.act(junk[:, 0:512], ps[0][:], AF.Square, [("ps", 0)], ["junk", "mst"], accum_out=st[:, 0:1])
                self.act(junk[:, 512:768], ps[1][:, 0:256], AF.Square, [("ps", 1)], ["junk", "mst"], accum_out=st[:, 1:2])
                self.act(junk[:, 768:1024], ps[1][:, 256:512], AF.Square, [("ps", 1)], ["junk", "mst"], accum_out=st[:, 2:3])
                self.vec("dve", "tensor_tensor", ["mst"], ["mst"], out=st[:, 3:4], in0=st[:, 0:1], in1=st[:, 1:2], op=ALU.add)
                self.vec("dve", "tensor_scalar", ["mst"], ["mst"], out=st[:, 4:5], in0=st[:, 3:4], scalar1=1.0 / 768, scalar2=EPS,
                         op0=ALU.mult, op1=ALU.add)
                self.vec("dve", "tensor_scalar", ["mst"], ["mst"], out=st[:, 5:6], in0=st[:, 2:3], scalar1=1.0 / 256, scalar2=EPS,
                         op0=ALU.mult, op1=ALU.add)
                self.act(st[:, 4:6], st[:, 4:6], AF.Sqrt, ["mst"], ["mst"])
                self.vec("dve", "reciprocal", ["mst"], ["mst"], out=st[:, 4:6], in_=st[:, 4:6])
                self.vec("dve", "tensor_scalar", [("ps", 0), "mst"], [("xn", jj)], out=xn[:, jj, 0:512], in0=ps[0][:],
                         scalar1=st[:, 4:5], scalar2=None, op0=ALU.mult)
                self.vec("dve", "tensor_scalar", [("ps", 1), "mst"], [("xn", jj)], out=xn[:, jj, 512:768], in0=ps[1][:, 0:256],
                         scalar1=st[:, 4:5], scalar2=None, op0=ALU.mult)
                self.vec("dve", "tensor_scalar", [("ps", 1), "mst"], [("xn", jj)], out=xn[:, jj, 768:1024], in0=ps[1][:, 256:512],
                         scalar1=st[:, 5:6], scalar2=None, op0=ALU.mult)
            for k in range(8):
                bank = 4 + (k % 4)
                pv = ps[bank][:].bitcast(BF16)
                for jj in range(4):
                    self.tr(pv[:, jj * 128:(jj + 1) * 128], xn[:, jj, k * 128:(k + 1) * 128], ident[:], [("xn", jj), "ident"],
                            [("ps", bank)])
                dst = qnT[:, k, :] if k < 6 else kvnT[:, k - 6, :]
                self.act(dst, pv[:, 0:512], AF.Copy, [("ps", bank), "mlaP"], ["qnT" if k < 6 else "kvnT"], scale=mlaP[:, k:k + 1])
            for half in range(2):
                for k in range(KC):
                    self.mm(ps[2 + half][0:16, :], w_in[:, k, 1024 + half * 16:1040 + half * 16], hT[:, k, :], k == 0, k == KC - 1,
                            ["mwin", "hT"], [("ps", 2 + half)])
            rope(ps[2][0:16, :], ps[3][0:16, :], 16, ("ps", 2), ("ps", 3), ro[0][0:16, :], ro[1][0:16, :], ("ro", 0), ("ro", 1))
            self.dma("sp", KR_d[0:16, cols], ro[0][0:16, :], [("ro", 0)], [("KR_d", t)], semkey=("scr", 0))
            self.dma("sp", KR_d[16:32, cols], ro[1][0:16, :], [("ro", 1)], [("KR_d", t)], semkey=("scr", 0))
            for c in range(8):
                bk = c % 2
                for k in range(6):
                    self.mm(ps[bk][:], w_uq[:, k, c * 128:(c + 1) * 128], qnT[:, k, :], k == 0, k == 5, ["mwuq", "qnT"], [("ps", bk)])
                self.act(qsb[bk][:], ps[bk][:], AF.Copy, [("ps", bk)], [("qsb", bk)])
                self.dma("sp", QN_d[c * 128:(c + 1) * 128, cols], qsb[bk][:], [("qsb", bk)], [("QN_d", t)], semkey=("scr", 1 + bk))
            for hg in range(2):
                for k in range(6):
                    self.mm(ps[2][:], w_uq[:, k, 1024 + hg * 128:1152 + hg * 128], qnT[:, k, :], k == 0, k == 5, ["mwuq", "qnT"],
                            [("ps", 2)])
                for k in range(6):
                    self.mm(ps[3][:], w_uq[:, k, 1280 + hg * 128:1408 + hg * 128], qnT[:, k, :], k == 0, k == 5, ["mwuq", "qnT"],
                            [("ps", 3)])
                rope(ps[2][:], ps[3][:], 128, ("ps", 2), ("ps", 3), ro[0][:], ro[1][:], ("ro", 0), ("ro", 1))
                self.dma("sp", QR_d[hg * 128:(hg + 1) * 128, cols], ro[0][:], [("ro", 0)], [("QR_d", t)], semkey=("scr", 0))
                self.dma("sp", QR_d[256 + hg * 128:256 + (hg + 1) * 128, cols], ro[1][:], [("ro", 1)], [("QR_d", t)], semkey=("scr", 0))
            for c in range(8):
                bk = c % 2
                for k in range(2):
                    self.mm(ps[bk][:], w_ukv[:, k, c * 128:(c + 1) * 128], kvnT[:, k, :], k == 0, k == 1, ["mwukv", "kvnT"], [("ps", bk)])
                self.vec("dve", "tensor_copy", [("ps", bk)], [("qsb", bk)], out=qsb[bk][:], in_=ps[bk][:])
                self.dma("sp", KN_d[c * 128:(c + 1) * 128, cols], qsb[bk][:], [("qsb", bk)], [("KN_d", t)], semkey=("scr", 1 + bk))
            for jj in range(4):
                j = t * 4 + jj
                tok = slice(jj * 128, (jj + 1) * 128)
                vb = vsb[jj % 2]
                for n in range(2):
                    for k in range(2):
                        self.mm(ps[2 + n][:], kvnT[:, k, tok], w_ukv[:, k, 1024 + n * 512:1536 + n * 512], k == 0, k == 1,
                                ["kvnT", "mwukv"], [("ps", 2 + n)])
                    if n == 0:
                        self.act(vb[:, 0:512], ps[2][:], AF.Copy, [("ps", 2)], [("vsb", jj % 2)])
                    else:
                        self.vec("dve", "tensor_copy", [("ps", 3)], [("vsb", jj % 2)], out=vb[:, 512:1024], in_=ps[3][:])
                self.dma("sp", V_d[j * 128:(j + 1) * 128, :], vb[:], [("vsb", jj % 2)], [("V_d", j)], semkey=("scr", 3 + jj % 2))
        S.barrier()
        self.release(mA)
        KhT = [self.sb("KhT%d" % i, [96, seq], BF16) for i in range(2)]
        QhT = [self.sb("QhT%d" % i, [96, seq], BF16) for i in range(2)]
        V1 = [self.sb("V1_%d" % i, [128, NB, 65], BF16) for i in range(2)]
        PT = [self.sb("PT%d" % i, [128, 512], BF16) for i in range(4)]
        rd = self.sb("rd", [128, 512], F32)
        bcs = self.sb("bcs", [64, 512], F32)
        osb = [self.sb("osb%d" % i, [64, 512], BF16) for i in range(2)]
        for i in range(2):
            self.vec("dve", "memset", [], [("V1", i)], V1[i][:, :, 64:65], 1.0)
        scale = 1.0 / math.sqrt(96.0)
        blocks = []
        for h in range(16):
            for qt in range(NT):
                nkb = 4 * (qt + 1)
                for kb in range(nkb):
                    blocks.append((h, qt, kb, nkb))
        loaded_heads = set()

        def load_head(h):
            if h in loaded_heads or h >= 16:
                return
            loaded_heads.add(h)
            hb = h % 2
            self.dma("sp", KhT[hb][0:64, :], KN_d[h * 64:(h + 1) * 64, :], [], [("KhT", hb, 0)], semkey=("ld", hb))
            self.dma("sp", KhT[hb][64:96, :], KR_d[0:32, :], [], [("KhT", hb, 1)], semkey=("ld", hb))
            self.dma("sp", QhT[hb][0:64, :], QN_d[h * 64:(h + 1) * 64, :], [], [("QhT", hb, 0)], semkey=("ld", hb))
            self.dma("sp", QhT[hb][64:80, :], QR_d[h * 16:(h + 1) * 16, :], [], [("QhT", hb, 1)], semkey=("ld", hb))
            self.dma("sp", QhT[hb][80:96, :], QR_d[256 + h * 16:256 + (h + 1) * 16, :], [], [("QhT", hb, 2)], semkey=("ld", hb))
            self.dma("sp", V1[hb][:, :, 0:64], V_d[:, h * 64:(h + 1) * 64].rearrange("(j p) c -> p j c", p=128), [],
                     [("V1", hb)], semkey=("ld", hb))

        def geom(i):
            h, qt, kb, nkb = blocks[i]
            jd = kb - 4 * qt
            c0 = 128 * jd if jd > 0 else 0
            return h, qt, kb, nkb, jd, c0, i % 4

        def stage1(i):
            h, qt, kb, nkb, jd, c0, sbk = geom(i)
            hb = h % 2
            kres = [("KhT", hb, 0), ("KhT", hb, 1)]
            qres = [("QhT", hb, 0), ("QhT", hb, 1), ("QhT", hb, 2)]
            pt = PT[sbk]
            self.mm(ps[sbk][:, c0:512], KhT[hb][:, kb * 128:(kb + 1) * 128], QhT[hb][:, qt * 512 + c0:(qt + 1) * 512],
                    True, True, kres + qres, [("ps", sbk)])
            self.act(pt[:, c0:512], ps[sbk][:, c0:512], AF.Exp, [("ps", sbk)], [("PT", sbk)], scale=scale)
            if jd >= 0:
                self.vec("dve", "tensor_tensor", [("PT", sbk), "tri"], [("PT", sbk)], out=pt[:, c0:c0 + 128],
                         in0=pt[:, c0:c0 + 128], in1=tri[:], op=ALU.mult)

        def stage2(i):
            h, qt, kb, nkb, jd, c0, sbk = geom(i)
            hb = h % 2
            ob = 4 + (qt % 2)
            pt = PT[sbk]
            self.mm(ps[ob][0:65, c0:512], V1[hb][:, kb, :], pt[:, c0:512], kb == 0, kb == nkb - 1,
                    [("V1", hb), ("PT", sbk)], [("ps", ob)])
            if kb == nkb - 1:
                self.vec("dve", "reciprocal", [("ps", ob)], ["rd"], out=rd[64:65, :], in_=ps[ob][64:65, :])
                self.mm(ps[6][0:64, :], cst[64:65, 256:320], rd[64:65, :], True, True, ["cst", "rd"], [("ps", 6)])
                self.act(bcs[:], ps[6][0:64, :], AF.Copy, [("ps", 6)], ["bcs"])
                o_ = osb[qt % 2]
                self.vec("dve", "tensor_tensor", [("ps", ob), "bcs"], [("osb", qt % 2)], out=o_[:], in0=ps[ob][0:64, :], in1=bcs[:],
                         op=ALU.mult)
                self.dma("sp", OT_d[h * 64:(h + 1) * 64, qt * 512:(qt + 1) * 512], o_[:], [("osb", qt % 2)], [("OT_d", qt)],
                         semkey=("ost", qt % 2))

        load_head(0)
        load_head(1)
        stage1(0)
        stage1(1)
        for i in range(len(blocks)):
            if i + 2 < len(blocks):
                stage1(i + 2)
            stage2(i)
            if i + 1 == len(blocks) or blocks[i + 1][0] != blocks[i][0]:
                load_head(blocks[i][0] + 2)
        S.barrier()
        self.release(mA)
        oT = [self.sb("oT%d" % i, [128, KC, 512], BF16) for i in range(2)]
        tmp = [self.sb("tmpm%d" % i, [128, 512], F32) for i in range(2)]
        for t in range(NT):
            cols = slice(t * 512, (t + 1) * 512)
            ot = oT[t % 2]
            self.dma("sp", ot[:], OT_d[:, cols].rearrange("(k p) t -> p k t", p=128), [], [("oT", t % 2)], semkey=("otl", t % 2))
            for jj in range(4):
                j = t * 4 + jj
                self.dma("sp", xt[:, jj, :], self.xs_d[j * 128:(j + 1) * 128, :], [("xd", j)], [("xt", jj)], semkey=("xtl", 0))
            for jj in range(4):
                j = t * 4 + jj
                tok = slice(jj * 128, (jj + 1) * 128)
                for n in range(2):
                    pb = ps[n]
                    for k in range(KC):
                        self.mm(pb[:], ot[:, k, tok], w_o[:, k, n * 512:(n + 1) * 512], k == 0, k == KC - 1, [("oT", t % 2), "mwo"],
                                [("ps", n)])
                    xs_ = xt[:, jj, n * 512:(n + 1) * 512]
                    self.vec("dve", "tensor_tensor", [("ps", n), "G"], [("tmpm", n)], out=tmp[n][:], in0=pb[:],
                             in1=G[:, n * 512:(n + 1) * 512], op=ALU.mult)
                    self.vec("dve", "tensor_tensor", [("tmpm", n), ("xt", jj)], [("xt", jj)], out=xs_, in0=tmp[n][:], in1=xs_,
                             op=ALU.add)
                self.dma("sp", self.xs_d[j * 128:(j + 1) * 128, :], xt[:, jj, :], [("xt", jj)], [("xd", j)], semkey=("xts", 0))
        self.sb_off = keep


    def final_norm(self, fin_in):
        m = self.mark()
        Gf = self.sb("Gf", [128, D], F32)
        junk = self.sb("junkf", [128, D], BF16)
        ob = [self.sb("ob%d" % i, [128, D], F32) for i in range(2)]
        self.dma("sp", Gf[:], fin_in.partition_broadcast(128), [], ["Gf"])
        for j in range(self.NB):
            xt = self.xres[:, j, :]
            ss = self.stat[:, 16 + j % 2:17 + j % 2]
            rs = self.stat[:, 18 + j % 2:19 + j % 2]
            o = ob[j % 2]
            self.act(junk[:], xt, AF.Square, [("x", j)], ["junkf", ("fss", j % 2)], accum_out=ss)
            self.vec("dve", "tensor_scalar", [("fss", j % 2)], [("frs", j % 2)], out=rs, in0=ss, scalar1=1.0 / D,
                     scalar2=EPS, op0=ALU.mult, op1=ALU.add)
            self.act(rs, rs, AF.Sqrt, [("frs", j % 2)], [("frs", j % 2)])
            self.vec("dve", "reciprocal", [("frs", j % 2)], [("frs", j % 2)], out=rs, in_=rs)
            self.vec("dve", "scalar_tensor_tensor", [("x", j), ("frs", j % 2), "Gf"], [("ob", j % 2)],
                     out=o[:], in0=xt, scalar=rs, in1=Gf[:], op0=ALU.mult, op1=ALU.mult)
            self.dma("sp", self.y_out[j * 128:(j + 1) * 128, :], o[:], [("ob", j % 2)], [("yout", j)],
                     semkey=("yout", j % 2))
        self.release(m)


def make_consts():
    c = np.zeros((128, 512), np.float32)
    c[:, 0:128] = np.eye(128, dtype=np.float32)
    ii = np.arange(128)
    c[:, 128:256] = (ii[:, None] <= ii[None, :]).astype(np.float32)
    c[:, 256:384] = 1.0
    for g, w in enumerate((2, 4, 8, 16)):
        c[:, 384 + g * 16: 384 + (g + 1) * 16] = 1.0 / np.minimum(np.arange(16) + 1.0, float(w))
    return c


def host_inputs(b, seq, x, c, positions, mod_w, mod_b, norm_g, ffn_w13, ffn_w2, ab_w_in, pool_w, pool_scale,
                ssd_conv_w, ssd_conv_b, ssd_dt_bias, ssd_a_log, ssd_d, ssd_norm_g, ab_w_out, mla_w_in,
                mla_q_norm_g, mla_w_uq, mla_kv_norm_g, mla_w_ukv, mla_w_o, final_norm_g):
    f = np.float32
    m = {}
    m["x"] = np.ascontiguousarray(x[b], dtype=f)
    m["c"] = np.ascontiguousarray(c[b].reshape(KC, 128).T, dtype=f)
    m["pos"] = np.ascontiguousarray(positions[b].reshape(1, seq), dtype=np.int32)
    m["mod_w"] = np.ascontiguousarray(mod_w, dtype=f)
    m["mod_b"] = np.ascontiguousarray(mod_b, dtype=f)
    m["ngP"] = np.ascontiguousarray(norm_g.reshape(2, 3, KC, 128).transpose(3, 0, 1, 2).reshape(128, 48), dtype=f)
    m["ffn_w13"] = np.ascontiguousarray(ffn_w13.reshape(4, D, 2 * DFF), dtype=f)
    m["ffn_w2"] = np.ascontiguousarray(ffn_w2.reshape(4, DFF, D), dtype=f)
    m["ab_w_in"] = np.ascontiguousarray(ab_w_in[0], dtype=f)
    m["pool_w"] = np.ascontiguousarray(pool_w[0].reshape(512, 128), dtype=f)
    abP = np.zeros((128, 80), f)
    abP[:, 0:4] = pool_scale[0].reshape(4, 128).T
    abP[:, 4:52] = ssd_conv_w[0].reshape(4, 12, 128).transpose(2, 1, 0).reshape(128, 48)
    abP[:, 52:64] = ssd_conv_b[0].reshape(12, 128).T
    abP[:, 64:72] = ssd_norm_g[0].reshape(8, 128).T
    m["abP"] = abP
    abR = np.zeros((1, 64), f)
    abR[0, 0:16] = ssd_dt_bias[0]
    abR[0, 16:32] = ssd_a_log[0]
    abR[0, 32:48] = ssd_d[0]
    m["abR"] = abR
    sel = np.zeros((16, 16, 128), f)
    for h in range(16):
        sel[h, h, :] = 1.0
    m["sel"] = sel.reshape(16, 2048)
    ii = np.arange(128)
    negm = np.where(ii[None, :] < ii[:, None], -30000.0, 0.0).astype(f)
    m["cmask"] = np.ascontiguousarray(np.tile(negm, (1, 4)))
    m["ab_w_out"] = np.ascontiguousarray(ab_w_out[0], dtype=f)
    m["mla_w_in"] = np.ascontiguousarray(mla_w_in[0], dtype=f)
    wq = mla_w_uq[0].reshape(768, 16, 96)
    m["mla_w_uq"] = np.ascontiguousarray(np.concatenate(
        [wq[:, :, 0:64].reshape(768, 1024), wq[:, :, 64:80].reshape(768, 256), wq[:, :, 80:96].reshape(768, 256)], axis=1), dtype=f)
    wkv = mla_w_ukv[0].reshape(256, 16, 128)
    m["mla_w_ukv"] = np.ascontiguousarray(np.concatenate(
        [wkv[:, :, 0:64].reshape(256, 1024), wkv[:, :, 64:128].reshape(256, 1024)], axis=1), dtype=f)
    m["mla_w_o"] = np.ascontiguousarray(mla_w_o[0], dtype=f)
    mp = np.zeros((128, 16), f)
    mp[:, 0:6] = mla_q_norm_g[0].reshape(6, 128).T
    mp[:, 6:8] = mla_kv_norm_g[0].reshape(2, 128).T
    mp[:, 8] = np.tile(INV_FREQ, 8)
    m["mlaP"] = mp
    m["final_g"] = np.ascontiguousarray(final_norm_g.reshape(1, D), dtype=f)
    m["consts"] = make_consts()
    return m


_NC_CACHE = {}


def kernel(**inputs):
    inputs = {k: np.asarray(v) for k, v in inputs.items()}
    B, seq, _ = inputs["x"].shape
    key = (seq,)
    if key not in _NC_CACHE:
        _NC_CACHE[key] = Builder(seq).build()
    nc = _NC_CACHE[key]
    in_maps = [host_inputs(b, seq, **inputs) for b in range(B)]
    res = run_bass_kernel_spmd(nc, in_maps, core_ids=list(range(B)))
    return np.stack([np.asarray(r["y"], dtype=np.float32) for r in res.results], axis=0)
```

```python
import numpy as np
import concourse.bass as bass
import concourse.mybir as mybir
from concourse.bass_utils import run_bass_kernel_spmd

F32 = mybir.dt.float32
BF16 = mybir.dt.bfloat16
I32 = mybir.dt.int32
AF = mybir.ActivationFunctionType
ALU = mybir.AluOpType
AX = mybir.AxisListType

D = 1024
KC = 8
DFF = 2816
NF = 22
EPS = 1e-6
SEQ_FULL = 4096
IN_AB = 3088
IN_MLA = 1056

COMPUTE = ("pe", "act", "dve", "pool")
INV_FREQ = (np.float32(10000.0) ** (-np.arange(0, 32, 2, dtype=np.float32) / np.float32(32))).astype(np.float32)


class Op:
    __slots__ = ("q", "fn", "reads", "writes", "dma", "semkey", "deps", "signal", "idx", "need_sig", "bar_dma")


class Sched:
    def __init__(self, nc):
        self.nc = nc
        self.ops = []
        self.last_w = {}
        self.readers = {}
        self.sems = {}
        self.dma_count = {}
        self.bar_deps = []
        self.bar_dma = {}

    def op(self, q, fn, reads=(), writes=(), dma=False, semkey=None):
        o = Op()
        o.q, o.fn, o.reads, o.writes, o.dma = q, fn, tuple(reads), tuple(writes), dma
        o.idx = len(self.ops)
        o.need_sig = False
        o.bar_dma = self.bar_dma
        deps = set(self.bar_deps)
        for r in o.reads:
            w = self.last_w.get(r)
            if w is not None:
                deps.add(w)
        for w_ in o.writes:
            w = self.last_w.get(w_)
            if w is not None:
                deps.add(w)
            for rd in self.readers.get(w_, ()):
                deps.add(rd)
        deps.discard(o.idx)
        o.deps = deps
        if dma:
            if semkey is None:
                semkey = ("dma",) + tuple(o.writes[:1])
            o.semkey = semkey
            n = self.dma_count.get(semkey, 0) + 1
            self.dma_count[semkey] = n
            o.signal = (semkey, 16 * n)
        else:
            o.semkey = q
            o.signal = None
        for r in o.reads:
            self.readers.setdefault(r, []).append(o.idx)
        for w_ in o.writes:
            self.last_w[w_] = o.idx
            self.readers[w_] = []
        self.ops.append(o)
        return o

    def barrier(self):
        lastq = {}
        for o in self.ops:
            if (not o.dma) and o.fn is not None:
                lastq[o.q] = o.idx
        self.bar_deps = list(lastq.values())
        self.bar_dma = {k: 16 * n for k, n in self.dma_count.items()}
        self.last_w = {}
        self.readers = {}

    def emit(self):
        nc = self.nc
        ops = self.ops
        for o in ops:
            for d in o.deps:
                p = ops[d]
                if p.q == "pe" and o.q == "pe" and not p.dma:
                    continue
                p.need_sig = True
        cnt = {q: 0 for q in COMPUTE}
        for o in ops:
            if not o.dma and o.fn is not None and o.need_sig:
                cnt[o.q] += 1
                o.signal = (o.q, cnt[o.q])
        def sem(key):
            s = self.sems.get(key)
            if s is None:
                s = nc.alloc_semaphore("s%d" % len(self.sems))
                self.sems[key] = s
            return s
        queues = {}
        for o in ops:
            queues.setdefault(o.q, []).append(o)
        dma_before = {}
        run = {}
        for o in ops:
            dma_before[o.idx] = dict(run) if False else None
        dma_positions = {}
        for o in ops:
            if o.dma:
                dma_positions.setdefault(o.semkey, []).append(o.idx)
        import bisect

        def emit_queue(qname, eng):
            waited = {}
            for o in queues.get(qname, []):
                need = dict(o.bar_dma)
                for d in o.deps:
                    p = ops[d]
                    if p.dma:
                        pos = dma_positions[p.semkey]
                        n = bisect.bisect_left(pos, o.idx)
                        key, val = p.semkey, 16 * n
                    else:
                        if p.fn is None:
                            continue
                        if p.q == "pe" and o.q == "pe" and not o.dma:
                            continue
                        key, val = p.signal
                    if need.get(key, 0) < val:
                        need[key] = val
                for key, val in need.items():
                    if waited.get(key, 0) >= val:
                        continue
                    waited[key] = val
                    eng.wait_ge(sem(key), val)
                if o.fn is None:
                    continue
                ins = o.fn(eng)
                if o.dma:
                    ins.then_inc(sem(o.semkey), 16)
                elif o.need_sig:
                    ins.then_inc(sem(o.q), 1)

        with nc.Block() as block:
            @block.tensor
            def _(e):
                emit_queue("pe", e)

            @block.scalar
            def _(e):
                emit_queue("act", e)

            @block.vector
            def _(e):
                emit_queue("dve", e)

            @block.gpsimd
            def _(e):
                emit_queue("pool", e)

            @block.sync
            def _(e):
                emit_queue("sp", e)


class Builder:
    def __init__(self, seq, subs=None, debug_out=None):
        self.seq = seq
        self.NT = seq // 512
        self.NB = seq // 128
        self.subs = subs
        nc = bass.Bass("TRN2", target_bir_lowering=False)
        self.nc = nc
        self.S = Sched(nc)
        self.sb_off = 16640
        self.sb_top = 229376
        self.nalloc = 0

    def sb(self, name, shape, dtype):
        esz = 4 if dtype in (F32, I32) else 2
        n = 1
        for s in shape[1:]:
            n *= s
        nbytes = (n * esz + 63) // 64 * 64
        off = self.sb_off
        assert off + nbytes <= self.sb_top, "SBUF overflow at %s: need %d have %d" % (name, nbytes, self.sb_top - off)
        self.sb_off += nbytes
        self.nalloc += 1
        return self.nc.alloc_sbuf_tensor_at("%s_%d" % (name, self.nalloc), list(shape), dtype, offset=off)

    def mark(self):
        return self.sb_off

    def release(self, m):
        self.sb_off = m

    def dram(self, name, shape, dtype, kind="Internal"):
        return self.nc.dram_tensor(name, list(shape), dtype, kind=kind).ap()

    def mm(self, out, lhsT, rhs, start, stop, reads, writes):
        self.S.op("pe", lambda e: e.matmul(out, lhsT, rhs, start=start, stop=stop), reads, writes)

    def tr(self, out, in_, ident, reads, writes):
        self.S.op("pe", lambda e: e.transpose(out, in_, ident), reads, writes)

    def act(self, out, in_, func, reads, writes, bias=None, scale=None, accum_out=None):
        kw = {}
        if bias is not None:
            kw["bias"] = bias
        if scale is not None:
            kw["scale"] = scale
        if accum_out is not None:
            kw["accum_out"] = accum_out
        self.S.op("act", lambda e: e.activation(out=out, in_=in_, func=func, **kw), reads, writes)

    def vec(self, q, method, reads, writes, *args, **kw):
        self.S.op(q, lambda e: getattr(e, method)(*args, **kw), reads, writes)

    def dma(self, q, out, in_, reads, writes, semkey=None):
        self.S.op(q, lambda e: e.dma_start(out=out, in_=in_), reads, writes, dma=True, semkey=semkey)

    def build(self):
        nc = self.nc
        seq, NT, NB = self.seq, self.NT, self.NB
        S = self.S
        x_in = self.dram("x", [seq, D], F32, "ExternalInput")
        c_in = self.dram("c", [128, KC], F32, "ExternalInput")
        pos_in = self.dram("pos", [1, seq], I32, "ExternalInput")
        mod_w = self.dram("mod_w", [2, D, 9 * D], F32, "ExternalInput")
        mod_b = self.dram("mod_b", [2, 9 * D], F32, "ExternalInput")
        ngP_in = self.dram("ngP", [128, 2 * 3 * KC], F32, "ExternalInput")
        w13_in = self.dram("ffn_w13", [4, D, 2 * DFF], F32, "ExternalInput")
        w2_in = self.dram("ffn_w2", [4, DFF, D], F32, "ExternalInput")
        abin_in = self.dram("ab_w_in", [D, IN_AB], F32, "ExternalInput")
        poolw_in = self.dram("pool_w", [512, 128], F32, "ExternalInput")
        abP_in = self.dram("abP", [128, 80], F32, "ExternalInput")
        abR_in = self.dram("abR", [1, 64], F32, "ExternalInput")
        about_in = self.dram("ab_w_out", [1536, D], F32, "ExternalInput")
        mlain_in = self.dram("mla_w_in", [D, IN_MLA], F32, "ExternalInput")
        wuq_in = self.dram("mla_w_uq", [768, 1536], F32, "ExternalInput")
        wukv_in = self.dram("mla_w_ukv", [256, 2048], F32, "ExternalInput")
        wo_in = self.dram("mla_w_o", [D, D], F32, "ExternalInput")
        mlaP_in = self.dram("mlaP", [128, 16], F32, "ExternalInput")
        fin_in = self.dram("final_g", [1, D], F32, "ExternalInput")
        cst_in = self.dram("consts", [128, 512], F32, "ExternalInput")
        sel_in = self.dram("sel", [16, 2048], F32, "ExternalInput")
        cmask_in = self.dram("cmask", [128, 512], F32, "ExternalInput")
        self.xs_d = self.dram("xs_d", [seq, D], F32)
        self.ins = dict(abin=abin_in, poolw=poolw_in, abP=abP_in, abR=abR_in, about=about_in, mlain=mlain_in,
                        wuq=wuq_in, wukv=wukv_in, wo=wo_in, mlaP=mlaP_in, sel=sel_in, cmask=cmask_in, pos=pos_in)
        y_out = self.dram("y", [seq, D], F32, "ExternalOutput")
        self.x_in, self.y_out = x_in, y_out

        w13_s = self.dram("w13_s", [4, D, 2 * DFF], BF16)
        w2_s = self.dram("w2_s", [4, DFF, D], BF16)
        mod_d = self.dram("mod_d", [2, 9 * D], F32)
        self.w13_s, self.w2_s, self.mod_d = w13_s, w2_s, mod_d

        self.ident = self.sb("ident", [128, 128], BF16)
        self.cst = self.sb("cst", [128, 512], F32)
        self.modP = self.sb("modP", [128, 2, 9, KC], F32)
        self.ngP = self.sb("ngP", [128, 2, 3, KC], F32)
        self.aP = self.sb("aP", [128, 2, 3, KC], F32)
        self.stat = self.sb("stat", [128, 64], F32)
        self.x_off = self.sb_off
        self.xres = self.sb("xres", [128, max(NB, 32), D], F32)
        self.ps = [nc.alloc_psum_tensor("ps%d" % i, [128, 512], F32) for i in range(8)]

        self.dma("sp", self.cst[:], cst_in, ["cst_in"], ["cst"])
        self.vec("dve", "tensor_copy", ["cst"], ["ident"], out=self.ident[:], in_=self.cst[:, 0:128])
        self.dma("sp", self.ngP[:].rearrange("p a b c -> p (a b c)"), ngP_in, [], ["ngP"])
        for j in range(NB):
            self.dma("sp", self.xres[:, j, :], x_in[j * 128:(j + 1) * 128, :], [], [("x", j)], semkey=("xload", j % 4))
        self.precast(w13_s[0], w13_in[0], D, 2 * DFF, ("w13s", 0))
        self.precast(w2_s[0], w2_in[0], DFF, D, ("w2s", 0))
        self.modulation(c_in, mod_w, mod_b)
        for i in range(1, 4):
            self.precast(w13_s[i], w13_in[i], D, 2 * DFF, ("w13s", i))
            self.precast(w2_s[i], w2_in[i], DFF, D, ("w2s", i))

        def want(l, s):
            return self.subs is None or (l, s) in self.subs

        for l in range(2):
            if want(l, 0):
                self.ffn(l, 0)
            if want(l, 1):
                self.spill_x()
                if l == 0:
                    self.mixer_ab()
                else:
                    self.mixer_mla()
                self.reload_x()
            if want(l, 2):
                self.ffn(l, 1)
        self.final_norm(fin_in)
        S.op("sp", None, reads=[("yout", j) for j in range(NB)])
        S.emit()
        return nc

    def precast(self, dst, src, rows, cols, res):
        a = rows // 128
        half = max(1, a // 2)
        for h0 in range(0, a, half):
            h1 = min(a, h0 + half)
            d = dst.rearrange("(p a) c -> p a c", p=128)[:, h0:h1, :]
            s = src.rearrange("(p a) c -> p a c", p=128)[:, h0:h1, :]
            self.dma("pool", d, s, [], [res], semkey=("pc",) + tuple(res))

    def modulation(self, c_in, mod_w, mod_b):
        m = self.mark()
        cact = self.sb("cact", [128, KC], F32)
        mrow = self.sb("mrow", [1, 9 * D], F32)
        modT = self.sb("modT", [128, 128], F32)
        wst = [self.sb("modw%d" % i, [128, KC, 512], F32) for i in range(2)]
        self.dma("sp", cact[:], c_in, [], ["cact"])
        self.act(cact[:], cact[:], AF.Silu, ["cact"], ["cact"])
        for l in range(2):
            self.dma("sp", mrow[:], mod_b[l:l + 1, :], [], ["mrow"])
            for cb in range(18):
                slot = (l * 18 + cb) % 2
                self.dma("sp", wst[slot][:], mod_w[l, :, cb * 512:(cb + 1) * 512].rearrange("(k p) c -> p k c", p=128),
                         [], [("modw", slot)])
                pst = self.ps[cb % 2]
                for k in range(KC):
                    self.mm(pst[0:1, :], cact[:, k:k + 1], wst[slot][:, k, :], k == 0, k == KC - 1,
                            ["cact", ("modw", slot)], [("ps", cb % 2)])
                self.vec("dve", "tensor_tensor", [("ps", cb % 2), "mrow"], ["mrow"],
                         out=mrow[0:1, cb * 512:(cb + 1) * 512], in0=pst[0:1, :], in1=mrow[0:1, cb * 512:(cb + 1) * 512],
                         op=ALU.add)
            self.dma("sp", self.mod_d[l:l + 1, :], mrow[:], ["mrow"], [("mod_d", l)])
            self.dma("sp", modT[0:72, :], self.mod_d[l, :].rearrange("(r p) -> r p", p=128), [("mod_d", l)], ["modT"])
            self.tr(self.ps[2][:, 0:72], modT[0:72, :], self.cst[0:72, 0:72], ["modT", "cst"], [("ps", 2)])
            self.vec("dve", "tensor_copy", [("ps", 2)], ["modP"],
                     out=self.modP[:, l, :, :].rearrange("p j k -> p (j k)"), in_=self.ps[2][:, 0:72])
        for l in range(2):
            for s in range(3):
                self.vec("dve", "scalar_tensor_tensor", ["modP", "ngP"], ["aP"],
                         out=self.aP[:, l, s, :], in0=self.modP[:, l, 3 * s + 1, :], scalar=1.0,
                         in1=self.ngP[:, l, s, :], op0=ALU.add, op1=ALU.mult)
        self.S.barrier()
        self.release(m)

    def norm_mod_T(self, t, l, s, hT, hres, xn, junk, xsrc=None, xr=None):
        xts, xrs = [], []
        for jj in range(4):
            j = t * 4 + jj
            xt = self.xres[:, j, :] if xsrc is None else xsrc(jj)
            xres_ = ("x", j) if xr is None else xr(jj)
            xts.append(xt)
            xrs.append(xres_)
            if junk is None:
                self.act(xn[:, jj, :], xt, AF.Square, [xres_], [("xn", jj), "ss"], accum_out=self.stat[:, jj:jj + 1])
            else:
                self.act(junk[:], xt, AF.Square, [xres_], ["junk", "ss"], accum_out=self.stat[:, jj:jj + 1])
        rs4 = self.stat[:, 8:12]
        self.vec("dve", "tensor_scalar", ["ss"], ["rs"], out=rs4, in0=self.stat[:, 0:4], scalar1=1.0 / D, scalar2=EPS,
                 op0=ALU.mult, op1=ALU.add)
        self.act(rs4, rs4, AF.Sqrt, ["rs"], ["rs"])
        self.vec("dve", "reciprocal", ["rs"], ["rs"], out=rs4, in_=rs4)
        for jj in range(4):
            self.vec("dve", "tensor_scalar", [xrs[jj], "rs"], [("xn", jj)], out=xn[:, jj, :], in0=xts[jj],
                     scalar1=self.stat[:, 8 + jj:9 + jj], scalar2=None, op0=ALU.mult)
        for k in range(KC):
            bank = 4 + (k % 4)
            pv = self.ps[bank][:].bitcast(BF16)
            for jj in range(4):
                self.tr(pv[:, jj * 128:(jj + 1) * 128], xn[:, jj, k * 128:(k + 1) * 128], self.ident[:],
                        [("xn", jj), "ident"], [("ps", bank)])
            self.act(hT[:, k, :], pv[:, 0:512], AF.Identity, [("ps", bank), "aP", "modP"], [hres],
                     scale=self.aP[:, l, s, k:k + 1], bias=self.modP[:, l, 3 * s, k:k + 1])

    def load_gate(self, G, l, s):
        self.dma("sp", G[:], self.mod_d[l:l + 1, (3 * s + 2) * D:(3 * s + 3) * D].partition_broadcast(128),
                 [("mod_d", l)], ["G"])

    def ffn(self, l, s2):
        S = self.S
        fi = l * 2 + s2
        s = 0 if s2 == 0 else 2
        NT = self.NT
        m = self.mark()
        hT = self.sb("hT", [128, KC, 512], BF16)
        gT = self.sb("gT", [128, NF, 512], BF16)
        xn = self.sb("xn", [128, 4, D], BF16)
        G = self.sb("G", [128, D], F32)
        sg = [self.sb("sg%d" % i, [128, 512], F32) for i in range(2)]
        NSL = 3
        ring = [self.sb("ring%d" % i, [128, 4096], BF16) for i in range(NSL)]
        self.load_gate(G, l, s)
        w2grp = [(0, 8), (8, 8), (16, 6)]
        pieces = []
        for t in range(NT):
            for pc in range(11):
                pieces.append(("w13", pc))
            for n in range(2):
                for g in range(3):
                    pieces.append(("w2", n, g))
        issued = [0]

        def issue():
            gi = issued[0]
            p = pieces[gi]
            sl = gi % NSL
            issued[0] += 1
            if p[0] == "w13":
                pc = p[1]
                v = ring[sl][:].rearrange("p (k a c) -> p k a c", k=KC, a=2)
                for ab in range(2):
                    src = self.w13_s[fi, :, ab * DFF + pc * 256: ab * DFF + (pc + 1) * 256]
                    self.dma("sp", v[:, :, ab, :], src.rearrange("(k p) c -> p k c", p=128),
                             [("w13s", fi)], [("ring", sl, ab)])
            else:
                n, g = p[1], p[2]
                f0, nf = w2grp[g]
                v = ring[sl][:].rearrange("p (f c) -> p f c", f=8)
                src = self.w2_s[fi, f0 * 128:(f0 + nf) * 128, n * 512:(n + 1) * 512]
                self.dma("sp", v[:, 0:nf, :], src.rearrange("(f p) c -> p f c", p=128), [("w2s", fi)],
                         [("ring", sl, 0), ("ring", sl, 1)])

        def ensure(gi):
            while issued[0] < len(pieces) and issued[0] < gi + NSL:
                issue()
            return gi % NSL

        pi = 0
        for t in range(NT):
            self.norm_mod_T(t, l, s, hT, "hT", xn, None)
            for pc in range(11):
                sl = ensure(pi)
                pi += 1
                wv = ring[sl][:].rearrange("p (k a c) -> p k a c", k=KC, a=2)
                for ff in range(2):
                    f = pc * 2 + ff
                    pa, pb = self.ps[ff * 2], self.ps[ff * 2 + 1]
                    for k in range(KC):
                        self.mm(pa[:], wv[:, k, 0, ff * 128:(ff + 1) * 128], hT[:, k, :], k == 0, k == KC - 1,
                                [("ring", sl, 0), "hT"], [("ps", ff * 2)])
                    for k in range(KC):
                        self.mm(pb[:], wv[:, k, 1, ff * 128:(ff + 1) * 128], hT[:, k, :], k == 0, k == KC - 1,
                                [("ring", sl, 1), "hT"], [("ps", ff * 2 + 1)])
                    self.act(sg[ff][:], pa[:], AF.Silu, [("ps", ff * 2)], [("sg", ff)])
                    self.vec("dve", "tensor_tensor", [("sg", ff), ("ps", ff * 2 + 1)], [("gT", f)],
                             out=gT[:, f, :], in0=sg[ff][:], in1=pb[:], op=ALU.mult)
            for n in range(2):
                for g in range(3):
                    sl = ensure(pi)
                    pi += 1
                    f0, nf = w2grp[g]
                    wv = ring[sl][:].rearrange("p (f c) -> p f c", f=8)
                    for fl in range(nf):
                        f = f0 + fl
                        for jj in range(4):
                            self.mm(self.ps[4 + jj][:], gT[:, f, jj * 128:(jj + 1) * 128], wv[:, fl, :],
                                    f == 0, f == NF - 1, [("gT", f), ("ring", sl, 0), ("ring", sl, 1)], [("ps", 4 + jj)])
                for jj in range(4):
                    j = t * 4 + jj
                    xs = self.xres[:, j, n * 512:(n + 1) * 512]
                    tb = sg[jj % 2]
                    self.vec("dve", "tensor_tensor", [("ps", 4 + jj), "G"], [("sg", jj % 2)],
                             out=tb[:], in0=self.ps[4 + jj][:], in1=G[:, n * 512:(n + 1) * 512], op=ALU.mult)
                    self.vec("dve", "scalar_tensor_tensor", [("sg", jj % 2), ("x", j)], [("x", j)],
                             out=xs, in0=tb[:], scalar=0.5, in1=xs, op0=ALU.mult, op1=ALU.add)
        S.barrier()
        self.release(m)

    def spill_x(self):
        for j in range(self.NB):
            self.dma("sp", self.xs_d[j * 128:(j + 1) * 128, :], self.xres[:, j, :], [("x", j)], [("xd", j)],
                     semkey=("xsp", j % 4))
        self.S.barrier()

    def reload_x(self):
        self.S.barrier()
        for j in range(self.NB):
            self.dma("sp", self.xres[:, j, :], self.xs_d[j * 128:(j + 1) * 128, :], [("xd", j)], [("x", j)],
                     semkey=("xload", j % 4))

    def mixer_ab(self):
        S = self.S
        ins = self.ins
        l, s = 0, 1
        NT = self.NT
        keep = self.sb_off
        self.sb_off = self.x_off
        cst = self.cst
        w_in = self.sb("abwin", [128, KC, IN_AB], BF16)
        w_out = self.sb("abwout", [128, 12, D], BF16)
        pw = self.sb("poolw", [128, 4, 128], BF16)
        abP = self.sb("abP", [128, 80], F32)
        rowp = self.sb("rowp", [128, 48], F32)
        a_bc = self.sb("a_bc", [128, 16], F32)
        sel = self.sb("sel", [16, 16, 128], BF16)
        negm = self.sb("negm", [128, 512], BF16)
        G = self.sb("G", [128, D], F32)
        hT = self.sb("hT", [128, KC, 512], BF16)
        xn = self.sb("xn", [128, 4, D], BF16)
        junk = self.sb("junk", [128, D], BF16)
        xt = [self.sb("xt0", [128, 4, D], F32)] * 2
        Up = self.sb("Up", [128, 528], F32)
        sA = self.sb("sA", [128, 528], F32)
        sB = self.sb("sB", [128, 528], F32)
        phalo = self.sb("phalo", [128, 4, 16], F32)
        dT = [self.sb("dT%d" % i, [128, 512], BF16) for i in range(2)]
        dfix = self.sb("dfix", [128, 16], F32)
        ypT = self.sb("ypT", [128, 4, 512], BF16)
        Uc = [self.sb("Uc0", [128, 515], F32)] * 2
        chalo = self.sb("chalo", [128, 12, 3], F32)
        acc = [self.sb("acc0", [128, 512], F32)] * 2
        xbcT = self.sb("xbcT", [128, 12, 512], BF16)
        zs = self.sb("zs", [128, D], F32)
        dtt = self.sb("dtt", [128, 96], F32)
        sm2 = self.sb("sm2", [128, 48], F32)
        acsT = self.sb("acsT", [16, 3, 128], F32)
        acsHL = self.sb("acsHL", [16, 2, 128], BF16)
        xs_tok = self.sb("xs_tok", [128, D], BF16)
        xw = self.sb("xw", [128, D], BF16)
        xsD = self.sb("xsD", [128, D], BF16)
        Btok = self.sb("Btok", [128, 2, 128], BF16)
        cb = self.sb("cb", [128, 2, 128], F32)
        Eexp = [self.sb("Eexp%d" % i, [128, 512], F32) for i in range(2)]
        Mt = [self.sb("Mt%d" % i, [128, 512], BF16) for i in range(2)]
        H = self.sb("H", [128, D], F32)
        Hb = self.sb("Hb", [128, D], BF16)
        yA = self.sb("yA", [128, D], F32)
        ssg = self.sb("ssg", [128, 4], F32)
        yn = self.sb("yn", [128, D], BF16)
        yT = self.sb("yT", [128, KC, 512], BF16)
        tmp = [self.sb("tmpo%d" % i, [128, 512], F32) for i in range(2)]
        ps = self.ps
        ident = self.ident

        self.dma("pool", w_in[:], ins["abin"].rearrange("(k p) c -> p k c", p=128), [], ["abwin"])
        self.dma("pool", w_out[:], ins["about"].rearrange("(k p) c -> p k c", p=128), [], ["abwout"])
        self.dma("pool", pw[:], ins["poolw"].rearrange("(g p) c -> p g c", p=128), [], ["poolw"])
        self.dma("pool", sel[:].rearrange("p a b -> p (a b)"), ins["sel"], [], ["sel"])
        self.dma("pool", negm[:], ins["cmask"], [], ["negm"])
        self.dma("sp", abP[:], ins["abP"], [], ["abP"])
        self.dma("sp", rowp[:], ins["abR"][0:1, 0:48].partition_broadcast(128), [], ["rowp"])
        self.act(a_bc[:], rowp[:, 16:32], AF.Exp, ["rowp"], ["a_bc"])
        self.vec("dve", "tensor_scalar", ["a_bc"], ["a_bc"], out=a_bc[:], in0=a_bc[:], scalar1=-1.0, scalar2=None,
                 op0=ALU.mult)
        self.load_gate(G, l, s)
        self.vec("dve", "memset", [], ["phalo"], phalo[:], 0.0)
        self.vec("dve", "memset", [], ["chalo"], chalo[:], 0.0)
        self.vec("dve", "memset", [], ["H"], H[:], 0.0)
        self.vec("dve", "memset", [], ["Hb"], Hb[:], 0.0)
        D_bc = rowp[:, 32:48]
        dtb_bc = rowp[:, 0:16]
        wins = (2, 4, 8, 16)

        for t in range(NT):
            xb = xt[t % 2]
            for jj in range(4):
                j = t * 4 + jj
                self.dma("sp", xb[:, jj, :], self.xs_d[j * 128:(j + 1) * 128, :], [("xd", j)], [("xt", 0, jj)],
                         semkey=("xtl", t % 2))
            self.norm_mod_T(t, l, s, hT, "hT", xn, junk, xsrc=lambda jj: xb[:, jj, :],
                            xr=lambda jj: ("xt", 0, jj))
            for g in range(4):
                bk = g % 2
                for k in range(KC):
                    self.mm(ps[bk][:], w_in[:, k, g * 128:(g + 1) * 128], hT[:, k, :], k == 0, k == KC - 1,
                            ["abwin", "hT"], [("ps", bk)])
                self.vec("dve", "tensor_copy", ["phalo"], ["Up"], out=Up[:, 0:16], in_=phalo[:, g, :])
                self.act(Up[:, 16:528], ps[bk][:], AF.Copy, [("ps", bk)], ["Up"])
                self.vec("dve", "tensor_copy", ["Up"], ["phalo"], out=phalo[:, g, :], in_=Up[:, 512:528])
                self.vec("dve", "tensor_tensor", ["Up"], ["sA"], out=sA[:, 1:528], in0=Up[:, 1:528], in1=Up[:, 0:527],
                         op=ALU.add)
                lvl = sA
                if g >= 1:
                    self.vec("dve", "tensor_tensor", ["sA"], ["sB"], out=sB[:, 3:528], in0=sA[:, 3:528],
                             in1=sA[:, 1:526], op=ALU.add)
                    lvl = sB
                if g >= 2:
                    self.vec("dve", "tensor_tensor", ["sB"], ["sA"], out=sA[:, 7:528], in0=sB[:, 7:528],
                             in1=sB[:, 3:524], op=ALU.add)
                    lvl = sA
                if g >= 3:
                    self.vec("dve", "tensor_tensor", ["sA"], ["sB"], out=sB[:, 15:528], in0=sA[:, 15:528],
                             in1=sA[:, 7:520], op=ALU.add)
                    lvl = sB
                lres = "sA" if lvl is sA else "sB"
                dd = dT[g % 2]
                self.vec("dve", "scalar_tensor_tensor", [lres, "Up"], [("dT", g % 2)], out=dd[:], in0=lvl[:, 16:528],
                         scalar=1.0 / wins[g], in1=Up[:, 16:528], op0=ALU.mult, op1=ALU.subtract)
                if t == 0:
                    self.vec("dve", "tensor_tensor", [lres, "cst"], ["dfix"], out=dfix[:], in0=lvl[:, 16:32],
                             in1=cst[:, 384 + g * 16:384 + (g + 1) * 16], op=ALU.mult)
                    self.vec("dve", "tensor_tensor", ["dfix", "Up"], [("dT", g % 2)], out=dd[:, 0:16], in0=dfix[:],
                             in1=Up[:, 16:32], op=ALU.subtract)
                self.mm(ps[2 + bk][:], pw[:, g, :], dd[:], True, True, ["poolw", ("dT", g % 2)], [("ps", 2 + bk)])
                self.act(ypT[:, g, :], ps[2 + bk][:], AF.Copy, [("ps", 2 + bk), "abP"], [("ypT", g)],
                         scale=abP[:, g:g + 1])
            for cc in range(12):
                bk = cc % 2
                c0 = 1536 + cc * 128
                for k in range(KC):
                    self.mm(ps[bk][:], w_in[:, k, c0:c0 + 128], hT[:, k, :], k == 0, k == KC - 1,
                            ["abwin", "hT"], [("ps", bk)])
                U = Uc[bk]
                ur = ("Uc", 0)
                self.vec("dve", "tensor_copy", ["chalo"], [ur], out=U[:, 0:3], in_=chalo[:, cc, :])
                self.act(U[:, 3:515], ps[bk][:], AF.Copy, [("ps", bk)], [ur])
                self.vec("dve", "tensor_copy", [ur], ["chalo"], out=chalo[:, cc, :], in_=U[:, 512:515])
                a_ = acc[bk]
                ar = ("acc", 0)
                self.vec("dve", "tensor_scalar", [ur, "abP"], [ar], out=a_[:], in0=U[:, 0:512],
                         scalar1=abP[:, 4 + cc * 4:5 + cc * 4], scalar2=abP[:, 52 + cc:53 + cc], op0=ALU.mult, op1=ALU.add)
                for kk in range(1, 4):
                    self.vec("dve", "scalar_tensor_tensor", [ur, "abP", ar], [ar], out=a_[:], in0=U[:, kk:kk + 512],
                             scalar=abP[:, 4 + cc * 4 + kk:5 + cc * 4 + kk], in1=a_[:], op0=ALU.mult, op1=ALU.add)
                self.act(xbcT[:, cc, :], a_[:], AF.Silu, [ar], [("xbcT", cc)])
            for jj in range(4):
                j = t * 4 + jj
                tok = slice(jj * 128, (jj + 1) * 128)
                for n in range(2):
                    for k in range(KC):
                        self.mm(ps[2 + n][:], hT[:, k, tok], w_in[:, k, 512 + n * 512:1024 + n * 512], k == 0,
                                k == KC - 1, ["hT", "abwin"], [("ps", 2 + n)])
                    self.act(zs[:, n * 512:(n + 1) * 512], ps[2 + n][:], AF.Silu, [("ps", 2 + n)], ["zs"])
                for k in range(KC):
                    self.mm(ps[4][:, 0:16], hT[:, k, tok], w_in[:, k, 3072:3088], k == 0, k == KC - 1,
                            ["hT", "abwin"], [("ps", 4)])
                dt_, lndt, adt, nb, wst, eacs = (dtt[:, i * 16:(i + 1) * 16] for i in range(6))
                cd_bc, acs_sb, arg = (sm2[:, i * 16:(i + 1) * 16] for i in range(3))
                self.vec("dve", "tensor_tensor", [("ps", 4), "rowp"], ["dt"], out=dt_, in0=ps[4][:, 0:16], in1=dtb_bc,
                         op=ALU.add)
                self.act(dt_, dt_, AF.Exp, ["dt"], ["dt"])
                self.act(dt_, dt_, AF.Ln, ["dt"], ["dt"], bias=1.0)
                self.act(lndt, dt_, AF.Ln, ["dt"], ["lndt"])
                self.vec("dve", "tensor_tensor", ["dt", "a_bc"], ["adt"], out=adt, in0=dt_, in1=a_bc[:], op=ALU.mult)
                self.mm(ps[4][:, 16:32], cst[:, 128:256], adt, True, True, ["cst", "adt"], [("ps", 4)])
                self.mm(ps[4][0:16, 128:256], adt, cst[:, 128:256], True, True, ["cst", "adt"], [("ps", 4)])
                self.mm(ps[4][:, 32:48], cst[:, 256:384], adt, True, True, ["cst", "adt"], [("ps", 4)])
                self.vec("dve", "tensor_copy", [("ps", 4)], ["acs_sb"], out=acs_sb, in_=ps[4][:, 16:32])
                self.vec("dve", "tensor_tensor", ["lndt", "acs_sb"], ["nb"], out=nb, in0=lndt, in1=acs_sb, op=ALU.subtract)
                self.act(eacs, ps[4][:, 16:32], AF.Exp, [("ps", 4)], ["eacs"])
                self.vec("dve", "tensor_tensor", [("ps", 4), "nb"], ["arg"], out=arg, in0=ps[4][:, 32:48], in1=nb,
                         op=ALU.add)
                self.act(wst, arg, AF.Exp, ["arg"], ["wst"])
                self.act(cd_bc, ps[4][:, 32:48], AF.Exp, [("ps", 4)], ["cd_bc"])
                self.vec("dve", "tensor_copy", [("ps", 4)], ["acsT0"], out=acsT[:, 0, :], in_=ps[4][0:16, 128:256])
                self.vec("dve", "tensor_copy", ["acsT0"], ["acsHL0"], out=acsHL[:, 0, :], in_=acsT[:, 0, :])
                self.vec("dve", "tensor_copy", ["acsHL0"], ["acsT1"], out=acsT[:, 1, :], in_=acsHL[:, 0, :])
                self.vec("dve", "tensor_tensor", ["acsT0", "acsT1"], ["acsHL1"], out=acsHL[:, 1, :], in0=acsT[:, 0, :],
                         in1=acsT[:, 1, :], op=ALU.subtract)
                pv5 = ps[5][:].bitcast(BF16)
                for cc in range(8):
                    self.tr(pv5[:, cc * 128:(cc + 1) * 128], xbcT[:, cc, tok], ident[:], [("xbcT", cc), "ident"],
                            [("ps", 5)])
                self.act(xs_tok[:], pv5[:, 0:1024], AF.Copy, [("ps", 5)], ["xs_tok"])
                self.vec("dve", "tensor_tensor", [("ps", 5), "wst"], ["xw"],
                         out=xw[:].rearrange("p (h q) -> p h q", h=16), in0=pv5[:, 0:1024].rearrange("p (h q) -> p h q", h=16),
                         in1=wst.unsqueeze(2).to_broadcast([128, 16, 64]), op=ALU.mult)
                self.vec("dve", "tensor_tensor", ["xs_tok", "rowp"], ["xsD"],
                         out=xsD[:].rearrange("p (h q) -> p h q", h=16), in0=xs_tok[:].rearrange("p (h q) -> p h q", h=16),
                         in1=D_bc.unsqueeze(2).to_broadcast([128, 16, 64]), op=ALU.mult)
                pv7 = ps[7][:].bitcast(BF16)
                for g in range(2):
                    self.tr(pv7[:, g * 128:(g + 1) * 128], xbcT[:, 8 + g, tok], ident[:], [("xbcT", 8 + g), "ident"],
                            [("ps", 7)])
                self.act(Btok[:].rearrange("p a b -> p (a b)"), pv7[:, 0:256], AF.Copy, [("ps", 7)], ["Btok"])
                for g in range(2):
                    self.mm(ps[4][:, 256 + g * 128:384 + g * 128], xbcT[:, 8 + g, tok], xbcT[:, 10 + g, tok], True, True,
                            [("xbcT", 8 + g), ("xbcT", 10 + g)], [("ps", 4)])
                self.vec("dve", "tensor_copy", [("ps", 4)], ["cb"], out=cb[:].rearrange("p a b -> p (a b)"),
                         in_=ps[4][:, 256:512])
                def e_phase(q4):
                    ebk = 6 + (q4 % 2)
                    eb = Eexp[q4 % 2]
                    mb = Mt[q4 % 2]
                    g = q4 // 2
                    self.mm(ps[ebk][:], ident[:], negm[:], True, False, ["ident", "negm"], [("ps", ebk)])
                    for hh in range(4):
                        h = q4 * 4 + hh
                        for part in range(2):
                            self.mm(ps[ebk][:, hh * 128:(hh + 1) * 128], sel[:, h, :], acsHL[:, part, :], False,
                                    hh == 3 and part == 1, ["sel", "acsHL0", "acsHL1"], [("ps", ebk)])
                    for hh in range(4):
                        h = q4 * 4 + hh
                        self.act(eb[:, hh * 128:(hh + 1) * 128], ps[ebk][:, hh * 128:(hh + 1) * 128], AF.Exp,
                                 [("ps", ebk), "nb"], [("Eexp", q4 % 2)], bias=nb[:, h:h + 1])
                    self.vec("dve", "tensor_tensor", [("Eexp", q4 % 2), "cb"], [("Mt", q4 % 2)],
                             out=mb[:].rearrange("p (a b) -> p a b", a=4), in0=eb[:].rearrange("p (a b) -> p a b", a=4),
                             in1=cb[:, g:g + 1, :].to_broadcast([128, 4, 128]), op=ALU.mult)

                def y_phase(q4):
                    g, qq = q4 // 2, q4 % 2
                    ydb = ps[g]
                    mb = Mt[q4 % 2]
                    if qq == 0:
                        self.mm(ydb[:], ident[:], xsD[:, g * 512:(g + 1) * 512], True, False, ["ident", "xsD"], [("ps", g)])
                    for hh in range(4):
                        h = q4 * 4 + hh
                        hl = h - g * 8
                        self.mm(ydb[:, hl * 64:(hl + 1) * 64], mb[:, hh * 128:(hh + 1) * 128],
                                xs_tok[:, h * 64:(h + 1) * 64], False, qq == 1 and hh == 3,
                                [("Mt", q4 % 2), "xs_tok"], [("ps", g)])

                e_phase(0)
                for q4 in range(4):
                    if q4 + 1 < 4:
                        e_phase(q4 + 1)
                    y_phase(q4)
                for g in range(2):
                    gs = slice(g * 512, (g + 1) * 512)
                    self.mm(ps[2 + g][:], xbcT[:, 10 + g, tok], Hb[:, gs], True, True, [("xbcT", 10 + g), "Hb"],
                            [("ps", 2 + g)])
                    self.vec("dve", "tensor_tensor", [("ps", 2 + g), "eacs"], ["yA"],
                             out=yA[:, gs].rearrange("p (h q) -> p h q", h=8),
                             in0=ps[2 + g][:].rearrange("p (h q) -> p h q", h=8),
                             in1=eacs[:, g * 8:(g + 1) * 8].unsqueeze(2).to_broadcast([128, 8, 64]), op=ALU.mult)
                    self.vec("dve", "tensor_tensor", [("ps", g), "yA"], ["yA"], out=yA[:, gs], in0=ps[g][:], in1=yA[:, gs],
                             op=ALU.add)
                    self.mm(ps[7][:], Btok[:, g, :], xw[:, gs], True, True, ["Btok", "xw"], [("ps", 7)])
                    self.vec("dve", "tensor_tensor", ["H", "cd_bc"], ["H"], out=H[:, gs].rearrange("p (h q) -> p h q", h=8),
                             in0=H[:, gs].rearrange("p (h q) -> p h q", h=8),
                             in1=cd_bc[:, g * 8:(g + 1) * 8].unsqueeze(2).to_broadcast([128, 8, 64]), op=ALU.mult)
                    self.vec("dve", "tensor_tensor", [("ps", 7), "H"], ["H"], out=H[:, gs], in0=ps[7][:], in1=H[:, gs],
                             op=ALU.add)
                    self.act(Hb[:, gs], H[:, gs], AF.Copy, ["H"], ["Hb"])
                self.vec("dve", "tensor_tensor", ["yA", "zs"], ["yA"], out=yA[:], in0=yA[:], in1=zs[:], op=ALU.mult)
                for g in range(2):
                    gs = slice(g * 512, (g + 1) * 512)
                    self.act(junk[:, gs], yA[:, gs], AF.Square, ["yA"], ["junk", "ssg"], accum_out=ssg[:, g:g + 1])
                self.vec("dve", "tensor_scalar", ["ssg"], ["ssg"], out=ssg[:, 2:4], in0=ssg[:, 0:2], scalar1=1.0 / 512,
                         scalar2=EPS, op0=ALU.mult, op1=ALU.add)
                self.act(ssg[:, 2:4], ssg[:, 2:4], AF.Sqrt, ["ssg"], ["ssg"])
                self.vec("dve", "reciprocal", ["ssg"], ["ssg"], out=ssg[:, 2:4], in_=ssg[:, 2:4])
                for g in range(2):
                    gs = slice(g * 512, (g + 1) * 512)
                    self.vec("dve", "tensor_scalar", ["yA", "ssg"], ["yn"], out=yn[:, gs], in0=yA[:, gs],
                             scalar1=ssg[:, 2 + g:3 + g], scalar2=None, op0=ALU.mult)
                for cc in range(8):
                    self.tr(pv5[:, cc * 128:(cc + 1) * 128], yn[:, cc * 128:(cc + 1) * 128], ident[:], ["yn", "ident"],
                            [("ps", 5)])
                for cc in range(8):
                    self.act(yT[:, cc, tok], pv5[:, cc * 128:(cc + 1) * 128], AF.Copy, [("ps", 5), "abP"], [("yT", jj)],
                             scale=abP[:, 64 + cc:65 + cc])
            for jj in range(4):
                j = t * 4 + jj
                tok = slice(jj * 128, (jj + 1) * 128)
                for n in range(2):
                    pb = ps[6 + n]
                    for k in range(12):
                        lhs = ypT[:, k, tok] if k < 4 else yT[:, k - 4, tok]
                        rr = [("ypT", k)] if k < 4 else [("yT", jj)]
                        self.mm(pb[:], lhs, w_out[:, k, n * 512:(n + 1) * 512], k == 0, k == 11, rr + ["abwout"],
                                [("ps", 6 + n)])
                    xs_ = xb[:, jj, n * 512:(n + 1) * 512]
                    self.vec("dve", "tensor_tensor", [("ps", 6 + n), "G"], [("tmpo", n)], out=tmp[n][:], in0=pb[:],
                             in1=G[:, n * 512:(n + 1) * 512], op=ALU.mult)
                    self.vec("dve", "tensor_tensor", [("tmpo", n), ("xt", 0, jj)], [("xt", 0, jj)], out=xs_,
                             in0=tmp[n][:], in1=xs_, op=ALU.add)
                self.dma("sp", self.xs_d[j * 128:(j + 1) * 128, :], xb[:, jj, :], [("xt", 0, jj)], [("xd", j)],
                         semkey=("xts", t % 2))
        self.sb_off = keep


    def mixer_mla(self):
        import math
        S = self.S
        ins = self.ins
        l, s = 1, 1
        NT, NB, seq = self.NT, self.NB, self.seq
        keep = self.sb_off
        self.sb_off = self.x_off
        cst, ident, ps = self.cst, self.ident, self.ps
        QN_d = self.dram("QN_d", [1024, seq], BF16)
        QR_d = self.dram("QR_d", [512, seq], BF16)
        KN_d = self.dram("KN_d", [1024, seq], BF16)
        KR_d = self.dram("KR_d", [32, seq], BF16)
        V_d = self.dram("V_d", [seq, 1024], BF16)
        OT_d = self.dram("OT_d", [1024, seq], BF16)
        w_in = self.sb("mwin", [128, KC, IN_MLA], BF16)
        w_uq = self.sb("mwuq", [128, 6, 1536], BF16)
        w_ukv = self.sb("mwukv", [128, 2, 2048], BF16)
        w_o = self.sb("mwo", [128, KC, D], BF16)
        mlaP = self.sb("mlaP", [128, 16], F32)
        G = self.sb("G", [128, D], F32)
        tri = self.sb("tri", [128, 128], BF16)
        xt = self.sb("xt", [128, 4, D], F32)
        self.dma("pool", w_in[:], ins["mlain"].rearrange("(k p) c -> p k c", p=128), [], ["mwin"])
        self.dma("pool", w_uq[:], ins["wuq"].rearrange("(k p) c -> p k c", p=128), [], ["mwuq"])
        self.dma("pool", w_ukv[:], ins["wukv"].rearrange("(k p) c -> p k c", p=128), [], ["mwukv"])
        self.dma("pool", w_o[:], ins["wo"].rearrange("(k p) c -> p k c", p=128), [], ["mwo"])
        self.dma("sp", mlaP[:], ins["mlaP"], [], ["mlaP"])
        self.load_gate(G, l, s)
        self.vec("dve", "tensor_copy", ["cst"], ["tri"], out=tri[:], in_=cst[:, 128:256])
        mA = self.mark()
        hT = self.sb("hT", [128, KC, 512], BF16)
        xn = self.sb("xn", [128, 4, D], BF16)
        junk = self.sb("junk", [128, D], BF16)
        qnT = self.sb("qnT", [128, 6, 512], BF16)
        kvnT = self.sb("kvnT", [128, 2, 512], BF16)
        posi = self.sb("posi", [128, 512], I32)
        ang = self.sb("ang", [128, 512], F32)
        rr = [self.sb("rr%d" % i, [128, 512], F32) for i in range(2)]
        cosT = self.sb("cosT", [128, 512], F32)
        sinT = self.sb("sinT", [128, 512], F32)
        x1s = self.sb("x1s", [128, 512], F32)
        x2s = self.sb("x2s", [128, 512], F32)
        t1 = self.sb("t1", [128, 512], F32)
        t2 = self.sb("t2", [128, 512], F32)
        ro = [self.sb("ro%d" % i, [128, 512], BF16) for i in range(2)]
        qsb = [self.sb("qsb%d" % i, [128, 512], BF16) for i in range(2)]
        vsb = [self.sb("vsb%d" % i, [128, D], BF16) for i in range(2)]
        st = self.sb("mst", [128, 16], F32)
        PI = math.pi
        pib = self.sb("pib", [128, 1], F32)
        self.vec("dve", "memset", [], ["pib"], pib[:], PI)

        def rope(x1ps, x2ps, P, res1, res2, o1, o2, o1res, o2res):
            self.act(x1s[0:P, :], x1ps, AF.Copy, [res1], ["x1s"])
            self.act(x2s[0:P, :], x2ps, AF.Copy, [res2], ["x2s"])
            self.vec("dve", "tensor_tensor", ["x1s", "cosT"], ["t1"], out=t1[0:P, :], in0=x1s[0:P, :], in1=cosT[0:P, :], op=ALU.mult)
            self.vec("dve", "tensor_tensor", ["x2s", "sinT"], ["t2"], out=t2[0:P, :], in0=x2s[0:P, :], in1=sinT[0:P, :], op=ALU.mult)
            self.vec("dve", "tensor_tensor", ["t1", "t2"], [o1res], out=o1, in0=t1[0:P, :], in1=t2[0:P, :], op=ALU.subtract)
            self.vec("dve", "tensor_tensor", ["x2s", "cosT"], ["t1"], out=t1[0:P, :], in0=x2s[0:P, :], in1=cosT[0:P, :], op=ALU.mult)
            self.vec("dve", "tensor_tensor", ["x1s", "sinT"], ["t2"], out=t2[0:P, :], in0=x1s[0:P, :], in1=sinT[0:P, :], op=ALU.mult)
            self.vec("dve", "tensor_tensor", ["t1", "t2"], [o2res], out=o2, in0=t1[0:P, :], in1=t2[0:P, :], op=ALU.add)

        for t in range(NT):
            cols = slice(t * 512, (t + 1) * 512)
            for jj in range(4):
                j = t * 4 + jj
                self.dma("sp", xt[:, jj, :], self.xs_d[j * 128:(j + 1) * 128, :], [("xd", j)], [("xt", jj)], semkey=("xtl", 0))
            self.norm_mod_T(t, l, s, hT, "hT", xn, junk, xsrc=lambda jj: xt[:, jj, :], xr=lambda jj: ("xt", jj))
            self.dma("sp", posi[:], ins["pos"][0:1, cols].partition_broadcast(128), [], ["posi"])
            self.vec("dve", "tensor_copy", ["posi"], ["ang"], out=ang[:], in_=posi[:])
            self.vec("dve", "tensor_scalar", ["ang", "mlaP"], ["ang"], out=ang[:], in0=ang[:], scalar1=mlaP[:, 8:9], scalar2=None,
                     op0=ALU.mult)
            C1 = 6.28125
            C2 = 2 * PI - C1
            for which, (dst, dres) in enumerate(((sinT, "sinT"), (cosT, "cosT"))):
                r = rr[which]
                rres = ("rr", which)
                src = ang
                if which == 1:
                    self.vec("dve", "tensor_scalar", ["ang"], ["t1"], out=t1[:], in0=ang[:], scalar1=PI / 2, scalar2=None, op0=ALU.add)
                    src = t1
                sres = "ang" if which == 0 else "t1"
                self.vec("dve", "tensor_scalar", [sres], ["t2"], out=t2[:], in0=src[:], scalar1=1.0 / (2 * PI), scalar2=None, op0=ALU.mult)
                self.vec("dve", "tensor_copy", ["t2"], ["posi"], out=posi[:], in_=t2[:])
                self.vec("dve", "tensor_copy", ["posi"], ["t2"], out=t2[:], in_=posi[:])
                self.vec("dve", "scalar_tensor_tensor", ["t2", sres], [rres], out=r[:], in0=t2[:], scalar=-C1, in1=src[:],
                         op0=ALU.mult, op1=ALU.add)
                self.vec("dve", "scalar_tensor_tensor", ["t2", rres], [rres], out=r[:], in0=t2[:], scalar=-C2, in1=r[:],
                         op0=ALU.mult, op1=ALU.add)
                self.vec("dve", "tensor_scalar", [rres], ["x1s"], out=x1s[:], in0=r[:], scalar1=PI, scalar2=-2 * PI, op0=ALU.is_gt,
                         op1=ALU.mult)
                self.vec("dve", "tensor_scalar", [rres], ["x2s"], out=x2s[:], in0=r[:], scalar1=-PI, scalar2=2 * PI, op0=ALU.is_lt,
                         op1=ALU.mult)
                self.vec("dve", "tensor_tensor", [rres, "x1s"], [rres], out=r[:], in0=r[:], in1=x1s[:], op=ALU.add)
                self.vec("dve", "tensor_tensor", [rres, "x2s"], [rres], out=r[:], in0=r[:], in1=x2s[:], op=ALU.add)
                self.act(dst[:], r[:], AF.Sin, [rres], [dres])
            for jj in range(4):
                tok = slice(jj * 128, (jj + 1) * 128)
                for n in range(2):
                    for k in range(KC):
                        self.mm(ps[n][:], hT[:, k, tok], w_in[:, k, n * 512:(n + 1) * 512], k == 0, k == KC - 1,
                                ["hT", "mwin"], [("ps", n)])
                self.act(junk[:, 0:512], ps[0][:], AF.Square, [("ps", 0)], ["junk", "mst"], accum_out=st[:, 0:1])
                self.act(junk[:, 512:768], ps[1][:, 0:256], AF.Square, [("ps", 1)], ["junk", "mst"], accum_out=st[:, 1:2])
                self.act(junk[:, 768:1024], ps[1][:, 256:512], AF.Square, [("ps", 1)], ["junk", "mst"], accum_out=st[:, 2:3])
                self.vec("dve", "tensor_tensor", ["mst"], ["mst"], out=st[:, 3:4], in0=st[:, 0:1], in1=st[:, 1:2], op=ALU.add)
                self.vec("dve", "tensor_scalar", ["mst"], ["mst"], out=st[:, 4:5], in0=st[:, 3:4], scalar1=1.0 / 768, scalar2=EPS,
                         op0=ALU.mult, op1=ALU.add)
                self.vec("dve", "tensor_scalar", ["mst"], ["mst"], out=st[:, 5:6], in0=st[:, 2:3], scalar1=1.0 / 256, scalar2=EPS,
                         op0=ALU.mult, op1=ALU.add)
                self.act(st[:, 4:6], st[:, 4:6], AF.Sqrt, ["mst"], ["mst"])
                self.vec("dve", "reciprocal", ["mst"], ["mst"], out=st[:, 4:6], in_=st[:, 4:6])
                self.vec("dve", "tensor_scalar", [("ps", 0), "mst"], [("xn", jj)], out=xn[:, jj, 0:512], in0=ps[0][:],
                         scalar1=st[:, 4:5], scalar2=None, op0=ALU.mult)
                self.vec("dve", "tensor_scalar", [("ps", 1), "mst"], [("xn", jj)], out=xn[:, jj, 512:768], in0=ps[1][:, 0:256],
                         scalar1=st[:, 4:5], scalar2=None, op0=ALU.mult)
                self.vec("dve", "tensor_scalar", [("ps", 1), "mst"], [("xn", jj)], out=xn[:, jj, 768:1024], in0=ps[1][:, 256:512],
                         scalar1=st[:, 5:6], scalar2=None, op0=ALU.mult)
            for k in range(8):
                bank = 4 + (k % 4)
                pv = ps[bank][:].bitcast(BF16)
                for jj in range(4):
                    self.tr(pv[:, jj * 128:(jj + 1) * 128], xn[:, jj, k * 128:(k + 1) * 128], ident[:], [("xn", jj), "ident"],
                            [("ps", bank)])
                dst = qnT[:, k, :] if k < 6 else kvnT[:, k - 6, :]
                self.act(dst, pv[:, 0:512], AF.Copy, [("ps", bank), "mlaP"], ["qnT" if k < 6 else "kvnT"], scale=mlaP[:, k:k + 1])
            for half in range(2):
                for k in range(KC):
                    self.mm(ps[2 + half][0:16, :], w_in[:, k, 1024 + half * 16:1040 + half * 16], hT[:, k, :], k == 0, k == KC - 1,
                            ["mwin", "hT"], [("ps", 2 + half)])
            rope(ps[2][0:16, :], ps[3][0:16, :], 16, ("ps", 2), ("ps", 3), ro[0][0:16, :], ro[1][0:16, :], ("ro", 0), ("ro", 1))
            self.dma("sp", KR_d[0:16, cols], ro[0][0:16, :], [("ro", 0)], [("KR_d", t)], semkey=("scr", 0))
            self.dma("sp", KR_d[16:32, cols], ro[1][0:16, :], [("ro", 1)], [("KR_d", t)], semkey=("scr", 0))
            for c in range(8):
                bk = c % 2
                for k in range(6):
                    self.mm(ps[bk][:], w_uq[:, k, c * 128:(c + 1) * 128], qnT[:, k, :], k == 0, k == 5, ["mwuq", "qnT"], [("ps", bk)])
                self.act(qsb[bk][:], ps[bk][:], AF.Copy, [("ps", bk)], [("qsb", bk)])
                self.dma("sp", QN_d[c * 128:(c + 1) * 128, cols], qsb[bk][:], [("qsb", bk)], [("QN_d", t)], semkey=("scr", 1 + bk))
            for hg in range(2):
                for k in range(6):
                    self.mm(ps[2][:], w_uq[:, k, 1024 + hg * 128:1152 + hg * 128], qnT[:, k, :], k == 0, k == 5, ["mwuq", "qnT"],
                            [("ps", 2)])
                for k in range(6):
                    self.mm(ps[3][:], w_uq[:, k, 1280 + hg * 128:1408 + hg * 128], qnT[:, k, :], k == 0, k == 5, ["mwuq", "qnT"],
                            [("ps", 3)])
                rope(ps[2][:], ps[3][:], 128, ("ps", 2), ("ps", 3), ro[0][:], ro[1][:], ("ro", 0), ("ro", 1))
                self.dma("sp", QR_d[hg * 128:(hg + 1) * 128, cols], ro[0][:], [("ro", 0)], [("QR_d", t)], semkey=("scr", 0))
                self.dma("sp", QR_d[256 + hg * 128:256 + (hg + 1) * 128, cols], ro[1][:], [("ro", 1)], [("QR_d", t)], semkey=("scr", 0))
            for c in range(8):
                bk = c % 2
                for k in range(2):
                    self.mm(ps[bk][:], w_ukv[:, k, c * 128:(c + 1) * 128], kvnT[:, k, :], k == 0, k == 1, ["mwukv", "kvnT"], [("ps", bk)])
                self.vec("dve", "tensor_copy", [("ps", bk)], [("qsb", bk)], out=qsb[bk][:], in_=ps[bk][:])
                self.dma("sp", KN_d[c * 128:(c + 1) * 128, cols], qsb[bk][:], [("qsb", bk)], [("KN_d", t)], semkey=("scr", 1 + bk))
            for jj in range(4):
                j = t * 4 + jj
                tok = slice(jj * 128, (jj + 1) * 128)
                vb = vsb[jj % 2]
                for n in range(2):
                    for k in range(2):
                        self.mm(ps[2 + n][:], kvnT[:, k, tok], w_ukv[:, k, 1024 + n * 512:1536 + n * 512], k == 0, k == 1,
                                ["kvnT", "mwukv"], [("ps", 2 + n)])
                    if n == 0:
                        self.act(vb[:, 0:512], ps[2][:], AF.Copy, [("ps", 2)], [("vsb", jj % 2)])
                    else:
                        self.vec("dve", "tensor_copy", [("ps", 3)], [("vsb", jj % 2)], out=vb[:, 512:1024], in_=ps[3][:])
                self.dma("sp", V_d[j * 128:(j + 1) * 128, :], vb[:], [("vsb", jj % 2)], [("V_d", j)], semkey=("scr", 3 + jj % 2))
        S.barrier()
        self.release(mA)
        KhT = [self.sb("KhT%d" % i, [96, seq], BF16) for i in range(2)]
        QhT = [self.sb("QhT%d" % i, [96, seq], BF16) for i in range(2)]
        V1 = [self.sb("V1_%d" % i, [128, NB, 65], BF16) for i in range(2)]
        PT = [self.sb("PT%d" % i, [128, 512], BF16) for i in range(4)]
        rd = self.sb("rd", [128, 512], F32)
        bcs = self.sb("bcs", [64, 512], F32)
        osb = [self.sb("osb%d" % i, [64, 512], BF16) for i in range(2)]
        for i in range(2):
            self.vec("dve", "memset", [], [("V1", i)], V1[i][:, :, 64:65], 1.0)
        negfill = self.sb("negfill", [128, 256], BF16)
        self.vec("dve", "memset", [], ["negfill"], negfill[:], 0.0)
        scale = 1.0 / math.sqrt(96.0)
        blocks = []
        for h in range(16):
            for qt in range(NT):
                nkb = 4 * (qt + 1)
                for kb in range(nkb):
                    blocks.append((h, qt, kb, nkb))
        loaded_heads = set()

        def load_head(h):
            if h in loaded_heads or h >= 16:
                return
            loaded_heads.add(h)
            hb = h % 2
            self.dma("sp", KhT[hb][0:64, :], KN_d[h * 64:(h + 1) * 64, :], [], [("KhT", hb, 0)], semkey=("ld", hb))
            self.dma("sp", KhT[hb][64:96, :], KR_d[0:32, :], [], [("KhT", hb, 1)], semkey=("ld", hb))
            self.dma("sp", QhT[hb][0:64, :], QN_d[h * 64:(h + 1) * 64, :], [], [("QhT", hb, 0)], semkey=("ld", hb))
            self.dma("sp", QhT[hb][64:80, :], QR_d[h * 16:(h + 1) * 16, :], [], [("QhT", hb, 1)], semkey=("ld", hb))
            self.dma("sp", QhT[hb][80:96, :], QR_d[256 + h * 16:256 + (h + 1) * 16, :], [], [("QhT", hb, 2)], semkey=("ld", hb))
            self.dma("sp", V1[hb][:, :, 0:64], V_d[:, h * 64:(h + 1) * 64].rearrange("(j p) c -> p j c", p=128), [],
                     [("V1", hb)], semkey=("ld", hb))

        def geom(i):
            h, qt, kb, nkb = blocks[i]
            jd = kb - 4 * qt
            c0 = 128 * jd if jd > 0 else 0
            return h, qt, kb, nkb, jd, c0, i % 4

        def stage1(i):
            h, qt, kb, nkb, jd, c0, sbk = geom(i)
            hb = h % 2
            kres = [("KhT", hb, 0), ("KhT", hb, 1)]
            qres = [("QhT", hb, 0), ("QhT", hb, 1), ("QhT", hb, 2)]
            pt = PT[sbk]
            self.mm(ps[sbk][:, c0:512], KhT[hb][:, kb * 128:(kb + 1) * 128], QhT[hb][:, qt * 512 + c0:(qt + 1) * 512],
                    True, True, kres + qres, [("ps", sbk)])
            self.act(pt[:, c0:512], ps[sbk][:, c0:512], AF.Exp, [("ps", sbk)], [("PT", sbk)], scale=scale)
            if jd >= 0:
                self.vec("dve", "tensor_tensor", [("PT", sbk), "tri"], [("PT", sbk)], out=pt[:, c0:c0 + 128],
                         in0=pt[:, c0:c0 + 128], in1=tri[:], op=ALU.mult)

        def stage2(i):
            h, qt, kb, nkb, jd, c0, sbk = geom(i)
            hb = h % 2
            ob = 4 + (qt % 2)
            pt = PT[sbk]
            self.mm(ps[7][:, 0:256], ident[:], negfill[:, 0:256], True, True, ["ident", "negfill"], [("ps", 7)])
            self.mm(ps[ob][0:65, c0:512], V1[hb][:, kb, :], pt[:, c0:512], kb == 0, kb == nkb - 1,
                    [("V1", hb), ("PT", sbk)], [("ps", ob)])
            if kb == nkb - 1:
                self.vec("dve", "reciprocal", [("ps", ob)], ["rd"], out=rd[64:65, :], in_=ps[ob][64:65, :])
                self.mm(ps[6][0:64, :], cst[64:65, 256:320], rd[64:65, :], True, True, ["cst", "rd"], [("ps", 6)])
                self.act(bcs[:], ps[6][0:64, :], AF.Copy, [("ps", 6)], ["bcs"])
                o_ = osb[qt % 2]
                self.vec("dve", "tensor_tensor", [("ps", ob), "bcs"], [("osb", qt % 2)], out=o_[:], in0=ps[ob][0:64, :], in1=bcs[:],
                         op=ALU.mult)
                self.dma("sp", OT_d[h * 64:(h + 1) * 64, qt * 512:(qt + 1) * 512], o_[:], [("osb", qt % 2)], [("OT_d", qt)],
                         semkey=("ost", qt % 2))

        load_head(0)
        load_head(1)
        stage1(0)
        stage1(1)
        for i in range(len(blocks)):
            if i + 2 < len(blocks):
                stage1(i + 2)
            stage2(i)
            if i + 1 == len(blocks) or blocks[i + 1][0] != blocks[i][0]:
                load_head(blocks[i][0] + 2)
        S.barrier()
        self.release(mA)
        oT = [self.sb("oT%d" % i, [128, KC, 512], BF16) for i in range(2)]
        tmp = [self.sb("tmpm%d" % i, [128, 512], F32) for i in range(2)]
        for t in range(NT):
            cols = slice(t * 512, (t + 1) * 512)
            ot = oT[t % 2]
            self.dma("sp", ot[:], OT_d[:, cols].rearrange("(k p) t -> p k t", p=128), [], [("oT", t % 2)], semkey=("otl", t % 2))
            for jj in range(4):
                j = t * 4 + jj
                self.dma("sp", xt[:, jj, :], self.xs_d[j * 128:(j + 1) * 128, :], [("xd", j)], [("xt", jj)], semkey=("xtl", 0))
            for jj in range(4):
                j = t * 4 + jj
                tok = slice(jj * 128, (jj + 1) * 128)
                for n in range(2):
                    pb = ps[n]
                    for k in range(KC):
                        self.mm(pb[:], ot[:, k, tok], w_o[:, k, n * 512:(n + 1) * 512], k == 0, k == KC - 1, [("oT", t % 2), "mwo"],
                                [("ps", n)])
                    xs_ = xt[:, jj, n * 512:(n + 1) * 512]
                    self.vec("dve", "tensor_tensor", [("ps", n), "G"], [("tmpm", n)], out=tmp[n][:], in0=pb[:],
                             in1=G[:, n * 512:(n + 1) * 512], op=ALU.mult)
                    self.vec("dve", "tensor_tensor", [("tmpm", n), ("xt", jj)], [("xt", jj)], out=xs_, in0=tmp[n][:], in1=xs_,
                             op=ALU.add)
                self.dma("sp", self.xs_d[j * 128:(j + 1) * 128, :], xt[:, jj, :], [("xt", jj)], [("xd", j)], semkey=("xts", 0))
        self.sb_off = keep


    def final_norm(self, fin_in):
        m = self.mark()
        Gf = self.sb("Gf", [128, D], F32)
        junk = self.sb("junkf", [128, D], BF16)
        ob = [self.sb("ob%d" % i, [128, D], F32) for i in range(2)]
        self.dma("sp", Gf[:], fin_in.partition_broadcast(128), [], ["Gf"])
        for j in range(self.NB):
            xt = self.xres[:, j, :]
            ss = self.stat[:, 16 + j % 2:17 + j % 2]
            rs = self.stat[:, 18 + j % 2:19 + j % 2]
            o = ob[j % 2]
            self.act(junk[:], xt, AF.Square, [("x", j)], ["junkf", ("fss", j % 2)], accum_out=ss)
            self.vec("dve", "tensor_scalar", [("fss", j % 2)], [("frs", j % 2)], out=rs, in0=ss, scalar1=1.0 / D,
                     scalar2=EPS, op0=ALU.mult, op1=ALU.add)
            self.act(rs, rs, AF.Sqrt, [("frs", j % 2)], [("frs", j % 2)])
            self.vec("dve", "reciprocal", [("frs", j % 2)], [("frs", j % 2)], out=rs, in_=rs)
            self.vec("dve", "scalar_tensor_tensor", [("x", j), ("frs", j % 2), "Gf"], [("ob", j % 2)],
                     out=o[:], in0=xt, scalar=rs, in1=Gf[:], op0=ALU.mult, op1=ALU.mult)
            self.dma("sp", self.y_out[j * 128:(j + 1) * 128, :], o[:], [("ob", j % 2)], [("yout", j)],
                     semkey=("yout", j % 2))
        self.release(m)


def make_consts():
    c = np.zeros((128, 512), np.float32)
    c[:, 0:128] = np.eye(128, dtype=np.float32)
    ii = np.arange(128)
    c[:, 128:256] = (ii[:, None] <= ii[None, :]).astype(np.float32)
    c[:, 256:384] = 1.0
    for g, w in enumerate((2, 4, 8, 16)):
        c[:, 384 + g * 16: 384 + (g + 1) * 16] = 1.0 / np.minimum(np.arange(16) + 1.0, float(w))
    return c


def host_inputs(b, seq, x, c, positions, mod_w, mod_b, norm_g, ffn_w13, ffn_w2, ab_w_in, pool_w, pool_scale,
                ssd_conv_w, ssd_conv_b, ssd_dt_bias, ssd_a_log, ssd_d, ssd_norm_g, ab_w_out, mla_w_in,
                mla_q_norm_g, mla_w_uq, mla_kv_norm_g, mla_w_ukv, mla_w_o, final_norm_g):
    f = np.float32
    m = {}
    m["x"] = np.ascontiguousarray(x[b], dtype=f)
    m["c"] = np.ascontiguousarray(c[b].reshape(KC, 128).T, dtype=f)
    m["pos"] = np.ascontiguousarray(positions[b].reshape(1, seq), dtype=np.int32)
    m["mod_w"] = np.ascontiguousarray(mod_w, dtype=f)
    m["mod_b"] = np.ascontiguousarray(mod_b, dtype=f)
    m["ngP"] = np.ascontiguousarray(norm_g.reshape(2, 3, KC, 128).transpose(3, 0, 1, 2).reshape(128, 48), dtype=f)
    m["ffn_w13"] = np.ascontiguousarray(ffn_w13.reshape(4, D, 2 * DFF), dtype=f)
    m["ffn_w2"] = np.ascontiguousarray(ffn_w2.reshape(4, DFF, D), dtype=f)
    m["ab_w_in"] = np.ascontiguousarray(ab_w_in[0], dtype=f)
    m["pool_w"] = np.ascontiguousarray(pool_w[0].reshape(512, 128), dtype=f)
    abP = np.zeros((128, 80), f)
    abP[:, 0:4] = pool_scale[0].reshape(4, 128).T
    abP[:, 4:52] = ssd_conv_w[0].reshape(4, 12, 128).transpose(2, 1, 0).reshape(128, 48)
    abP[:, 52:64] = ssd_conv_b[0].reshape(12, 128).T
    abP[:, 64:72] = ssd_norm_g[0].reshape(8, 128).T
    m["abP"] = abP
    abR = np.zeros((1, 64), f)
    abR[0, 0:16] = ssd_dt_bias[0]
    abR[0, 16:32] = ssd_a_log[0]
    abR[0, 32:48] = ssd_d[0]
    m["abR"] = abR
    sel = np.zeros((16, 16, 128), f)
    for h in range(16):
        sel[h, h, :] = 1.0
    m["sel"] = sel.reshape(16, 2048)
    ii = np.arange(128)
    negm = np.where(ii[None, :] < ii[:, None], -30000.0, 0.0).astype(f)
    m["cmask"] = np.ascontiguousarray(np.tile(negm, (1, 4)))
    m["ab_w_out"] = np.ascontiguousarray(ab_w_out[0], dtype=f)
    m["mla_w_in"] = np.ascontiguousarray(mla_w_in[0], dtype=f)
    wq = mla_w_uq[0].reshape(768, 16, 96)
    m["mla_w_uq"] = np.ascontiguousarray(np.concatenate(
        [wq[:, :, 0:64].reshape(768, 1024), wq[:, :, 64:80].reshape(768, 256), wq[:, :, 80:96].reshape(768, 256)], axis=1), dtype=f)
    wkv = mla_w_ukv[0].reshape(256, 16, 128)
    m["mla_w_ukv"] = np.ascontiguousarray(np.concatenate(
        [wkv[:, :, 0:64].reshape(256, 1024), wkv[:, :, 64:128].reshape(256, 1024)], axis=1), dtype=f)
    m["mla_w_o"] = np.ascontiguousarray(mla_w_o[0], dtype=f)
    mp = np.zeros((128, 16), f)
    mp[:, 0:6] = mla_q_norm_g[0].reshape(6, 128).T
    mp[:, 6:8] = mla_kv_norm_g[0].reshape(2, 128).T
    mp[:, 8] = np.tile(INV_FREQ, 8)
    m["mlaP"] = mp
    m["final_g"] = np.ascontiguousarray(final_norm_g.reshape(1, D), dtype=f)
    m["consts"] = make_consts()
    return m


_NC_CACHE = {}


def kernel(**inputs):
    inputs = {k: np.asarray(v) for k, v in inputs.items()}
    B, seq, _ = inputs["x"].shape
    key = (seq,)
    if key not in _NC_CACHE:
        _NC_CACHE[key] = Builder(seq).build()
    nc = _NC_CACHE[key]
    in_maps = [host_inputs(b, seq, **inputs) for b in range(B)]
    res = run_bass_kernel_spmd(nc, in_maps, core_ids=list(range(B)))
    return np.stack([np.asarray(r["y"], dtype=np.float32) for r in res.results], axis=0)
```

```python
import numpy as np
import concourse.bass as bass
import concourse.mybir as mybir
from concourse.bass_utils import run_bass_kernel_spmd

F32 = mybir.dt.float32
BF16 = mybir.dt.bfloat16
I32 = mybir.dt.int32
AF = mybir.ActivationFunctionType
ALU = mybir.AluOpType
AX = mybir.AxisListType

D = 1024
KC = 8
DFF = 2816
NF = 22
EPS = 1e-6
SEQ_FULL = 4096
IN_AB = 3088
IN_MLA = 1056

COMPUTE = ("pe", "act", "dve", "pool")
INV_FREQ = (np.float32(10000.0) ** (-np.arange(0, 32, 2, dtype=np.float32) / np.float32(32))).astype(np.float32)


class Op:
    __slots__ = ("q", "fn", "reads", "writes", "dma", "semkey", "deps", "signal", "idx", "need_sig", "bar_dma")


class Sched:
    def __init__(self, nc):
        self.nc = nc
        self.ops = []
        self.last_w = {}
        self.readers = {}
        self.sems = {}
        self.dma_count = {}
        self.bar_deps = []
        self.bar_dma = {}

    def op(self, q, fn, reads=(), writes=(), dma=False, semkey=None):
        o = Op()
        o.q, o.fn, o.reads, o.writes, o.dma = q, fn, tuple(reads), tuple(writes), dma
        o.idx = len(self.ops)
        o.need_sig = False
        o.bar_dma = self.bar_dma
        deps = set(self.bar_deps)
        for r in o.reads:
            w = self.last_w.get(r)
            if w is not None:
                deps.add(w)
        for w_ in o.writes:
            w = self.last_w.get(w_)
            if w is not None:
                deps.add(w)
            for rd in self.readers.get(w_, ()):
                deps.add(rd)
        deps.discard(o.idx)
        o.deps = deps
        if dma:
            if semkey is None:
                semkey = ("dma",) + tuple(o.writes[:1])
            o.semkey = semkey
            n = self.dma_count.get(semkey, 0) + 1
            self.dma_count[semkey] = n
            o.signal = (semkey, 16 * n)
        else:
            o.semkey = q
            o.signal = None
        for r in o.reads:
            self.readers.setdefault(r, []).append(o.idx)
        for w_ in o.writes:
            self.last_w[w_] = o.idx
            self.readers[w_] = []
        self.ops.append(o)
        return o

    def barrier(self):
        lastq = {}
        for o in self.ops:
            if (not o.dma) and o.fn is not None:
                lastq[o.q] = o.idx
        self.bar_deps = list(lastq.values())
        self.bar_dma = {k: 16 * n for k, n in self.dma_count.items()}
        self.last_w = {}
        self.readers = {}

    def emit(self):
        nc = self.nc
        ops = self.ops
        for o in ops:
            for d in o.deps:
                p = ops[d]
                if p.q == "pe" and o.q == "pe" and not p.dma:
                    continue
                p.need_sig = True
        cnt = {q: 0 for q in COMPUTE}
        for o in ops:
            if not o.dma and o.fn is not None and o.need_sig:
                cnt[o.q] += 1
                o.signal = (o.q, cnt[o.q])
        def sem(key):
            s = self.sems.get(key)
            if s is None:
                s = nc.alloc_semaphore("s%d" % len(self.sems))
                self.sems[key] = s
            return s
        queues = {}
        for o in ops:
            queues.setdefault(o.q, []).append(o)
        dma_before = {}
        run = {}
        for o in ops:
            dma_before[o.idx] = dict(run) if False else None
        dma_positions = {}
        for o in ops:
            if o.dma:
                dma_positions.setdefault(o.semkey, []).append(o.idx)
        import bisect

        def emit_queue(qname, eng):
            waited = {}
            for o in queues.get(qname, []):
                need = dict(o.bar_dma)
                for d in o.deps:
                    p = ops[d]
                    if p.dma:
                        pos = dma_positions[p.semkey]
                        n = bisect.bisect_left(pos, o.idx)
                        key, val = p.semkey, 16 * n
                    else:
                        if p.fn is None:
                            continue
                        if p.q == "pe" and o.q == "pe" and not o.dma:
                            continue
                        key, val = p.signal
                    if need.get(key, 0) < val:
                        need[key] = val
                for key, val in need.items():
                    if waited.get(key, 0) >= val:
                        continue
                    waited[key] = val
                    eng.wait_ge(sem(key), val)
                if o.fn is None:
                    continue
                ins = o.fn(eng)
                if o.dma:
                    ins.then_inc(sem(o.semkey), 16)
                elif o.need_sig:
                    ins.then_inc(sem(o.q), 1)

        with nc.Block() as block:
            @block.tensor
            def _(e):
                emit_queue("pe", e)

            @block.scalar
            def _(e):
                emit_queue("act", e)

            @block.vector
            def _(e):
                emit_queue("dve", e)

            @block.gpsimd
            def _(e):
                emit_queue("pool", e)

            @block.sync
            def _(e):
                emit_queue("sp", e)


class Builder:
    def __init__(self, seq, subs=None, debug_out=None):
        self.seq = seq
        self.NT = seq // 512
        self.NB = seq // 128
        self.subs = subs
        nc = bass.Bass("TRN2", target_bir_lowering=False)
        self.nc = nc
        self.S = Sched(nc)
        self.sb_off = 16640
        self.sb_top = 229376
        self.nalloc = 0

    def sb(self, name, shape, dtype):
        esz = 4 if dtype in (F32, I32) else 2
        n = 1
        for s in shape[1:]:
            n *= s
        nbytes = (n * esz + 63) // 64 * 64
        off = self.sb_off
        assert off + nbytes <= self.sb_top, "SBUF overflow at %s: need %d have %d" % (name, nbytes, self.sb_top - off)
        self.sb_off += nbytes
        self.nalloc += 1
        return self.nc.alloc_sbuf_tensor_at("%s_%d" % (name, self.nalloc), list(shape), dtype, offset=off)

    def mark(self):
        return self.sb_off

    def release(self, m):
        self.sb_off = m

    def dram(self, name, shape, dtype, kind="Internal"):
        return self.nc.dram_tensor(name, list(shape), dtype, kind=kind).ap()

    def mm(self, out, lhsT, rhs, start, stop, reads, writes):
        self.S.op("pe", lambda e: e.matmul(out, lhsT, rhs, start=start, stop=stop), reads, writes)

    def tr(self, out, in_, ident, reads, writes):
        self.S.op("pe", lambda e: e.transpose(out, in_, ident), reads, writes)

    def act(self, out, in_, func, reads, writes, bias=None, scale=None, accum_out=None):
        kw = {}
        if bias is not None:
            kw["bias"] = bias
        if scale is not None:
            kw["scale"] = scale
        if accum_out is not None:
            kw["accum_out"] = accum_out
        self.S.op("act", lambda e: e.activation(out=out, in_=in_, func=func, **kw), reads, writes)

    def vec(self, q, method, reads, writes, *args, **kw):
        self.S.op(q, lambda e: getattr(e, method)(*args, **kw), reads, writes)

    def dma(self, q, out, in_, reads, writes, semkey=None):
        self.S.op(q, lambda e: e.dma_start(out=out, in_=in_), reads, writes, dma=True, semkey=semkey)

    def build(self):
        nc = self.nc
        seq, NT, NB = self.seq, self.NT, self.NB
        S = self.S
        x_in = self.dram("x", [seq, D], F32, "ExternalInput")
        c_in = self.dram("c", [128, KC], F32, "ExternalInput")
        pos_in = self.dram("pos", [1, seq], I32, "ExternalInput")
        mod_w = self.dram("mod_w", [2, D, 9 * D], F32, "ExternalInput")
        mod_b = self.dram("mod_b", [2, 9 * D], F32, "ExternalInput")
        ngP_in = self.dram("ngP", [128, 2 * 3 * KC], F32, "ExternalInput")
        w13_in = self.dram("ffn_w13", [4, D, 2 * DFF], F32, "ExternalInput")
        w2_in = self.dram("ffn_w2", [4, DFF, D], F32, "ExternalInput")
        abin_in = self.dram("ab_w_in", [D, IN_AB], F32, "ExternalInput")
        poolw_in = self.dram("pool_w", [512, 128], F32, "ExternalInput")
        abP_in = self.dram("abP", [128, 80], F32, "ExternalInput")
        abR_in = self.dram("abR", [1, 64], F32, "ExternalInput")
        about_in = self.dram("ab_w_out", [1536, D], F32, "ExternalInput")
        mlain_in = self.dram("mla_w_in", [D, IN_MLA], F32, "ExternalInput")
        wuq_in = self.dram("mla_w_uq", [768, 1536], F32, "ExternalInput")
        wukv_in = self.dram("mla_w_ukv", [256, 2048], F32, "ExternalInput")
        wo_in = self.dram("mla_w_o", [D, D], F32, "ExternalInput")
        mlaP_in = self.dram("mlaP", [128, 16], F32, "ExternalInput")
        fin_in = self.dram("final_g", [1, D], F32, "ExternalInput")
        cst_in = self.dram("consts", [128, 512], F32, "ExternalInput")
        sel_in = self.dram("sel", [16, 2048], F32, "ExternalInput")
        cmask_in = self.dram("cmask", [128, 512], F32, "ExternalInput")
        self.xs_d = self.dram("xs_d", [seq, D], F32)
        self.ins = dict(abin=abin_in, poolw=poolw_in, abP=abP_in, abR=abR_in, about=about_in, mlain=mlain_in,
                        wuq=wuq_in, wukv=wukv_in, wo=wo_in, mlaP=mlaP_in, sel=sel_in, cmask=cmask_in, pos=pos_in)
        y_out = self.dram("y", [seq, D], F32, "ExternalOutput")
        self.x_in, self.y_out = x_in, y_out

        w13_s = self.dram("w13_s", [4, D, 2 * DFF], BF16)
        w2_s = self.dram("w2_s", [4, DFF, D], BF16)
        mod_d = self.dram("mod_d", [2, 9 * D], F32)
        self.w13_s, self.w2_s, self.mod_d = w13_s, w2_s, mod_d

        self.ident = self.sb("ident", [128, 128], BF16)
        self.cst = self.sb("cst", [128, 512], F32)
        self.modP = self.sb("modP", [128, 2, 9, KC], F32)
        self.ngP = self.sb("ngP", [128, 2, 3, KC], F32)
        self.aP = self.sb("aP", [128, 2, 3, KC], F32)
        self.stat = self.sb("stat", [128, 64], F32)
        self.x_off = self.sb_off
        self.xres = self.sb("xres", [128, max(NB, 32), D], F32)
        self.ps = [nc.alloc_psum_tensor("ps%d" % i, [128, 512], F32) for i in range(8)]

        self.dma("sp", self.cst[:], cst_in, ["cst_in"], ["cst"])
        self.vec("dve", "tensor_copy", ["cst"], ["ident"], out=self.ident[:], in_=self.cst[:, 0:128])
        self.dma("sp", self.ngP[:].rearrange("p a b c -> p (a b c)"), ngP_in, [], ["ngP"])
        for j in range(NB):
            self.dma("sp", self.xres[:, j, :], x_in[j * 128:(j + 1) * 128, :], [], [("x", j)], semkey=("xload", j % 4))
        self.precast(w13_s[0], w13_in[0], D, 2 * DFF, ("w13s", 0))
        self.precast(w2_s[0], w2_in[0], DFF, D, ("w2s", 0))
        self.modulation(c_in, mod_w, mod_b)
        for i in range(1, 4):
            self.precast(w13_s[i], w13_in[i], D, 2 * DFF, ("w13s", i))
            self.precast(w2_s[i], w2_in[i], DFF, D, ("w2s", i))

        def want(l, s):
            return self.subs is None or (l, s) in self.subs

        for l in range(2):
            if want(l, 0):
                self.ffn(l, 0)
            if want(l, 1):
                self.spill_x()
                if l == 0:
                    self.mixer_ab()
                else:
                    self.mixer_mla()
                self.reload_x()
            if want(l, 2):
                self.ffn(l, 1)
        self.final_norm(fin_in)
        S.op("sp", None, reads=[("yout", j) for j in range(NB)])
        S.emit()
        return nc

    def precast(self, dst, src, rows, cols, res):
        a = rows // 128
        half = max(1, a // 2)
        for h0 in range(0, a, half):
            h1 = min(a, h0 + half)
            d = dst.rearrange("(p a) c -> p a c", p=128)[:, h0:h1, :]
            s = src.rearrange("(p a) c -> p a c", p=128)[:, h0:h1, :]
            self.dma("pool", d, s, [], [res], semkey=("pc",) + tuple(res))

    def modulation(self, c_in, mod_w, mod_b):
        m = self.mark()
        cact = self.sb("cact", [128, KC], F32)
        mrow = self.sb("mrow", [1, 9 * D], F32)
        modT = self.sb("modT", [128, 128], F32)
        wst = [self.sb("modw%d" % i, [128, KC, 512], F32) for i in range(2)]
        self.dma("sp", cact[:], c_in, [], ["cact"])
        self.act(cact[:], cact[:], AF.Silu, ["cact"], ["cact"])
        for l in range(2):
            self.dma("sp", mrow[:], mod_b[l:l + 1, :], [], ["mrow"])
            for cb in range(18):
                slot = (l * 18 + cb) % 2
                self.dma("sp", wst[slot][:], mod_w[l, :, cb * 512:(cb + 1) * 512].rearrange("(k p) c -> p k c", p=128),
                         [], [("modw", slot)])
                pst = self.ps[cb % 2]
                for k in range(KC):
                    self.mm(pst[0:1, :], cact[:, k:k + 1], wst[slot][:, k, :], k == 0, k == KC - 1,
                            ["cact", ("modw", slot)], [("ps", cb % 2)])
                self.vec("dve", "tensor_tensor", [("ps", cb % 2), "mrow"], ["mrow"],
                         out=mrow[0:1, cb * 512:(cb + 1) * 512], in0=pst[0:1, :], in1=mrow[0:1, cb * 512:(cb + 1) * 512],
                         op=ALU.add)
            self.dma("sp", self.mod_d[l:l + 1, :], mrow[:], ["mrow"], [("mod_d", l)])
            self.dma("sp", modT[0:72, :], self.mod_d[l, :].rearrange("(r p) -> r p", p=128), [("mod_d", l)], ["modT"])
            self.tr(self.ps[2][:, 0:72], modT[0:72, :], self.cst[0:72, 0:72], ["modT", "cst"], [("ps", 2)])
            self.vec("dve", "tensor_copy", [("ps", 2)], ["modP"],
                     out=self.modP[:, l, :, :].rearrange("p j k -> p (j k)"), in_=self.ps[2][:, 0:72])
        for l in range(2):
            for s in range(3):
                self.vec("dve", "scalar_tensor_tensor", ["modP", "ngP"], ["aP"],
                         out=self.aP[:, l, s, :], in0=self.modP[:, l, 3 * s + 1, :], scalar=1.0,
                         in1=self.ngP[:, l, s, :], op0=ALU.add, op1=ALU.mult)
        self.S.barrier()
        self.release(m)

    def norm_mod_T(self, t, l, s, hT, hres, xn, junk, xsrc=None, xr=None):
        self.norm_stats(t, xn, junk, xsrc, xr)
        self.norm_transpose(l, s, hT, hres, xn, 4)

    def norm_stats(self, t, xn, junk, xsrc=None, xr=None):
        xts, xrs = [], []
        for jj in range(4):
            j = t * 4 + jj
            xt = self.xres[:, j, :] if xsrc is None else xsrc(jj)
            xres_ = ("x", j) if xr is None else xr(jj)
            xts.append(xt)
            xrs.append(xres_)
            if junk is None:
                self.act(xn[:, jj, :], xt, AF.Square, [xres_], [("xn", jj), "ss"], accum_out=self.stat[:, jj:jj + 1])
            else:
                self.act(junk[:], xt, AF.Square, [xres_], ["junk", "ss"], accum_out=self.stat[:, jj:jj + 1])
        rs4 = self.stat[:, 8:12]
        self.vec("dve", "tensor_scalar", ["ss"], ["rs"], out=rs4, in0=self.stat[:, 0:4], scalar1=1.0 / D, scalar2=EPS,
                 op0=ALU.mult, op1=ALU.add)
        self.act(rs4, rs4, AF.Sqrt, ["rs"], ["rs"])
        self.vec("dve", "reciprocal", ["rs"], ["rs"], out=rs4, in_=rs4)
        for jj in range(4):
            self.vec("dve", "tensor_scalar", [xrs[jj], "rs"], [("xn", jj)], out=xn[:, jj, :], in0=xts[jj],
                     scalar1=self.stat[:, 8 + jj:9 + jj], scalar2=None, op0=ALU.mult)

    def norm_transpose(self, l, s, hT, hres, xn, bank0):
        for k in range(KC):
            bank = bank0 + (k % 4)
            pv = self.ps[bank][:].bitcast(BF16)
            for jj in range(4):
                self.tr(pv[:, jj * 128:(jj + 1) * 128], xn[:, jj, k * 128:(k + 1) * 128], self.ident[:],
                        [("xn", jj), "ident"], [("ps", bank)])
            self.act(hT[:, k, :], pv[:, 0:512], AF.Identity, [("ps", bank), "aP", "modP"], [hres],
                     scale=self.aP[:, l, s, k:k + 1], bias=self.modP[:, l, 3 * s, k:k + 1])

    def load_gate(self, G, l, s):
        self.dma("sp", G[:], self.mod_d[l:l + 1, (3 * s + 2) * D:(3 * s + 3) * D].partition_broadcast(128),
                 [("mod_d", l)], ["G"])

    def ffn(self, l, s2):
        S = self.S
        fi = l * 2 + s2
        s = 0 if s2 == 0 else 2
        NT = self.NT
        m = self.mark()
        hT = self.sb("hT", [128, KC, 512], BF16)
        gT = self.sb("gT", [128, NF, 512], BF16)
        xn = self.sb("xn", [128, 4, D], BF16)
        G = self.sb("G", [128, D], F32)
        sg = [self.sb("sg%d" % i, [128, 512], F32) for i in range(2)]
        NSL = 3
        ring = [self.sb("ring%d" % i, [128, 4096], BF16) for i in range(NSL)]
        self.load_gate(G, l, s)
        w2grp = [(0, 8), (8, 8), (16, 6)]
        pieces = []
        for t in range(NT):
            for pc in range(11):
                pieces.append(("w13", pc))
            for n in range(2):
                for g in range(3):
                    pieces.append(("w2", n, g))
        issued = [0]

        def issue():
            gi = issued[0]
            p = pieces[gi]
            sl = gi % NSL
            issued[0] += 1
            if p[0] == "w13":
                pc = p[1]
                v = ring[sl][:].rearrange("p (k a c) -> p k a c", k=KC, a=2)
                for ab in range(2):
                    src = self.w13_s[fi, :, ab * DFF + pc * 256: ab * DFF + (pc + 1) * 256]
                    self.dma("sp", v[:, :, ab, :], src.rearrange("(k p) c -> p k c", p=128),
                             [("w13s", fi)], [("ring", sl, ab)])
            else:
                n, g = p[1], p[2]
                f0, nf = w2grp[g]
                v = ring[sl][:].rearrange("p (f c) -> p f c", f=8)
                src = self.w2_s[fi, f0 * 128:(f0 + nf) * 128, n * 512:(n + 1) * 512]
                self.dma("sp", v[:, 0:nf, :], src.rearrange("(f p) c -> p f c", p=128), [("w2s", fi)],
                         [("ring", sl, 0), ("ring", sl, 1)])

        def ensure(gi):
            while issued[0] < len(pieces) and issued[0] < gi + NSL:
                issue()
            return gi % NSL

        pi = 0
        self.norm_stats(0, xn, None)
        for t in range(NT):
            self.norm_transpose(l, s, hT, "hT", xn, 4)
            for pc in range(11):
                sl = ensure(pi)
                pi += 1
                wv = ring[sl][:].rearrange("p (k a c) -> p k a c", k=KC, a=2)
                for ff in range(2):
                    f = pc * 2 + ff
                    pa, pb = self.ps[ff * 2], self.ps[ff * 2 + 1]
                    for k in range(KC):
                        self.mm(pa[:], wv[:, k, 0, ff * 128:(ff + 1) * 128], hT[:, k, :], k == 0, k == KC - 1,
                                [("ring", sl, 0), "hT"], [("ps", ff * 2)])
                    for k in range(KC):
                        self.mm(pb[:], wv[:, k, 1, ff * 128:(ff + 1) * 128], hT[:, k, :], k == 0, k == KC - 1,
                                [("ring", sl, 1), "hT"], [("ps", ff * 2 + 1)])
                    self.act(sg[ff][:], pa[:], AF.Silu, [("ps", ff * 2)], [("sg", ff)])
                    self.vec("dve", "tensor_tensor", [("sg", ff), ("ps", ff * 2 + 1)], [("gT", f)],
                             out=gT[:, f, :], in0=sg[ff][:], in1=pb[:], op=ALU.mult)
            for n in range(2):
                if n == 1 and t + 1 < NT:
                    self.norm_stats(t + 1, xn, None)
                for g in range(3):
                    sl = ensure(pi)
                    pi += 1
                    f0, nf = w2grp[g]
                    wv = ring[sl][:].rearrange("p (f c) -> p f c", f=8)
                    for fl in range(nf):
                        f = f0 + fl
                        for jj in range(4):
                            self.mm(self.ps[4 + jj][:], gT[:, f, jj * 128:(jj + 1) * 128], wv[:, fl, :],
                                    f == 0, f == NF - 1, [("gT", f), ("ring", sl, 0), ("ring", sl, 1)], [("ps", 4 + jj)])
                for jj in range(4):
                    j = t * 4 + jj
                    xs = self.xres[:, j, n * 512:(n + 1) * 512]
                    tb = sg[jj % 2]
                    self.vec("dve", "tensor_tensor", [("ps", 4 + jj), "G"], [("sg", jj % 2)],
                             out=tb[:], in0=self.ps[4 + jj][:], in1=G[:, n * 512:(n + 1) * 512], op=ALU.mult)
                    self.vec("dve", "scalar_tensor_tensor", [("sg", jj % 2), ("x", j)], [("x", j)],
                             out=xs, in0=tb[:], scalar=0.5, in1=xs, op0=ALU.mult, op1=ALU.add)
        S.barrier()
        self.release(m)

    def spill_x(self):
        for j in range(self.NB):
            self.dma("sp", self.xs_d[j * 128:(j + 1) * 128, :], self.xres[:, j, :], [("x", j)], [("xd", j)],
                     semkey=("xsp", j % 4))
        self.S.barrier()

    def reload_x(self):
        self.S.barrier()
        for j in range(self.NB):
            self.dma("sp", self.xres[:, j, :], self.xs_d[j * 128:(j + 1) * 128, :], [("xd", j)], [("x", j)],
                     semkey=("xload", j % 4))

    def mixer_ab(self):
        S = self.S
        ins = self.ins
        l, s = 0, 1
        NT = self.NT
        keep = self.sb_off
        self.sb_off = self.x_off
        cst = self.cst
        w_in = self.sb("abwin", [128, KC, IN_AB], BF16)
        w_out = self.sb("abwout", [128, 12, D], BF16)
        pw = self.sb("poolw", [128, 4, 128], BF16)
        abP = self.sb("abP", [128, 80], F32)
        rowp = self.sb("rowp", [128, 48], F32)
        a_bc = self.sb("a_bc", [128, 16], F32)
        sel = self.sb("sel", [16, 16, 128], BF16)
        negm = self.sb("negm", [128, 512], BF16)
        G = self.sb("G", [128, D], F32)
        hT = self.sb("hT", [128, KC, 512], BF16)
        xn = self.sb("xn", [128, 4, D], BF16)
        junk = self.sb("junk", [128, D], BF16)
        xt = [self.sb("xt0", [128, 4, D], F32)] * 2
        Up = self.sb("Up", [128, 528], F32)
        sA = self.sb("sA", [128, 528], F32)
        sB = self.sb("sB", [128, 528], F32)
        phalo = self.sb("phalo", [128, 4, 16], F32)
        dT = [self.sb("dT%d" % i, [128, 512], BF16) for i in range(2)]
        dfix = self.sb("dfix", [128, 16], F32)
        ypT = self.sb("ypT", [128, 4, 512], BF16)
        Uc = [self.sb("Uc0", [128, 515], F32)] * 2
        chalo = self.sb("chalo", [128, 12, 3], F32)
        acc = [self.sb("acc0", [128, 512], F32)] * 2
        xbcT = self.sb("xbcT", [128, 12, 512], BF16)
        zs = self.sb("zs", [128, D], F32)
        dtt = self.sb("dtt", [128, 96], F32)
        sm2 = self.sb("sm2", [128, 48], F32)
        acsT = self.sb("acsT", [16, 3, 128], F32)
        acsHL = self.sb("acsHL", [16, 2, 128], BF16)
        xs_tok = self.sb("xs_tok", [128, D], BF16)
        xw = self.sb("xw", [128, D], BF16)
        xsD = self.sb("xsD", [128, D], BF16)
        Btok = self.sb("Btok", [128, 2, 128], BF16)
        cb = self.sb("cb", [128, 2, 128], F32)
        Eexp = [self.sb("Eexp%d" % i, [128, 512], F32) for i in range(2)]
        Mt = [self.sb("Mt%d" % i, [128, 512], BF16) for i in range(2)]
        H = self.sb("H", [128, D], F32)
        Hb = self.sb("Hb", [128, D], BF16)
        yA = self.sb("yA", [128, D], F32)
        ssg = self.sb("ssg", [128, 4], F32)
        yn = self.sb("yn", [128, D], BF16)
        yT = self.sb("yT", [128, KC, 512], BF16)
        tmp = [self.sb("tmpo%d" % i, [128, 512], F32) for i in range(2)]
        ps = self.ps
        ident = self.ident

        self.dma("pool", w_in[:], ins["abin"].rearrange("(k p) c -> p k c", p=128), [], ["abwin"])
        self.dma("pool", w_out[:], ins["about"].rearrange("(k p) c -> p k c", p=128), [], ["abwout"])
        self.dma("pool", pw[:], ins["poolw"].rearrange("(g p) c -> p g c", p=128), [], ["poolw"])
        self.dma("pool", sel[:].rearrange("p a b -> p (a b)"), ins["sel"], [], ["sel"])
        self.dma("pool", negm[:], ins["cmask"], [], ["negm"])
        self.dma("sp", abP[:], ins["abP"], [], ["abP"])
        self.dma("sp", rowp[:], ins["abR"][0:1, 0:48].partition_broadcast(128), [], ["rowp"])
        self.act(a_bc[:], rowp[:, 16:32], AF.Exp, ["rowp"], ["a_bc"])
        self.vec("dve", "tensor_scalar", ["a_bc"], ["a_bc"], out=a_bc[:], in0=a_bc[:], scalar1=-1.0, scalar2=None,
                 op0=ALU.mult)
        self.load_gate(G, l, s)
        self.vec("dve", "memset", [], ["phalo"], phalo[:], 0.0)
        self.vec("dve", "memset", [], ["chalo"], chalo[:], 0.0)
        self.vec("dve", "memset", [], ["H"], H[:], 0.0)
        self.vec("dve", "memset", [], ["Hb"], Hb[:], 0.0)
        D_bc = rowp[:, 32:48]
        dtb_bc = rowp[:, 0:16]
        wins = (2, 4, 8, 16)

        for t in range(NT):
            xb = xt[t % 2]
            for jj in range(4):
                j = t * 4 + jj
                self.dma("sp", xb[:, jj, :], self.xs_d[j * 128:(j + 1) * 128, :], [("xd", j)], [("xt", 0, jj)],
                         semkey=("xtl", t % 2))
            self.norm_mod_T(t, l, s, hT, "hT", xn, junk, xsrc=lambda jj: xb[:, jj, :],
                            xr=lambda jj: ("xt", 0, jj))
            for g in range(4):
                bk = g % 2
                for k in range(KC):
                    self.mm(ps[bk][:], w_in[:, k, g * 128:(g + 1) * 128], hT[:, k, :], k == 0, k == KC - 1,
                            ["abwin", "hT"], [("ps", bk)])
                self.vec("dve", "tensor_copy", ["phalo"], ["Up"], out=Up[:, 0:16], in_=phalo[:, g, :])
                self.act(Up[:, 16:528], ps[bk][:], AF.Copy, [("ps", bk)], ["Up"])
                self.vec("dve", "tensor_copy", ["Up"], ["phalo"], out=phalo[:, g, :], in_=Up[:, 512:528])
                self.vec("dve", "tensor_tensor", ["Up"], ["sA"], out=sA[:, 1:528], in0=Up[:, 1:528], in1=Up[:, 0:527],
                         op=ALU.add)
                lvl = sA
                if g >= 1:
                    self.vec("dve", "tensor_tensor", ["sA"], ["sB"], out=sB[:, 3:528], in0=sA[:, 3:528],
                             in1=sA[:, 1:526], op=ALU.add)
                    lvl = sB
                if g >= 2:
                    self.vec("dve", "tensor_tensor", ["sB"], ["sA"], out=sA[:, 7:528], in0=sB[:, 7:528],
                             in1=sB[:, 3:524], op=ALU.add)
                    lvl = sA
                if g >= 3:
                    self.vec("dve", "tensor_tensor", ["sA"], ["sB"], out=sB[:, 15:528], in0=sA[:, 15:528],
                             in1=sA[:, 7:520], op=ALU.add)
                    lvl = sB
                lres = "sA" if lvl is sA else "sB"
                dd = dT[g % 2]
                self.vec("dve", "scalar_tensor_tensor", [lres, "Up"], [("dT", g % 2)], out=dd[:], in0=lvl[:, 16:528],
                         scalar=1.0 / wins[g], in1=Up[:, 16:528], op0=ALU.mult, op1=ALU.subtract)
                if t == 0:
                    self.vec("dve", "tensor_tensor", [lres, "cst"], ["dfix"], out=dfix[:], in0=lvl[:, 16:32],
                             in1=cst[:, 384 + g * 16:384 + (g + 1) * 16], op=ALU.mult)
                    self.vec("dve", "tensor_tensor", ["dfix", "Up"], [("dT", g % 2)], out=dd[:, 0:16], in0=dfix[:],
                             in1=Up[:, 16:32], op=ALU.subtract)
                self.mm(ps[2 + bk][:], pw[:, g, :], dd[:], True, True, ["poolw", ("dT", g % 2)], [("ps", 2 + bk)])
                self.act(ypT[:, g, :], ps[2 + bk][:], AF.Copy, [("ps", 2 + bk), "abP"], [("ypT", g)],
                         scale=abP[:, g:g + 1])
            for cc in range(12):
                bk = cc % 2
                c0 = 1536 + cc * 128
                for k in range(KC):
                    self.mm(ps[bk][:], w_in[:, k, c0:c0 + 128], hT[:, k, :], k == 0, k == KC - 1,
                            ["abwin", "hT"], [("ps", bk)])
                U = Uc[bk]
                ur = ("Uc", 0)
                self.vec("dve", "tensor_copy", ["chalo"], [ur], out=U[:, 0:3], in_=chalo[:, cc, :])
                self.act(U[:, 3:515], ps[bk][:], AF.Copy, [("ps", bk)], [ur])
                self.vec("dve", "tensor_copy", [ur], ["chalo"], out=chalo[:, cc, :], in_=U[:, 512:515])
                a_ = acc[bk]
                ar = ("acc", 0)
                self.vec("dve", "tensor_scalar", [ur, "abP"], [ar], out=a_[:], in0=U[:, 0:512],
                         scalar1=abP[:, 4 + cc * 4:5 + cc * 4], scalar2=abP[:, 52 + cc:53 + cc], op0=ALU.mult, op1=ALU.add)
                for kk in range(1, 4):
                    self.vec("dve", "scalar_tensor_tensor", [ur, "abP", ar], [ar], out=a_[:], in0=U[:, kk:kk + 512],
                             scalar=abP[:, 4 + cc * 4 + kk:5 + cc * 4 + kk], in1=a_[:], op0=ALU.mult, op1=ALU.add)
                self.act(xbcT[:, cc, :], a_[:], AF.Silu, [ar], [("xbcT", cc)])
            for jj in range(4):
                j = t * 4 + jj
                tok = slice(jj * 128, (jj + 1) * 128)
                for n in range(2):
                    for k in range(KC):
                        self.mm(ps[2 + n][:], hT[:, k, tok], w_in[:, k, 512 + n * 512:1024 + n * 512], k == 0,
                                k == KC - 1, ["hT", "abwin"], [("ps", 2 + n)])
                    self.act(zs[:, n * 512:(n + 1) * 512], ps[2 + n][:], AF.Silu, [("ps", 2 + n)], ["zs"])
                for k in range(KC):
                    self.mm(ps[4][:, 0:16], hT[:, k, tok], w_in[:, k, 3072:3088], k == 0, k == KC - 1,
                            ["hT", "abwin"], [("ps", 4)])
                dt_, lndt, adt, nb, wst, eacs = (dtt[:, i * 16:(i + 1) * 16] for i in range(6))
                cd_bc, acs_sb, arg = (sm2[:, i * 16:(i + 1) * 16] for i in range(3))
                self.vec("dve", "tensor_tensor", [("ps", 4), "rowp"], ["dt"], out=dt_, in0=ps[4][:, 0:16], in1=dtb_bc,
                         op=ALU.add)
                self.act(dt_, dt_, AF.Exp, ["dt"], ["dt"])
                self.act(dt_, dt_, AF.Ln, ["dt"], ["dt"], bias=1.0)
                self.act(lndt, dt_, AF.Ln, ["dt"], ["lndt"])
                self.vec("dve", "tensor_tensor", ["dt", "a_bc"], ["adt"], out=adt, in0=dt_, in1=a_bc[:], op=ALU.mult)
                self.mm(ps[4][:, 16:32], cst[:, 128:256], adt, True, True, ["cst", "adt"], [("ps", 4)])
                self.mm(ps[4][0:16, 128:256], adt, cst[:, 128:256], True, True, ["cst", "adt"], [("ps", 4)])
                self.mm(ps[4][:, 32:48], cst[:, 256:384], adt, True, True, ["cst", "adt"], [("ps", 4)])
                self.vec("dve", "tensor_copy", [("ps", 4)], ["acs_sb"], out=acs_sb, in_=ps[4][:, 16:32])
                self.vec("dve", "tensor_tensor", ["lndt", "acs_sb"], ["nb"], out=nb, in0=lndt, in1=acs_sb, op=ALU.subtract)
                self.act(eacs, ps[4][:, 16:32], AF.Exp, [("ps", 4)], ["eacs"])
                self.vec("dve", "tensor_tensor", [("ps", 4), "nb"], ["arg"], out=arg, in0=ps[4][:, 32:48], in1=nb,
                         op=ALU.add)
                self.act(wst, arg, AF.Exp, ["arg"], ["wst"])
                self.act(cd_bc, ps[4][:, 32:48], AF.Exp, [("ps", 4)], ["cd_bc"])
                self.vec("dve", "tensor_copy", [("ps", 4)], ["acsT0"], out=acsT[:, 0, :], in_=ps[4][0:16, 128:256])
                self.vec("dve", "tensor_copy", ["acsT0"], ["acsHL0"], out=acsHL[:, 0, :], in_=acsT[:, 0, :])
                self.vec("dve", "tensor_copy", ["acsHL0"], ["acsT1"], out=acsT[:, 1, :], in_=acsHL[:, 0, :])
                self.vec("dve", "tensor_tensor", ["acsT0", "acsT1"], ["acsHL1"], out=acsHL[:, 1, :], in0=acsT[:, 0, :],
                         in1=acsT[:, 1, :], op=ALU.subtract)
                pv5 = ps[5][:].bitcast(BF16)
                for cc in range(8):
                    self.tr(pv5[:, cc * 128:(cc + 1) * 128], xbcT[:, cc, tok], ident[:], [("xbcT", cc), "ident"],
                            [("ps", 5)])
                self.act(xs_tok[:], pv5[:, 0:1024], AF.Copy, [("ps", 5)], ["xs_tok"])
                self.vec("dve", "tensor_tensor", [("ps", 5), "wst"], ["xw"],
                         out=xw[:].rearrange("p (h q) -> p h q", h=16), in0=pv5[:, 0:1024].rearrange("p (h q) -> p h q", h=16),
                         in1=wst.unsqueeze(2).to_broadcast([128, 16, 64]), op=ALU.mult)
                self.vec("dve", "tensor_tensor", ["xs_tok", "rowp"], ["xsD"],
                         out=xsD[:].rearrange("p (h q) -> p h q", h=16), in0=xs_tok[:].rearrange("p (h q) -> p h q", h=16),
                         in1=D_bc.unsqueeze(2).to_broadcast([128, 16, 64]), op=ALU.mult)
                pv7 = ps[7][:].bitcast(BF16)
                for g in range(2):
                    self.tr(pv7[:, g * 128:(g + 1) * 128], xbcT[:, 8 + g, tok], ident[:], [("xbcT", 8 + g), "ident"],
                            [("ps", 7)])
                self.act(Btok[:].rearrange("p a b -> p (a b)"), pv7[:, 0:256], AF.Copy, [("ps", 7)], ["Btok"])
                for g in range(2):
                    self.mm(ps[4][:, 256 + g * 128:384 + g * 128], xbcT[:, 8 + g, tok], xbcT[:, 10 + g, tok], True, True,
                            [("xbcT", 8 + g), ("xbcT", 10 + g)], [("ps", 4)])
                self.vec("dve", "tensor_copy", [("ps", 4)], ["cb"], out=cb[:].rearrange("p a b -> p (a b)"),
                         in_=ps[4][:, 256:512])
                def e_phase(q4):
                    ebk = 6 + (q4 % 2)
                    eb = Eexp[q4 % 2]
                    mb = Mt[q4 % 2]
                    g = q4 // 2
                    self.mm(ps[ebk][:], ident[:], negm[:], True, False, ["ident", "negm"], [("ps", ebk)])
                    for hh in range(4):
                        h = q4 * 4 + hh
                        for part in range(2):
                            self.mm(ps[ebk][:, hh * 128:(hh + 1) * 128], sel[:, h, :], acsHL[:, part, :], False,
                                    hh == 3 and part == 1, ["sel", "acsHL0", "acsHL1"], [("ps", ebk)])
                    for hh in range(4):
                        h = q4 * 4 + hh
                        self.act(eb[:, hh * 128:(hh + 1) * 128], ps[ebk][:, hh * 128:(hh + 1) * 128], AF.Exp,
                                 [("ps", ebk), "nb"], [("Eexp", q4 % 2)], bias=nb[:, h:h + 1])
                    self.vec("dve", "tensor_tensor", [("Eexp", q4 % 2), "cb"], [("Mt", q4 % 2)],
                             out=mb[:].rearrange("p (a b) -> p a b", a=4), in0=eb[:].rearrange("p (a b) -> p a b", a=4),
                             in1=cb[:, g:g + 1, :].to_broadcast([128, 4, 128]), op=ALU.mult)

                def y_phase(q4):
                    g, qq = q4 // 2, q4 % 2
                    ydb = ps[g]
                    mb = Mt[q4 % 2]
                    if qq == 0:
                        self.mm(ydb[:], ident[:], xsD[:, g * 512:(g + 1) * 512], True, False, ["ident", "xsD"], [("ps", g)])
                    for hh in range(4):
                        h = q4 * 4 + hh
                        hl = h - g * 8
                        self.mm(ydb[:, hl * 64:(hl + 1) * 64], mb[:, hh * 128:(hh + 1) * 128],
                                xs_tok[:, h * 64:(h + 1) * 64], False, qq == 1 and hh == 3,
                                [("Mt", q4 % 2), "xs_tok"], [("ps", g)])

                e_phase(0)
                for q4 in range(4):
                    if q4 + 1 < 4:
                        e_phase(q4 + 1)
                    y_phase(q4)
                for g in range(2):
                    gs = slice(g * 512, (g + 1) * 512)
                    self.mm(ps[2 + g][:], xbcT[:, 10 + g, tok], Hb[:, gs], True, True, [("xbcT", 10 + g), "Hb"],
                            [("ps", 2 + g)])
                    self.vec("dve", "tensor_tensor", [("ps", 2 + g), "eacs"], ["yA"],
                             out=yA[:, gs].rearrange("p (h q) -> p h q", h=8),
                             in0=ps[2 + g][:].rearrange("p (h q) -> p h q", h=8),
                             in1=eacs[:, g * 8:(g + 1) * 8].unsqueeze(2).to_broadcast([128, 8, 64]), op=ALU.mult)
                    self.vec("dve", "tensor_tensor", [("ps", g), "yA"], ["yA"], out=yA[:, gs], in0=ps[g][:], in1=yA[:, gs],
                             op=ALU.add)
                    self.mm(ps[7][:], Btok[:, g, :], xw[:, gs], True, True, ["Btok", "xw"], [("ps", 7)])
                    self.vec("dve", "tensor_tensor", ["H", "cd_bc"], ["H"], out=H[:, gs].rearrange("p (h q) -> p h q", h=8),
                             in0=H[:, gs].rearrange("p (h q) -> p h q", h=8),
                             in1=cd_bc[:, g * 8:(g + 1) * 8].unsqueeze(2).to_broadcast([128, 8, 64]), op=ALU.mult)
                    self.vec("dve", "tensor_tensor", [("ps", 7), "H"], ["H"], out=H[:, gs], in0=ps[7][:], in1=H[:, gs],
                             op=ALU.add)
                    self.act(Hb[:, gs], H[:, gs], AF.Copy, ["H"], ["Hb"])
                self.vec("dve", "tensor_tensor", ["yA", "zs"], ["yA"], out=yA[:], in0=yA[:], in1=zs[:], op=ALU.mult)
                for g in range(2):
                    gs = slice(g * 512, (g + 1) * 512)
                    self.act(junk[:, gs], yA[:, gs], AF.Square, ["yA"], ["junk", "ssg"], accum_out=ssg[:, g:g + 1])
                self.vec("dve", "tensor_scalar", ["ssg"], ["ssg"], out=ssg[:, 2:4], in0=ssg[:, 0:2], scalar1=1.0 / 512,
                         scalar2=EPS, op0=ALU.mult, op1=ALU.add)
                self.act(ssg[:, 2:4], ssg[:, 2:4], AF.Sqrt, ["ssg"], ["ssg"])
                self.vec("dve", "reciprocal", ["ssg"], ["ssg"], out=ssg[:, 2:4], in_=ssg[:, 2:4])
                for g in range(2):
                    gs = slice(g * 512, (g + 1) * 512)
                    self.vec("dve", "tensor_scalar", ["yA", "ssg"], ["yn"], out=yn[:, gs], in0=yA[:, gs],
                             scalar1=ssg[:, 2 + g:3 + g], scalar2=None, op0=ALU.mult)
                for cc in range(8):
                    self.tr(pv5[:, cc * 128:(cc + 1) * 128], yn[:, cc * 128:(cc + 1) * 128], ident[:], ["yn", "ident"],
                            [("ps", 5)])
                for cc in range(8):
                    self.act(yT[:, cc, tok], pv5[:, cc * 128:(cc + 1) * 128], AF.Copy, [("ps", 5), "abP"], [("yT", jj)],
                             scale=abP[:, 64 + cc:65 + cc])
            for jj in range(4):
                j = t * 4 + jj
                tok = slice(jj * 128, (jj + 1) * 128)
                for n in range(2):
                    pb = ps[6 + n]
                    for k in range(12):
                        lhs = ypT[:, k, tok] if k < 4 else yT[:, k - 4, tok]
                        rr = [("ypT", k)] if k < 4 else [("yT", jj)]
                        self.mm(pb[:], lhs, w_out[:, k, n * 512:(n + 1) * 512], k == 0, k == 11, rr + ["abwout"],
                                [("ps", 6 + n)])
                    xs_ = xb[:, jj, n * 512:(n + 1) * 512]
                    self.vec("dve", "tensor_tensor", [("ps", 6 + n), "G"], [("tmpo", n)], out=tmp[n][:], in0=pb[:],
                             in1=G[:, n * 512:(n + 1) * 512], op=ALU.mult)
                    self.vec("dve", "tensor_tensor", [("tmpo", n), ("xt", 0, jj)], [("xt", 0, jj)], out=xs_,
                             in0=tmp[n][:], in1=xs_, op=ALU.add)
                self.dma("sp", self.xs_d[j * 128:(j + 1) * 128, :], xb[:, jj, :], [("xt", 0, jj)], [("xd", j)],
                         semkey=("xts", t % 2))
        self.sb_off = keep


    def mixer_mla(self):
        import math
        S = self.S
        ins = self.ins
        l, s = 1, 1
        NT, NB, seq = self.NT, self.NB, self.seq
        keep = self.sb_off
        self.sb_off = self.x_off
        cst, ident, ps = self.cst, self.ident, self.ps
        QN_d = self.dram("QN_d", [1024, seq], BF16)
        QR_d = self.dram("QR_d", [512, seq], BF16)
        KN_d = self.dram("KN_d", [1024, seq], BF16)
        KR_d = self.dram("KR_d", [32, seq], BF16)
        V_d = self.dram("V_d", [seq, 1024], BF16)
        OT_d = self.dram("OT_d", [1024, seq], BF16)
        w_in = self.sb("mwin", [128, KC, IN_MLA], BF16)
        w_uq = self.sb("mwuq", [128, 6, 1536], BF16)
        w_ukv = self.sb("mwukv", [128, 2, 2048], BF16)
        w_o = self.sb("mwo", [128, KC, D], BF16)
        mlaP = self.sb("mlaP", [128, 16], F32)
        G = self.sb("G", [128, D], F32)
        tri = self.sb("tri", [128, 128], BF16)
        xt = self.sb("xt", [128, 4, D], F32)
        self.dma("pool", w_in[:], ins["mlain"].rearrange("(k p) c -> p k c", p=128), [], ["mwin"])
        self.dma("pool", w_uq[:], ins["wuq"].rearrange("(k p) c -> p k c", p=128), [], ["mwuq"])
        self.dma("pool", w_ukv[:], ins["wukv"].rearrange("(k p) c -> p k c", p=128), [], ["mwukv"])
        self.dma("pool", w_o[:], ins["wo"].rearrange("(k p) c -> p k c", p=128), [], ["mwo"])
        self.dma("sp", mlaP[:], ins["mlaP"], [], ["mlaP"])
        self.load_gate(G, l, s)
        self.vec("dve", "tensor_copy", ["cst"], ["tri"], out=tri[:], in_=cst[:, 128:256])
        mA = self.mark()
        hT = self.sb("hT", [128, KC, 512], BF16)
        xn = self.sb("xn", [128, 4, D], BF16)
        junk = self.sb("junk", [128, D], BF16)
        qnT = self.sb("qnT", [128, 6, 512], BF16)
        kvnT = self.sb("kvnT", [128, 2, 512], BF16)
        posi = self.sb("posi", [128, 512], I32)
        ang = self.sb("ang", [128, 512], F32)
        rr = [self.sb("rr%d" % i, [128, 512], F32) for i in range(2)]
        cosT = self.sb("cosT", [128, 512], F32)
        sinT = self.sb("sinT", [128, 512], F32)
        x1s = self.sb("x1s", [128, 512], F32)
        x2s = self.sb("x2s", [128, 512], F32)
        t1 = self.sb("t1", [128, 512], F32)
        t2 = self.sb("t2", [128, 512], F32)
        ro = [self.sb("ro%d" % i, [128, 512], BF16) for i in range(2)]
        qsb = [self.sb("qsb%d" % i, [128, 512], BF16) for i in range(2)]
        vsb = [self.sb("vsb%d" % i, [128, D], BF16) for i in range(2)]
        st = self.sb("mst", [128, 16], F32)
        PI = math.pi
        pib = self.sb("pib", [128, 1], F32)
        self.vec("dve", "memset", [], ["pib"], pib[:], PI)

        def rope(x1ps, x2ps, P, res1, res2, o1, o2, o1res, o2res):
            self.act(x1s[0:P, :], x1ps, AF.Copy, [res1], ["x1s"])
            self.act(x2s[0:P, :], x2ps, AF.Copy, [res2], ["x2s"])
            self.vec("dve", "tensor_tensor", ["x1s", "cosT"], ["t1"], out=t1[0:P, :], in0=x1s[0:P, :], in1=cosT[0:P, :], op=ALU.mult)
            self.vec("dve", "tensor_tensor", ["x2s", "sinT"], ["t2"], out=t2[0:P, :], in0=x2s[0:P, :], in1=sinT[0:P, :], op=ALU.mult)
            self.vec("dve", "tensor_tensor", ["t1", "t2"], [o1res], out=o1, in0=t1[0:P, :], in1=t2[0:P, :], op=ALU.subtract)
            self.vec("dve", "tensor_tensor", ["x2s", "cosT"], ["t1"], out=t1[0:P, :], in0=x2s[0:P, :], in1=cosT[0:P, :], op=ALU.mult)
            self.vec("dve", "tensor_tensor", ["x1s", "sinT"], ["t2"], out=t2[0:P, :], in0=x1s[0:P, :], in1=sinT[0:P, :], op=ALU.mult)
            self.vec("dve", "tensor_tensor", ["t1", "t2"], [o2res], out=o2, in0=t1[0:P, :], in1=t2[0:P, :], op=ALU.add)

        for t in range(NT):
            cols = slice(t * 512, (t + 1) * 512)
            for jj in range(4):
                j = t * 4 + jj
                self.dma("sp", xt[:, jj, :], self.xs_d[j * 128:(j + 1) * 128, :], [("xd", j)], [("xt", jj)], semkey=("xtl", 0))
            self.norm_mod_T(t, l, s, hT, "hT", xn, junk, xsrc=lambda jj: xt[:, jj, :], xr=lambda jj: ("xt", jj))
            self.dma("sp", posi[:], ins["pos"][0:1, cols].partition_broadcast(128), [], ["posi"])
            self.vec("dve", "tensor_copy", ["posi"], ["ang"], out=ang[:], in_=posi[:])
            self.vec("dve", "tensor_scalar", ["ang", "mlaP"], ["ang"], out=ang[:], in0=ang[:], scalar1=mlaP[:, 8:9], scalar2=None,
                     op0=ALU.mult)
            C1 = 6.28125
            C2 = 2 * PI - C1
            for which, (dst, dres) in enumerate(((sinT, "sinT"), (cosT, "cosT"))):
                r = rr[which]
                rres = ("rr", which)
                src = ang
                if which == 1:
                    self.vec("dve", "tensor_scalar", ["ang"], ["t1"], out=t1[:], in0=ang[:], scalar1=PI / 2, scalar2=None, op0=ALU.add)
                    src = t1
                sres = "ang" if which == 0 else "t1"
                self.vec("dve", "tensor_scalar", [sres], ["t2"], out=t2[:], in0=src[:], scalar1=1.0 / (2 * PI), scalar2=None, op0=ALU.mult)
                self.vec("dve", "tensor_copy", ["t2"], ["posi"], out=posi[:], in_=t2[:])
                self.vec("dve", "tensor_copy", ["posi"], ["t2"], out=t2[:], in_=posi[:])
                self.vec("dve", "scalar_tensor_tensor", ["t2", sres], [rres], out=r[:], in0=t2[:], scalar=-C1, in1=src[:],
                         op0=ALU.mult, op1=ALU.add)
                self.vec("dve", "scalar_tensor_tensor", ["t2", rres], [rres], out=r[:], in0=t2[:], scalar=-C2, in1=r[:],
                         op0=ALU.mult, op1=ALU.add)
                self.vec("dve", "tensor_scalar", [rres], ["x1s"], out=x1s[:], in0=r[:], scalar1=PI, scalar2=-2 * PI, op0=ALU.is_gt,
                         op1=ALU.mult)
                self.vec("dve", "tensor_scalar", [rres], ["x2s"], out=x2s[:], in0=r[:], scalar1=-PI, scalar2=2 * PI, op0=ALU.is_lt,
                         op1=ALU.mult)
                self.vec("dve", "tensor_tensor", [rres, "x1s"], [rres], out=r[:], in0=r[:], in1=x1s[:], op=ALU.add)
                self.vec("dve", "tensor_tensor", [rres, "x2s"], [rres], out=r[:], in0=r[:], in1=x2s[:], op=ALU.add)
                self.act(dst[:], r[:], AF.Sin, [rres], [dres])
            for jj in range(4):
                tok = slice(jj * 128, (jj + 1) * 128)
                for n in range(2):
                    for k in range(KC):
                        self.mm(ps[n][:], hT[:, k, tok], w_in[:, k, n * 512:(n + 1) * 512], k == 0, k == KC - 1,
                                ["hT", "mwin"], [("ps", n)])
                self.act(junk[:, 0:512], ps[0][:], AF.Square, [("ps", 0)], ["junk", "mst"], accum_out=st[:, 0:1])
                self.act(junk[:, 512:768], ps[1][:, 0:256], AF.Square, [("ps", 1)], ["junk", "mst"], accum_out=st[:, 1:2])
                self.act(junk[:, 768:1024], ps[1][:, 256:512], AF.Square, [("ps", 1)], ["junk", "mst"], accum_out=st[:, 2:3])
                self.vec("dve", "tensor_tensor", ["mst"], ["mst"], out=st[:, 3:4], in0=st[:, 0:1], in1=st[:, 1:2], op=ALU.add)
                self.vec("dve", "tensor_scalar", ["mst"], ["mst"], out=st[:, 4:5], in0=st[:, 3:4], scalar1=1.0 / 768, scalar2=EPS,
                         op0=ALU.mult, op1=ALU.add)
                self.vec("dve", "tensor_scalar", ["mst"], ["mst"], out=st[:, 5:6], in0=st[:, 2:3], scalar1=1.0 / 256, scalar2=EPS,
                         op0=ALU.mult, op1=ALU.add)
                self.act(st[:, 4:6], st[:, 4:6], AF.Sqrt, ["mst"], ["mst"])
                self.vec("dve", "reciprocal", ["mst"], ["mst"], out=st[:, 4:6], in_=st[:, 4:6])
                self.vec("dve", "tensor_scalar", [("ps", 0), "mst"], [("xn", jj)], out=xn[:, jj, 0:512], in0=ps[0][:],
                         scalar1=st[:, 4:5], scalar2=None, op0=ALU.mult)
                self.vec("dve", "tensor_scalar", [("ps", 1), "mst"], [("xn", jj)], out=xn[:, jj, 512:768], in0=ps[1][:, 0:256],
                         scalar1=st[:, 4:5], scalar2=None, op0=ALU.mult)
                self.vec("dve", "tensor_scalar", [("ps", 1), "mst"], [("xn", jj)], out=xn[:, jj, 768:1024], in0=ps[1][:, 256:512],
                         scalar1=st[:, 5:6], scalar2=None, op0=ALU.mult)
            for k in range(8):
                bank = 4 + (k % 4)
                pv = ps[bank][:].bitcast(BF16)
                for jj in range(4):
                    self.tr(pv[:, jj * 128:(jj + 1) * 128], xn[:, jj, k * 128:(k + 1) * 128], ident[:], [("xn", jj), "ident"],
                            [("ps", bank)])
                dst = qnT[:, k, :] if k < 6 else kvnT[:, k - 6, :]
                self.act(dst, pv[:, 0:512], AF.Copy, [("ps", bank), "mlaP"], ["qnT" if k < 6 else "kvnT"], scale=mlaP[:, k:k + 1])
            for half in range(2):
                for k in range(KC):
                    self.mm(ps[2 + half][0:16, :], w_in[:, k, 1024 + half * 16:1040 + half * 16], hT[:, k, :], k == 0, k == KC - 1,
                            ["mwin", "hT"], [("ps", 2 + half)])
            rope(ps[2][0:16, :], ps[3][0:16, :], 16, ("ps", 2), ("ps", 3), ro[0][0:16, :], ro[1][0:16, :], ("ro", 0), ("ro", 1))
            self.dma("sp", KR_d[0:16, cols], ro[0][0:16, :], [("ro", 0)], [("KR_d", t)], semkey=("scr", 0))
            self.dma("sp", KR_d[16:32, cols], ro[1][0:16, :], [("ro", 1)], [("KR_d", t)], semkey=("scr", 0))
            for c in range(8):
                bk = c % 2
                for k in range(6):
                    self.mm(ps[bk][:], w_uq[:, k, c * 128:(c + 1) * 128], qnT[:, k, :], k == 0, k == 5, ["mwuq", "qnT"], [("ps", bk)])
                self.act(qsb[bk][:], ps[bk][:], AF.Copy, [("ps", bk)], [("qsb", bk)])
                self.dma("sp", QN_d[c * 128:(c + 1) * 128, cols], qsb[bk][:], [("qsb", bk)], [("QN_d", t)], semkey=("scr", 1 + bk))
            for hg in range(2):
                for k in range(6):
                    self.mm(ps[2][:], w_uq[:, k, 1024 + hg * 128:1152 + hg * 128], qnT[:, k, :], k == 0, k == 5, ["mwuq", "qnT"],
                            [("ps", 2)])
                for k in range(6):
                    self.mm(ps[3][:], w_uq[:, k, 1280 + hg * 128:1408 + hg * 128], qnT[:, k, :], k == 0, k == 5, ["mwuq", "qnT"],
                            [("ps", 3)])
                rope(ps[2][:], ps[3][:], 128, ("ps", 2), ("ps", 3), ro[0][:], ro[1][:], ("ro", 0), ("ro", 1))
                self.dma("sp", QR_d[hg * 128:(hg + 1) * 128, cols], ro[0][:], [("ro", 0)], [("QR_d", t)], semkey=("scr", 0))
                self.dma("sp", QR_d[256 + hg * 128:256 + (hg + 1) * 128, cols], ro[1][:], [("ro", 1)], [("QR_d", t)], semkey=("scr", 0))
            for c in range(8):
                bk = c % 2
                for k in range(2):
                    self.mm(ps[bk][:], w_ukv[:, k, c * 128:(c + 1) * 128], kvnT[:, k, :], k == 0, k == 1, ["mwukv", "kvnT"], [("ps", bk)])
                self.vec("dve", "tensor_copy", [("ps", bk)], [("qsb", bk)], out=qsb[bk][:], in_=ps[bk][:])
                self.dma("sp", KN_d[c * 128:(c + 1) * 128, cols], qsb[bk][:], [("qsb", bk)], [("KN_d", t)], semkey=("scr", 1 + bk))
            for jj in range(4):
                j = t * 4 + jj
                tok = slice(jj * 128, (jj + 1) * 128)
                vb = vsb[jj % 2]
                for n in range(2):
                    for k in range(2):
                        self.mm(ps[2 + n][:], kvnT[:, k, tok], w_ukv[:, k, 1024 + n * 512:1536 + n * 512], k == 0, k == 1,
                                ["kvnT", "mwukv"], [("ps", 2 + n)])
                    if n == 0:
                        self.act(vb[:, 0:512], ps[2][:], AF.Copy, [("ps", 2)], [("vsb", jj % 2)])
                    else:
                        self.vec("dve", "tensor_copy", [("ps", 3)], [("vsb", jj % 2)], out=vb[:, 512:1024], in_=ps[3][:])
                self.dma("sp", V_d[j * 128:(j + 1) * 128, :], vb[:], [("vsb", jj % 2)], [("V_d", j)], semkey=("scr", 3 + jj % 2))
        S.barrier()
        self.release(mA)
        KhT = [self.sb("KhT%d" % i, [96, seq], BF16) for i in range(2)]
        QhT = [self.sb("QhT%d" % i, [96, seq], BF16) for i in range(2)]
        V1 = [self.sb("V1_%d" % i, [128, NB, 65], BF16) for i in range(2)]
        PT = [self.sb("PT%d" % i, [128, 512], BF16) for i in range(4)]
        rd = self.sb("rd", [128, 512], F32)
        bcs = self.sb("bcs", [64, 512], F32)
        osb = [self.sb("osb%d" % i, [64, 512], BF16) for i in range(2)]
        for i in range(2):
            self.vec("dve", "memset", [], [("V1", i)], V1[i][:, :, 64:65], 1.0)
        scale = 1.0 / math.sqrt(96.0)
        blocks = []
        for h in range(16):
            for qt in range(NT):
                nkb = 4 * (qt + 1)
                for kb in range(nkb):
                    blocks.append((h, qt, kb, nkb))
        loaded_heads = set()

        def load_head(h):
            if h in loaded_heads or h >= 16:
                return
            loaded_heads.add(h)
            hb = h % 2
            self.dma("sp", KhT[hb][0:64, :], KN_d[h * 64:(h + 1) * 64, :], [], [("KhT", hb, 0)], semkey=("ld", hb))
            self.dma("sp", KhT[hb][64:96, :], KR_d[0:32, :], [], [("KhT", hb, 1)], semkey=("ld", hb))
            self.dma("sp", QhT[hb][0:64, :], QN_d[h * 64:(h + 1) * 64, :], [], [("QhT", hb, 0)], semkey=("ld", hb))
            self.dma("sp", QhT[hb][64:80, :], QR_d[h * 16:(h + 1) * 16, :], [], [("QhT", hb, 1)], semkey=("ld", hb))
            self.dma("sp", QhT[hb][80:96, :], QR_d[256 + h * 16:256 + (h + 1) * 16, :], [], [("QhT", hb, 2)], semkey=("ld", hb))
            self.dma("sp", V1[hb][:, :, 0:64], V_d[:, h * 64:(h + 1) * 64].rearrange("(j p) c -> p j c", p=128), [],
                     [("V1", hb)], semkey=("ld", hb))

        def geom(i):
            h, qt, kb, nkb = blocks[i]
            jd = kb - 4 * qt
            c0 = 128 * jd if jd > 0 else 0
            return h, qt, kb, nkb, jd, c0, i % 4

        def stage1(i):
            h, qt, kb, nkb, jd, c0, sbk = geom(i)
            hb = h % 2
            kres = [("KhT", hb, 0), ("KhT", hb, 1)]
            qres = [("QhT", hb, 0), ("QhT", hb, 1), ("QhT", hb, 2)]
            pt = PT[sbk]
            self.mm(ps[sbk][:, c0:512], KhT[hb][:, kb * 128:(kb + 1) * 128], QhT[hb][:, qt * 512 + c0:(qt + 1) * 512],
                    True, True, kres + qres, [("ps", sbk)])
            self.act(pt[:, c0:512], ps[sbk][:, c0:512], AF.Exp, [("ps", sbk)], [("PT", sbk)], scale=scale)
            if jd >= 0:
                self.vec("dve", "tensor_tensor", [("PT", sbk), "tri"], [("PT", sbk)], out=pt[:, c0:c0 + 128],
                         in0=pt[:, c0:c0 + 128], in1=tri[:], op=ALU.mult)

        def stage2(i):
            h, qt, kb, nkb, jd, c0, sbk = geom(i)
            hb = h % 2
            ob = 4 + (qt % 2)
            pt = PT[sbk]
            self.mm(ps[ob][0:65, c0:512], V1[hb][:, kb, :], pt[:, c0:512], kb == 0, kb == nkb - 1,
                    [("V1", hb), ("PT", sbk)], [("ps", ob)])
            if kb == nkb - 1:
                self.vec("dve", "reciprocal", [("ps", ob)], ["rd"], out=rd[64:65, :], in_=ps[ob][64:65, :])
                self.mm(ps[6][0:64, :], cst[64:65, 256:320], rd[64:65, :], True, True, ["cst", "rd"], [("ps", 6)])
                self.act(bcs[:], ps[6][0:64, :], AF.Copy, [("ps", 6)], ["bcs"])
                o_ = osb[qt % 2]
                self.vec("dve", "tensor_tensor", [("ps", ob), "bcs"], [("osb", qt % 2)], out=o_[:], in0=ps[ob][0:64, :], in1=bcs[:],
                         op=ALU.mult)
                self.dma("sp", OT_d[h * 64:(h + 1) * 64, qt * 512:(qt + 1) * 512], o_[:], [("osb", qt % 2)], [("OT_d", qt)],
                         semkey=("ost", qt % 2))

        load_head(0)
        load_head(1)
        stage1(0)
        stage1(1)
        for i in range(len(blocks)):
            if i + 2 < len(blocks):
                stage1(i + 2)
            stage2(i)
            if i + 1 == len(blocks) or blocks[i + 1][0] != blocks[i][0]:
                load_head(blocks[i][0] + 2)
        S.barrier()
        self.release(mA)
        oT = [self.sb("oT%d" % i, [128, KC, 512], BF16) for i in range(2)]
        tmp = [self.sb("tmpm%d" % i, [128, 512], F32) for i in range(2)]
        for t in range(NT):
            cols = slice(t * 512, (t + 1) * 512)
            ot = oT[t % 2]
            self.dma("sp", ot[:], OT_d[:, cols].rearrange("(k p) t -> p k t", p=128), [], [("oT", t % 2)], semkey=("otl", t % 2))
            for jj in range(4):
                j = t * 4 + jj
                self.dma("sp", xt[:, jj, :], self.xs_d[j * 128:(j + 1) * 128, :], [("xd", j)], [("xt", jj)], semkey=("xtl", 0))
            for jj in range(4):
                j = t * 4 + jj
                tok = slice(jj * 128, (jj + 1) * 128)
                for n in range(2):
                    pb = ps[n]
                    for k in range(KC):
                        self.mm(pb[:], ot[:, k, tok], w_o[:, k, n * 512:(n + 1) * 512], k == 0, k == KC - 1, [("oT", t % 2), "mwo"],
                                [("ps", n)])
                    xs_ = xt[:, jj, n * 512:(n + 1) * 512]
                    self.vec("dve", "tensor_tensor", [("ps", n), "G"], [("tmpm", n)], out=tmp[n][:], in0=pb[:],
                             in1=G[:, n * 512:(n + 1) * 512], op=ALU.mult)
                    self.vec("dve", "tensor_tensor", [("tmpm", n), ("xt", jj)], [("xt", jj)], out=xs_, in0=tmp[n][:], in1=xs_,
                             op=ALU.add)
                self.dma("sp", self.xs_d[j * 128:(j + 1) * 128, :], xt[:, jj, :], [("xt", jj)], [("xd", j)], semkey=("xts", 0))
        self.sb_off = keep


    def final_norm(self, fin_in):
        m = self.mark()
        Gf = self.sb("Gf", [128, D], F32)
        junk = self.sb("junkf", [128, D], BF16)
        ob = [self.sb("ob%d" % i, [128, D], F32) for i in range(2)]
        self.dma("sp", Gf[:], fin_in.partition_broadcast(128), [], ["Gf"])
        for j in range(self.NB):
            xt = self.xres[:, j, :]
            ss = self.stat[:, 16 + j % 2:17 + j % 2]
            rs = self.stat[:, 18 + j % 2:19 + j % 2]
            o = ob[j % 2]
            self.act(junk[:], xt, AF.Square, [("x", j)], ["junkf", ("fss", j % 2)], accum_out=ss)
            self.vec("dve", "tensor_scalar", [("fss", j % 2)], [("frs", j % 2)], out=rs, in0=ss, scalar1=1.0 / D,
                     scalar2=EPS, op0=ALU.mult, op1=ALU.add)
            self.act(rs, rs, AF.Sqrt, [("frs", j % 2)], [("frs", j % 2)])
            self.vec("dve", "reciprocal", [("frs", j % 2)], [("frs", j % 2)], out=rs, in_=rs)
            self.vec("dve", "scalar_tensor_tensor", [("x", j), ("frs", j % 2), "Gf"], [("ob", j % 2)],
                     out=o[:], in0=xt, scalar=rs, in1=Gf[:], op0=ALU.mult, op1=ALU.mult)
            self.dma("sp", self.y_out[j * 128:(j + 1) * 128, :], o[:], [("ob", j % 2)], [("yout", j)],
                     semkey=("yout", j % 2))
        self.release(m)


def make_consts():
    c = np.zeros((128, 512), np.float32)
    c[:, 0:128] = np.eye(128, dtype=np.float32)
    ii = np.arange(128)
    c[:, 128:256] = (ii[:, None] <= ii[None, :]).astype(np.float32)
    c[:, 256:384] = 1.0
    for g, w in enumerate((2, 4, 8, 16)):
        c[:, 384 + g * 16: 384 + (g + 1) * 16] = 1.0 / np.minimum(np.arange(16) + 1.0, float(w))
    return c


def host_inputs(b, seq, x, c, positions, mod_w, mod_b, norm_g, ffn_w13, ffn_w2, ab_w_in, pool_w, pool_scale,
                ssd_conv_w, ssd_conv_b, ssd_dt_bias, ssd_a_log, ssd_d, ssd_norm_g, ab_w_out, mla_w_in,
                mla_q_norm_g, mla_w_uq, mla_kv_norm_g, mla_w_ukv, mla_w_o, final_norm_g):
    f = np.float32
    m = {}
    m["x"] = np.ascontiguousarray(x[b], dtype=f)
    m["c"] = np.ascontiguousarray(c[b].reshape(KC, 128).T, dtype=f)
    m["pos"] = np.ascontiguousarray(positions[b].reshape(1, seq), dtype=np.int32)
    m["mod_w"] = np.ascontiguousarray(mod_w, dtype=f)
    m["mod_b"] = np.ascontiguousarray(mod_b, dtype=f)
    m["ngP"] = np.ascontiguousarray(norm_g.reshape(2, 3, KC, 128).transpose(3, 0, 1, 2).reshape(128, 48), dtype=f)
    m["ffn_w13"] = np.ascontiguousarray(ffn_w13.reshape(4, D, 2 * DFF), dtype=f)
    m["ffn_w2"] = np.ascontiguousarray(ffn_w2.reshape(4, DFF, D), dtype=f)
    m["ab_w_in"] = np.ascontiguousarray(ab_w_in[0], dtype=f)
    m["pool_w"] = np.ascontiguousarray(pool_w[0].reshape(512, 128), dtype=f)
    abP = np.zeros((128, 80), f)
    abP[:, 0:4] = pool_scale[0].reshape(4, 128).T
    abP[:, 4:52] = ssd_conv_w[0].reshape(4, 12, 128).transpose(2, 1, 0).reshape(128, 48)
    abP[:, 52:64] = ssd_conv_b[0].reshape(12, 128).T
    abP[:, 64:72] = ssd_norm_g[0].reshape(8, 128).T
    m["abP"] = abP
    abR = np.zeros((1, 64), f)
    abR[0, 0:16] = ssd_dt_bias[0]
    abR[0, 16:32] = ssd_a_log[0]
    abR[0, 32:48] = ssd_d[0]
    m["abR"] = abR
    sel = np.zeros((16, 16, 128), f)
    for h in range(16):
        sel[h, h, :] = 1.0
    m["sel"] = sel.reshape(16, 2048)
    ii = np.arange(128)
    negm = np.where(ii[None, :] < ii[:, None], -30000.0, 0.0).astype(f)
    m["cmask"] = np.ascontiguousarray(np.tile(negm, (1, 4)))
    m["ab_w_out"] = np.ascontiguousarray(ab_w_out[0], dtype=f)
    m["mla_w_in"] = np.ascontiguousarray(mla_w_in[0], dtype=f)
    wq = mla_w_uq[0].reshape(768, 16, 96)
    m["mla_w_uq"] = np.ascontiguousarray(np.concatenate(
        [wq[:, :, 0:64].reshape(768, 1024), wq[:, :, 64:80].reshape(768, 256), wq[:, :, 80:96].reshape(768, 256)], axis=1), dtype=f)
    wkv = mla_w_ukv[0].reshape(256, 16, 128)
    m["mla_w_ukv"] = np.ascontiguousarray(np.concatenate(
        [wkv[:, :, 0:64].reshape(256, 1024), wkv[:, :, 64:128].reshape(256, 1024)], axis=1), dtype=f)
    m["mla_w_o"] = np.ascontiguousarray(mla_w_o[0], dtype=f)
    mp = np.zeros((128, 16), f)
    mp[:, 0:6] = mla_q_norm_g[0].reshape(6, 128).T
    mp[:, 6:8] = mla_kv_norm_g[0].reshape(2, 128).T
    mp[:, 8] = np.tile(INV_FREQ, 8)
    m["mlaP"] = mp
    m["final_g"] = np.ascontiguousarray(final_norm_g.reshape(1, D), dtype=f)
    m["consts"] = make_consts()
    return m


_NC_CACHE = {}


def kernel(**inputs):
    inputs = {k: np.asarray(v) for k, v in inputs.items()}
    B, seq, _ = inputs["x"].shape
    key = (seq,)
    if key not in _NC_CACHE:
        _NC_CACHE[key] = Builder(seq).build()
    nc = _NC_CACHE[key]
    in_maps = [host_inputs(b, seq, **inputs) for b in range(B)]
    res = run_bass_kernel_spmd(nc, in_maps, core_ids=list(range(B)))
    return np.stack([np.asarray(r["y"], dtype=np.float32) for r in res.results], axis=0)
```
